# Optimizing a Trainium2 kernel written in Bass

```python
import jax, jax.numpy as jnp
from jax import lax
import numpy as np

D_MODEL = 1024
BATCH = 1
SEQ = 16384
DEPTH = 2

N_A_LAYERS = DEPTH // 2
N_B_LAYERS = DEPTH - N_A_LAYERS
MLA_HEADS = 8
QK_NOPE = 128
QK_ROPE = 64
V_HEAD = 128
Q_LORA = 512
KV_LORA = 256
ROPE_THETA = 10000.0
DIFF_HEADS = 8
DIFF_HEAD = 64
DIFF_V = 2 * DIFF_HEAD
LAMBDA_STD = 0.1
D_FF = ((8 * D_MODEL // 3 + 255) // 256) * 256
EPS = 1e-6
Q_BLOCK = 128
NEG_INF = -1e30

kernel_name = "yoco_mla_diffattn_alibi_swiglu"


def _rmsnorm(x, g):
    xf = x.astype(jnp.float32)
    r = lax.rsqrt(jnp.mean(xf * xf, axis=-1, keepdims=True) + EPS)
    return (xf * r).astype(x.dtype) * g


def _rope(x, pos):
    half = QK_ROPE // 2
    inv = ROPE_THETA ** (-jnp.arange(half, dtype=jnp.float32) * 2.0 / QK_ROPE)
    ang = pos.astype(jnp.float32)[..., None] * inv
    cos = jnp.cos(ang)[:, :, None, :]
    sin = jnp.sin(ang)[:, :, None, :]
    x1 = x[..., :half].astype(jnp.float32)
    x2 = x[..., half:].astype(jnp.float32)
    return jnp.concatenate([x1 * cos - x2 * sin, x2 * cos + x1 * sin], axis=-1).astype(x.dtype)


def _alibi_slopes(n_heads):
    return 2.0 ** (-8.0 * jnp.arange(1, n_heads + 1, dtype=jnp.float32) / n_heads)


def _sweep(block_fn, *q_arrays):
    b, s = q_arrays[0].shape[:2]
    nb = s // Q_BLOCK

    def to_blocks(a):
        return jnp.moveaxis(a.reshape((b, nb, Q_BLOCK) + a.shape[2:]), 1, 0)

    starts = jnp.arange(nb, dtype=jnp.int32) * Q_BLOCK
    out = lax.map(lambda args: block_fn(*args), (starts,) + tuple(to_blocks(a) for a in q_arrays))
    return jnp.moveaxis(out, 0, 1).reshape((b, s) + out.shape[3:])


def _mla(h, pos, w_dq, q_norm, w_uq, w_dkv, kv_norm, w_ukv, w_o):
    b, s, _ = h.shape
    cq = _rmsnorm(h @ w_dq, q_norm)
    q = (cq @ w_uq).reshape(b, s, MLA_HEADS, QK_NOPE + QK_ROPE)
    q = jnp.concatenate([q[..., :QK_NOPE], _rope(q[..., QK_NOPE:], pos)], axis=-1)
    ckv = h @ w_dkv
    c_kv = _rmsnorm(ckv[..., :KV_LORA], kv_norm)
    k_pe = _rope(ckv[..., None, KV_LORA:], pos)
    kv = (c_kv @ w_ukv).reshape(b, s, MLA_HEADS, QK_NOPE + V_HEAD)
    k = jnp.concatenate([kv[..., :QK_NOPE],
                         jnp.broadcast_to(k_pe, (b, s, MLA_HEADS, QK_ROPE))], axis=-1)
    v = kv[..., QK_NOPE:]
    scale = (QK_NOPE + QK_ROPE) ** -0.5
    key_idx = jnp.arange(s, dtype=jnp.int32)

    def block(start, qb):
        sc = jnp.einsum('bqhd,bkhd->bhqk', qb, k).astype(jnp.float32) * scale
        mask = (start + jnp.arange(Q_BLOCK, dtype=jnp.int32))[:, None] >= key_idx[None, :]
        p = jax.nn.softmax(jnp.where(mask, sc, NEG_INF), axis=-1).astype(v.dtype)
        return jnp.einsum('bhqk,bkhd->bqhd', p, v)

    o = _sweep(block, q)
    return o.reshape(b, s, MLA_HEADS * V_HEAD) @ w_o


def _shared_kv(h, kv_norm, w_k, w_v):
    b, s, _ = h.shape
    hk = _rmsnorm(h, kv_norm)
    k = (hk @ w_k).reshape(b, s, DIFF_HEADS, 2, DIFF_HEAD)
    v = (hk @ w_v).reshape(b, s, DIFF_HEADS, DIFF_V)
    return k[..., 0, :], k[..., 1, :], v


def _diff_attn(h, pos, k1, k2, v, w_q, lq1, lk1, lq2, lk2, subln, w_o, lambda_init):
    b, s, _ = h.shape
    q = (h @ w_q).reshape(b, s, DIFF_HEADS, 2, DIFF_HEAD)
    q1, q2 = q[..., 0, :], q[..., 1, :]
    lam = (jnp.exp(jnp.sum(lq1.astype(jnp.float32) * lk1.astype(jnp.float32)))
           - jnp.exp(jnp.sum(lq2.astype(jnp.float32) * lk2.astype(jnp.float32)))
           + lambda_init)
    slopes = _alibi_slopes(DIFF_HEADS)
    scale = DIFF_HEAD ** -0.5
    key_idx = jnp.arange(s, dtype=jnp.int32)

    def block(start, q1b, q2b, posb):
        dist = (posb[:, :, None] - pos[:, None, :]).astype(jnp.float32)
        bias = -slopes[None, :, None, None] * dist[:, None]
        mask = (start + jnp.arange(Q_BLOCK, dtype=jnp.int32))[:, None] >= key_idx[None, :]

        def probs(qb, kk):
            sc = jnp.einsum('bqhd,bkhd->bhqk', qb, kk).astype(jnp.float32) * scale + bias
            return jax.nn.softmax(jnp.where(mask, sc, NEG_INF), axis=-1)

        a = probs(q1b, k1) - lam * probs(q2b, k2)
        return jnp.einsum('bhqk,bkhd->bqhd', a.astype(v.dtype), v)

    o = _sweep(block, q1, q2, pos)
    o = _rmsnorm(o, subln) * (1.0 - lambda_init)
    return o.reshape(b, s, DIFF_HEADS * DIFF_V) @ w_o


def _swiglu(h, w_gate_up, w_down):
    gu = h @ w_gate_up
    return (jax.nn.silu(gu[..., :D_FF]) * gu[..., D_FF:]) @ w_down


def setup_inputs(seed: int = 0) -> dict:
    key = jax.random.key(seed)
    ks = jax.random.split(key, 32)

    def w(k, shape, fan_in):
        return jax.random.normal(k, shape, jnp.float32) * (fan_in ** -0.5)

    def gain(k, shape):
        return 1.0 + 0.02 * jax.random.normal(k, shape, jnp.float32)

    nA, nB = N_A_LAYERS, N_B_LAYERS
    return {
        "x": jax.random.normal(ks[0], (BATCH, SEQ, D_MODEL), jnp.float32),
        "positions": jnp.broadcast_to(jnp.arange(SEQ, dtype=jnp.int32)[None, :], (BATCH, SEQ)),
        "attn_norm": gain(ks[1], (DEPTH, D_MODEL)),
        "ffn_norm": gain(ks[2], (DEPTH, D_MODEL)),
        "final_norm": gain(ks[3], (D_MODEL,)),
        "mla_w_dq": w(ks[4], (nA, D_MODEL, Q_LORA), D_MODEL),
        "mla_q_norm": gain(ks[5], (nA, Q_LORA)),
        "mla_w_uq": w(ks[6], (nA, Q_LORA, MLA_HEADS * (QK_NOPE + QK_ROPE)), Q_LORA),
        "mla_w_dkv": w(ks[7], (nA, D_MODEL, KV_LORA + QK_ROPE), D_MODEL),
        "mla_kv_norm": gain(ks[8], (nA, KV_LORA)),
        "mla_w_ukv": w(ks[9], (nA, KV_LORA, MLA_HEADS * (QK_NOPE + V_HEAD)), KV_LORA),
        "mla_w_o": w(ks[10], (nA, MLA_HEADS * V_HEAD, D_MODEL), MLA_HEADS * V_HEAD),
        "diff_kv_norm": gain(ks[11], (D_MODEL,)),
        "diff_w_k": w(ks[12], (D_MODEL, DIFF_HEADS * 2 * DIFF_HEAD), D_MODEL),
        "diff_w_v": w(ks[13], (D_MODEL, DIFF_HEADS * DIFF_V), D_MODEL),
        "diff_w_q": w(ks[14], (nB, D_MODEL, DIFF_HEADS * 2 * DIFF_HEAD), D_MODEL),
        "diff_lambda_q1": LAMBDA_STD * jax.random.normal(ks[15], (nB, DIFF_HEAD), jnp.float32),
        "diff_lambda_k1": LAMBDA_STD * jax.random.normal(ks[16], (nB, DIFF_HEAD), jnp.float32),
        "diff_lambda_q2": LAMBDA_STD * jax.random.normal(ks[17], (nB, DIFF_HEAD), jnp.float32),
        "diff_lambda_k2": LAMBDA_STD * jax.random.normal(ks[18], (nB, DIFF_HEAD), jnp.float32),
        "diff_subln": gain(ks[19], (nB, DIFF_V)),
        "diff_w_o": w(ks[20], (nB, DIFF_HEADS * DIFF_V, D_MODEL), DIFF_HEADS * DIFF_V),
        "ffn_w_gate_up": w(ks[21], (DEPTH, D_MODEL, 2 * D_FF), D_MODEL),
        "ffn_w_down": w(ks[22], (DEPTH, D_FF, D_MODEL), D_FF),
    }


def reference(x, positions, attn_norm, ffn_norm, final_norm,
              mla_w_dq, mla_q_norm, mla_w_uq, mla_w_dkv, mla_kv_norm, mla_w_ukv, mla_w_o,
              diff_kv_norm, diff_w_k, diff_w_v, diff_w_q,
              diff_lambda_q1, diff_lambda_k1, diff_lambda_q2, diff_lambda_k2,
              diff_subln, diff_w_o, ffn_w_gate_up, ffn_w_down):
    h = x
    shared = None
    for l in range(DEPTH):
        hn = _rmsnorm(h, attn_norm[l])
        if l < N_A_LAYERS:
            i = l
            h = h + _mla(hn, positions, mla_w_dq[i], mla_q_norm[i], mla_w_uq[i],
                         mla_w_dkv[i], mla_kv_norm[i], mla_w_ukv[i], mla_w_o[i])
        else:
            i = l - N_A_LAYERS
            if shared is None:
                shared = _shared_kv(h, diff_kv_norm, diff_w_k, diff_w_v)
                hn = _rmsnorm(h, attn_norm[l])
            k1, k2, v = shared
            lambda_init = 0.8 - 0.6 * float(np.exp(-0.3 * l))
            h = h + _diff_attn(hn, positions, k1, k2, v, diff_w_q[i],
                               diff_lambda_q1[i], diff_lambda_k1[i],
                               diff_lambda_q2[i], diff_lambda_k2[i],
                               diff_subln[i], diff_w_o[i], lambda_init)
        h = h + _swiglu(_rmsnorm(h, ffn_norm[l]), ffn_w_gate_up[l], ffn_w_down[l])
    return _rmsnorm(h, final_norm)
```

```python
from contextlib import ExitStack
import numpy as np
import ml_dtypes
import concourse.bass as bass
import concourse.mybir as mybir
from concourse.bass_utils import run_bass_kernel_spmd

F32 = mybir.dt.float32
BF16 = mybir.dt.bfloat16
I32 = mybir.dt.int32
AF = mybir.ActivationFunctionType
ALU = mybir.AluOpType

NCORES = 8
S = 16384
D = 1024
TPC = S // NCORES
NT = TPC // 128
H = 8
DFF = 2816
EPS = 1e-6
NEG = -30000.0
TWO_PI = 6.283185307179586
C1 = 6.28125
C2 = TWO_PI - C1
PI = 3.141592653589793
PI_SAFE = 3.1415925
SAME_ENGINE_SYNC = True


class Tok:
    __slots__ = ("w", "r", "sem", "cnt")

    def __init__(self):
        self.w = {}
        self.r = {}
        self.sem = None
        self.cnt = 0


class KB:
    def __init__(self):
        self.nc = bass.Bass("TRN2", target_bir_lowering=False)
        self.st = ExitStack()
        self.sems = []
        self.eng = {}
        nc = self.nc
        for n, h in (("pe", nc.tensor), ("act", nc.scalar), ("dve", nc.vector),
                     ("pool", nc.gpsimd), ("sp", nc.sync)):
            si = self.new_sem("e_" + n)
            self.eng[n] = {"h": h, "sem": si, "cnt": 0, "seen": {}}
        self.nbank = 0
        self.banks = []
        self.dma_toks = []
        self.scopes = []

    def tok(self):
        t = Tok()
        self.dma_toks.append(t)
        return t

    def push_scope(self):
        self.scopes.append(ExitStack())

    def pop_scope(self):
        self.scopes.pop().close()

    def new_sem(self, name):
        s = self.st.enter_context(self.nc.semaphore(name))
        self.sems.append(s)
        return len(self.sems) - 1

    def sb(self, name, shape, dt):
        st = self.scopes[-1] if self.scopes else self.st
        return st.enter_context(self.nc.sbuf_tensor(name, shape, dt))

    def ps(self, name, shape, dt):
        return self.st.enter_context(self.nc.psum_tensor(name, shape, dt))

    def dram(self, name, shape, dt, kind):
        return self.nc.dram_tensor(name, shape, dt, kind=kind).ap()

    def make_banks(self):
        for i in range(8):
            t = self.ps("bank%d" % i, [128, 512], F32)
            self.banks.append((t, Tok()))

    def bank(self):
        b = self.banks[self.nbank % 8]
        self.nbank += 1
        return b

    def _wait(self, e, deps, dma=False):
        E = self.eng[e]
        for sem, val in deps.items():
            if sem == E["sem"] and not dma and (e == "pe" or not SAME_ENGINE_SYNC):
                continue
            if E["seen"].get(sem, 0) >= val:
                continue
            E["h"].wait_ge(self.sems[sem], val)
            E["seen"][sem] = val

    def _deps(self, reads, writes, pwrites=(), skip_sem=None):
        deps = {}
        for t in reads:
            for s, v in t.w.items():
                if v > deps.get(s, 0):
                    deps[s] = v
        for t in writes:
            for d in (t.w, t.r):
                for s, v in d.items():
                    if v > deps.get(s, 0):
                        deps[s] = v
        for t in pwrites:
            for d in (t.w, t.r):
                for s, v in d.items():
                    if v > deps.get(s, 0):
                        deps[s] = v
        if skip_sem is not None:
            deps.pop(skip_sem, None)
        return deps

    def _post(self, ticket, reads, writes, pwrites=()):
        s, v = ticket
        for t in reads:
            t.r[s] = v
        for t in writes:
            t.w = {s: v}
            t.r = {}
        for t in pwrites:
            t.w[s] = v

    def op(self, e, fn, reads=(), writes=(), pwrites=()):
        E = self.eng[e]
        self._wait(e, self._deps(reads, writes, pwrites))
        inst = fn()
        E["cnt"] += 1
        inst.then_inc(self.sems[E["sem"]], 1)
        self._post((E["sem"], E["cnt"]), reads, writes, pwrites)

    def dma(self, q, out, in_, owner, reads=(), writes=(), pwrites=()):
        E = self.eng[q]
        if owner.sem is None:
            owner.sem = self.new_sem("d%d" % len(self.sems))
        self._wait(q, self._deps(reads, writes, pwrites, skip_sem=owner.sem), dma=True)
        owner.cnt += 16
        E["h"].dma_start(out=out, in_=in_).then_inc(self.sems[owner.sem], 16)
        self._post((owner.sem, owner.cnt), reads, writes, pwrites)

    def finish(self, toks):
        deps = {}
        for t in toks:
            for d in (t.w, t.r):
                for s, v in d.items():
                    if v > deps.get(s, 0):
                        deps[s] = v
        self._wait("sp", deps, dma=True)


def rms_scale(kb, ss_ap, r_ap, n, tok_ss, tok_r):
    nc = kb.nc
    kb.op("act", lambda: nc.scalar.activation(out=r_ap, in_=ss_ap, func=AF.Ln, scale=1.0 / n, bias=kb.eps_ap),
          reads=[tok_ss, kb.eps_tok], writes=[tok_r])
    kb.op("act", lambda: nc.scalar.activation(out=r_ap, in_=r_ap, func=AF.Exp, scale=-0.5),
          reads=[tok_r], writes=[tok_r])


def setup_consts(kb, ident_dram):
    nc = kb.nc
    kb.ident = kb.sb("ident_sb", [128, 128], BF16)
    kb.ident_tok = Tok()
    kb.dma("pool", kb.ident[:], ident_dram, owner=kb.ident_tok, writes=[kb.ident_tok])
    kb.eps_t = kb.sb("eps_t", [128, 1], F32)
    kb.eps_tok = Tok()
    kb.eps_ap = kb.eps_t[:, 0:1]
    kb.op("dve", lambda: nc.vector.memset(kb.eps_t[:], EPS), writes=[kb.eps_tok])


def transpose_to(kb, src_ap, src_tok, nblk, kpart, dst_fn, dst_tok, evac="act", pw=False):
    nc = kb.nc
    bt, btok = kb.bank()
    bv = bt[:].bitcast(BF16)
    for j in range(nblk):
        kb.op("pe", lambda j=j: nc.tensor.transpose(bv[0:kpart, j * 128:(j + 1) * 128], src_ap(j), kb.ident[:]),
              reads=[src_tok, kb.ident_tok], writes=[btok])
    srcv = bv[0:kpart, 0:nblk * 128].rearrange("p (j t) -> p j t", j=nblk)
    wr = dict(pwrites=[dst_tok]) if pw else dict(writes=[dst_tok])
    if evac == "act":
        kb.op("act", lambda: nc.scalar.copy(out=dst_fn(), in_=srcv), reads=[btok], **wr)
    else:
        kb.op("dve", lambda: nc.vector.tensor_copy(out=dst_fn(), in_=srcv), reads=[btok], **wr)


def build_L1():
    kb = KB()
    nc = kb.nc
    x = kb.dram("x", [TPC, D], F32, "ExternalInput")
    pos = kb.dram("pos", [1, TPC], I32, "ExternalInput")
    invf = kb.dram("invf", [64, 1], F32, "ExternalInput")
    ident_d = kb.dram("ident", [128, 128], F32, "ExternalInput")
    g_attn = kb.dram("g_attn", [D], F32, "ExternalInput")
    g_q = kb.dram("g_q", [512], F32, "ExternalInput")
    g_kv = kb.dram("g_kv", [256], F32, "ExternalInput")
    w_dq = kb.dram("w_dq", [D, 512], F32, "ExternalInput")
    w_uq = kb.dram("w_uq", [512, 1536], F32, "ExternalInput")
    w_dkv = kb.dram("w_dkv", [D, 320], F32, "ExternalInput")
    w_ukv = kb.dram("w_ukv", [256, 2048], F32, "ExternalInput")
    QTa_o = kb.dram("QTa", [H, 128, TPC], BF16, "ExternalOutput")
    QTb_o = kb.dram("QTb", [H, 64, TPC], BF16, "ExternalOutput")
    KTa_o = kb.dram("KTa", [H, 128, TPC], BF16, "ExternalOutput")
    KTb_o = kb.dram("KTb", [64, TPC], BF16, "ExternalOutput")
    V_o = kb.dram("V", [H, 128, NT, 129], BF16, "ExternalOutput")

    kb.make_banks()
    setup_consts(kb, ident_d)
    scale = (128 + 64) ** -0.5

    gA = kb.sb("gA", [128, 8], F32); gA_t = Tok()
    gQ = kb.sb("gQ", [128, 4], F32); gQ_t = Tok()
    gK = kb.sb("gK", [128, 2], F32); gK_t = Tok()
    with nc.allow_non_contiguous_dma(reason="tiny gain vectors"):
        kb.dma("sp", gA[:], g_attn.rearrange("(j p) -> p j", p=128), owner=gA_t, writes=[gA_t])
        kb.dma("sp", gQ[:], g_q.rearrange("(j p) -> p j", p=128), owner=gQ_t, writes=[gQ_t])
        kb.dma("sp", gK[:], g_kv.rearrange("(j p) -> p j", p=128), owner=gK_t, writes=[gK_t])

    stage = kb.sb("stage", [128, 6144], F32); stage_t = Tok()
    Wdq = kb.sb("Wdq", [128, 8, 512], BF16)
    Wdkv = kb.sb("Wdkv", [128, 8, 320], BF16)
    Wkrot = kb.sb("Wkrot", [128, 8, 64], BF16)
    Wuq = kb.sb("Wuq", [128, 4, 1536], BF16)
    Wqrot = kb.sb("Wqrot", [128, 4, 8, 64], BF16)
    Wukv = kb.sb("Wukv", [128, 2, 2048], BF16)

    def prep(w_dram, g_sb, g_tok, out_bf, kc, ncol, extra=None):
        sview = stage[:, 0:kc * ncol].rearrange("p (j n) -> p j n", j=kc)
        kb.dma("sp", sview, w_dram.rearrange("(j p) n -> p j n", p=128), owner=stage_t, writes=[stage_t])
        wt = Tok()
        for j in range(kc):
            kb.op("dve", lambda j=j: nc.vector.tensor_scalar(
                out=out_bf[:, j, :], in0=sview[:, j, :], scalar1=g_sb[:, j:j + 1],
                scalar2=(None if extra is None else float(extra)), op0=ALU.mult,
                **({} if extra is None else {"op1": ALU.mult})),
                reads=[stage_t, g_tok], writes=[wt])
        return wt

    Wdq_t = prep(w_dq, gA, gA_t, Wdq, 8, 512)
    Wdkv_t = prep(w_dkv, gA, gA_t, Wdkv, 8, 320)
    Wuq_t = prep(w_uq, gQ, gQ_t, Wuq, 4, 1536, extra=scale)
    Wukv_t = prep(w_ukv, gK, gK_t, Wukv, 2, 2048)
    Wkrot_t = Tok()
    kb.op("dve", lambda: nc.vector.tensor_scalar(out=Wkrot[:, :, 0:32], in0=Wdkv[:, :, 288:320], scalar1=-1.0,
                                                 scalar2=None, op0=ALU.mult), reads=[Wdkv_t], writes=[Wkrot_t])
    kb.op("dve", lambda: nc.vector.tensor_copy(out=Wkrot[:, :, 32:64], in_=Wdkv[:, :, 256:288]),
          reads=[Wdkv_t], writes=[Wkrot_t])
    Wqrot_t = Tok()
    Wuq_v = Wuq[:].rearrange("p j (h d) -> p j h d", h=8)
    for j in range(4):
        kb.op("dve", lambda j=j: nc.vector.tensor_scalar(out=Wqrot[:, j, :, 0:32], in0=Wuq_v[:, j, :, 160:192],
                                                         scalar1=-1.0, scalar2=None, op0=ALU.mult),
              reads=[Wuq_t], writes=[Wqrot_t])
        kb.op("dve", lambda j=j: nc.vector.tensor_copy(out=Wqrot[:, j, :, 32:64], in_=Wuq_v[:, j, :, 128:160]),
              reads=[Wuq_t], writes=[Wqrot_t])

    cosT = kb.sb("cosT", [64, TPC], F32); sinT = kb.sb("sinT", [64, TPC], F32); tab_t = Tok()
    invf_sb = kb.sb("invf_sb", [64, 1], F32); invf_t = Tok()
    kb.dma("sp", invf_sb[:], invf, owner=invf_t, writes=[invf_t])
    posi = kb.sb("posi", [64, TPC], I32); posi_t = Tok()
    kb.dma("sp", posi[:], pos.partition_broadcast(64), owner=posi_t, writes=[posi_t])
    A0 = stage[0:64, 0:TPC]; A1 = stage[0:64, TPC:2 * TPC]; A2 = stage[0:64, 2 * TPC:3 * TPC]
    A1i = A1.bitcast(I32)
    V = nc.vector
    sq = [
        (lambda: V.tensor_copy(out=A0, in_=posi[:]), [posi_t]),
        (lambda: V.tensor_scalar(out=A0, in0=A0, scalar1=invf_sb[:, 0:1], scalar2=None, op0=ALU.mult), [invf_t]),
        (lambda: V.tensor_scalar(out=A2, in0=A0, scalar1=1.0 / TWO_PI, scalar2=None, op0=ALU.mult), []),
        (lambda: V.tensor_copy(out=A1i, in_=A2), []),
        (lambda: V.tensor_copy(out=A2, in_=A1i), []),
        (lambda: V.scalar_tensor_tensor(out=A0, in0=A2, scalar=-C1, in1=A0, op0=ALU.mult, op1=ALU.add), []),
        (lambda: V.scalar_tensor_tensor(out=A0, in0=A2, scalar=-C2, in1=A0, op0=ALU.mult, op1=ALU.add), []),
        (lambda: V.tensor_scalar(out=A2, in0=A0, scalar1=PI, scalar2=-TWO_PI, op0=ALU.is_gt, op1=ALU.mult), []),
        (lambda: V.tensor_tensor(out=A0, in0=A0, in1=A2, op=ALU.add), []),
        (lambda: V.tensor_scalar(out=A2, in0=A0, scalar1=-PI, scalar2=TWO_PI, op0=ALU.is_lt, op1=ALU.mult), []),
        (lambda: V.tensor_tensor(out=A0, in0=A0, in1=A2, op=ALU.add), []),
        (lambda: V.tensor_scalar(out=A1, in0=A0, scalar1=PI / 2, scalar2=None, op0=ALU.add), []),
        (lambda: V.tensor_scalar(out=A2, in0=A1, scalar1=PI, scalar2=-TWO_PI, op0=ALU.is_gt, op1=ALU.mult), []),
        (lambda: V.tensor_tensor(out=A1, in0=A1, in1=A2, op=ALU.add), []),
        (lambda: V.tensor_scalar(out=A0, in0=A0, scalar1=PI_SAFE, scalar2=-PI_SAFE, op0=ALU.min, op1=ALU.max), []),
        (lambda: V.tensor_scalar(out=A1, in0=A1, scalar1=PI_SAFE, scalar2=-PI_SAFE, op0=ALU.min, op1=ALU.max), []),
    ]
    for fn, rd in sq:
        kb.op("dve", fn, reads=[stage_t] + rd, writes=[stage_t])
    kb.op("act", lambda: nc.scalar.activation(out=sinT[:], in_=A0, func=AF.Sin), reads=[stage_t], writes=[tab_t])
    kb.op("act", lambda: nc.scalar.activation(out=cosT[:], in_=A1, func=AF.Sin), reads=[stage_t], pwrites=[tab_t])

    xt = [kb.sb("xt%d" % i, [128, D], F32) for i in range(2)]; xt_t = [Tok(), Tok()]
    junk = kb.sb("junk", [128, D], BF16); junk_t = Tok()
    xs = kb.sb("xs", [128, D], BF16); xs_t = Tok()
    st4 = kb.sb("st4", [128, 8], F32); st_t = Tok()
    hnT = kb.sb("hnT", [128, 8, 512], BF16); hnT_t = Tok()
    cqs = kb.sb("cqs", [128, 512], BF16); cqs_t = Tok()
    ckvs = kb.sb("ckvs", [128, 256], BF16); ckvs_t = Tok()
    cqT = kb.sb("cqT", [128, 4, 512], BF16); cqT_t = Tok()
    ckvT = kb.sb("ckvT", [128, 2, 512], BF16); ckvT_t = Tok()
    Vaug = kb.sb("Vaug", [128, H, NT, 129], BF16); Vaug_t = Tok()
    kb.op("pool", lambda: nc.gpsimd.memset(Vaug[:], 1.0), writes=[Vaug_t])
    qa_st = kb.sb("qa_st", [128, H, 512], BF16); qa_t = Tok()
    qb_st = kb.sb("qb_st", [64, H, 512], BF16); qb_t = Tok()
    ka_st = kb.sb("ka_st", [128, H, 512], BF16); ka_t = Tok()
    kb_st = kb.sb("kb_st", [64, 512], BF16); kbs_t = Tok()
    tmp1 = kb.sb("tmp1", [64, 512], F32); tmp1_t = Tok()
    tmp2 = kb.sb("tmp2", [64, 512], F32); tmp2_t = Tok()

    Wukv_v = Wukv[:].rearrange("p j (h d) -> p j h d", h=8)

    def rope_combine(raw_b, raw_tok, rot_b, rot_tok, c, out_ap, out_tok, pw):
        cs = cosT[:, c * 512:(c + 1) * 512]
        sn = sinT[:, c * 512:(c + 1) * 512]
        kb.op("dve", lambda: nc.vector.tensor_tensor(out=tmp1[:], in0=raw_b[0:64, :], in1=cs, op=ALU.mult),
              reads=[raw_tok, tab_t], writes=[tmp1_t])
        kb.op("dve", lambda: nc.vector.tensor_tensor(out=tmp2[:], in0=rot_b[0:64, :], in1=sn, op=ALU.mult),
              reads=[rot_tok, tab_t], writes=[tmp2_t])
        wr = dict(pwrites=[out_tok]) if pw else dict(writes=[out_tok])
        kb.op("pool", lambda: nc.gpsimd.tensor_tensor(out=out_ap, in0=tmp1[:], in1=tmp2[:], op=ALU.add),
              reads=[tmp1_t, tmp2_t], **wr)

    for c in range(NT // 4):
        for i in range(4):
            tt = 4 * c + i
            xb = xt[tt % 2]; xbt = xt_t[tt % 2]
            kb.dma("sp", xb[:], x[tt * 128:(tt + 1) * 128, :], owner=xbt, writes=[xbt])
            kb.op("act", lambda: nc.scalar.activation(out=junk[:], in_=xb[:], func=AF.Square,
                                                      accum_out=st4[:, 0:1]),
                  reads=[xbt], writes=[junk_t, st_t])
            rms_scale(kb, st4[:, 0:1], st4[:, 1:2], D, st_t, st_t)
            kb.op("dve", lambda: nc.vector.tensor_scalar(out=xs[:], in0=xb[:], scalar1=st4[:, 1:2], scalar2=None,
                                                         op0=ALU.mult), reads=[xbt, st_t], writes=[xs_t])
            transpose_to(kb, lambda j: xs[:, j * 128:(j + 1) * 128], xs_t, 8, 128,
                         lambda: hnT[:, :, i * 128:(i + 1) * 128], hnT_t, evac="act", pw=(i > 0))
            cq_b, cq_bt = kb.bank()
            for j in range(8):
                kb.op("pe", lambda j=j: nc.tensor.matmul(cq_b[:, 0:512], hnT[:, j, i * 128:(i + 1) * 128],
                                                         Wdq[:, j, :], start=(j == 0), stop=(j == 7)),
                      reads=[hnT_t, Wdq_t], writes=[cq_bt])
            ck_b, ck_bt = kb.bank()
            for j in range(8):
                kb.op("pe", lambda j=j: nc.tensor.matmul(ck_b[:, 0:256], hnT[:, j, i * 128:(i + 1) * 128],
                                                         Wdkv[:, j, 0:256], start=(j == 0), stop=(j == 7)),
                      reads=[hnT_t, Wdkv_t], writes=[ck_bt])
            kb.op("act", lambda: nc.scalar.activation(out=junk[:, 0:512], in_=cq_b[:, 0:512], func=AF.Square,
                                                      accum_out=st4[:, 2:3]), reads=[cq_bt], writes=[junk_t, st_t])
            rms_scale(kb, st4[:, 2:3], st4[:, 3:4], 512, st_t, st_t)
            kb.op("dve", lambda: nc.vector.tensor_scalar(out=cqs[:], in0=cq_b[:, 0:512], scalar1=st4[:, 3:4],
                                                         scalar2=None, op0=ALU.mult),
                  reads=[cq_bt, st_t], writes=[cqs_t])
            kb.op("act", lambda: nc.scalar.activation(out=junk[:, 0:256], in_=ck_b[:, 0:256], func=AF.Square,
                                                      accum_out=st4[:, 4:5]), reads=[ck_bt], writes=[junk_t, st_t])
            rms_scale(kb, st4[:, 4:5], st4[:, 5:6], 256, st_t, st_t)
            kb.op("dve", lambda: nc.vector.tensor_scalar(out=ckvs[:], in0=ck_b[:, 0:256], scalar1=st4[:, 5:6],
                                                         scalar2=None, op0=ALU.mult),
                  reads=[ck_bt, st_t], writes=[ckvs_t])
            transpose_to(kb, lambda j: cqs[:, j * 128:(j + 1) * 128], cqs_t, 4, 128,
                         lambda: cqT[:, :, i * 128:(i + 1) * 128], cqT_t, evac="dve", pw=(i > 0))
            transpose_to(kb, lambda j: ckvs[:, j * 128:(j + 1) * 128], ckvs_t, 2, 128,
                         lambda: ckvT[:, :, i * 128:(i + 1) * 128], ckvT_t, evac="dve", pw=(i > 0))
            for hh in range(2):
                vb, vbt = kb.bank()
                for j in range(2):
                    kb.op("pe", lambda j=j: nc.tensor.matmul(vb[:, 0:512].rearrange("p (h d) -> p h d", h=4),
                                                             ckvT[:, j, i * 128:(i + 1) * 128],
                                                             Wukv_v[:, j, 4 * hh:4 * hh + 4, 128:256],
                                                             start=(j == 0), stop=(j == 1)),
                          reads=[ckvT_t, Wukv_t], writes=[vbt])
                kb.op("act", lambda: nc.scalar.copy(out=Vaug[:, 4 * hh:4 * hh + 4, tt, 0:128],
                                                    in_=vb[:, 0:512].rearrange("p (h d) -> p h d", h=4)),
                      reads=[vbt], pwrites=[Vaug_t])
        csl = slice(c * 512, (c + 1) * 512)
        kp_b, kp_bt = kb.bank()
        for j in range(8):
            kb.op("pe", lambda j=j: nc.tensor.matmul(kp_b[0:64, :], Wdkv[:, j, 256:320], hnT[:, j, :],
                                                     start=(j == 0), stop=(j == 7)),
                  reads=[hnT_t, Wdkv_t], writes=[kp_bt])
        kr_b, kr_bt = kb.bank()
        for j in range(8):
            kb.op("pe", lambda j=j: nc.tensor.matmul(kr_b[0:64, :], Wkrot[:, j, :], hnT[:, j, :],
                                                     start=(j == 0), stop=(j == 7)),
                  reads=[hnT_t, Wkrot_t], writes=[kr_bt])
        rope_combine(kp_b, kp_bt, kr_b, kr_bt, c, kb_st[:], kbs_t, False)
        kb.dma("pool", KTb_o[:, csl], kb_st[:], owner=kbs_t, reads=[kbs_t])
        for h in range(H):
            qa_b, qa_bt = kb.bank()
            for j in range(4):
                kb.op("pe", lambda j=j: nc.tensor.matmul(qa_b[:, :], Wuq[:, j, h * 192:h * 192 + 128], cqT[:, j, :],
                                                         start=(j == 0), stop=(j == 3)),
                      reads=[cqT_t, Wuq_t], writes=[qa_bt])
            kb.op("act", lambda: nc.scalar.copy(out=qa_st[:, h, :], in_=qa_b[:, :]), reads=[qa_bt],
                  **(dict(writes=[qa_t]) if h == 0 else dict(pwrites=[qa_t])))
            qp_b, qp_bt = kb.bank()
            for j in range(4):
                kb.op("pe", lambda j=j: nc.tensor.matmul(qp_b[0:64, :], Wuq[:, j, h * 192 + 128:h * 192 + 192],
                                                         cqT[:, j, :], start=(j == 0), stop=(j == 3)),
                      reads=[cqT_t, Wuq_t], writes=[qp_bt])
            qr_b, qr_bt = kb.bank()
            for j in range(4):
                kb.op("pe", lambda j=j: nc.tensor.matmul(qr_b[0:64, :], Wqrot[:, j, h, :], cqT[:, j, :],
                                                         start=(j == 0), stop=(j == 3)),
                      reads=[cqT_t, Wqrot_t], writes=[qr_bt])
            rope_combine(qp_b, qp_bt, qr_b, qr_bt, c, qb_st[:, h, :], qb_t, h > 0)
            ka_b, ka_bt = kb.bank()
            for j in range(2):
                kb.op("pe", lambda j=j: nc.tensor.matmul(ka_b[:, :], Wukv[:, j, h * 256:h * 256 + 128], ckvT[:, j, :],
                                                         start=(j == 0), stop=(j == 1)),
                      reads=[ckvT_t, Wukv_t], writes=[ka_bt])
            kb.op("dve", lambda: nc.vector.tensor_copy(out=ka_st[:, h, :], in_=ka_b[:, :]), reads=[ka_bt],
                  **(dict(writes=[ka_t]) if h == 0 else dict(pwrites=[ka_t])))
        kb.dma("pool", QTa_o[:, :, csl].rearrange("h p t -> p h t"), qa_st[:], owner=qa_t, reads=[qa_t])
        kb.dma("pool", QTb_o[:, :, csl].rearrange("h p t -> p h t"), qb_st[:], owner=qb_t, reads=[qb_t])
        kb.dma("pool", KTa_o[:, :, csl].rearrange("h p t -> p h t"), ka_st[:], owner=ka_t, reads=[ka_t])
    for h in range(H):
        kb.dma("pool", V_o[h], Vaug[:, h, :, :], owner=Vaug_t, reads=[Vaug_t])
    kb.finish([kbs_t, qa_t, qb_t, ka_t, Vaug_t])
    return kb


def attention_core(kb, passes, V_sb, kv_tok_of_block, epilogue, nq_chunks=S // 512):
    nc = kb.nc
    npass = len(passes)
    zeros = kb.sb("zeros", [128, 512], BF16); zeros_t = Tok()
    kb.op("pool", lambda: nc.gpsimd.memset(zeros[:], 0.0), writes=[zeros_t])
    NQS = 3
    qslots = []
    for pi, ps_ in enumerate(passes):
        sl = []
        for s in range(NQS):
            parts = [kb.sb("q%d_%d_%d" % (pi, s, k), [rows, 512], BF16) for k, (_, rows) in enumerate(ps_["Q"])]
            sl.append((parts, Tok()))
        qslots.append(sl)
    NP = 3
    pbuf = [(kb.sb("pT%d" % i, [128, 512], BF16), Tok()) for i in range(NP)]
    sbank = [kb.banks[0], kb.banks[1]]
    osets = [(kb.banks[2], kb.banks[3]), (kb.banks[4], kb.banks[5])]

    def load_q(qc):
        for pi, ps_ in enumerate(passes):
            parts, tok = qslots[pi][qc % NQS]
            for k, (qd, rows) in enumerate(ps_["Q"]):
                kb.dma("pool", parts[k][:], qd[:, qc * 512:(qc + 1) * 512], owner=tok,
                       **(dict(writes=[tok]) if k == 0 else dict(pwrites=[tok])))

    units = [(qc, pi) for qc in range(nq_chunks) for pi in range(npass)]
    tiles = []
    for ui, (qc, pi) in enumerate(units):
        for kbi in range(4 * qc + 4):
            tiles.append((ui, qc, pi, kbi))

    def emit_qk(ti):
        ui, qc, pi, kbi = tiles[ti]
        sb_t, sb_tok = sbank[ti % 2]
        parts, qtok = qslots[pi][qc % NQS]
        j = kbi - 4 * qc
        lo = 128 * j if j > 0 else 0
        kparts = passes[pi]["K"]
        n = len(kparts)
        for k, (ksb, rows) in enumerate(kparts):
            kb.op("pe", lambda k=k, ksb=ksb, rows=rows: nc.tensor.matmul(
                sb_t[:, lo:512], ksb[0:rows, kbi * 128:(kbi + 1) * 128], parts[k][0:rows, lo:512],
                start=(k == 0), stop=(k == n - 1 and j < 0)),
                reads=[kv_tok_of_block(kbi), qtok], writes=[sb_tok])
        if j >= 0:
            kb.op("pe", lambda: nc.tensor.matmul(sb_t[:, lo:lo + 128], kb.ident[:], kb.mask[:], start=False, stop=True),
                  reads=[kb.ident_tok, kb.mask_tok], writes=[sb_tok])

    def emit_exp(ti):
        ui, qc, pi, kbi = tiles[ti]
        sb_t, sb_tok = sbank[ti % 2]
        pb, ptok = pbuf[ti % NP]
        j = kbi - 4 * qc
        lo = 128 * j if j > 0 else 0
        kb.op("act", lambda: nc.scalar.activation(out=pb[:, lo:512], in_=sb_t[:, lo:512], func=AF.Exp),
              reads=[sb_tok], writes=[ptok])

    def emit_pv(ti):
        ui, qc, pi, kbi = tiles[ti]
        pb, ptok = pbuf[ti % NP]
        oset = osets[ui % 2]
        j = kbi - 4 * qc
        last = (kbi == 4 * qc + 3)
        if kbi == 0:
            for (ob, obtok) in oset:
                kb.op("pe", lambda ob=ob: nc.tensor.matmul(ob[:, :], zeros[:, 0:128], zeros[:, :], start=True, stop=False),
                      reads=[zeros_t], writes=[obtok])
        for i in range(max(j, 0), 4):
            ob, obtok = oset[i // 2]
            c0 = (i % 2) * 129
            kb.op("pe", lambda i=i, ob=ob, c0=c0: nc.tensor.matmul(
                ob[:, c0:c0 + 129], pb[:, 128 * i:128 * (i + 1)], V_sb[:, kbi, :], start=False,
                stop=(kbi == 4 * qc + i)),
                reads=[ptok, kv_tok_of_block(kbi)], writes=[obtok])
        if last:
            epilogue(qc, pi, oset)

    load_q(0)
    if nq_chunks > 1:
        load_q(1)
    nt = len(tiles)
    for ti in range(nt):
        ui, qc, pi, kbi = tiles[ti]
        if kbi == 0 and pi == 0 and qc + 2 < nq_chunks:
            load_q(qc + 2)
        emit_qk(ti)
        emit_exp(ti)
        if ti >= 1:
            emit_pv(ti - 1)
    emit_pv(nt - 1)


def setup_mask(kb, mask_dram):
    kb.mask = kb.sb("mask_sb", [128, 128], BF16)
    kb.mask_tok = Tok()
    kb.dma("pool", kb.mask[:], mask_dram, owner=kb.mask_tok, writes=[kb.mask_tok])


def build_L2(nq_chunks=S // 512):
    kb = KB()
    nc = kb.nc
    QTa = kb.dram("QTa", [128, S], BF16, "ExternalInput")
    QTb = kb.dram("QTb", [64, S], BF16, "ExternalInput")
    KTa = kb.dram("KTa", [128, S], BF16, "ExternalInput")
    KTb = kb.dram("KTb", [64, S], BF16, "ExternalInput")
    Vd = kb.dram("V", [128, S // 128, 129], BF16, "ExternalInput")
    ident_d = kb.dram("ident", [128, 128], F32, "ExternalInput")
    mask_d = kb.dram("mask", [128, 128], F32, "ExternalInput")
    O = kb.dram("O", [S, 128], BF16, "ExternalOutput")
    kb.make_banks()
    setup_consts(kb, ident_d)
    setup_mask(kb, mask_d)
    KTa_sb = kb.sb("KTa_sb", [128, S], BF16)
    KTb_sb = kb.sb("KTb_sb", [64, S], BF16)
    V_sb = kb.sb("V_sb", [128, S // 128, 129], BF16)
    ptoks = [Tok() for _ in range(8)]
    for r in range(8):
        t = ptoks[r]
        kb.dma("sp", KTa_sb[:, r * 2048:(r + 1) * 2048], KTa[:, r * 2048:(r + 1) * 2048], owner=t, writes=[t])
        kb.dma("sp", KTb_sb[:, r * 2048:(r + 1) * 2048], KTb[:, r * 2048:(r + 1) * 2048], owner=t, pwrites=[t])
        kb.dma("sp", V_sb[:, r * 16:(r + 1) * 16, :], Vd[:, r * 16:(r + 1) * 16, :], owner=t, pwrites=[t])
    ost = [(kb.sb("ost%d" % i, [128, 4, 128], BF16), Tok()) for i in range(2)]
    rc = kb.sb("rc", [128, 4], F32); rc_t = Tok()
    Ov = O.rearrange("(c i p) d -> c p i d", i=4, p=128)

    def epilogue(qc, pi, oset):
        o_sb, o_tok = ost[qc % 2]
        for b, (ob, obtok) in enumerate(oset):
            sums = ob[:, 0:258].rearrange("p (i c) -> p i c", c=129)[:, :, 128:129]
            kb.op("dve", lambda: nc.vector.reciprocal(out=rc[:, 2 * b:2 * b + 2].rearrange("p (i c) -> p i c", c=1),
                                                     in_=sums), reads=[obtok], writes=[rc_t])
            for i in range(2):
                kb.op("dve", lambda i=i: nc.vector.tensor_scalar(
                    out=o_sb[:, 2 * b + i, :], in0=ob[:, i * 129:i * 129 + 128],
                    scalar1=rc[:, 2 * b + i:2 * b + i + 1], scalar2=None, op0=ALU.mult),
                    reads=[obtok, rc_t], **(dict(writes=[o_tok]) if (b == 0 and i == 0) else dict(pwrites=[o_tok])))
        kb.dma("sp", Ov[qc], o_sb[:], owner=o_tok, reads=[o_tok])

    attention_core(kb, [dict(K=[(KTa_sb, 128), (KTb_sb, 64)], Q=[(QTa, 128), (QTb, 64)])], V_sb,
                   lambda kbi: ptoks[kbi // 16], epilogue, nq_chunks=nq_chunks)
    kb.finish([t for _, t in ost])
    return kb


def kb_barrier(kb):
    deps = {}
    for n, E in kb.eng.items():
        if E["cnt"] > 0:
            deps[E["sem"]] = E["cnt"]
    for t in kb.dma_toks:
        if t.sem is not None and t.cnt > 0:
            deps[t.sem] = t.cnt
    for n in kb.eng:
        kb._wait(n, dict(deps), dma=True)


def load_gain(kb, name, g_dram, kc):
    nc = kb.nc
    g = kb.sb(name, [128, kc], F32)
    t = kb.tok()
    with nc.allow_non_contiguous_dma(reason="tiny gain vector"):
        kb.dma("sp", g[:], g_dram.rearrange("(j p) -> p j", p=128), owner=t, writes=[t])
    return g, t


def prep_weight(kb, w_view, g_sb, g_tok, out_bf, kc, ncol, stage, stage_tok, extra=None, q="sp"):
    nc = kb.nc
    sview = stage[:, 0:kc * ncol].rearrange("p (j n) -> p j n", j=kc)
    kb.dma(q, sview, w_view.rearrange("(j p) n -> p j n", p=128), owner=stage_tok, writes=[stage_tok])
    wt = kb.tok()
    for j in range(kc):
        if g_sb is None:
            kb.op("dve", lambda j=j: nc.vector.tensor_copy(out=out_bf[:, j, :], in_=sview[:, j, :]),
                  reads=[stage_tok], **(dict(writes=[wt]) if j == 0 else dict(pwrites=[wt])))
        else:
            kb.op("dve", lambda j=j: nc.vector.tensor_scalar(
                out=out_bf[:, j, :], in0=sview[:, j, :], scalar1=g_sb[:, j:j + 1],
                scalar2=(None if extra is None else float(extra)), op0=ALU.mult,
                **({} if extra is None else {"op1": ALU.mult})),
                reads=[stage_tok, g_tok], **(dict(writes=[wt]) if j == 0 else dict(pwrites=[wt])))
    return wt


def norm_transpose(kb, h_res, h_tok, hnT_all, hnT_tok, work):
    nc = kb.nc
    junk, junk_t, xs, xs_t, st4, st_t = work
    for tt in range(NT):
        kb.op("act", lambda: nc.scalar.activation(out=junk[:], in_=h_res[:, tt, :], func=AF.Square,
                                                  accum_out=st4[:, 0:1]), reads=[h_tok[tt]], writes=[junk_t, st_t])
        rms_scale(kb, st4[:, 0:1], st4[:, 1:2], D, st_t, st_t)
        kb.op("dve", lambda: nc.vector.tensor_scalar(out=xs[:], in0=h_res[:, tt, :], scalar1=st4[:, 1:2],
                                                     scalar2=None, op0=ALU.mult),
              reads=[h_tok[tt], st_t], writes=[xs_t])
        transpose_to(kb, lambda j: xs[:, j * 128:(j + 1) * 128], xs_t, 8, 128,
                     lambda: hnT_all[:, :, tt * 128:(tt + 1) * 128], hnT_tok, evac="act", pw=(tt > 0))


def phase_attn_out(kb, resid_load, O_d, w_o, h_res, h_tok):
    nc = kb.nc
    kb.push_scope()
    stage = kb.sb("pa_stage", [128, 8 * 1024], F32); stage_t = kb.tok()
    Wo = kb.sb("pa_Wo", [128, 8, 1024], BF16)
    Wo_t = prep_weight(kb, w_o, None, None, Wo, 8, 1024, stage, stage_t)
    ot = [(kb.sb("pa_o%d" % i, [128, D], BF16), kb.tok()) for i in range(2)]
    oT = kb.sb("pa_oT", [128, 8, 128], BF16); oT_t = kb.tok()
    for tt in range(NT):
        resid_load(tt)
        ob, obt = ot[tt % 2]
        kb.dma("pool", ob[:], O_d[tt * 128:(tt + 1) * 128, :], owner=obt, writes=[obt])
        transpose_to(kb, lambda j: ob[:, j * 128:(j + 1) * 128], obt, 8, 128, lambda: oT[:, :, :], oT_t, evac="act")
        for half in range(2):
            pb, pbt = kb.bank()
            for j in range(8):
                kb.op("pe", lambda j=j: nc.tensor.matmul(pb[:, :], oT[:, j, :], Wo[:, j, half * 512:(half + 1) * 512],
                                                         start=(j == 0), stop=(j == 7)),
                      reads=[oT_t, Wo_t], writes=[pbt])
            hs = h_res[:, tt, half * 512:(half + 1) * 512]
            kb.op("dve", lambda: nc.vector.tensor_tensor(out=hs, in0=pb[:, :], in1=hs, op=ALU.add),
                  reads=[pbt], pwrites=[h_tok[tt]])
    kb_barrier(kb)
    kb.pop_scope()


def phase_ffn(kb, h_res, h_tok, g_ffn, w_gu, w_d):
    nc = kb.nc
    kb.push_scope()
    gF, gF_t = load_gain(kb, "ff_g", g_ffn, 8)
    junk = kb.sb("ff_junk", [128, D], BF16); xs = kb.sb("ff_xs", [128, D], BF16); st4 = kb.sb("ff_st", [128, 4], F32)
    work = (junk, kb.tok(), xs, kb.tok(), st4, kb.tok())
    hnT = kb.sb("ff_hnT", [128, 8, TPC], BF16); hnT_t = kb.tok()
    norm_transpose(kb, h_res, h_tok, hnT, hnT_t, work)
    HB = 256
    nhb = DFF // HB
    stages = [(kb.sb("ff_stage%d" % i, [128, 8 * HB], F32), kb.tok()) for i in range(3)]
    Wg = [kb.sb("ff_Wg%d" % i, [128, 8, HB], BF16) for i in range(2)]
    Wu = [kb.sb("ff_Wu%d" % i, [128, 8, HB], BF16) for i in range(2)]
    Wd = [kb.sb("ff_Wd%d" % i, [128, 2, D], BF16) for i in range(2)]
    sg = [(kb.sb("ff_sg%d" % i, [128, 512], F32), kb.tok()) for i in range(2)]
    act = [(kb.sb("ff_act%d" % i, [128, 2, 512], BF16), kb.tok()) for i in range(2)]
    si = 0
    for hb in range(nhb):
        s0, s0t = stages[si % 3]; si += 1
        Wg_t = prep_weight(kb, w_gu[:, hb * HB:(hb + 1) * HB], gF, gF_t, Wg[hb % 2], 8, HB, s0, s0t)
        s1, s1t = stages[si % 3]; si += 1
        Wu_t = prep_weight(kb, w_gu[:, DFF + hb * HB:DFF + (hb + 1) * HB], gF, gF_t, Wu[hb % 2], 8, HB, s1, s1t)
        s2, s2t = stages[si % 3]; si += 1
        Wd_t = prep_weight(kb, w_d[hb * HB:(hb + 1) * HB, :], None, None, Wd[hb % 2], 2, D, s2, s2t)
        for c in range(TPC // 512):
            ab, abt = act[(hb * 4 + c) % 2]
            for sub in range(2):
                gb, gbt = kb.bank()
                for j in range(8):
                    kb.op("pe", lambda j=j: nc.tensor.matmul(gb[:, :], Wg[hb % 2][:, j, sub * 128:(sub + 1) * 128],
                                                             hnT[:, j, c * 512:(c + 1) * 512], start=(j == 0), stop=(j == 7)),
                          reads=[hnT_t, Wg_t], writes=[gbt])
                ub, ubt = kb.bank()
                for j in range(8):
                    kb.op("pe", lambda j=j: nc.tensor.matmul(ub[:, :], Wu[hb % 2][:, j, sub * 128:(sub + 1) * 128],
                                                             hnT[:, j, c * 512:(c + 1) * 512], start=(j == 0), stop=(j == 7)),
                          reads=[hnT_t, Wu_t], writes=[ubt])
                sgb, sgt = sg[sub]
                kb.op("act", lambda: nc.scalar.activation(out=sgb[:], in_=gb[:, :], func=AF.Silu),
                      reads=[gbt], writes=[sgt])
                kb.op("dve", lambda: nc.vector.tensor_tensor(out=ab[:, sub, :], in0=ub[:, :], in1=sgb[:], op=ALU.mult),
                      reads=[ubt, sgt], **(dict(writes=[abt]) if sub == 0 else dict(pwrites=[abt])))
            for i in range(4):
                tt = c * 4 + i
                for half in range(2):
                    db, dbt = kb.bank()
                    for sub in range(2):
                        kb.op("pe", lambda sub=sub: nc.tensor.matmul(db[:, :], ab[:, sub, i * 128:(i + 1) * 128],
                                                                     Wd[hb % 2][:, sub, half * 512:(half + 1) * 512],
                                                                     start=(sub == 0), stop=(sub == 1)),
                              reads=[abt, Wd_t], writes=[dbt])
                    hs = h_res[:, tt, half * 512:(half + 1) * 512]
                    kb.op("dve", lambda: nc.vector.tensor_tensor(out=hs, in0=db[:, :], in1=hs, op=ALU.add),
                          reads=[dbt], pwrites=[h_tok[tt]])
    kb_barrier(kb)
    kb.pop_scope()


def make_h(kb):
    h_res = kb.sb("h_res", [128, NT, D], F32)
    h_tok = [kb.tok() for _ in range(NT)]
    return h_res, h_tok


def build_L3():
    kb = KB()
    nc = kb.nc
    x = kb.dram("x", [TPC, D], F32, "ExternalInput")
    O_d = kb.dram("O", [TPC, D], BF16, "ExternalInput")
    pos = kb.dram("pos", [1, TPC], I32, "ExternalInput")
    coef = kb.dram("coef", [4, 8], F32, "ExternalInput")
    ident_d = kb.dram("ident", [128, 128], F32, "ExternalInput")
    w_o = kb.dram("w_o", [D, D], F32, "ExternalInput")
    g_ffn = kb.dram("g_ffn", [D], F32, "ExternalInput")
    w_gu = kb.dram("w_gu", [D, 2 * DFF], F32, "ExternalInput")
    w_d = kb.dram("w_d", [DFF, D], F32, "ExternalInput")
    g_kv = kb.dram("g_kv", [D], F32, "ExternalInput")
    g_a1 = kb.dram("g_a1", [D], F32, "ExternalInput")
    w_k = kb.dram("w_k", [D, D], F32, "ExternalInput")
    w_v = kb.dram("w_v", [D, D], F32, "ExternalInput")
    w_q = kb.dram("w_q", [D, D], F32, "ExternalInput")
    h1_o = kb.dram("h1", [TPC, D], F32, "ExternalOutput")
    Q1_o = kb.dram("Q1", [H, 68, TPC], BF16, "ExternalOutput")
    Q2_o = kb.dram("Q2", [H, 68, TPC], BF16, "ExternalOutput")
    K1_o = kb.dram("K1", [H, 68, TPC], BF16, "ExternalOutput")
    K2_o = kb.dram("K2", [H, 68, TPC], BF16, "ExternalOutput")
    V_o = kb.dram("V", [H, 128, NT, 129], BF16, "ExternalOutput")
    kb.make_banks()
    setup_consts(kb, ident_d)
    h_res, h_tok = make_h(kb)

    def resid_load(tt):
        kb.dma("sp", h_res[:, tt, :], x[tt * 128:(tt + 1) * 128, :], owner=h_tok[tt], writes=[h_tok[tt]])

    phase_attn_out(kb, resid_load, O_d, w_o, h_res, h_tok)
    phase_ffn(kb, h_res, h_tok, g_ffn, w_gu, w_d)
    for tt in range(NT):
        kb.dma("sp", h1_o[tt * 128:(tt + 1) * 128, :], h_res[:, tt, :], owner=h_tok[tt], reads=[h_tok[tt]])

    kb.push_scope()
    cf = kb.sb("d_cf", [4, 8], F32); cf_t = kb.tok()
    kb.dma("sp", cf[:], coef, owner=cf_t, writes=[cf_t])
    pi_ = kb.sb("d_pi", [4, TPC], I32); pi_t = kb.tok()
    kb.dma("sp", pi_[:], pos.partition_broadcast(4), owner=pi_t, writes=[pi_t])
    ai = kb.sb("d_ai", [4, TPC], I32)
    af = kb.sb("d_af", [4, TPC], F32); bf_ = kb.sb("d_bf", [4, TPC], F32); aug_t = kb.tok()
    t1 = kb.sb("d_t1", [4, TPC], F32)
    kaug = kb.sb("d_kaug", [4, TPC], BF16); qbase = kb.sb("d_qbase", [4, TPC], F32)
    qaug = [(kb.sb("d_qaug%d" % i, [4, TPC], BF16), kb.tok()) for i in range(2)]
    kaug_t = kb.tok()
    V_ = nc.vector
    seq = [
        lambda: V_.tensor_scalar(out=ai[:], in0=pi_[:], scalar1=7, scalar2=None, op0=ALU.arith_shift_right),
        lambda: V_.tensor_copy(out=af[:], in_=ai[:]),
        lambda: V_.tensor_scalar(out=ai[:], in0=pi_[:], scalar1=127, scalar2=None, op0=ALU.bitwise_and),
        lambda: V_.tensor_copy(out=bf_[:], in_=ai[:]),
        lambda: V_.tensor_scalar(out=t1[:], in0=af[:], scalar1=cf[:, 0:1], scalar2=cf[:, 2:3], op0=ALU.mult, op1=ALU.add),
        lambda: V_.scalar_tensor_tensor(out=kaug[:], in0=bf_[:], scalar=cf[:, 1:2], in1=t1[:], op0=ALU.mult, op1=ALU.add),
        lambda: V_.tensor_scalar(out=t1[:], in0=af[:], scalar1=cf[:, 3:4], scalar2=cf[:, 5:6], op0=ALU.mult, op1=ALU.add),
        lambda: V_.scalar_tensor_tensor(out=qbase[:], in0=bf_[:], scalar=cf[:, 4:5], in1=t1[:], op0=ALU.mult, op1=ALU.add),
    ]
    for fn in seq:
        kb.op("dve", fn, reads=[pi_t, cf_t], writes=[aug_t])
    kb.op("dve", lambda: V_.tensor_copy(out=kaug[:], in_=kaug[:]), reads=[aug_t], writes=[kaug_t])
    for h in range(H):
        slope = 2.0 ** (-8.0 * (h + 1) / H)
        qa_, qa_t_ = qaug[h % 2]
        kb.op("dve", lambda: V_.tensor_scalar(out=qa_[:], in0=qbase[:], scalar1=slope, scalar2=None, op0=ALU.mult),
              reads=[aug_t], writes=[qa_t_])
        kb.dma("pool", Q1_o[h, 64:68, :], qa_[:], owner=qa_t_, reads=[qa_t_])
        kb.dma("pool", Q2_o[h, 64:68, :], qa_[:], owner=qa_t_, reads=[qa_t_])
        kb.dma("pool", K1_o[h, 64:68, :], kaug[:], owner=kaug_t, reads=[kaug_t])
        kb.dma("pool", K2_o[h, 64:68, :], kaug[:], owner=kaug_t, reads=[kaug_t])
    kb_barrier(kb)
    kb.pop_scope()

    kb.push_scope()
    gK, gK_t = load_gain(kb, "d_gk", g_kv, 8)
    gA, gA_t = load_gain(kb, "d_ga", g_a1, 8)
    Wk = kb.sb("d_Wk", [128, 8, D], BF16); Wv = kb.sb("d_Wv", [128, 8, D], BF16); Wq = kb.sb("d_Wq", [128, 8, D], BF16)
    kb.push_scope()
    stage = kb.sb("d_stage", [128, 8 * 1024], F32); stage_t = kb.tok()
    Wk_t = prep_weight(kb, w_k, gK, gK_t, Wk, 8, D, stage, stage_t)
    Wv_t = prep_weight(kb, w_v, gK, gK_t, Wv, 8, D, stage, stage_t)
    Wq_t = prep_weight(kb, w_q, gA, gA_t, Wq, 8, D, stage, stage_t, extra=64 ** -0.5)
    kb_barrier(kb)
    kb.pop_scope()
    junk = kb.sb("d_junk", [128, D], BF16); xs = kb.sb("d_xs", [128, D], BF16); st4 = kb.sb("d_st", [128, 4], F32)
    work = (junk, kb.tok(), xs, kb.tok(), st4, kb.tok())
    hT = kb.sb("d_hT", [128, 8, TPC], BF16); hT_t = kb.tok()
    norm_transpose(kb, h_res, h_tok, hT, hT_t, work)
    vst = [(kb.sb("d_vst%d" % i, [128, H, 129], BF16), kb.tok()) for i in range(2)]
    for vs_, vs_t in vst:
        kb.op("pool", lambda: nc.gpsimd.memset(vs_[:], 1.0), writes=[vs_t])
    kst = [(kb.sb("d_kst%d" % i, [128, H, 512], BF16), kb.tok()) for i in range(1)]
    qst = [(kb.sb("d_qst%d" % i, [128, H, 512], BF16), kb.tok()) for i in range(1)]
    for c in range(TPC // 512):
        csl = slice(c * 512, (c + 1) * 512)
        for (W, W_t, stl, o1, o2, ev) in ((Wk, Wk_t, kst, K1_o, K2_o, "act"), (Wq, Wq_t, qst, Q1_o, Q2_o, "dve")):
            sb_, sb_t = stl[0]
            for h in range(H):
                pb, pbt = kb.bank()
                for j in range(8):
                    kb.op("pe", lambda j=j: nc.tensor.matmul(pb[:, :], W[:, j, h * 128:(h + 1) * 128], hT[:, j, csl],
                                                             start=(j == 0), stop=(j == 7)),
                          reads=[hT_t, W_t], writes=[pbt])
                wr = dict(writes=[sb_t]) if h == 0 else dict(pwrites=[sb_t])
                if ev == "act":
                    kb.op("act", lambda: nc.scalar.copy(out=sb_[:, h, :], in_=pb[:, :]), reads=[pbt], **wr)
                else:
                    kb.op("dve", lambda: nc.vector.tensor_copy(out=sb_[:, h, :], in_=pb[:, :]), reads=[pbt], **wr)
            kb.dma("pool", o1[:, 0:64, csl].rearrange("h p t -> p h t"), sb_[0:64, :, :], owner=sb_t, reads=[sb_t])
            kb.dma("pool", o2[:, 0:64, csl].rearrange("h p t -> p h t"), sb_[64:128, :, :], owner=sb_t, reads=[sb_t])
        for i in range(4):
            tt = c * 4 + i
            vs_, vs_t = vst[tt % 2]
            for hh in range(2):
                vb, vbt = kb.bank()
                for j in range(8):
                    kb.op("pe", lambda j=j: nc.tensor.matmul(vb[:, :], hT[:, j, tt * 128:(tt + 1) * 128],
                                                             Wv[:, j, hh * 512:(hh + 1) * 512], start=(j == 0), stop=(j == 7)),
                          reads=[hT_t, Wv_t], writes=[vbt])
                kb.op("act", lambda: nc.scalar.copy(out=vs_[:, 4 * hh:4 * hh + 4, 0:128],
                                                    in_=vb[:, :].rearrange("p (h d) -> p h d", h=4)),
                      reads=[vbt], pwrites=[vs_t])
            kb.dma("pool", V_o[:, :, tt, :].rearrange("h p c -> p h c"), vs_[:], owner=vs_t, reads=[vs_t])
    kb.finish(h_tok + [kaug_t] + [t for _, t in qaug] + [t for _, t in vst] + [t for _, t in kst] + [t for _, t in qst])
    kb_barrier(kb)
    kb.pop_scope()
    return kb


def build_L4(lambda_init, nq_chunks=S // 512):
    kb = KB()
    nc = kb.nc
    Q1 = kb.dram("Q1", [68, S], BF16, "ExternalInput")
    Q2 = kb.dram("Q2", [68, S], BF16, "ExternalInput")
    K1 = kb.dram("K1", [68, S], BF16, "ExternalInput")
    K2 = kb.dram("K2", [68, S], BF16, "ExternalInput")
    Vd = kb.dram("V", [128, S // 128, 129], BF16, "ExternalInput")
    ident_d = kb.dram("ident", [128, 128], F32, "ExternalInput")
    mask_d = kb.dram("mask", [128, 128], F32, "ExternalInput")
    lam_d = kb.dram("lam", [4, 64], F32, "ExternalInput")
    subln_d = kb.dram("subln", [1, 128], F32, "ExternalInput")
    O = kb.dram("O", [S, 128], BF16, "ExternalOutput")
    kb.make_banks()
    setup_consts(kb, ident_d)
    setup_mask(kb, mask_d)
    lam_sb = kb.sb("lam_sb", [128, 4, 64], F32); lam_t = kb.tok()
    kb.dma("sp", lam_sb[:].rearrange("p a d -> p (a d)"), lam_d.rearrange("a d -> (a d)").partition_broadcast(128),
           owner=lam_t, writes=[lam_t])
    lj = kb.sb("lam_j", [128, 64], F32); lv = kb.sb("lam_v", [128, 4], F32); lv_t = kb.tok()
    kb.op("dve", lambda: nc.vector.tensor_tensor(out=lj[:], in0=lam_sb[:, 0, :], in1=lam_sb[:, 1, :], op=ALU.mult),
          reads=[lam_t], writes=[lv_t])
    kb.op("dve", lambda: nc.vector.tensor_reduce(out=lv[:, 0:1], in_=lj[:], axis=mybir.AxisListType.X, op=ALU.add),
          reads=[lv_t], writes=[lv_t])
    kb.op("dve", lambda: nc.vector.tensor_tensor(out=lj[:], in0=lam_sb[:, 2, :], in1=lam_sb[:, 3, :], op=ALU.mult),
          reads=[lam_t, lv_t], writes=[lv_t])
    kb.op("dve", lambda: nc.vector.tensor_reduce(out=lv[:, 1:2], in_=lj[:], axis=mybir.AxisListType.X, op=ALU.add),
          reads=[lv_t], writes=[lv_t])
    kb.op("act", lambda: nc.scalar.activation(out=lv[:, 0:2], in_=lv[:, 0:2], func=AF.Exp), reads=[lv_t], writes=[lv_t])
    kb.op("dve", lambda: nc.vector.scalar_tensor_tensor(out=lv[:, 2:3], in0=lv[:, 1:2], scalar=-float(lambda_init),
                                                        in1=lv[:, 0:1], op0=ALU.add, op1=ALU.subtract),
          reads=[lv_t], writes=[lv_t])
    sub_sb = kb.sb("sub_sb", [128, 128], F32); sub_t = kb.tok()
    kb.dma("sp", sub_sb[:], subln_d.partition_broadcast(128), owner=sub_t, writes=[sub_t])
    kb.op("dve", lambda: nc.vector.tensor_scalar(out=sub_sb[:], in0=sub_sb[:], scalar1=float(1.0 - lambda_init),
                                                 scalar2=None, op0=ALU.mult), reads=[sub_t], writes=[sub_t])

    K1_sb = kb.sb("K1_sb", [68, S], BF16)
    K2_sb = kb.sb("K2_sb", [68, S], BF16)
    V_sb = kb.sb("V_sb", [128, S // 128, 129], BF16)
    ptoks = [kb.tok() for _ in range(8)]
    for r in range(8):
        t = ptoks[r]
        kb.dma("sp", K1_sb[:, r * 2048:(r + 1) * 2048], K1[:, r * 2048:(r + 1) * 2048], owner=t, writes=[t])
        kb.dma("sp", K2_sb[:, r * 2048:(r + 1) * 2048], K2[:, r * 2048:(r + 1) * 2048], owner=t, pwrites=[t])
        kb.dma("sp", V_sb[:, r * 16:(r + 1) * 16, :], Vd[:, r * 16:(r + 1) * 16, :], owner=t, pwrites=[t])
    o1 = kb.sb("o1n", [128, 4, 128], F32); o1_t = kb.tok()
    att = kb.sb("attd", [128, 4, 128], F32); att_t = kb.tok()
    junk = kb.sb("junk4", [128, 128], F32); junk_t = kb.tok()
    ost = [(kb.sb("ost%d" % i, [128, 4, 128], BF16), kb.tok()) for i in range(2)]
    rc = kb.sb("rc", [128, 8], F32); rc_t = kb.tok()
    ss = kb.sb("ss4", [128, 8], F32); ss_t = kb.tok()
    Ov = O.rearrange("(c i p) d -> c p i d", i=4, p=128)

    def epilogue(qc, pi, oset):
        for b, (ob, obtok) in enumerate(oset):
            sums = ob[:, 0:258].rearrange("p (i c) -> p i c", c=129)[:, :, 128:129]
            kb.op("dve", lambda: nc.vector.reciprocal(out=rc[:, 2 * b:2 * b + 2].rearrange("p (i c) -> p i c", c=1),
                                                     in_=sums), reads=[obtok], writes=[rc_t])
            for i in range(2):
                qi = 2 * b + i
                if pi == 0:
                    kb.op("dve", lambda: nc.vector.tensor_scalar(
                        out=o1[:, qi, :], in0=ob[:, i * 129:i * 129 + 128], scalar1=rc[:, qi:qi + 1], scalar2=None,
                        op0=ALU.mult), reads=[obtok, rc_t], **(dict(writes=[o1_t]) if qi == 0 else dict(pwrites=[o1_t])))
                else:
                    kb.op("dve", lambda: nc.vector.tensor_scalar(
                        out=att[:, qi, :], in0=ob[:, i * 129:i * 129 + 128], scalar1=rc[:, qi:qi + 1],
                        scalar2=lv[:, 2:3], op0=ALU.mult, op1=ALU.mult),
                        reads=[obtok, rc_t, lv_t], **(dict(writes=[att_t]) if qi == 0 else dict(pwrites=[att_t])))
                    kb.op("pool", lambda: nc.gpsimd.tensor_tensor(out=att[:, qi, :], in0=att[:, qi, :], in1=o1[:, qi, :],
                                                                  op=ALU.add), reads=[o1_t], pwrites=[att_t])
        if pi == 1:
            o_sb, o_tok = ost[qc % 2]
            for qi in range(4):
                kb.op("act", lambda: nc.scalar.activation(out=junk[:], in_=att[:, qi, :], func=AF.Square,
                                                          accum_out=ss[:, qi:qi + 1]),
                      reads=[att_t], writes=[junk_t], **(dict(pwrites=[ss_t])))
            kb.op("act", lambda: nc.scalar.activation(out=ss[:, 4:8], in_=ss[:, 0:4], func=AF.Ln, scale=1.0 / 128,
                                                      bias=kb.eps_ap), reads=[ss_t, kb.eps_tok], writes=[ss_t])
            kb.op("act", lambda: nc.scalar.activation(out=ss[:, 4:8], in_=ss[:, 4:8], func=AF.Exp, scale=-0.5),
                  reads=[ss_t], writes=[ss_t])
            for qi in range(4):
                kb.op("dve", lambda: nc.vector.scalar_tensor_tensor(
                    out=o_sb[:, qi, :], in0=att[:, qi, :], scalar=ss[:, 4 + qi:5 + qi], in1=sub_sb[:],
                    op0=ALU.mult, op1=ALU.mult), reads=[att_t, ss_t, sub_t],
                    **(dict(writes=[o_tok]) if qi == 0 else dict(pwrites=[o_tok])))
            kb.dma("sp", Ov[qc], o_sb[:], owner=o_tok, reads=[o_tok])

    attention_core(kb, [dict(K=[(K1_sb, 68)], Q=[(Q1, 68)]), dict(K=[(K2_sb, 68)], Q=[(Q2, 68)])], V_sb,
                   lambda kbi: ptoks[kbi // 16], epilogue, nq_chunks=nq_chunks)
    kb.finish([t for _, t in ost])
    return kb


def build_L5():
    kb = KB()
    nc = kb.nc
    h1 = kb.dram("h1", [TPC, D], F32, "ExternalInput")
    O_d = kb.dram("O", [TPC, D], BF16, "ExternalInput")
    ident_d = kb.dram("ident", [128, 128], F32, "ExternalInput")
    w_o = kb.dram("w_o", [D, D], F32, "ExternalInput")
    g_ffn = kb.dram("g_ffn", [D], F32, "ExternalInput")
    w_gu = kb.dram("w_gu", [D, 2 * DFF], F32, "ExternalInput")
    w_d = kb.dram("w_d", [DFF, D], F32, "ExternalInput")
    g_fin = kb.dram("g_fin", [1, D], F32, "ExternalInput")
    out = kb.dram("out", [TPC, D], F32, "ExternalOutput")
    kb.make_banks()
    setup_consts(kb, ident_d)
    h_res, h_tok = make_h(kb)

    def resid_load(tt):
        kb.dma("sp", h_res[:, tt, :], h1[tt * 128:(tt + 1) * 128, :], owner=h_tok[tt], writes=[h_tok[tt]])

    phase_attn_out(kb, resid_load, O_d, w_o, h_res, h_tok)
    phase_ffn(kb, h_res, h_tok, g_ffn, w_gu, w_d)
    gb = kb.sb("f_g", [128, D], F32); gb_t = kb.tok()
    kb.dma("sp", gb[:], g_fin.partition_broadcast(128), owner=gb_t, writes=[gb_t])
    junk = kb.sb("f_junk", [128, D], BF16); junk_t = kb.tok()
    st4 = kb.sb("f_st", [128, 4], F32); st_t = kb.tok()
    ob = [(kb.sb("f_o%d" % i, [128, D], F32), kb.tok()) for i in range(2)]
    for tt in range(NT):
        kb.op("act", lambda: nc.scalar.activation(out=junk[:], in_=h_res[:, tt, :], func=AF.Square,
                                                  accum_out=st4[:, 0:1]), reads=[h_tok[tt]], writes=[junk_t, st_t])
        rms_scale(kb, st4[:, 0:1], st4[:, 1:2], D, st_t, st_t)
        o_sb, o_t = ob[tt % 2]
        kb.op("dve", lambda: nc.vector.scalar_tensor_tensor(out=o_sb[:], in0=h_res[:, tt, :], scalar=st4[:, 1:2],
                                                            in1=gb[:], op0=ALU.mult, op1=ALU.mult),
              reads=[h_tok[tt], st_t, gb_t], writes=[o_t])
        kb.dma("sp", out[tt * 128:(tt + 1) * 128, :], o_sb[:], owner=o_t, reads=[o_t])
    kb.finish([t for _, t in ob])
    return kb


def _run(kb, maps):
    res = run_bass_kernel_spmd(kb.nc, maps, core_ids=list(range(NCORES)))
    return res.results


def _cat_tokens(results, key, h):
    return np.ascontiguousarray(np.concatenate([np.asarray(results[r][key])[h] for r in range(NCORES)], axis=-1))


def _cat_v(results, h):
    return np.ascontiguousarray(np.concatenate([np.asarray(results[r]["V"])[h] for r in range(NCORES)], axis=1))


def kernel(x, positions, attn_norm, ffn_norm, final_norm,
           mla_w_dq, mla_q_norm, mla_w_uq, mla_w_dkv, mla_kv_norm, mla_w_ukv, mla_w_o,
           diff_kv_norm, diff_w_k, diff_w_v, diff_w_q,
           diff_lambda_q1, diff_lambda_k1, diff_lambda_q2, diff_lambda_k2,
           diff_subln, diff_w_o, ffn_w_gate_up, ffn_w_down):
    f32 = np.float32
    x = np.asarray(x, f32)
    positions = np.asarray(positions, np.int32)
    A = lambda a: np.ascontiguousarray(np.asarray(a, f32))
    inv = (10000.0 ** (-np.arange(32, dtype=f32) * 2.0 / 64)).astype(f32)
    invf = np.concatenate([inv, inv]).reshape(64, 1).astype(f32)
    ident = np.eye(128, dtype=f32)
    mask = np.where(np.arange(128)[:, None] > np.arange(128)[None, :], NEG, 0.0).astype(f32)
    coef = np.array([[128, 0, 0, 0, 0, 1],
                     [0, 1, 0, 0, 0, 1],
                     [0, 0, 1, -128, 0, 0],
                     [0, 0, 1, 0, -1, 0]], f32)
    coef = np.ascontiguousarray(np.concatenate([coef, np.zeros((4, 2), f32)], axis=1))
    sl = [slice(r * TPC, (r + 1) * TPC) for r in range(NCORES)]
    xs = [np.ascontiguousarray(x[0, s]) for s in sl]
    ps = [np.ascontiguousarray(positions[:, s]) for s in sl]

    r1 = _run(build_L1(), [dict(x=xs[r], pos=ps[r], invf=invf, ident=ident, g_attn=A(attn_norm[0]),
                                g_q=A(mla_q_norm[0]), g_kv=A(mla_kv_norm[0]), w_dq=A(mla_w_dq[0]),
                                w_uq=A(mla_w_uq[0]), w_dkv=A(mla_w_dkv[0]), w_ukv=A(mla_w_ukv[0]))
                           for r in range(NCORES)])
    ktb = np.ascontiguousarray(np.concatenate([np.asarray(r1[r]["KTb"]) for r in range(NCORES)], axis=-1))
    r2 = _run(build_L2(), [dict(QTa=_cat_tokens(r1, "QTa", h), QTb=_cat_tokens(r1, "QTb", h),
                                KTa=_cat_tokens(r1, "KTa", h), KTb=ktb, V=_cat_v(r1, h), ident=ident, mask=mask)
                           for h in range(NCORES)])
    del r1
    O0 = np.concatenate([np.asarray(r2[h]["O"]) for h in range(H)], axis=1)
    del r2
    r3 = _run(build_L3(), [dict(x=xs[r], O=np.ascontiguousarray(O0[sl[r]]), pos=ps[r], coef=coef, ident=ident,
                                w_o=A(mla_w_o[0]), g_ffn=A(ffn_norm[0]), w_gu=A(ffn_w_gate_up[0]),
                                w_d=A(ffn_w_down[0]), g_kv=A(diff_kv_norm), g_a1=A(attn_norm[1]),
                                w_k=A(diff_w_k), w_v=A(diff_w_v), w_q=A(diff_w_q[0]))
                           for r in range(NCORES)])
    lambda_init = 0.8 - 0.6 * float(np.exp(-0.3 * 1))
    lam = np.ascontiguousarray(np.stack([A(diff_lambda_q1[0]), A(diff_lambda_k1[0]),
                                         A(diff_lambda_q2[0]), A(diff_lambda_k2[0])]))
    r4 = _run(build_L4(lambda_init), [dict(Q1=_cat_tokens(r3, "Q1", h), Q2=_cat_tokens(r3, "Q2", h),
                                           K1=_cat_tokens(r3, "K1", h), K2=_cat_tokens(r3, "K2", h),
                                           V=_cat_v(r3, h), ident=ident, mask=mask, lam=lam,
                                           subln=A(diff_subln[0]).reshape(1, 128))
                                      for h in range(NCORES)])
    h1 = [np.asarray(r3[r]["h1"]) for r in range(NCORES)]
    del r3
    O1 = np.concatenate([np.asarray(r4[h]["O"]) for h in range(H)], axis=1)
    del r4
    r5 = _run(build_L5(), [dict(h1=h1[r], O=np.ascontiguousarray(O1[sl[r]]), ident=ident, w_o=A(diff_w_o[0]),
                                g_ffn=A(ffn_norm[1]), w_gu=A(ffn_w_gate_up[1]), w_d=A(ffn_w_down[1]),
                                g_fin=A(final_norm).reshape(1, D))
                           for r in range(NCORES)])
    out = np.concatenate([np.asarray(r5[r]["out"]) for r in range(NCORES)], axis=0)
    return out.reshape(1, S, D).astype(f32)
```

```python
from contextlib import ExitStack
import numpy as np
import ml_dtypes
import concourse.bass as bass
import concourse.mybir as mybir
from concourse.bass_utils import run_bass_kernel_spmd

F32 = mybir.dt.float32
BF16 = mybir.dt.bfloat16
I32 = mybir.dt.int32
AF = mybir.ActivationFunctionType
ALU = mybir.AluOpType

NCORES = 8
S = 16384
D = 1024
TPC = S // NCORES
NT = TPC // 128
H = 8
DFF = 2816
EPS = 1e-6
NEG = -30000.0
TWO_PI = 6.283185307179586
C1 = 6.28125
C2 = TWO_PI - C1
PI = 3.141592653589793
PI_SAFE = 3.1415925
SAME_ENGINE_SYNC = True


class Tok:
    __slots__ = ("w", "r", "sem", "cnt")

    def __init__(self):
        self.w = {}
        self.r = {}
        self.sem = None
        self.cnt = 0


class KB:
    def __init__(self):
        self.nc = bass.Bass("TRN2", target_bir_lowering=False)
        self.st = ExitStack()
        self.sems = []
        self.eng = {}
        nc = self.nc
        for n, h in (("pe", nc.tensor), ("act", nc.scalar), ("dve", nc.vector),
                     ("pool", nc.gpsimd), ("sp", nc.sync)):
            si = self.new_sem("e_" + n)
            self.eng[n] = {"h": h, "sem": si, "cnt": 0, "seen": {}}
        self.nbank = 0
        self.banks = []
        self.dma_toks = []
        self.scopes = []
        self.scope_toks = []
        self.free_sems = []

    def tok(self):
        t = Tok()
        self.dma_toks.append(t)
        if self.scope_toks:
            self.scope_toks[-1].append(t)
        return t

    def push_scope(self):
        self.scopes.append(ExitStack())
        self.scope_toks.append([])

    def pop_scope(self):
        for t in self.scope_toks.pop():
            if t.sem is not None:
                self.free_sems.append((t.sem, t.cnt))
                t.sem = None
        self.scopes.pop().close()

    def new_sem(self, name):
        s = self.st.enter_context(self.nc.semaphore(name))
        self.sems.append(s)
        return len(self.sems) - 1

    def sb(self, name, shape, dt):
        st = self.scopes[-1] if self.scopes else self.st
        self.nsb = getattr(self, "nsb", 0) + 1
        return st.enter_context(self.nc.sbuf_tensor("%s_%d" % (name, self.nsb), shape, dt))

    def ps(self, name, shape, dt):
        return self.st.enter_context(self.nc.psum_tensor(name, shape, dt))

    def dram(self, name, shape, dt, kind):
        return self.nc.dram_tensor(name, shape, dt, kind=kind).ap()

    def make_banks(self):
        for i in range(8):
            t = self.ps("bank%d" % i, [128, 512], F32)
            self.banks.append((t, Tok()))

    def bank(self):
        b = self.banks[self.nbank % 8]
        self.nbank += 1
        return b

    def _wait(self, e, deps, dma=False):
        E = self.eng[e]
        for sem, val in deps.items():
            if sem == E["sem"] and not dma and (e == "pe" or not SAME_ENGINE_SYNC):
                continue
            if E["seen"].get(sem, 0) >= val:
                continue
            E["h"].wait_ge(self.sems[sem], val)
            E["seen"][sem] = val

    def _deps(self, reads, writes, pwrites=(), skip_sem=None):
        deps = {}
        for t in reads:
            for s, v in t.w.items():
                if v > deps.get(s, 0):
                    deps[s] = v
        for t in writes:
            for d in (t.w, t.r):
                for s, v in d.items():
                    if v > deps.get(s, 0):
                        deps[s] = v
        for t in pwrites:
            for d in (t.w, t.r):
                for s, v in d.items():
                    if v > deps.get(s, 0):
                        deps[s] = v
        if skip_sem is not None:
            deps.pop(skip_sem, None)
        return deps

    def _post(self, ticket, reads, writes, pwrites=()):
        s, v = ticket
        for t in reads:
            t.r[s] = v
        for t in writes:
            t.w = {s: v}
            t.r = {}
        for t in pwrites:
            t.w[s] = v

    def op(self, e, fn, reads=(), writes=(), pwrites=()):
        E = self.eng[e]
        self._wait(e, self._deps(reads, writes, pwrites))
        inst = fn()
        E["cnt"] += 1
        inst.then_inc(self.sems[E["sem"]], 1)
        self._post((E["sem"], E["cnt"]), reads, writes, pwrites)

    def dma(self, q, out, in_, owner, reads=(), writes=(), pwrites=()):
        E = self.eng[q]
        if owner.sem is None:
            if self.free_sems:
                owner.sem, owner.cnt = self.free_sems.pop()
            else:
                owner.sem = self.new_sem("d%d" % len(self.sems))
        self._wait(q, self._deps(reads, writes, pwrites, skip_sem=owner.sem), dma=True)
        owner.cnt += 16
        E["h"].dma_start(out=out, in_=in_).then_inc(self.sems[owner.sem], 16)
        self._post((owner.sem, owner.cnt), reads, writes, pwrites)

    def finish(self, toks):
        deps = {}
        for t in toks:
            for d in (t.w, t.r):
                for s, v in d.items():
                    if v > deps.get(s, 0):
                        deps[s] = v
        self._wait("sp", deps, dma=True)


def rms_scale(kb, ss_ap, r_ap, n, tok_ss, tok_r):
    nc = kb.nc
    kb.op("act", lambda: nc.scalar.activation(out=r_ap, in_=ss_ap, func=AF.Ln, scale=1.0 / n, bias=kb.eps_ap),
          reads=[tok_ss, kb.eps_tok], writes=[tok_r])
    kb.op("act", lambda: nc.scalar.activation(out=r_ap, in_=r_ap, func=AF.Exp, scale=-0.5),
          reads=[tok_r], writes=[tok_r])


def setup_consts(kb, ident_dram):
    nc = kb.nc
    kb.ident = kb.sb("ident_sb", [128, 128], BF16)
    kb.ident_tok = Tok()
    kb.dma("pool", kb.ident[:], ident_dram, owner=kb.ident_tok, writes=[kb.ident_tok])
    kb.eps_t = kb.sb("eps_t", [128, 1], F32)
    kb.eps_tok = Tok()
    kb.eps_ap = kb.eps_t[:, 0:1]
    kb.op("dve", lambda: nc.vector.memset(kb.eps_t[:], EPS), writes=[kb.eps_tok])


def transpose_to(kb, src_ap, src_tok, nblk, kpart, dst_fn, dst_tok, evac="act", pw=False):
    nc = kb.nc
    bt, btok = kb.bank()
    bv = bt[:].bitcast(BF16)
    for j in range(nblk):
        kb.op("pe", lambda j=j: nc.tensor.transpose(bv[0:kpart, j * 128:(j + 1) * 128], src_ap(j), kb.ident[:]),
              reads=[src_tok, kb.ident_tok], writes=[btok])
    srcv = bv[0:kpart, 0:nblk * 128].rearrange("p (j t) -> p j t", j=nblk)
    wr = dict(pwrites=[dst_tok]) if pw else dict(writes=[dst_tok])
    if evac == "act":
        kb.op("act", lambda: nc.scalar.copy(out=dst_fn(), in_=srcv), reads=[btok], **wr)
    else:
        kb.op("dve", lambda: nc.vector.tensor_copy(out=dst_fn(), in_=srcv), reads=[btok], **wr)


def build_L1():
    kb = KB()
    nc = kb.nc
    x = kb.dram("x", [TPC, D], F32, "ExternalInput")
    pos = kb.dram("pos", [1, TPC], I32, "ExternalInput")
    invf = kb.dram("invf", [64, 1], F32, "ExternalInput")
    ident_d = kb.dram("ident", [128, 128], F32, "ExternalInput")
    g_attn = kb.dram("g_attn", [D], F32, "ExternalInput")
    g_q = kb.dram("g_q", [512], F32, "ExternalInput")
    g_kv = kb.dram("g_kv", [256], F32, "ExternalInput")
    w_dq = kb.dram("w_dq", [D, 512], F32, "ExternalInput")
    w_uq = kb.dram("w_uq", [512, 1536], F32, "ExternalInput")
    w_dkv = kb.dram("w_dkv", [D, 320], F32, "ExternalInput")
    w_ukv = kb.dram("w_ukv", [256, 2048], F32, "ExternalInput")
    QTa_o = kb.dram("QTa", [H, 128, TPC], BF16, "ExternalOutput")
    QTb_o = kb.dram("QTb", [H, 64, TPC], BF16, "ExternalOutput")
    KTa_o = kb.dram("KTa", [H, 128, TPC], BF16, "ExternalOutput")
    KTb_o = kb.dram("KTb", [64, TPC], BF16, "ExternalOutput")
    V_o = kb.dram("V", [H, 128, NT, 129], BF16, "ExternalOutput")

    kb.make_banks()
    setup_consts(kb, ident_d)
    scale = (128 + 64) ** -0.5

    gA = kb.sb("gA", [128, 8], F32); gA_t = Tok()
    gQ = kb.sb("gQ", [128, 4], F32); gQ_t = Tok()
    gK = kb.sb("gK", [128, 2], F32); gK_t = Tok()
    with nc.allow_non_contiguous_dma(reason="tiny gain vectors"):
        kb.dma("sp", gA[:], g_attn.rearrange("(j p) -> p j", p=128), owner=gA_t, writes=[gA_t])
        kb.dma("sp", gQ[:], g_q.rearrange("(j p) -> p j", p=128), owner=gQ_t, writes=[gQ_t])
        kb.dma("sp", gK[:], g_kv.rearrange("(j p) -> p j", p=128), owner=gK_t, writes=[gK_t])

    stage = kb.sb("stage", [128, 6144], F32); stage_t = Tok()
    Wdq = kb.sb("Wdq", [128, 8, 512], BF16)
    Wdkv = kb.sb("Wdkv", [128, 8, 320], BF16)
    Wkrot = kb.sb("Wkrot", [128, 8, 64], BF16)
    Wuq = kb.sb("Wuq", [128, 4, 1536], BF16)
    Wqrot = kb.sb("Wqrot", [128, 4, 8, 64], BF16)
    Wukv = kb.sb("Wukv", [128, 2, 2048], BF16)

    def prep(w_dram, g_sb, g_tok, out_bf, kc, ncol, extra=None):
        sview = stage[:, 0:kc * ncol].rearrange("p (j n) -> p j n", j=kc)
        kb.dma("sp", sview, w_dram.rearrange("(j p) n -> p j n", p=128), owner=stage_t, writes=[stage_t])
        wt = Tok()
        for j in range(kc):
            kb.op("dve", lambda j=j: nc.vector.tensor_scalar(
                out=out_bf[:, j, :], in0=sview[:, j, :], scalar1=g_sb[:, j:j + 1],
                scalar2=(None if extra is None else float(extra)), op0=ALU.mult,
                **({} if extra is None else {"op1": ALU.mult})),
                reads=[stage_t, g_tok], writes=[wt])
        return wt

    Wdq_t = prep(w_dq, gA, gA_t, Wdq, 8, 512)
    Wdkv_t = prep(w_dkv, gA, gA_t, Wdkv, 8, 320)
    Wuq_t = prep(w_uq, gQ, gQ_t, Wuq, 4, 1536, extra=scale)
    Wukv_t = prep(w_ukv, gK, gK_t, Wukv, 2, 2048)
    Wkrot_t = Tok()
    kb.op("dve", lambda: nc.vector.tensor_scalar(out=Wkrot[:, :, 0:32], in0=Wdkv[:, :, 288:320], scalar1=-1.0,
                                                 scalar2=None, op0=ALU.mult), reads=[Wdkv_t], writes=[Wkrot_t])
    kb.op("dve", lambda: nc.vector.tensor_copy(out=Wkrot[:, :, 32:64], in_=Wdkv[:, :, 256:288]),
          reads=[Wdkv_t], writes=[Wkrot_t])
    Wqrot_t = Tok()
    Wuq_v = Wuq[:].rearrange("p j (h d) -> p j h d", h=8)
    for j in range(4):
        kb.op("dve", lambda j=j: nc.vector.tensor_scalar(out=Wqrot[:, j, :, 0:32], in0=Wuq_v[:, j, :, 160:192],
                                                         scalar1=-1.0, scalar2=None, op0=ALU.mult),
              reads=[Wuq_t], writes=[Wqrot_t])
        kb.op("dve", lambda j=j: nc.vector.tensor_copy(out=Wqrot[:, j, :, 32:64], in_=Wuq_v[:, j, :, 128:160]),
              reads=[Wuq_t], writes=[Wqrot_t])

    cosT = kb.sb("cosT", [64, TPC], F32); sinT = kb.sb("sinT", [64, TPC], F32); tab_t = Tok()
    invf_sb = kb.sb("invf_sb", [64, 1], F32); invf_t = Tok()
    kb.dma("sp", invf_sb[:], invf, owner=invf_t, writes=[invf_t])
    posi = kb.sb("posi", [64, TPC], I32); posi_t = Tok()
    kb.dma("sp", posi[:], pos.partition_broadcast(64), owner=posi_t, writes=[posi_t])
    A0 = stage[0:64, 0:TPC]; A1 = stage[0:64, TPC:2 * TPC]; A2 = stage[0:64, 2 * TPC:3 * TPC]
    A1i = A1.bitcast(I32)
    V = nc.vector
    sq = [
        (lambda: V.tensor_copy(out=A0, in_=posi[:]), [posi_t]),
        (lambda: V.tensor_scalar(out=A0, in0=A0, scalar1=invf_sb[:, 0:1], scalar2=None, op0=ALU.mult), [invf_t]),
        (lambda: V.tensor_scalar(out=A2, in0=A0, scalar1=1.0 / TWO_PI, scalar2=None, op0=ALU.mult), []),
        (lambda: V.tensor_copy(out=A1i, in_=A2), []),
        (lambda: V.tensor_copy(out=A2, in_=A1i), []),
        (lambda: V.scalar_tensor_tensor(out=A0, in0=A2, scalar=-C1, in1=A0, op0=ALU.mult, op1=ALU.add), []),
        (lambda: V.scalar_tensor_tensor(out=A0, in0=A2, scalar=-C2, in1=A0, op0=ALU.mult, op1=ALU.add), []),
        (lambda: V.tensor_scalar(out=A2, in0=A0, scalar1=PI, scalar2=-TWO_PI, op0=ALU.is_gt, op1=ALU.mult), []),
        (lambda: V.tensor_tensor(out=A0, in0=A0, in1=A2, op=ALU.add), []),
        (lambda: V.tensor_scalar(out=A2, in0=A0, scalar1=-PI, scalar2=TWO_PI, op0=ALU.is_lt, op1=ALU.mult), []),
        (lambda: V.tensor_tensor(out=A0, in0=A0, in1=A2, op=ALU.add), []),
        (lambda: V.tensor_scalar(out=A1, in0=A0, scalar1=PI / 2, scalar2=None, op0=ALU.add), []),
        (lambda: V.tensor_scalar(out=A2, in0=A1, scalar1=PI, scalar2=-TWO_PI, op0=ALU.is_gt, op1=ALU.mult), []),
        (lambda: V.tensor_tensor(out=A1, in0=A1, in1=A2, op=ALU.add), []),
        (lambda: V.tensor_scalar(out=A0, in0=A0, scalar1=PI_SAFE, scalar2=-PI_SAFE, op0=ALU.min, op1=ALU.max), []),
        (lambda: V.tensor_scalar(out=A1, in0=A1, scalar1=PI_SAFE, scalar2=-PI_SAFE, op0=ALU.min, op1=ALU.max), []),
    ]
    for fn, rd in sq:
        kb.op("dve", fn, reads=[stage_t] + rd, writes=[stage_t])
    kb.op("act", lambda: nc.scalar.activation(out=sinT[:], in_=A0, func=AF.Sin), reads=[stage_t], writes=[tab_t])
    kb.op("act", lambda: nc.scalar.activation(out=cosT[:], in_=A1, func=AF.Sin), reads=[stage_t], pwrites=[tab_t])

    xt = [kb.sb("xt%d" % i, [128, D], F32) for i in range(2)]; xt_t = [Tok(), Tok()]
    junk = kb.sb("junk", [128, D], BF16); junk_t = Tok()
    xs = kb.sb("xs", [128, D], BF16); xs_t = Tok()
    st4 = kb.sb("st4", [128, 8], F32); st_t = Tok()
    hnT = kb.sb("hnT", [128, 8, 512], BF16); hnT_t = Tok()
    cqs = kb.sb("cqs", [128, 512], BF16); cqs_t = Tok()
    ckvs = kb.sb("ckvs", [128, 256], BF16); ckvs_t = Tok()
    cqT = kb.sb("cqT", [128, 4, 512], BF16); cqT_t = Tok()
    ckvT = kb.sb("ckvT", [128, 2, 512], BF16); ckvT_t = Tok()
    Vaug = kb.sb("Vaug", [128, H, NT, 129], BF16); Vaug_t = Tok()
    kb.op("pool", lambda: nc.gpsimd.memset(Vaug[:], 1.0), writes=[Vaug_t])
    qa_st = kb.sb("qa_st", [128, H, 512], BF16); qa_t = Tok()
    qb_st = kb.sb("qb_st", [64, H, 512], BF16); qb_t = Tok()
    ka_st = kb.sb("ka_st", [128, H, 512], BF16); ka_t = Tok()
    kb_st = kb.sb("kb_st", [64, 512], BF16); kbs_t = Tok()
    tmp1 = kb.sb("tmp1", [64, 512], F32); tmp1_t = Tok()
    tmp2 = kb.sb("tmp2", [64, 512], F32); tmp2_t = Tok()

    Wukv_v = Wukv[:].rearrange("p j (h d) -> p j h d", h=8)

    def rope_combine(raw_b, raw_tok, rot_b, rot_tok, c, out_ap, out_tok, pw):
        cs = cosT[:, c * 512:(c + 1) * 512]
        sn = sinT[:, c * 512:(c + 1) * 512]
        kb.op("dve", lambda: nc.vector.tensor_tensor(out=tmp1[:], in0=raw_b[0:64, :], in1=cs, op=ALU.mult),
              reads=[raw_tok, tab_t], writes=[tmp1_t])
        kb.op("dve", lambda: nc.vector.tensor_tensor(out=tmp2[:], in0=rot_b[0:64, :], in1=sn, op=ALU.mult),
              reads=[rot_tok, tab_t], writes=[tmp2_t])
        wr = dict(pwrites=[out_tok]) if pw else dict(writes=[out_tok])
        kb.op("pool", lambda: nc.gpsimd.tensor_tensor(out=out_ap, in0=tmp1[:], in1=tmp2[:], op=ALU.add),
              reads=[tmp1_t, tmp2_t], **wr)

    for c in range(NT // 4):
        for i in range(4):
            tt = 4 * c + i
            xb = xt[tt % 2]; xbt = xt_t[tt % 2]
            kb.dma("sp", xb[:], x[tt * 128:(tt + 1) * 128, :], owner=xbt, writes=[xbt])
            kb.op("act", lambda: nc.scalar.activation(out=junk[:], in_=xb[:], func=AF.Square,
                                                      accum_out=st4[:, 0:1]),
                  reads=[xbt], writes=[junk_t, st_t])
            rms_scale(kb, st4[:, 0:1], st4[:, 1:2], D, st_t, st_t)
            kb.op("dve", lambda: nc.vector.tensor_scalar(out=xs[:], in0=xb[:], scalar1=st4[:, 1:2], scalar2=None,
                                                         op0=ALU.mult), reads=[xbt, st_t], writes=[xs_t])
            transpose_to(kb, lambda j: xs[:, j * 128:(j + 1) * 128], xs_t, 8, 128,
                         lambda: hnT[:, :, i * 128:(i + 1) * 128], hnT_t, evac="act", pw=(i > 0))
            cq_b, cq_bt = kb.bank()
            for j in range(8):
                kb.op("pe", lambda j=j: nc.tensor.matmul(cq_b[:, 0:512], hnT[:, j, i * 128:(i + 1) * 128],
                                                         Wdq[:, j, :], start=(j == 0), stop=(j == 7)),
                      reads=[hnT_t, Wdq_t], writes=[cq_bt])
            ck_b, ck_bt = kb.bank()
            for j in range(8):
                kb.op("pe", lambda j=j: nc.tensor.matmul(ck_b[:, 0:256], hnT[:, j, i * 128:(i + 1) * 128],
                                                         Wdkv[:, j, 0:256], start=(j == 0), stop=(j == 7)),
                      reads=[hnT_t, Wdkv_t], writes=[ck_bt])
            kb.op("act", lambda: nc.scalar.activation(out=junk[:, 0:512], in_=cq_b[:, 0:512], func=AF.Square,
                                                      accum_out=st4[:, 2:3]), reads=[cq_bt], writes=[junk_t, st_t])
            rms_scale(kb, st4[:, 2:3], st4[:, 3:4], 512, st_t, st_t)
            kb.op("dve", lambda: nc.vector.tensor_scalar(out=cqs[:], in0=cq_b[:, 0:512], scalar1=st4[:, 3:4],
                                                         scalar2=None, op0=ALU.mult),
                  reads=[cq_bt, st_t], writes=[cqs_t])
            kb.op("act", lambda: nc.scalar.activation(out=junk[:, 0:256], in_=ck_b[:, 0:256], func=AF.Square,
                                                      accum_out=st4[:, 4:5]), reads=[ck_bt], writes=[junk_t, st_t])
            rms_scale(kb, st4[:, 4:5], st4[:, 5:6], 256, st_t, st_t)
            kb.op("dve", lambda: nc.vector.tensor_scalar(out=ckvs[:], in0=ck_b[:, 0:256], scalar1=st4[:, 5:6],
                                                         scalar2=None, op0=ALU.mult),
                  reads=[ck_bt, st_t], writes=[ckvs_t])
            transpose_to(kb, lambda j: cqs[:, j * 128:(j + 1) * 128], cqs_t, 4, 128,
                         lambda: cqT[:, :, i * 128:(i + 1) * 128], cqT_t, evac="dve", pw=(i > 0))
            transpose_to(kb, lambda j: ckvs[:, j * 128:(j + 1) * 128], ckvs_t, 2, 128,
                         lambda: ckvT[:, :, i * 128:(i + 1) * 128], ckvT_t, evac="dve", pw=(i > 0))
            for hh in range(2):
                vb, vbt = kb.bank()
                for j in range(2):
                    kb.op("pe", lambda j=j: nc.tensor.matmul(vb[:, 0:512].rearrange("p (h d) -> p h d", h=4),
                                                             ckvT[:, j, i * 128:(i + 1) * 128],
                                                             Wukv_v[:, j, 4 * hh:4 * hh + 4, 128:256],
                                                             start=(j == 0), stop=(j == 1)),
                          reads=[ckvT_t, Wukv_t], writes=[vbt])
                kb.op("act", lambda: nc.scalar.copy(out=Vaug[:, 4 * hh:4 * hh + 4, tt, 0:128],
                                                    in_=vb[:, 0:512].rearrange("p (h d) -> p h d", h=4)),
                      reads=[vbt], pwrites=[Vaug_t])
        csl = slice(c * 512, (c + 1) * 512)
        kp_b, kp_bt = kb.bank()
        for j in range(8):
            kb.op("pe", lambda j=j: nc.tensor.matmul(kp_b[0:64, :], Wdkv[:, j, 256:320], hnT[:, j, :],
                                                     start=(j == 0), stop=(j == 7)),
                  reads=[hnT_t, Wdkv_t], writes=[kp_bt])
        kr_b, kr_bt = kb.bank()
        for j in range(8):
            kb.op("pe", lambda j=j: nc.tensor.matmul(kr_b[0:64, :], Wkrot[:, j, :], hnT[:, j, :],
                                                     start=(j == 0), stop=(j == 7)),
                  reads=[hnT_t, Wkrot_t], writes=[kr_bt])
        rope_combine(kp_b, kp_bt, kr_b, kr_bt, c, kb_st[:], kbs_t, False)
        kb.dma("pool", KTb_o[:, csl], kb_st[:], owner=kbs_t, reads=[kbs_t])
        for h in range(H):
            qa_b, qa_bt = kb.bank()
            for j in range(4):
                kb.op("pe", lambda j=j: nc.tensor.matmul(qa_b[:, :], Wuq[:, j, h * 192:h * 192 + 128], cqT[:, j, :],
                                                         start=(j == 0), stop=(j == 3)),
                      reads=[cqT_t, Wuq_t], writes=[qa_bt])
            kb.op("act", lambda: nc.scalar.copy(out=qa_st[:, h, :], in_=qa_b[:, :]), reads=[qa_bt],
                  **(dict(writes=[qa_t]) if h == 0 else dict(pwrites=[qa_t])))
            qp_b, qp_bt = kb.bank()
            for j in range(4):
                kb.op("pe", lambda j=j: nc.tensor.matmul(qp_b[0:64, :], Wuq[:, j, h * 192 + 128:h * 192 + 192],
                                                         cqT[:, j, :], start=(j == 0), stop=(j == 3)),
                      reads=[cqT_t, Wuq_t], writes=[qp_bt])
            qr_b, qr_bt = kb.bank()
            for j in range(4):
                kb.op("pe", lambda j=j: nc.tensor.matmul(qr_b[0:64, :], Wqrot[:, j, h, :], cqT[:, j, :],
                                                         start=(j == 0), stop=(j == 3)),
                      reads=[cqT_t, Wqrot_t], writes=[qr_bt])
            rope_combine(qp_b, qp_bt, qr_b, qr_bt, c, qb_st[:, h, :], qb_t, h > 0)
            ka_b, ka_bt = kb.bank()
            for j in range(2):
                kb.op("pe", lambda j=j: nc.tensor.matmul(ka_b[:, :], Wukv[:, j, h * 256:h * 256 + 128], ckvT[:, j, :],
                                                         start=(j == 0), stop=(j == 1)),
                      reads=[ckvT_t, Wukv_t], writes=[ka_bt])
            kb.op("dve", lambda: nc.vector.tensor_copy(out=ka_st[:, h, :], in_=ka_b[:, :]), reads=[ka_bt],
                  **(dict(writes=[ka_t]) if h == 0 else dict(pwrites=[ka_t])))
        kb.dma("pool", QTa_o[:, :, csl].rearrange("h p t -> p h t"), qa_st[:], owner=qa_t, reads=[qa_t])
        kb.dma("pool", QTb_o[:, :, csl].rearrange("h p t -> p h t"), qb_st[:], owner=qb_t, reads=[qb_t])
        kb.dma("pool", KTa_o[:, :, csl].rearrange("h p t -> p h t"), ka_st[:], owner=ka_t, reads=[ka_t])
    for h in range(H):
        kb.dma("pool", V_o[h], Vaug[:, h, :, :], owner=Vaug_t, reads=[Vaug_t])
    kb.finish([kbs_t, qa_t, qb_t, ka_t, Vaug_t])
    return kb


def attention_core(kb, passes, V_sb, kv_tok_of_block, epilogue, nq_chunks=S // 512, q_resident=None, qc_order=None, extra_q_reads=(), filler=0):
    nc = kb.nc
    npass = len(passes)
    kb.ac_n = getattr(kb, "ac_n", 0) + 1
    pfx = "ac%d_" % kb.ac_n
    zeros = kb.sb(pfx + "zeros", [128, 512], BF16); zeros_t = Tok()
    kb.op("pool", lambda: nc.gpsimd.memset(zeros[:], 0.0), writes=[zeros_t])
    NQS = 3
    order = list(range(nq_chunks)) if qc_order is None else list(qc_order)
    qslots = []
    for pi, ps_ in enumerate(passes if q_resident is None else []):
        sl = []
        for s in range(NQS):
            parts = [kb.sb(pfx + "q%d_%d_%d" % (pi, s, k), [rows, 512], BF16) for k, (_, rows) in enumerate(ps_["Q"])]
            sl.append((parts, Tok()))
        qslots.append(sl)
    NP = 3
    pbuf = [(kb.sb(pfx + "pT%d" % i, [128, 512], BF16), Tok()) for i in range(NP)]
    sbank = [kb.banks[0], kb.banks[1], kb.banks[7]]
    NS = 3
    osets = [(kb.banks[2], kb.banks[3]), (kb.banks[4], kb.banks[5])]

    def load_q(oi):
        qc = order[oi]
        for pi, ps_ in enumerate(passes):
            parts, tok = qslots[pi][oi % NQS]
            for k, (qd, rows) in enumerate(ps_["Q"]):
                kb.dma("pool", parts[k][:], qd[:, qc * 512:(qc + 1) * 512], owner=tok, reads=list(extra_q_reads),
                       **(dict(writes=[tok]) if k == 0 else dict(pwrites=[tok])))

    units = [(qc, pi) for qc in order for pi in range(npass)]
    oidx = {qc: i for i, qc in enumerate(order)}
    tiles = []
    for ui, (qc, pi) in enumerate(units):
        for kbi in range(4 * qc + 4):
            tiles.append((ui, qc, pi, kbi))

    def emit_qk(ti):
        ui, qc, pi, kbi = tiles[ti]
        sb_t, sb_tok = sbank[ti % NS]
        if q_resident is None:
            parts, qtok = qslots[pi][oidx[qc] % NQS]
            qsl = lambda k, rows, lo: parts[k][0:rows, lo:512]
        else:
            qtok = q_resident(qc)
            qsl = lambda k, rows, lo: passes[pi]["Q"][k][0][0:rows, qc * 512 + lo:(qc + 1) * 512]
        j = kbi - 4 * qc
        lo = 128 * j if j > 0 else 0
        kparts = passes[pi]["K"]
        n = len(kparts)
        for k, (ksb, rows) in enumerate(kparts):
            kb.op("pe", lambda k=k, ksb=ksb, rows=rows: nc.tensor.matmul(
                sb_t[:, lo:512], ksb[0:rows, kbi * 128:(kbi + 1) * 128], qsl(k, rows, lo),
                start=(k == 0), stop=(k == n - 1 and j < 0)),
                reads=[kv_tok_of_block(kbi), qtok], writes=[sb_tok])
        if j >= 0:
            kb.op("pe", lambda: nc.tensor.matmul(sb_t[:, lo:lo + 128], kb.ident[:], kb.mask[:], start=False, stop=True),
                  reads=[kb.ident_tok, kb.mask_tok], writes=[sb_tok])
        if filler:
            fb, fbt = kb.banks[6]
            kb.op("pe", lambda: nc.tensor.matmul(fb[:, 0:filler], zeros[:, 0:128], zeros[:, 0:filler], start=True, stop=True),
                  reads=[zeros_t], writes=[fbt])

    def emit_exp(ti):
        ui, qc, pi, kbi = tiles[ti]
        sb_t, sb_tok = sbank[ti % NS]
        pb, ptok = pbuf[ti % NP]
        j = kbi - 4 * qc
        lo = 128 * j if j > 0 else 0
        kb.op("act", lambda: nc.scalar.activation(out=pb[:, lo:512], in_=sb_t[:, lo:512], func=AF.Exp),
              reads=[sb_tok], writes=[ptok])

    def emit_pv(ti):
        ui, qc, pi, kbi = tiles[ti]
        pb, ptok = pbuf[ti % NP]
        oset = osets[ui % 2]
        j = kbi - 4 * qc
        last = (kbi == 4 * qc + 3)
        if kbi == 0:
            for (ob, obtok) in oset:
                kb.op("pe", lambda ob=ob: nc.tensor.matmul(ob[:, :], zeros[:, 0:128], zeros[:, :], start=True, stop=False),
                      reads=[zeros_t], writes=[obtok])
        for i in range(max(j, 0), 4):
            ob, obtok = oset[i // 2]
            c0 = (i % 2) * 129
            kb.op("pe", lambda i=i, ob=ob, c0=c0: nc.tensor.matmul(
                ob[:, c0:c0 + 129], pb[:, 128 * i:128 * (i + 1)], V_sb[:, kbi, :], start=False,
                stop=(kbi == 4 * qc + i)),
                reads=[ptok, kv_tok_of_block(kbi)], writes=[obtok])
        if last:
            epilogue(qc, pi, oset)

    if q_resident is None:
        load_q(0)
        if len(order) > 1:
            load_q(1)
    nt = len(tiles)

    def qk_with_prefetch(x):
        ui, qc, pi, kbi = tiles[x]
        if q_resident is None and kbi == 0 and pi == 0 and oidx[qc] + 2 < len(order):
            load_q(oidx[qc] + 2)
        emit_qk(x)

    for x in range(min(2, nt)):
        qk_with_prefetch(x)
    for ti in range(nt):
        if ti + 2 < nt:
            qk_with_prefetch(ti + 2)
        emit_exp(ti)
        if ti >= 1:
            emit_pv(ti - 1)
    emit_pv(nt - 1)


def setup_mask(kb, mask_dram):
    kb.mask = kb.sb("mask_sb", [128, 128], BF16)
    kb.mask_tok = Tok()
    kb.dma("pool", kb.mask[:], mask_dram, owner=kb.mask_tok, writes=[kb.mask_tok])


def build_L2(nq_chunks=S // 512):
    kb = KB()
    nc = kb.nc
    QTa = kb.dram("QTa", [128, S], BF16, "ExternalInput")
    QTb = kb.dram("QTb", [64, S], BF16, "ExternalInput")
    KTa = kb.dram("KTa", [128, S], BF16, "ExternalInput")
    KTb = kb.dram("KTb", [64, S], BF16, "ExternalInput")
    Vd = kb.dram("V", [128, S // 128, 129], BF16, "ExternalInput")
    ident_d = kb.dram("ident", [128, 128], F32, "ExternalInput")
    mask_d = kb.dram("mask", [128, 128], F32, "ExternalInput")
    O = kb.dram("O", [S, 128], BF16, "ExternalOutput")
    kb.make_banks()
    setup_consts(kb, ident_d)
    setup_mask(kb, mask_d)
    KTa_sb = kb.sb("KTa_sb", [128, S], BF16)
    KTb_sb = kb.sb("KTb_sb", [64, S], BF16)
    V_sb = kb.sb("V_sb", [128, S // 128, 129], BF16)
    ptoks = [Tok() for _ in range(8)]
    for r in range(8):
        t = ptoks[r]
        kb.dma("sp", KTa_sb[:, r * 2048:(r + 1) * 2048], KTa[:, r * 2048:(r + 1) * 2048], owner=t, writes=[t])
        kb.dma("sp", KTb_sb[:, r * 2048:(r + 1) * 2048], KTb[:, r * 2048:(r + 1) * 2048], owner=t, pwrites=[t])
        kb.dma("sp", V_sb[:, r * 16:(r + 1) * 16, :], Vd[:, r * 16:(r + 1) * 16, :], owner=t, pwrites=[t])
    ost = [(kb.sb("ost%d" % i, [128, 4, 128], BF16), Tok()) for i in range(2)]
    rc = kb.sb("rc", [128, 4], F32); rc_t = Tok()
    Ov = O.rearrange("(c i p) d -> c p i d", i=4, p=128)

    def epilogue(qc, pi, oset):
        o_sb, o_tok = ost[qc % 2]
        for b, (ob, obtok) in enumerate(oset):
            sums = ob[:, 0:258].rearrange("p (i c) -> p i c", c=129)[:, :, 128:129]
            kb.op("dve", lambda: nc.vector.reciprocal(out=rc[:, 2 * b:2 * b + 2].rearrange("p (i c) -> p i c", c=1),
                                                     in_=sums), reads=[obtok], writes=[rc_t])
            for i in range(2):
                kb.op("dve", lambda i=i: nc.vector.tensor_scalar(
                    out=o_sb[:, 2 * b + i, :], in0=ob[:, i * 129:i * 129 + 128],
                    scalar1=rc[:, 2 * b + i:2 * b + i + 1], scalar2=None, op0=ALU.mult),
                    reads=[obtok, rc_t], **(dict(writes=[o_tok]) if (b == 0 and i == 0) else dict(pwrites=[o_tok])))
        kb.dma("sp", Ov[qc], o_sb[:], owner=o_tok, reads=[o_tok])

    attention_core(kb, [dict(K=[(KTa_sb, 128), (KTb_sb, 64)], Q=[(QTa, 128), (QTb, 64)])], V_sb,
                   lambda kbi: ptoks[kbi // 16], epilogue, nq_chunks=nq_chunks)
    kb.finish([t for _, t in ost])
    return kb


def kb_barrier(kb):
    deps = {}
    for n, E in kb.eng.items():
        if E["cnt"] > 0:
            deps[E["sem"]] = E["cnt"]
    for t in kb.dma_toks:
        if t.sem is not None and t.cnt > 0:
            deps[t.sem] = t.cnt
    for n in kb.eng:
        kb._wait(n, dict(deps), dma=True)


def load_gain(kb, name, g_dram, kc):
    nc = kb.nc
    g = kb.sb(name, [128, kc], F32)
    t = kb.tok()
    with nc.allow_non_contiguous_dma(reason="tiny gain vector"):
        kb.dma("sp", g[:], g_dram.rearrange("(j p) -> p j", p=128), owner=t, writes=[t])
    return g, t


def prep_weight(kb, w_view, g_sb, g_tok, out_bf, kc, ncol, stage, stage_tok, extra=None, q="sp"):
    nc = kb.nc
    sview = stage[:, 0:kc * ncol].rearrange("p (j n) -> p j n", j=kc)
    kb.dma(q, sview, w_view.rearrange("(j p) n -> p j n", p=128), owner=stage_tok, writes=[stage_tok])
    wt = kb.tok()
    for j in range(kc):
        if g_sb is None:
            kb.op("dve", lambda j=j: nc.vector.tensor_copy(out=out_bf[:, j, :], in_=sview[:, j, :]),
                  reads=[stage_tok], **(dict(writes=[wt]) if j == 0 else dict(pwrites=[wt])))
        else:
            kb.op("dve", lambda j=j: nc.vector.tensor_scalar(
                out=out_bf[:, j, :], in0=sview[:, j, :], scalar1=g_sb[:, j:j + 1],
                scalar2=(None if extra is None else float(extra)), op0=ALU.mult,
                **({} if extra is None else {"op1": ALU.mult})),
                reads=[stage_tok, g_tok], **(dict(writes=[wt]) if j == 0 else dict(pwrites=[wt])))
    return wt


def norm_transpose(kb, h_res, h_tok, hnT_all, hnT_tok, work):
    nc = kb.nc
    junk, junk_t, xs, xs_t, st4, st_t = work
    for tt in range(NT):
        kb.op("act", lambda: nc.scalar.activation(out=junk[:], in_=h_res[:, tt, :], func=AF.Square,
                                                  accum_out=st4[:, 0:1]), reads=[h_tok[tt]], writes=[junk_t, st_t])
        rms_scale(kb, st4[:, 0:1], st4[:, 1:2], D, st_t, st_t)
        kb.op("dve", lambda: nc.vector.tensor_scalar(out=xs[:], in0=h_res[:, tt, :], scalar1=st4[:, 1:2],
                                                     scalar2=None, op0=ALU.mult),
              reads=[h_tok[tt], st_t], writes=[xs_t])
        transpose_to(kb, lambda j: xs[:, j * 128:(j + 1) * 128], xs_t, 8, 128,
                     lambda: hnT_all[:, :, tt * 128:(tt + 1) * 128], hnT_tok, evac="act", pw=(tt > 0))


def phase_attn_out(kb, resid_load, O_d, w_o, h_res, h_tok):
    nc = kb.nc
    kb.push_scope()
    stage = kb.sb("pa_stage", [128, 8 * 1024], F32); stage_t = kb.tok()
    Wo = kb.sb("pa_Wo", [128, 8, 1024], BF16)
    Wo_t = prep_weight(kb, w_o, None, None, Wo, 8, 1024, stage, stage_t)
    ot = [(kb.sb("pa_o%d" % i, [128, D], BF16), kb.tok()) for i in range(2)]
    oT = kb.sb("pa_oT", [128, 8, 128], BF16); oT_t = kb.tok()
    for tt in range(NT):
        resid_load(tt)
        ob, obt = ot[tt % 2]
        kb.dma("pool", ob[:], O_d[tt * 128:(tt + 1) * 128, :], owner=obt, writes=[obt])
        transpose_to(kb, lambda j: ob[:, j * 128:(j + 1) * 128], obt, 8, 128, lambda: oT[:, :, :], oT_t, evac="act")
        for half in range(2):
            pb, pbt = kb.bank()
            for j in range(8):
                kb.op("pe", lambda j=j: nc.tensor.matmul(pb[:, :], oT[:, j, :], Wo[:, j, half * 512:(half + 1) * 512],
                                                         start=(j == 0), stop=(j == 7)),
                      reads=[oT_t, Wo_t], writes=[pbt])
            hs = h_res[:, tt, half * 512:(half + 1) * 512]
            kb.op("dve", lambda: nc.vector.tensor_tensor(out=hs, in0=pb[:, :], in1=hs, op=ALU.add),
                  reads=[pbt], pwrites=[h_tok[tt]])
    kb_barrier(kb)
    kb.pop_scope()


def phase_ffn(kb, h_res, h_tok, g_ffn, w_gu, w_d):
    nc = kb.nc
    kb.push_scope()
    gF, gF_t = load_gain(kb, "ff_g", g_ffn, 8)
    junk = kb.sb("ff_junk", [128, D], BF16); xs = kb.sb("ff_xs", [128, D], BF16); st4 = kb.sb("ff_st", [128, 4], F32)
    work = (junk, kb.tok(), xs, kb.tok(), st4, kb.tok())
    hnT = kb.sb("ff_hnT", [128, 8, TPC], BF16); hnT_t = kb.tok()
    norm_transpose(kb, h_res, h_tok, hnT, hnT_t, work)
    HB = 256
    nhb = DFF // HB
    stages = [(kb.sb("ff_stage%d" % i, [128, 8 * HB], F32), kb.tok()) for i in range(3)]
    Wg = [kb.sb("ff_Wg%d" % i, [128, 8, HB], BF16) for i in range(2)]
    Wu = [kb.sb("ff_Wu%d" % i, [128, 8, HB], BF16) for i in range(2)]
    Wd = [kb.sb("ff_Wd%d" % i, [128, 2, D], BF16) for i in range(2)]
    sg = [(kb.sb("ff_sg%d" % i, [128, 512], F32), kb.tok()) for i in range(2)]
    act = [(kb.sb("ff_act%d" % i, [128, 2, 512], BF16), kb.tok()) for i in range(2)]
    si = 0
    wts = {}

    def load_w(hb):
        nonlocal si
        s0, s0t = stages[si % 3]; si += 1
        Wg_t = prep_weight(kb, w_gu[:, hb * HB:(hb + 1) * HB], gF, gF_t, Wg[hb % 2], 8, HB, s0, s0t)
        s1, s1t = stages[si % 3]; si += 1
        Wu_t = prep_weight(kb, w_gu[:, DFF + hb * HB:DFF + (hb + 1) * HB], gF, gF_t, Wu[hb % 2], 8, HB, s1, s1t)
        s2, s2t = stages[si % 3]; si += 1
        Wd_t = prep_weight(kb, w_d[hb * HB:(hb + 1) * HB, :], None, None, Wd[hb % 2], 2, D, s2, s2t)
        wts[hb] = (Wg_t, Wu_t, Wd_t)

    units = [(hb, c) for hb in range(nhb) for c in range(TPC // 512)]

    def stage1(u):
        hb, c = units[u]
        if c == 0:
            load_w(hb)
        Wg_t, Wu_t, Wd_t = wts[hb]
        ab, abt = act[u % 2]
        for sub in range(2):
            gb, gbt = kb.bank()
            for j in range(8):
                kb.op("pe", lambda j=j: nc.tensor.matmul(gb[:, :], Wg[hb % 2][:, j, sub * 128:(sub + 1) * 128],
                                                         hnT[:, j, c * 512:(c + 1) * 512], start=(j == 0), stop=(j == 7)),
                      reads=[hnT_t, Wg_t], writes=[gbt])
            ub, ubt = kb.bank()
            for j in range(8):
                kb.op("pe", lambda j=j: nc.tensor.matmul(ub[:, :], Wu[hb % 2][:, j, sub * 128:(sub + 1) * 128],
                                                         hnT[:, j, c * 512:(c + 1) * 512], start=(j == 0), stop=(j == 7)),
                      reads=[hnT_t, Wu_t], writes=[ubt])
            sgb, sgt = sg[sub]
            kb.op("act", lambda: nc.scalar.activation(out=sgb[:], in_=gb[:, :], func=AF.Silu),
                  reads=[gbt], writes=[sgt])
            kb.op("dve", lambda: nc.vector.tensor_tensor(out=ab[:, sub, :], in0=ub[:, :], in1=sgb[:], op=ALU.mult),
                  reads=[ubt, sgt], **(dict(writes=[abt]) if sub == 0 else dict(pwrites=[abt])))

    def stage2(u):
        hb, c = units[u]
        Wg_t, Wu_t, Wd_t = wts[hb]
        ab, abt = act[u % 2]
        for i in range(4):
            tt = c * 4 + i
            for half in range(2):
                db, dbt = kb.bank()
                for sub in range(2):
                    kb.op("pe", lambda sub=sub: nc.tensor.matmul(db[:, :], ab[:, sub, i * 128:(i + 1) * 128],
                                                                 Wd[hb % 2][:, sub, half * 512:(half + 1) * 512],
                                                                 start=(sub == 0), stop=(sub == 1)),
                          reads=[abt, Wd_t], writes=[dbt])
                hs = h_res[:, tt, half * 512:(half + 1) * 512]
                kb.op("dve", lambda: nc.vector.tensor_tensor(out=hs, in0=db[:, :], in1=hs, op=ALU.add),
                      reads=[dbt], pwrites=[h_tok[tt]])

    stage1(0)
    for u in range(len(units)):
        if u + 1 < len(units):
            stage1(u + 1)
        stage2(u)
    kb_barrier(kb)
    kb.pop_scope()


def make_h(kb):
    h_res = kb.sb("h_res", [128, NT, D], F32)
    h_tok = [kb.tok() for _ in range(NT)]
    return h_res, h_tok


def build_L3():
    kb = KB()
    nc = kb.nc
    x = kb.dram("x", [TPC, D], F32, "ExternalInput")
    O_d = kb.dram("O", [TPC, D], BF16, "ExternalInput")
    pos = kb.dram("pos", [1, TPC], I32, "ExternalInput")
    coef = kb.dram("coef", [4, 8], F32, "ExternalInput")
    ident_d = kb.dram("ident", [128, 128], F32, "ExternalInput")
    w_o = kb.dram("w_o", [D, D], F32, "ExternalInput")
    g_ffn = kb.dram("g_ffn", [D], F32, "ExternalInput")
    w_gu = kb.dram("w_gu", [D, 2 * DFF], F32, "ExternalInput")
    w_d = kb.dram("w_d", [DFF, D], F32, "ExternalInput")
    g_kv = kb.dram("g_kv", [D], F32, "ExternalInput")
    g_a1 = kb.dram("g_a1", [D], F32, "ExternalInput")
    w_k = kb.dram("w_k", [D, D], F32, "ExternalInput")
    w_v = kb.dram("w_v", [D, D], F32, "ExternalInput")
    w_q = kb.dram("w_q", [D, D], F32, "ExternalInput")
    h1_o = kb.dram("h1", [TPC, D], F32, "ExternalOutput")
    Q1_o = kb.dram("Q1", [H, 68, TPC], BF16, "ExternalOutput")
    Q2_o = kb.dram("Q2", [H, 68, TPC], BF16, "ExternalOutput")
    K1_o = kb.dram("K1", [H, 68, TPC], BF16, "ExternalOutput")
    K2_o = kb.dram("K2", [H, 68, TPC], BF16, "ExternalOutput")
    V_o = kb.dram("V", [H, 128, NT, 129], BF16, "ExternalOutput")
    kb.make_banks()
    setup_consts(kb, ident_d)
    h_res, h_tok = make_h(kb)

    def resid_load(tt):
        kb.dma("sp", h_res[:, tt, :], x[tt * 128:(tt + 1) * 128, :], owner=h_tok[tt], writes=[h_tok[tt]])

    phase_attn_out(kb, resid_load, O_d, w_o, h_res, h_tok)
    phase_ffn(kb, h_res, h_tok, g_ffn, w_gu, w_d)
    for tt in range(NT):
        kb.dma("sp", h1_o[tt * 128:(tt + 1) * 128, :], h_res[:, tt, :], owner=h_tok[tt], reads=[h_tok[tt]])

    kb.push_scope()
    cf = kb.sb("d_cf", [4, 8], F32); cf_t = kb.tok()
    kb.dma("sp", cf[:], coef, owner=cf_t, writes=[cf_t])
    pi_ = kb.sb("d_pi", [4, TPC], I32); pi_t = kb.tok()
    kb.dma("sp", pi_[:], pos.partition_broadcast(4), owner=pi_t, writes=[pi_t])
    ai = kb.sb("d_ai", [4, TPC], I32)
    af = kb.sb("d_af", [4, TPC], F32); bf_ = kb.sb("d_bf", [4, TPC], F32); aug_t = kb.tok()
    t1 = kb.sb("d_t1", [4, TPC], F32)
    kaug = kb.sb("d_kaug", [4, TPC], BF16); qbase = kb.sb("d_qbase", [4, TPC], F32)
    qaug = [(kb.sb("d_qaug%d" % i, [4, TPC], BF16), kb.tok()) for i in range(2)]
    kaug_t = kb.tok()
    V_ = nc.vector
    seq = [
        lambda: V_.tensor_scalar(out=ai[:], in0=pi_[:], scalar1=7, scalar2=None, op0=ALU.arith_shift_right),
        lambda: V_.tensor_copy(out=af[:], in_=ai[:]),
        lambda: V_.tensor_scalar(out=ai[:], in0=pi_[:], scalar1=127, scalar2=None, op0=ALU.bitwise_and),
        lambda: V_.tensor_copy(out=bf_[:], in_=ai[:]),
        lambda: V_.tensor_scalar(out=t1[:], in0=af[:], scalar1=cf[:, 0:1], scalar2=cf[:, 2:3], op0=ALU.mult, op1=ALU.add),
        lambda: V_.scalar_tensor_tensor(out=kaug[:], in0=bf_[:], scalar=cf[:, 1:2], in1=t1[:], op0=ALU.mult, op1=ALU.add),
        lambda: V_.tensor_scalar(out=t1[:], in0=af[:], scalar1=cf[:, 3:4], scalar2=cf[:, 5:6], op0=ALU.mult, op1=ALU.add),
        lambda: V_.scalar_tensor_tensor(out=qbase[:], in0=bf_[:], scalar=cf[:, 4:5], in1=t1[:], op0=ALU.mult, op1=ALU.add),
    ]
    for fn in seq:
        kb.op("dve", fn, reads=[pi_t, cf_t], writes=[aug_t])
    kb.op("dve", lambda: V_.tensor_copy(out=kaug[:], in_=kaug[:]), reads=[aug_t], writes=[kaug_t])
    for h in range(H):
        slope = 2.0 ** (-8.0 * (h + 1) / H)
        qa_, qa_t_ = qaug[h % 2]
        kb.op("dve", lambda: V_.tensor_scalar(out=qa_[:], in0=qbase[:], scalar1=slope, scalar2=None, op0=ALU.mult),
              reads=[aug_t], writes=[qa_t_])
        kb.dma("pool", Q1_o[h, 64:68, :], qa_[:], owner=qa_t_, reads=[qa_t_])
        kb.dma("pool", Q2_o[h, 64:68, :], qa_[:], owner=qa_t_, reads=[qa_t_])
        kb.dma("pool", K1_o[h, 64:68, :], kaug[:], owner=kaug_t, reads=[kaug_t])
        kb.dma("pool", K2_o[h, 64:68, :], kaug[:], owner=kaug_t, reads=[kaug_t])
    kb_barrier(kb)
    kb.pop_scope()

    kb.push_scope()
    gK, gK_t = load_gain(kb, "d_gk", g_kv, 8)
    gA, gA_t = load_gain(kb, "d_ga", g_a1, 8)
    Wk = kb.sb("d_Wk", [128, 8, D], BF16); Wv = kb.sb("d_Wv", [128, 8, D], BF16); Wq = kb.sb("d_Wq", [128, 8, D], BF16)
    kb.push_scope()
    stage = kb.sb("d_stage", [128, 8 * 1024], F32); stage_t = kb.tok()
    Wk_t = prep_weight(kb, w_k, gK, gK_t, Wk, 8, D, stage, stage_t)
    Wv_t = prep_weight(kb, w_v, gK, gK_t, Wv, 8, D, stage, stage_t)
    Wq_t = prep_weight(kb, w_q, gA, gA_t, Wq, 8, D, stage, stage_t, extra=64 ** -0.5)
    kb_barrier(kb)
    kb.pop_scope()
    junk = kb.sb("d_junk", [128, D], BF16); xs = kb.sb("d_xs", [128, D], BF16); st4 = kb.sb("d_st", [128, 4], F32)
    work = (junk, kb.tok(), xs, kb.tok(), st4, kb.tok())
    hT = kb.sb("d_hT", [128, 8, TPC], BF16); hT_t = kb.tok()
    norm_transpose(kb, h_res, h_tok, hT, hT_t, work)
    vst = [(kb.sb("d_vst%d" % i, [128, H, 129], BF16), kb.tok()) for i in range(2)]
    for vs_, vs_t in vst:
        kb.op("pool", lambda: nc.gpsimd.memset(vs_[:], 1.0), writes=[vs_t])
    kst = [(kb.sb("d_kst%d" % i, [128, H, 512], BF16), kb.tok()) for i in range(1)]
    qst = [(kb.sb("d_qst%d" % i, [128, H, 512], BF16), kb.tok()) for i in range(1)]
    for c in range(TPC // 512):
        csl = slice(c * 512, (c + 1) * 512)
        for (W, W_t, stl, o1, o2, ev) in ((Wk, Wk_t, kst, K1_o, K2_o, "act"), (Wq, Wq_t, qst, Q1_o, Q2_o, "dve")):
            sb_, sb_t = stl[0]
            for h in range(H):
                pb, pbt = kb.bank()
                for j in range(8):
                    kb.op("pe", lambda j=j: nc.tensor.matmul(pb[:, :], W[:, j, h * 128:(h + 1) * 128], hT[:, j, csl],
                                                             start=(j == 0), stop=(j == 7)),
                          reads=[hT_t, W_t], writes=[pbt])
                wr = dict(writes=[sb_t]) if h == 0 else dict(pwrites=[sb_t])
                if ev == "act":
                    kb.op("act", lambda: nc.scalar.copy(out=sb_[:, h, :], in_=pb[:, :]), reads=[pbt], **wr)
                else:
                    kb.op("dve", lambda: nc.vector.tensor_copy(out=sb_[:, h, :], in_=pb[:, :]), reads=[pbt], **wr)
            kb.dma("pool", o1[:, 0:64, csl].rearrange("h p t -> p h t"), sb_[0:64, :, :], owner=sb_t, reads=[sb_t])
            kb.dma("pool", o2[:, 0:64, csl].rearrange("h p t -> p h t"), sb_[64:128, :, :], owner=sb_t, reads=[sb_t])
        for i in range(4):
            tt = c * 4 + i
            vs_, vs_t = vst[tt % 2]
            for hh in range(2):
                vb, vbt = kb.bank()
                for j in range(8):
                    kb.op("pe", lambda j=j: nc.tensor.matmul(vb[:, :], hT[:, j, tt * 128:(tt + 1) * 128],
                                                             Wv[:, j, hh * 512:(hh + 1) * 512], start=(j == 0), stop=(j == 7)),
                          reads=[hT_t, Wv_t], writes=[vbt])
                kb.op("act", lambda: nc.scalar.copy(out=vs_[:, 4 * hh:4 * hh + 4, 0:128],
                                                    in_=vb[:, :].rearrange("p (h d) -> p h d", h=4)),
                      reads=[vbt], pwrites=[vs_t])
            kb.dma("pool", V_o[:, :, tt, :].rearrange("h p c -> p h c"), vs_[:], owner=vs_t, reads=[vs_t])
    kb.finish(h_tok + [kaug_t] + [t for _, t in qaug] + [t for _, t in vst] + [t for _, t in kst] + [t for _, t in qst])
    kb_barrier(kb)
    kb.pop_scope()
    return kb


def build_L4(lambda_init, nq_chunks=S // 512):
    kb = KB()
    nc = kb.nc
    Q1 = kb.dram("Q1", [68, S], BF16, "ExternalInput")
    Q2 = kb.dram("Q2", [68, S], BF16, "ExternalInput")
    K1 = kb.dram("K1", [68, S], BF16, "ExternalInput")
    K2 = kb.dram("K2", [68, S], BF16, "ExternalInput")
    Vd = kb.dram("V", [128, S // 128, 129], BF16, "ExternalInput")
    ident_d = kb.dram("ident", [128, 128], F32, "ExternalInput")
    mask_d = kb.dram("mask", [128, 128], F32, "ExternalInput")
    lam_d = kb.dram("lam", [4, 64], F32, "ExternalInput")
    subln_d = kb.dram("subln", [1, 128], F32, "ExternalInput")
    O = kb.dram("O", [S, 128], BF16, "ExternalOutput")
    kb.make_banks()
    setup_consts(kb, ident_d)
    setup_mask(kb, mask_d)
    lam_sb = kb.sb("lam_sb", [128, 4, 64], F32); lam_t = kb.tok()
    kb.dma("sp", lam_sb[:].rearrange("p a d -> p (a d)"), lam_d.rearrange("a d -> (a d)").partition_broadcast(128),
           owner=lam_t, writes=[lam_t])
    lj = kb.sb("lam_j", [128, 64], F32); lv = kb.sb("lam_v", [128, 4], F32); lv_t = kb.tok()
    kb.op("dve", lambda: nc.vector.tensor_tensor(out=lj[:], in0=lam_sb[:, 0, :], in1=lam_sb[:, 1, :], op=ALU.mult),
          reads=[lam_t], writes=[lv_t])
    kb.op("dve", lambda: nc.vector.tensor_reduce(out=lv[:, 0:1], in_=lj[:], axis=mybir.AxisListType.X, op=ALU.add),
          reads=[lv_t], writes=[lv_t])
    kb.op("dve", lambda: nc.vector.tensor_tensor(out=lj[:], in0=lam_sb[:, 2, :], in1=lam_sb[:, 3, :], op=ALU.mult),
          reads=[lam_t, lv_t], writes=[lv_t])
    kb.op("dve", lambda: nc.vector.tensor_reduce(out=lv[:, 1:2], in_=lj[:], axis=mybir.AxisListType.X, op=ALU.add),
          reads=[lv_t], writes=[lv_t])
    kb.op("act", lambda: nc.scalar.activation(out=lv[:, 0:2], in_=lv[:, 0:2], func=AF.Exp), reads=[lv_t], writes=[lv_t])
    kb.op("dve", lambda: nc.vector.scalar_tensor_tensor(out=lv[:, 2:3], in0=lv[:, 1:2], scalar=-float(lambda_init),
                                                        in1=lv[:, 0:1], op0=ALU.add, op1=ALU.subtract),
          reads=[lv_t], writes=[lv_t])
    sub_sb = kb.sb("sub_sb", [128, 128], F32); sub_t = kb.tok()
    kb.dma("sp", sub_sb[:], subln_d.partition_broadcast(128), owner=sub_t, writes=[sub_t])
    kb.op("dve", lambda: nc.vector.tensor_scalar(out=sub_sb[:], in0=sub_sb[:], scalar1=float(1.0 - lambda_init),
                                                 scalar2=None, op0=ALU.mult), reads=[sub_t], writes=[sub_t])

    K1_sb = kb.sb("K1_sb", [68, S], BF16)
    K2_sb = kb.sb("K2_sb", [68, S], BF16)
    V_sb = kb.sb("V_sb", [128, S // 128, 129], BF16)
    ptoks = [kb.tok() for _ in range(8)]
    for r in range(8):
        t = ptoks[r]
        kb.dma("sp", K1_sb[:, r * 2048:(r + 1) * 2048], K1[:, r * 2048:(r + 1) * 2048], owner=t, writes=[t])
        kb.dma("sp", K2_sb[:, r * 2048:(r + 1) * 2048], K2[:, r * 2048:(r + 1) * 2048], owner=t, pwrites=[t])
        kb.dma("sp", V_sb[:, r * 16:(r + 1) * 16, :], Vd[:, r * 16:(r + 1) * 16, :], owner=t, pwrites=[t])
    o1 = kb.sb("o1n", [128, 4, 128], F32); o1_t = kb.tok()
    att = kb.sb("attd", [128, 4, 128], F32); att_t = kb.tok()
    junk = kb.sb("junk4", [128, 128], F32); junk_t = kb.tok()
    ost = [(kb.sb("ost%d" % i, [128, 4, 128], BF16), kb.tok()) for i in range(2)]
    rc = kb.sb("rc", [128, 8], F32); rc_t = kb.tok()
    ss = kb.sb("ss4", [128, 8], F32); ss_t = kb.tok()
    Ov = O.rearrange("(c i p) d -> c p i d", i=4, p=128)

    def epilogue(qc, pi, oset):
        for b, (ob, obtok) in enumerate(oset):
            sums = ob[:, 0:258].rearrange("p (i c) -> p i c", c=129)[:, :, 128:129]
            kb.op("dve", lambda: nc.vector.reciprocal(out=rc[:, 2 * b:2 * b + 2].rearrange("p (i c) -> p i c", c=1),
                                                     in_=sums), reads=[obtok], writes=[rc_t])
            for i in range(2):
                qi = 2 * b + i
                if pi == 0:
                    kb.op("dve", lambda: nc.vector.tensor_scalar(
                        out=o1[:, qi, :], in0=ob[:, i * 129:i * 129 + 128], scalar1=rc[:, qi:qi + 1], scalar2=None,
                        op0=ALU.mult), reads=[obtok, rc_t], **(dict(writes=[o1_t]) if qi == 0 else dict(pwrites=[o1_t])))
                else:
                    kb.op("dve", lambda: nc.vector.tensor_scalar(
                        out=att[:, qi, :], in0=ob[:, i * 129:i * 129 + 128], scalar1=rc[:, qi:qi + 1],
                        scalar2=lv[:, 2:3], op0=ALU.mult, op1=ALU.mult),
                        reads=[obtok, rc_t, lv_t], **(dict(writes=[att_t]) if qi == 0 else dict(pwrites=[att_t])))
                    kb.op("pool", lambda: nc.gpsimd.tensor_tensor(out=att[:, qi, :], in0=att[:, qi, :], in1=o1[:, qi, :],
                                                                  op=ALU.add), reads=[o1_t], pwrites=[att_t])
        if pi == 1:
            o_sb, o_tok = ost[qc % 2]
            for qi in range(4):
                kb.op("act", lambda: nc.scalar.activation(out=junk[:], in_=att[:, qi, :], func=AF.Square,
                                                          accum_out=ss[:, qi:qi + 1]),
                      reads=[att_t], writes=[junk_t], **(dict(pwrites=[ss_t])))
            kb.op("act", lambda: nc.scalar.activation(out=ss[:, 4:8], in_=ss[:, 0:4], func=AF.Ln, scale=1.0 / 128,
                                                      bias=kb.eps_ap), reads=[ss_t, kb.eps_tok], writes=[ss_t])
            kb.op("act", lambda: nc.scalar.activation(out=ss[:, 4:8], in_=ss[:, 4:8], func=AF.Exp, scale=-0.5),
                  reads=[ss_t], writes=[ss_t])
            for qi in range(4):
                kb.op("dve", lambda: nc.vector.scalar_tensor_tensor(
                    out=o_sb[:, qi, :], in0=att[:, qi, :], scalar=ss[:, 4 + qi:5 + qi], in1=sub_sb[:],
                    op0=ALU.mult, op1=ALU.mult), reads=[att_t, ss_t, sub_t],
                    **(dict(writes=[o_tok]) if qi == 0 else dict(pwrites=[o_tok])))
            kb.dma("sp", Ov[qc], o_sb[:], owner=o_tok, reads=[o_tok])

    attention_core(kb, [dict(K=[(K1_sb, 68)], Q=[(Q1, 68)]), dict(K=[(K2_sb, 68)], Q=[(Q2, 68)])], V_sb,
                   lambda kbi: ptoks[kbi // 16], epilogue, nq_chunks=nq_chunks)
    kb.finish([t for _, t in ost])
    return kb


def build_L5():
    kb = KB()
    nc = kb.nc
    h1 = kb.dram("h1", [TPC, D], F32, "ExternalInput")
    O_d = kb.dram("O", [TPC, D], BF16, "ExternalInput")
    ident_d = kb.dram("ident", [128, 128], F32, "ExternalInput")
    w_o = kb.dram("w_o", [D, D], F32, "ExternalInput")
    g_ffn = kb.dram("g_ffn", [D], F32, "ExternalInput")
    w_gu = kb.dram("w_gu", [D, 2 * DFF], F32, "ExternalInput")
    w_d = kb.dram("w_d", [DFF, D], F32, "ExternalInput")
    g_fin = kb.dram("g_fin", [1, D], F32, "ExternalInput")
    out = kb.dram("out", [TPC, D], F32, "ExternalOutput")
    kb.make_banks()
    setup_consts(kb, ident_d)
    h_res, h_tok = make_h(kb)

    def resid_load(tt):
        kb.dma("sp", h_res[:, tt, :], h1[tt * 128:(tt + 1) * 128, :], owner=h_tok[tt], writes=[h_tok[tt]])

    phase_attn_out(kb, resid_load, O_d, w_o, h_res, h_tok)
    phase_ffn(kb, h_res, h_tok, g_ffn, w_gu, w_d)
    gb = kb.sb("f_g", [128, D], F32); gb_t = kb.tok()
    kb.dma("sp", gb[:], g_fin.partition_broadcast(128), owner=gb_t, writes=[gb_t])
    junk = kb.sb("f_junk", [128, D], BF16); junk_t = kb.tok()
    st4 = kb.sb("f_st", [128, 4], F32); st_t = kb.tok()
    ob = [(kb.sb("f_o%d" % i, [128, D], F32), kb.tok()) for i in range(2)]
    for tt in range(NT):
        kb.op("act", lambda: nc.scalar.activation(out=junk[:], in_=h_res[:, tt, :], func=AF.Square,
                                                  accum_out=st4[:, 0:1]), reads=[h_tok[tt]], writes=[junk_t, st_t])
        rms_scale(kb, st4[:, 0:1], st4[:, 1:2], D, st_t, st_t)
        o_sb, o_t = ob[tt % 2]
        kb.op("dve", lambda: nc.vector.scalar_tensor_tensor(out=o_sb[:], in0=h_res[:, tt, :], scalar=st4[:, 1:2],
                                                            in1=gb[:], op0=ALU.mult, op1=ALU.mult),
              reads=[h_tok[tt], st_t, gb_t], writes=[o_t])
        kb.dma("sp", out[tt * 128:(tt + 1) * 128, :], o_sb[:], owner=o_t, reads=[o_t])
    kb.finish([t for _, t in ob])
    return kb


def _run(kb, maps):
    res = run_bass_kernel_spmd(kb.nc, maps, core_ids=list(range(NCORES)))
    return res.results


def _cat_tokens(results, key, h):
    return np.ascontiguousarray(np.concatenate([np.asarray(results[r][key])[h] for r in range(NCORES)], axis=-1))


def _cat_v(results, h):
    return np.ascontiguousarray(np.concatenate([np.asarray(results[r]["V"])[h] for r in range(NCORES)], axis=1))


def kernel_unfused(x, positions, attn_norm, ffn_norm, final_norm,
           mla_w_dq, mla_q_norm, mla_w_uq, mla_w_dkv, mla_kv_norm, mla_w_ukv, mla_w_o,
           diff_kv_norm, diff_w_k, diff_w_v, diff_w_q,
           diff_lambda_q1, diff_lambda_k1, diff_lambda_q2, diff_lambda_k2,
           diff_subln, diff_w_o, ffn_w_gate_up, ffn_w_down):
    f32 = np.float32
    x = np.asarray(x, f32)
    positions = np.asarray(positions, np.int32)
    A = lambda a: np.ascontiguousarray(np.asarray(a, f32))
    inv = (10000.0 ** (-np.arange(32, dtype=f32) * 2.0 / 64)).astype(f32)
    invf = np.concatenate([inv, inv]).reshape(64, 1).astype(f32)
    ident = np.eye(128, dtype=f32)
    mask = np.where(np.arange(128)[:, None] > np.arange(128)[None, :], NEG, 0.0).astype(f32)
    coef = np.array([[128, 0, 0, 0, 0, 1],
                     [0, 1, 0, 0, 0, 1],
                     [0, 0, 1, -128, 0, 0],
                     [0, 0, 1, 0, -1, 0]], f32)
    coef = np.ascontiguousarray(np.concatenate([coef, np.zeros((4, 2), f32)], axis=1))
    sl = [slice(r * TPC, (r + 1) * TPC) for r in range(NCORES)]
    xs = [np.ascontiguousarray(x[0, s]) for s in sl]
    ps = [np.ascontiguousarray(positions[:, s]) for s in sl]

    r1 = _run(build_L1(), [dict(x=xs[r], pos=ps[r], invf=invf, ident=ident, g_attn=A(attn_norm[0]),
                                g_q=A(mla_q_norm[0]), g_kv=A(mla_kv_norm[0]), w_dq=A(mla_w_dq[0]),
                                w_uq=A(mla_w_uq[0]), w_dkv=A(mla_w_dkv[0]), w_ukv=A(mla_w_ukv[0]))
                           for r in range(NCORES)])
    ktb = np.ascontiguousarray(np.concatenate([np.asarray(r1[r]["KTb"]) for r in range(NCORES)], axis=-1))
    r2 = _run(build_L2(), [dict(QTa=_cat_tokens(r1, "QTa", h), QTb=_cat_tokens(r1, "QTb", h),
                                KTa=_cat_tokens(r1, "KTa", h), KTb=ktb, V=_cat_v(r1, h), ident=ident, mask=mask)
                           for h in range(NCORES)])
    del r1
    O0 = np.concatenate([np.asarray(r2[h]["O"]) for h in range(H)], axis=1)
    del r2
    r3 = _run(build_L3(), [dict(x=xs[r], O=np.ascontiguousarray(O0[sl[r]]), pos=ps[r], coef=coef, ident=ident,
                                w_o=A(mla_w_o[0]), g_ffn=A(ffn_norm[0]), w_gu=A(ffn_w_gate_up[0]),
                                w_d=A(ffn_w_down[0]), g_kv=A(diff_kv_norm), g_a1=A(attn_norm[1]),
                                w_k=A(diff_w_k), w_v=A(diff_w_v), w_q=A(diff_w_q[0]))
                           for r in range(NCORES)])
    lambda_init = 0.8 - 0.6 * float(np.exp(-0.3 * 1))
    lam = np.ascontiguousarray(np.stack([A(diff_lambda_q1[0]), A(diff_lambda_k1[0]),
                                         A(diff_lambda_q2[0]), A(diff_lambda_k2[0])]))
    r4 = _run(build_L4(lambda_init), [dict(Q1=_cat_tokens(r3, "Q1", h), Q2=_cat_tokens(r3, "Q2", h),
                                           K1=_cat_tokens(r3, "K1", h), K2=_cat_tokens(r3, "K2", h),
                                           V=_cat_v(r3, h), ident=ident, mask=mask, lam=lam,
                                           subln=A(diff_subln[0]).reshape(1, 128))
                                      for h in range(NCORES)])
    h1 = [np.asarray(r3[r]["h1"]) for r in range(NCORES)]
    del r3
    O1 = np.concatenate([np.asarray(r4[h]["O"]) for h in range(H)], axis=1)
    del r4
    r5 = _run(build_L5(), [dict(h1=h1[r], O=np.ascontiguousarray(O1[sl[r]]), ident=ident, w_o=A(diff_w_o[0]),
                                g_ffn=A(ffn_norm[1]), w_gu=A(ffn_w_gate_up[1]), w_d=A(ffn_w_down[1]),
                                g_fin=A(final_norm).reshape(1, D))
                           for r in range(NCORES)])
    out = np.concatenate([np.asarray(r5[r]["out"]) for r in range(NCORES)], axis=0)
    return out.reshape(1, S, D).astype(f32)


GROWS = 1024


def kb_gather(kb, gin, gout, reads, writes):
    nc = kb.nc
    E = kb.eng["pool"]
    kb._wait("pool", kb._deps(reads, writes), dma=True)
    kb.cc_cnt += 1
    nc.gpsimd.collective_compute("AllGather", ALU.bypass, replica_groups=[list(range(NCORES))],
                                 ins=[gin.opt()], outs=[gout.opt()]).then_inc(kb.sems[kb.cc_sem])
    kb._post((kb.cc_sem, kb.cc_cnt), reads, writes)


def rope_tables(kb, pos_ap, n, scr, posi, cos_out, sin_out, invf_sb, invf_t, out_tok, scr_tok):
    nc = kb.nc
    V = nc.vector
    A0, A1, A2 = scr
    A1i = A1.bitcast(I32)
    kb.dma("sp", posi, pos_ap.partition_broadcast(64), owner=scr_tok, writes=[scr_tok])
    sq = [
        lambda: V.tensor_copy(out=A0, in_=posi),
        lambda: V.tensor_scalar(out=A0, in0=A0, scalar1=invf_sb[:, 0:1], scalar2=None, op0=ALU.mult),
        lambda: V.tensor_scalar(out=A2, in0=A0, scalar1=1.0 / TWO_PI, scalar2=None, op0=ALU.mult),
        lambda: V.tensor_copy(out=A1i, in_=A2),
        lambda: V.tensor_copy(out=A2, in_=A1i),
        lambda: V.scalar_tensor_tensor(out=A0, in0=A2, scalar=-C1, in1=A0, op0=ALU.mult, op1=ALU.add),
        lambda: V.scalar_tensor_tensor(out=A0, in0=A2, scalar=-C2, in1=A0, op0=ALU.mult, op1=ALU.add),
        lambda: V.tensor_scalar(out=A2, in0=A0, scalar1=PI, scalar2=-TWO_PI, op0=ALU.is_gt, op1=ALU.mult),
        lambda: V.tensor_tensor(out=A0, in0=A0, in1=A2, op=ALU.add),
        lambda: V.tensor_scalar(out=A2, in0=A0, scalar1=-PI, scalar2=TWO_PI, op0=ALU.is_lt, op1=ALU.mult),
        lambda: V.tensor_tensor(out=A0, in0=A0, in1=A2, op=ALU.add),
        lambda: V.tensor_scalar(out=A1, in0=A0, scalar1=PI / 2, scalar2=None, op0=ALU.add),
        lambda: V.tensor_scalar(out=A2, in0=A1, scalar1=PI, scalar2=-TWO_PI, op0=ALU.is_gt, op1=ALU.mult),
        lambda: V.tensor_tensor(out=A1, in0=A1, in1=A2, op=ALU.add),
        lambda: V.tensor_scalar(out=A0, in0=A0, scalar1=PI_SAFE, scalar2=-PI_SAFE, op0=ALU.min, op1=ALU.max),
        lambda: V.tensor_scalar(out=A1, in0=A1, scalar1=PI_SAFE, scalar2=-PI_SAFE, op0=ALU.min, op1=ALU.max),
    ]
    for fn in sq:
        kb.op("dve", fn, reads=[invf_t], writes=[scr_tok])
    kb.op("act", lambda: nc.scalar.activation(out=sin_out, in_=A0, func=AF.Sin), reads=[scr_tok], writes=[out_tok])
    kb.op("act", lambda: nc.scalar.activation(out=cos_out, in_=A1, func=AF.Sin), reads=[scr_tok], pwrites=[out_tok])


def rope_apply(kb, raw_b, raw_tok, rot_b, rot_tok, cs, sn, tab_t, tmp, out_ap, out_tok, pw):
    nc = kb.nc
    (tmp1, tmp1_t), (tmp2, tmp2_t) = tmp
    kb.op("dve", lambda: nc.vector.tensor_tensor(out=tmp1[:], in0=raw_b[0:64, :], in1=cs, op=ALU.mult),
          reads=[raw_tok, tab_t], writes=[tmp1_t])
    kb.op("dve", lambda: nc.vector.tensor_tensor(out=tmp2[:], in0=rot_b[0:64, :], in1=sn, op=ALU.mult),
          reads=[rot_tok, tab_t], writes=[tmp2_t])
    wr = dict(pwrites=[out_tok]) if pw else dict(writes=[out_tok])
    kb.op("pool", lambda: nc.gpsimd.tensor_tensor(out=out_ap, in0=tmp1[:], in1=tmp2[:], op=ALU.add),
          reads=[tmp1_t, tmp2_t], **wr)


def build_fused(stop=None):
    kb = KB()
    nc = kb.nc
    kb.cc_sem = kb.new_sem("cc")
    kb.cc_cnt = 0
    pid = nc.partition_id()
    di = lambda n, sh, dt=F32: kb.dram(n, sh, dt, "ExternalInput")
    x = di("x", [TPC, D]); pos = di("pos", [1, TPC], I32); pos_all = di("pos_all", [1, S], I32)
    invf = di("invf", [64, 1]); ident_d = di("ident", [128, 128]); mask_d = di("mask", [128, 128])
    kcoef = di("kcoef", [4, 4]); qcoef = di("qcoef", [4, 4])
    g_attn0 = di("g_attn0", [D]); g_q = di("g_q", [512]); g_kvl = di("g_kvl", [256])
    w_dq = di("w_dq", [D, 512]); w_dkv = di("w_dkv", [D, 320])
    w_uq_h = di("w_uq_h", [512, 192]); w_ukv_h = di("w_ukv_h", [256, 256])
    w_o0 = di("w_o0", [D, D]); g_ffn0 = di("g_ffn0", [D]); w_gu0 = di("w_gu0", [D, 2 * DFF]); w_d0 = di("w_d0", [DFF, D])
    g_dkv = di("g_dkv", [D]); g_attn1 = di("g_attn1", [D])
    w_k_h = di("w_k_h", [D, 128]); w_v_h = di("w_v_h", [D, 128]); w_q_h = di("w_q_h", [D, 128])
    lam_d = di("lam", [4, 64]); subln_d = di("subln", [1, 128])
    w_o1 = di("w_o1", [D, D]); g_ffn1 = di("g_ffn1", [D]); w_gu1 = di("w_gu1", [D, 2 * DFF]); w_d1 = di("w_d1", [DFF, D])
    g_fin = di("g_fin", [1, D])
    out = kb.dram("out", [TPC, D], F32, "ExternalOutput")
    gin = kb.dram("gin", [GROWS, 512], F32, "Internal")
    gout = kb.dram("gout", [NCORES * GROWS, 512], F32, "Internal")
    gin_bf = gin.bitcast(BF16)
    gout_bf = gout.bitcast(BF16).rearrange("(r a) b -> r a b", r=NCORES)
    gin_O = gin_bf.rearrange("a (b d) -> (a b) d", d=128)
    gout_O = gout_bf.rearrange("r a (b d) -> r (a b) d", d=128)
    out_bf = out.bitcast(BF16)
    gin_t = kb.tok(); gout_t = kb.tok(); outd_t = kb.tok()
    lambda_init = 0.8 - 0.6 * float(np.exp(-0.3 * 1))

    kb.make_banks()
    setup_consts(kb, ident_d)
    setup_mask(kb, mask_d)
    invf_sb = kb.sb("invf_sb", [64, 1], F32); invf_t = kb.tok()
    kb.dma("sp", invf_sb[:], invf, owner=invf_t, writes=[invf_t])

    kb.push_scope()
    gA, gA_t = load_gain(kb, "a_gA", g_attn0, 8)
    stage = kb.sb("a_stage", [128, 6144], F32); stage_t = kb.tok()
    Wdq = kb.sb("a_Wdq", [128, 8, 512], BF16); Wdkv = kb.sb("a_Wdkv", [128, 8, 320], BF16)
    Wkrot = kb.sb("a_Wkrot", [128, 8, 64], BF16)
    Wdq_t = prep_weight(kb, w_dq, gA, gA_t, Wdq, 8, 512, stage, stage_t)
    Wdkv_t = prep_weight(kb, w_dkv, gA, gA_t, Wdkv, 8, 320, stage, stage_t)
    Wkrot_t = kb.tok()
    kb.op("dve", lambda: nc.vector.tensor_scalar(out=Wkrot[:, :, 0:32], in0=Wdkv[:, :, 288:320], scalar1=-1.0,
                                                 scalar2=None, op0=ALU.mult), reads=[Wdkv_t], writes=[Wkrot_t])
    kb.op("dve", lambda: nc.vector.tensor_copy(out=Wkrot[:, :, 32:64], in_=Wdkv[:, :, 256:288]),
          reads=[Wdkv_t], pwrites=[Wkrot_t])
    gQ, gQ_t = load_gain(kb, "a_gQ", g_q, 4)
    gK, gK_t = load_gain(kb, "a_gK", g_kvl, 2)
    cosT = kb.sb("a_cos", [64, TPC], F32); sinT = kb.sb("a_sin", [64, TPC], F32); tab_t = kb.tok()
    posi = kb.sb("a_posi", [64, TPC], I32)
    rope_tables(kb, pos, TPC, (stage[0:64, 0:TPC], stage[0:64, TPC:2 * TPC], stage[0:64, 2 * TPC:3 * TPC]),
                posi[:], cosT[:], sinT[:], invf_sb, invf_t, tab_t, stage_t)
    xt = [(kb.sb("a_xt%d" % i, [128, D], F32), kb.tok()) for i in range(2)]
    junk = kb.sb("a_junk", [128, D], BF16); junk_t = kb.tok()
    xs = kb.sb("a_xs", [128, D], BF16); xs_t = kb.tok()
    st4 = kb.sb("a_st4", [128, 8], F32); st_t = kb.tok()
    hnT = kb.sb("a_hnT", [128, 8, 512], BF16); hnT_t = kb.tok()
    cqs = kb.sb("a_cqs", [128, 512], BF16); cqs_t = kb.tok()
    ckvs = kb.sb("a_ckvs", [128, 256], BF16); ckvs_t = kb.tok()
    cqT = [(kb.sb("a_cqT%d" % i, [128, 4, 512], BF16), kb.tok()) for i in range(2)]
    ckvT = [(kb.sb("a_ckvT%d" % i, [128, 2, 512], BF16), kb.tok()) for i in range(2)]
    kpe_st = [(kb.sb("a_kpe%d" % i, [64, 512], BF16), kb.tok()) for i in range(2)]
    tmp = ((kb.sb("a_tmp1", [64, 512], F32), kb.tok()), (kb.sb("a_tmp2", [64, 512], F32), kb.tok()))
    for c in range(NT // 4):
        cq_c, cq_ct = cqT[c % 2]; ck_c, ck_ct = ckvT[c % 2]; kp_c, kp_ct = kpe_st[c % 2]
        for i in range(4):
            tt = 4 * c + i
            xb, xbt = xt[tt % 2]
            kb.dma("sp", xb[:], x[tt * 128:(tt + 1) * 128, :], owner=xbt, writes=[xbt])
            kb.op("act", lambda: nc.scalar.activation(out=junk[:], in_=xb[:], func=AF.Square, accum_out=st4[:, 0:1]),
                  reads=[xbt], writes=[junk_t, st_t])
            rms_scale(kb, st4[:, 0:1], st4[:, 1:2], D, st_t, st_t)
            kb.op("dve", lambda: nc.vector.tensor_scalar(out=xs[:], in0=xb[:], scalar1=st4[:, 1:2], scalar2=None,
                                                         op0=ALU.mult), reads=[xbt, st_t], writes=[xs_t])
            transpose_to(kb, lambda j: xs[:, j * 128:(j + 1) * 128], xs_t, 8, 128,
                         lambda: hnT[:, :, i * 128:(i + 1) * 128], hnT_t, evac="act", pw=(i > 0))
            cq_b, cq_bt = kb.bank()
            for j in range(8):
                kb.op("pe", lambda j=j: nc.tensor.matmul(cq_b[:, 0:512], hnT[:, j, i * 128:(i + 1) * 128],
                                                         Wdq[:, j, :], start=(j == 0), stop=(j == 7)),
                      reads=[hnT_t, Wdq_t], writes=[cq_bt])
            ck_b, ck_bt = kb.bank()
            for j in range(8):
                kb.op("pe", lambda j=j: nc.tensor.matmul(ck_b[:, 0:256], hnT[:, j, i * 128:(i + 1) * 128],
                                                         Wdkv[:, j, 0:256], start=(j == 0), stop=(j == 7)),
                      reads=[hnT_t, Wdkv_t], writes=[ck_bt])
            kb.op("act", lambda: nc.scalar.activation(out=junk[:, 0:512], in_=cq_b[:, 0:512], func=AF.Square,
                                                      accum_out=st4[:, 2:3]), reads=[cq_bt], writes=[junk_t, st_t])
            rms_scale(kb, st4[:, 2:3], st4[:, 3:4], 512, st_t, st_t)
            kb.op("dve", lambda: nc.vector.tensor_scalar(out=cqs[:], in0=cq_b[:, 0:512], scalar1=st4[:, 3:4],
                                                         scalar2=None, op0=ALU.mult), reads=[cq_bt, st_t], writes=[cqs_t])
            kb.op("act", lambda: nc.scalar.activation(out=junk[:, 0:256], in_=ck_b[:, 0:256], func=AF.Square,
                                                      accum_out=st4[:, 4:5]), reads=[ck_bt], writes=[junk_t, st_t])
            rms_scale(kb, st4[:, 4:5], st4[:, 5:6], 256, st_t, st_t)
            kb.op("dve", lambda: nc.vector.tensor_scalar(out=ckvs[:], in0=ck_b[:, 0:256], scalar1=st4[:, 5:6],
                                                         scalar2=None, op0=ALU.mult), reads=[ck_bt, st_t], writes=[ckvs_t])
            transpose_to(kb, lambda j: cqs[:, j * 128:(j + 1) * 128], cqs_t, 4, 128,
                         lambda: cq_c[:, :, i * 128:(i + 1) * 128], cq_ct, evac="dve", pw=(i > 0))
            transpose_to(kb, lambda j: ckvs[:, j * 128:(j + 1) * 128], ckvs_t, 2, 128,
                         lambda: ck_c[:, :, i * 128:(i + 1) * 128], ck_ct, evac="dve", pw=(i > 0))
        kp_b, kp_bt = kb.bank()
        for j in range(8):
            kb.op("pe", lambda j=j: nc.tensor.matmul(kp_b[0:64, :], Wdkv[:, j, 256:320], hnT[:, j, :],
                                                     start=(j == 0), stop=(j == 7)), reads=[hnT_t, Wdkv_t], writes=[kp_bt])
        kr_b, kr_bt = kb.bank()
        for j in range(8):
            kb.op("pe", lambda j=j: nc.tensor.matmul(kr_b[0:64, :], Wkrot[:, j, :], hnT[:, j, :],
                                                     start=(j == 0), stop=(j == 7)), reads=[hnT_t, Wkrot_t], writes=[kr_bt])
        rope_apply(kb, kp_b, kp_bt, kr_b, kr_bt, cosT[:, c * 512:(c + 1) * 512], sinT[:, c * 512:(c + 1) * 512], tab_t,
                   tmp, kp_c[:], kp_ct, False)
        hf, cc = c // 2, c % 2
        dst, dtok = (gin_bf, gin_t) if hf == 0 else (out_bf, outd_t)
        csl = slice(cc * 512, (cc + 1) * 512)
        kb.dma("pool", dst[0:512, csl].rearrange("(j p) t -> p j t", p=128), cq_c[:], owner=cq_ct, reads=[cq_ct], pwrites=[dtok])
        kb.dma("pool", dst[512:768, csl].rearrange("(j p) t -> p j t", p=128), ck_c[:], owner=ck_ct, reads=[ck_ct], pwrites=[dtok])
        kb.dma("pool", dst[768:832, csl], kp_c[:], owner=kp_ct, reads=[kp_ct], pwrites=[dtok])
    kb_barrier(kb)
    kb.pop_scope()

    if stop == "A":
        return kb
    kb.push_scope()
    QTa_sb = kb.sb("QTa_sb", [128, S], BF16); QTb_sb = kb.sb("QTb_sb", [128, S], BF16)
    KTa_sb = kb.sb("KTa_sb", [128, S], BF16); KTb_sb = kb.sb("KTb_sb", [128, S], BF16)
    V_sb = kb.sb("V_sb", [128, S // 128, 129], BF16)
    ctok = [kb.tok() for _ in range(S // 512)]
    vinit_t = kb.tok()
    kb.op("pool", lambda: nc.gpsimd.memset(V_sb[:], 1.0), writes=[vinit_t])
    kb.op("pool", lambda: nc.gpsimd.memset(QTb_sb[64:128, :], 0.0), pwrites=[vinit_t])
    kb.op("pool", lambda: nc.gpsimd.memset(KTb_sb[64:128, :], 0.0), pwrites=[vinit_t])
    kb.push_scope()
    gQ, gQ_t = load_gain(kb, "b_gQ", g_q, 4)
    gK, gK_t = load_gain(kb, "b_gK", g_kvl, 2)
    stage = kb.sb("b_stage", [128, 1024], F32); stage_t = kb.tok()
    Wuq = kb.sb("b_Wuq", [128, 4, 192], BF16); Wqrot = kb.sb("b_Wqrot", [128, 4, 64], BF16)
    Wukv = kb.sb("b_Wukv", [128, 2, 256], BF16)
    Wuq_t = prep_weight(kb, w_uq_h, gQ, gQ_t, Wuq, 4, 192, stage, stage_t, extra=(128 + 64) ** -0.5)
    Wukv_t = prep_weight(kb, w_ukv_h, gK, gK_t, Wukv, 2, 256, stage, stage_t)
    Wqrot_t = kb.tok()
    kb.op("dve", lambda: nc.vector.tensor_scalar(out=Wqrot[:, :, 0:32], in0=Wuq[:, :, 160:192], scalar1=-1.0,
                                                 scalar2=None, op0=ALU.mult), reads=[Wuq_t], writes=[Wqrot_t])
    kb.op("dve", lambda: nc.vector.tensor_copy(out=Wqrot[:, :, 32:64], in_=Wuq[:, :, 128:160]),
          reads=[Wuq_t], pwrites=[Wqrot_t])
    cqc = [(kb.sb("b_cq%d" % i, [128, 4, 512], BF16), kb.tok()) for i in range(2)]
    ckc = [(kb.sb("b_ck%d" % i, [128, 2, 512], BF16), kb.tok()) for i in range(2)]
    tabs = [(kb.sb("b_cos%d" % i, [64, 512], F32), kb.sb("b_sin%d" % i, [64, 512], F32), kb.tok()) for i in range(2)]
    scr = [(kb.sb("b_A0_%d" % i, [64, 512], F32), kb.sb("b_A1_%d" % i, [64, 512], F32),
            kb.sb("b_A2_%d" % i, [64, 512], F32), kb.sb("b_pi_%d" % i, [64, 512], I32), kb.tok()) for i in range(1)]
    tmp = ((kb.sb("b_tmp1", [64, 512], F32), kb.tok()), (kb.sb("b_tmp2", [64, 512], F32), kb.tok()))
    n = 0
    for hf in range(2):
        if hf == 1:
            kb.dma("pool", gin_bf[0:832, :], out_bf[0:832, 0:1024], owner=gin_t, reads=[outd_t], writes=[gin_t])
        kb_gather(kb, gin, gout, reads=[gin_t], writes=[gout_t])
        if stop == "G1":
            kb_barrier(kb)
            return kb
        if stop == "Ba" and hf == 1:
            kb_barrier(kb)
            return kb
        for r in range(NCORES):
            for cc in range(2):
                gc = r * 4 + hf * 2 + cc
                tsl = slice(gc * 512, (gc + 1) * 512)
                csl = slice(cc * 512, (cc + 1) * 512)
                cq_c, cq_ct = cqc[n % 2]; ck_c, ck_ct = ckc[n % 2]
                cs_, sn_, tb_t = tabs[n % 2]; A0, A1, A2, pi_, sc_t = scr[0]
                n += 1
                kb.dma("sp", cq_c[:], gout_bf[r, 0:512, csl].rearrange("(j p) t -> p j t", p=128), owner=cq_ct,
                       reads=[gout_t], writes=[cq_ct])
                kb.dma("sp", ck_c[:], gout_bf[r, 512:768, csl].rearrange("(j p) t -> p j t", p=128), owner=ck_ct,
                       reads=[gout_t], writes=[ck_ct])
                ct = ctok[gc]
                kb.dma("sp", KTb_sb[0:64, tsl], gout_bf[r, 768:832, csl], owner=ct, reads=[gout_t, vinit_t], writes=[ct])
                rope_tables(kb, pos_all[:, tsl], 512, (A0[:], A1[:], A2[:]), pi_[:], cs_[:], sn_[:], invf_sb, invf_t, tb_t, sc_t)
                qa_b, qa_bt = kb.bank()
                for j in range(4):
                    kb.op("pe", lambda j=j: nc.tensor.matmul(qa_b[:, :], Wuq[:, j, 0:128], cq_c[:, j, :],
                                                             start=(j == 0), stop=(j == 3)), reads=[cq_ct, Wuq_t], writes=[qa_bt])
                kb.op("act", lambda: nc.scalar.copy(out=QTa_sb[:, tsl], in_=qa_b[:, :]), reads=[qa_bt], pwrites=[ct])
                qp_b, qp_bt = kb.bank()
                for j in range(4):
                    kb.op("pe", lambda j=j: nc.tensor.matmul(qp_b[0:64, :], Wuq[:, j, 128:192], cq_c[:, j, :],
                                                             start=(j == 0), stop=(j == 3)), reads=[cq_ct, Wuq_t], writes=[qp_bt])
                qr_b, qr_bt = kb.bank()
                for j in range(4):
                    kb.op("pe", lambda j=j: nc.tensor.matmul(qr_b[0:64, :], Wqrot[:, j, :], cq_c[:, j, :],
                                                             start=(j == 0), stop=(j == 3)), reads=[cq_ct, Wqrot_t], writes=[qr_bt])
                rope_apply(kb, qp_b, qp_bt, qr_b, qr_bt, cs_[:], sn_[:], tb_t, tmp, QTb_sb[0:64, tsl], ct, True)
                ka_b, ka_bt = kb.bank()
                for j in range(2):
                    kb.op("pe", lambda j=j: nc.tensor.matmul(ka_b[:, :], Wukv[:, j, 0:128], ck_c[:, j, :],
                                                             start=(j == 0), stop=(j == 1)), reads=[ck_ct, Wukv_t], writes=[ka_bt])
                kb.op("dve", lambda: nc.vector.tensor_copy(out=KTa_sb[:, tsl], in_=ka_b[:, :]), reads=[ka_bt], pwrites=[ct])
                vb, vbt = kb.bank()
                for i in range(4):
                    for j in range(2):
                        kb.op("pe", lambda i=i, j=j: nc.tensor.matmul(vb[:, i * 128:(i + 1) * 128], ck_c[:, j, i * 128:(i + 1) * 128],
                                                                      Wukv[:, j, 128:256], start=(j == 0), stop=(j == 1)),
                              reads=[ck_ct, Wukv_t], writes=[vbt])
                kb.op("act", lambda: nc.scalar.copy(out=V_sb[:, gc * 4:gc * 4 + 4, 0:128],
                                                    in_=vb[:, :].rearrange("p (i d) -> p i d", i=4)),
                      reads=[vbt, vinit_t], pwrites=[ct])
    kb_barrier(kb)
    kb.pop_scope()

    if stop == "B":
        return kb
    qorder = [qc for qc in range(S // 512) if (qc % 4) < 2] + [qc for qc in range(S // 512) if (qc % 4) >= 2]

    def o_row0(qc):
        return (qc // 4) * 1024 + (qc % 2) * 512

    outO = out_bf[0:512, :].rearrange("a (b d) -> (a b) d", d=128)

    def attention_phase(passes, finalize_fn, ost, q_res, extra_q_reads=()):
        done = {0: 0, 1: 0}

        def epilogue(qc, pi, oset):
            hf = (qc % 4) // 2
            o_sb, o_tok = ost[done[0] % 2 if hf == 0 else done[1] % 2]
            fin = finalize_fn(qc, pi, oset, o_sb, o_tok)
            if fin:
                r0 = o_row0(qc)
                dstv, dtok = (gin_O, gin_t) if hf == 0 else (outO, outd_t)
                with nc.allow_non_contiguous_dma(reason="256B rows"):
                    kb.dma("pool", dstv[r0:r0 + 512, :].rearrange("(i p) d -> p i d", p=128), o_sb[:], owner=o_tok,
                           reads=[o_tok], pwrites=[dtok])
                done[hf] += 1
                if hf == 0 and done[0] == 16:
                    kb_gather(kb, gin, gout, reads=[gin_t], writes=[gout_t])

        attention_core(kb, passes, V_sb, lambda kbi: ctok[kbi // 4], epilogue,
                       q_resident=((lambda qc: ctok[qc]) if q_res else None), qc_order=qorder,
                       extra_q_reads=extra_q_reads, filler=(0 if q_res else 384))

    Ost = [(kb.sb("Ost%d" % i, [128, 4, 128], BF16), kb.tok()) for i in range(2)]
    rc = kb.sb("rc", [128, 8], F32); rc_t = kb.tok()
    gathers = []

    def fin_mla(qc, pi, oset, dst, dtok):
        for b, (ob, obtok) in enumerate(oset):
            sums = ob[:, 0:258].rearrange("p (i c) -> p i c", c=129)[:, :, 128:129]
            kb.op("dve", lambda: nc.vector.reciprocal(out=rc[:, 2 * b:2 * b + 2].rearrange("p (i c) -> p i c", c=1),
                                                     in_=sums), reads=[obtok], writes=[rc_t])
            for i in range(2):
                kb.op("dve", lambda i=i: nc.vector.tensor_scalar(
                    out=dst[:, 2 * b + i, :], in0=ob[:, i * 129:i * 129 + 128],
                    scalar1=rc[:, 2 * b + i:2 * b + i + 1], scalar2=None, op0=ALU.mult),
                    reads=[obtok, rc_t], **(dict(writes=[dtok]) if (b == 0 and i == 0) else dict(pwrites=[dtok])))
        return True

    attention_phase([dict(K=[(KTa_sb, 128), (KTb_sb, 128)], Q=[(QTa_sb, 128), (QTb_sb, 128)])], fin_mla, Ost, True)
    kb_barrier(kb)
    kb.pop_scope()

    if stop == "L2":
        return kb
    h_res, h_tok = make_h(kb)
    for tt in range(NT):
        kb.dma("sp", h_res[:, tt, :], x[tt * 128:(tt + 1) * 128, :], owner=h_tok[tt], writes=[h_tok[tt]])

    def o_gather_and_project(w_o):
        kb.push_scope()
        stage = kb.sb("pa_stage", [128, 8 * 1024], F32); stage_t = kb.tok()
        Wo = kb.sb("pa_Wo", [128, 8, 1024], BF16)
        Wo_t = prep_weight(kb, w_o, None, None, Wo, 8, 1024, stage, stage_t)
        oall = [(kb.sb("pa_oall%d" % k, [128, D], BF16), kb.tok()) for k in range(8)]
        oT = kb.sb("pa_oT", [128, 8, 128], BF16); oT_t = kb.tok()

        def load_tiles():
            for k in range(8):
                ob, obt = oall[k]
                with nc.allow_non_contiguous_dma(reason="256B head rows"):
                    kb.dma("sp", ob[:].rearrange("p (h d) -> p h d", h=H),
                           gout_O[:, bass.ds(pid * 1024 + k * 128, 128), :].rearrange("h p d -> p h d"),
                           owner=obt, reads=[gout_t], writes=[obt])

        def project(hf):
            for k in range(8):
                tt = hf * 8 + k
                ob, obt = oall[k]
                transpose_to(kb, lambda j: ob[:, j * 128:(j + 1) * 128], obt, 8, 128, lambda: oT[:, :, :], oT_t, evac="act")
                for half in range(2):
                    pb, pbt = kb.bank()
                    for j in range(8):
                        kb.op("pe", lambda j=j: nc.tensor.matmul(pb[:, :], oT[:, j, :], Wo[:, j, half * 512:(half + 1) * 512],
                                                                 start=(j == 0), stop=(j == 7)),
                              reads=[oT_t, Wo_t], writes=[pbt])
                    hs = h_res[:, tt, half * 512:(half + 1) * 512]
                    kb.op("dve", lambda: nc.vector.tensor_tensor(out=hs, in0=pb[:, :], in1=hs, op=ALU.add),
                          reads=[pbt], pwrites=[h_tok[tt]])

        load_tiles()
        kb.dma("pool", gin_bf, out_bf[0:512, :].rearrange("a (b c) -> (a b) c", c=1024), owner=gin_t,
               reads=[outd_t], writes=[gin_t])
        kb_gather(kb, gin, gout, reads=[gin_t], writes=[gout_t])
        project(0)
        load_tiles()
        project(1)
        kb_barrier(kb)
        kb.pop_scope()

    o_gather_and_project(w_o0)
    if stop == "C1":
        kb_barrier(kb)
        return kb
    phase_ffn(kb, h_res, h_tok, g_ffn0, w_gu0, w_d0)
    if stop == "C":
        return kb

    kb.push_scope()
    junk = kb.sb("d_junk", [128, D], BF16); xs = kb.sb("d_xs", [128, D], BF16); st4 = kb.sb("d_st", [128, 4], F32)
    work = (junk, kb.tok(), xs, kb.tok(), st4, kb.tok())
    hT = kb.sb("d_hT", [128, 8, TPC], BF16); hT_t = kb.tok()
    norm_transpose(kb, h_res, h_tok, hT, hT_t, work)
    kb.dma("pool", gin_bf.rearrange("(j p) t -> p j t", p=128), hT[:, :, 0:1024], owner=gin_t, reads=[hT_t], writes=[gin_t])
    stg = out_bf[0:512, :].rearrange("a (b c) -> (a b) c", c=1024)
    kb.dma("pool", stg.rearrange("(j p) t -> p j t", p=128), hT[:, :, 1024:2048], owner=outd_t, reads=[hT_t], writes=[outd_t])
    kb_barrier(kb)
    kb.pop_scope()

    kb.push_scope()
    K1_sb = kb.sb("K1_sb", [68, S], BF16); K2_sb = kb.sb("K2_sb", [68, S], BF16)
    V_sb = kb.sb("V4_sb", [128, S // 128, 129], BF16)
    ctok = [kb.tok() for _ in range(S // 512)]
    vinit_t = kb.tok()
    kb.op("pool", lambda: nc.gpsimd.memset(V_sb[:], 1.0), writes=[vinit_t])
    Q1d = out_bf[512:1056, :].rearrange("(q a) b -> q (a b)", q=68)
    Q2d = out_bf[1056:1600, :].rearrange("(q a) b -> q (a b)", q=68)
    qd_t = kb.tok()
    kb.push_scope()
    gDK, gDK_t = load_gain(kb, "d_gk", g_dkv, 8)
    gA1, gA1_t = load_gain(kb, "d_ga", g_attn1, 8)
    stage = kb.sb("d_stage", [128, 1024], F32); stage_t = kb.tok()
    Wk = kb.sb("d_Wk", [128, 8, 128], BF16); Wv = kb.sb("d_Wv", [128, 8, 128], BF16); Wq = kb.sb("d_Wq", [128, 8, 128], BF16)
    Wk_t = prep_weight(kb, w_k_h, gDK, gDK_t, Wk, 8, 128, stage, stage_t)
    Wv_t = prep_weight(kb, w_v_h, gDK, gDK_t, Wv, 8, 128, stage, stage_t)
    Wq_t = prep_weight(kb, w_q_h, gA1, gA1_t, Wq, 8, 128, stage, stage_t, extra=64 ** -0.5)
    cf = kb.sb("d_cf", [68, 8], F32); cf_t = kb.tok()
    kb.dma("sp", cf[64:68, 0:4], kcoef, owner=cf_t, writes=[cf_t])
    kb.dma("sp", cf[64:68, 4:8], qcoef, owner=cf_t, pwrites=[cf_t])
    xc = [(kb.sb("d_xc%d" % i, [128, 8, 512], BF16), kb.tok()) for i in range(2)]
    qst = [(kb.sb("d_qst%d" % i, [68, 2, 512], BF16), kb.tok()) for i in range(2)]
    pI = kb.sb("d_pI", [68, 512], I32); aI = kb.sb("d_aI", [68, 512], I32)
    aF = kb.sb("d_aF", [68, 512], F32); bF = kb.sb("d_bF", [68, 512], F32); t1 = kb.sb("d_t1", [68, 512], F32)
    aug_t = kb.tok()
    V_ = nc.vector
    R = slice(64, 68)
    n = 0
    for hf in range(2):
        if hf == 1:
            kb.dma("pool", gin_bf, stg, owner=gin_t, reads=[outd_t], writes=[gin_t])
        kb_gather(kb, gin, gout, reads=[gin_t], writes=[gout_t])
        for r in range(NCORES):
            for cc in range(2):
                gc = r * 4 + hf * 2 + cc
                tsl = slice(gc * 512, (gc + 1) * 512)
                csl = slice(cc * 512, (cc + 1) * 512)
                x_c, x_ct = xc[n % 2]; q_s, q_st = qst[n % 2]
                n += 1
                ct = ctok[gc]
                kb.dma("sp", x_c[:], gout_bf[r, :, csl].rearrange("(j p) t -> p j t", p=128), owner=x_ct,
                       reads=[gout_t], writes=[x_ct])
                kb.dma("sp", pI[R, :], pos_all[:, tsl].partition_broadcast(4), owner=aug_t, writes=[aug_t])
                seq = [
                    lambda: V_.tensor_scalar(out=aI[R, :], in0=pI[R, :], scalar1=7, scalar2=None, op0=ALU.arith_shift_right),
                    lambda: V_.tensor_copy(out=aF[R, :], in_=aI[R, :]),
                    lambda: V_.tensor_scalar(out=aI[R, :], in0=pI[R, :], scalar1=127, scalar2=None, op0=ALU.bitwise_and),
                    lambda: V_.tensor_copy(out=bF[R, :], in_=aI[R, :]),
                    lambda: V_.tensor_scalar(out=t1[R, :], in0=aF[R, :], scalar1=cf[R, 0:1], scalar2=cf[R, 2:3], op0=ALU.mult, op1=ALU.add),
                ]
                for fn in seq:
                    kb.op("dve", fn, reads=[cf_t], writes=[aug_t])
                kb.op("dve", lambda: V_.scalar_tensor_tensor(out=K1_sb[R, tsl], in0=bF[R, :], scalar=cf[R, 1:2], in1=t1[R, :],
                                                             op0=ALU.mult, op1=ALU.add), reads=[aug_t, cf_t], writes=[ct])
                kb.op("dve", lambda: V_.tensor_copy(out=K2_sb[R, tsl], in_=K1_sb[R, tsl]), reads=[ct], pwrites=[ct])
                kb.op("dve", lambda: V_.tensor_scalar(out=t1[R, :], in0=aF[R, :], scalar1=cf[R, 4:5], scalar2=cf[R, 6:7],
                                                      op0=ALU.mult, op1=ALU.add), reads=[aug_t, cf_t], writes=[aug_t])
                kb.op("dve", lambda: V_.scalar_tensor_tensor(out=q_s[R, 0, :], in0=bF[R, :], scalar=cf[R, 5:6], in1=t1[R, :],
                                                             op0=ALU.mult, op1=ALU.add), reads=[aug_t, cf_t], writes=[q_st])
                kb.op("dve", lambda: V_.tensor_copy(out=q_s[R, 1, :], in_=q_s[R, 0, :]), reads=[q_st], pwrites=[q_st])
                for i in range(2):
                    kb_, kbt_ = kb.bank()
                    for j in range(8):
                        kb.op("pe", lambda j=j: nc.tensor.matmul(kb_[0:64, :], Wk[:, j, i * 64:(i + 1) * 64], x_c[:, j, :],
                                                                 start=(j == 0), stop=(j == 7)), reads=[x_ct, Wk_t], writes=[kbt_])
                    Ki = K1_sb if i == 0 else K2_sb
                    kb.op("act", lambda: nc.scalar.copy(out=Ki[0:64, tsl], in_=kb_[0:64, :]), reads=[kbt_], pwrites=[ct])
                    qb_, qbt_ = kb.bank()
                    for j in range(8):
                        kb.op("pe", lambda j=j: nc.tensor.matmul(qb_[0:64, :], Wq[:, j, i * 64:(i + 1) * 64], x_c[:, j, :],
                                                                 start=(j == 0), stop=(j == 7)), reads=[x_ct, Wq_t], writes=[qbt_])
                    kb.op("dve", lambda: nc.vector.tensor_copy(out=q_s[0:64, i, :], in_=qb_[0:64, :]), reads=[qbt_], pwrites=[q_st])
                vb, vbt = kb.bank()
                for i in range(4):
                    for j in range(8):
                        kb.op("pe", lambda i=i, j=j: nc.tensor.matmul(vb[:, i * 128:(i + 1) * 128], x_c[:, j, i * 128:(i + 1) * 128],
                                                                      Wv[:, j, :], start=(j == 0), stop=(j == 7)),
                              reads=[x_ct, Wv_t], writes=[vbt])
                kb.op("act", lambda: nc.scalar.copy(out=V_sb[:, gc * 4:gc * 4 + 4, 0:128],
                                                    in_=vb[:, :].rearrange("p (i d) -> p i d", i=4)),
                      reads=[vbt, vinit_t], pwrites=[ct])
                kb.dma("pool", Q1d[:, tsl], q_s[:, 0, :], owner=q_st, reads=[q_st], pwrites=[qd_t])
                kb.dma("pool", Q2d[:, tsl], q_s[:, 1, :], owner=q_st, reads=[q_st], pwrites=[qd_t])
    kb_barrier(kb)
    kb.pop_scope()

    lam_sb = kb.sb("lam_sb", [128, 4, 64], F32); lam_t = kb.tok()
    kb.dma("sp", lam_sb[:].rearrange("p a d -> p (a d)"), lam_d.rearrange("a d -> (a d)").partition_broadcast(128),
           owner=lam_t, writes=[lam_t])
    lj = kb.sb("lam_j", [128, 64], F32); lv = kb.sb("lam_v", [128, 4], F32); lv_t = kb.tok()
    kb.op("dve", lambda: nc.vector.tensor_tensor(out=lj[:], in0=lam_sb[:, 0, :], in1=lam_sb[:, 1, :], op=ALU.mult),
          reads=[lam_t], writes=[lv_t])
    kb.op("dve", lambda: nc.vector.tensor_reduce(out=lv[:, 0:1], in_=lj[:], axis=mybir.AxisListType.X, op=ALU.add),
          reads=[lv_t], writes=[lv_t])
    kb.op("dve", lambda: nc.vector.tensor_tensor(out=lj[:], in0=lam_sb[:, 2, :], in1=lam_sb[:, 3, :], op=ALU.mult),
          reads=[lam_t, lv_t], writes=[lv_t])
    kb.op("dve", lambda: nc.vector.tensor_reduce(out=lv[:, 1:2], in_=lj[:], axis=mybir.AxisListType.X, op=ALU.add),
          reads=[lv_t], writes=[lv_t])
    kb.op("act", lambda: nc.scalar.activation(out=lv[:, 0:2], in_=lv[:, 0:2], func=AF.Exp), reads=[lv_t], writes=[lv_t])
    kb.op("dve", lambda: nc.vector.scalar_tensor_tensor(out=lv[:, 2:3], in0=lv[:, 1:2], scalar=-float(lambda_init),
                                                        in1=lv[:, 0:1], op0=ALU.add, op1=ALU.subtract),
          reads=[lv_t], writes=[lv_t])
    sub_sb = kb.sb("sub_sb", [128, 128], F32); sub_t = kb.tok()
    kb.dma("sp", sub_sb[:], subln_d.partition_broadcast(128), owner=sub_t, writes=[sub_t])
    kb.op("dve", lambda: nc.vector.tensor_scalar(out=sub_sb[:], in0=sub_sb[:], scalar1=float(1.0 - lambda_init),
                                                 scalar2=None, op0=ALU.mult), reads=[sub_t], writes=[sub_t])
    o1 = kb.sb("o1n", [128, 4, 128], F32); o1_t = kb.tok()
    att = kb.sb("attd", [128, 4, 128], F32); att_t = kb.tok()
    junk4 = kb.sb("junk4", [128, 128], F32); junk4_t = kb.tok()
    Ost4 = [(kb.sb("Ost4_%d" % i, [128, 4, 128], BF16), kb.tok()) for i in range(2)]
    rc4 = kb.sb("rc4", [128, 8], F32); rc4_t = kb.tok()
    ss = kb.sb("ss4", [128, 8], F32); ss_t = kb.tok()

    def fin_diff(qc, pi, oset, dst, dtok):
        for b, (ob, obtok) in enumerate(oset):
            sums = ob[:, 0:258].rearrange("p (i c) -> p i c", c=129)[:, :, 128:129]
            kb.op("dve", lambda: nc.vector.reciprocal(out=rc4[:, 2 * b:2 * b + 2].rearrange("p (i c) -> p i c", c=1),
                                                     in_=sums), reads=[obtok], writes=[rc4_t])
            for i in range(2):
                qi = 2 * b + i
                if pi == 0:
                    kb.op("dve", lambda: nc.vector.tensor_scalar(
                        out=o1[:, qi, :], in0=ob[:, i * 129:i * 129 + 128], scalar1=rc4[:, qi:qi + 1], scalar2=None,
                        op0=ALU.mult), reads=[obtok, rc4_t], **(dict(writes=[o1_t]) if qi == 0 else dict(pwrites=[o1_t])))
                else:
                    kb.op("dve", lambda: nc.vector.tensor_scalar(
                        out=att[:, qi, :], in0=ob[:, i * 129:i * 129 + 128], scalar1=rc4[:, qi:qi + 1],
                        scalar2=lv[:, 2:3], op0=ALU.mult, op1=ALU.mult),
                        reads=[obtok, rc4_t, lv_t], **(dict(writes=[att_t]) if qi == 0 else dict(pwrites=[att_t])))
                    kb.op("pool", lambda: nc.gpsimd.tensor_tensor(out=att[:, qi, :], in0=att[:, qi, :], in1=o1[:, qi, :],
                                                                  op=ALU.add), reads=[o1_t], pwrites=[att_t])
        if pi == 0:
            return False
        for qi in range(4):
            kb.op("act", lambda: nc.scalar.activation(out=junk4[:], in_=att[:, qi, :], func=AF.Square,
                                                      accum_out=ss[:, qi:qi + 1]),
                  reads=[att_t], writes=[junk4_t], pwrites=[ss_t])
        kb.op("act", lambda: nc.scalar.activation(out=ss[:, 4:8], in_=ss[:, 0:4], func=AF.Ln, scale=1.0 / 128,
                                                  bias=kb.eps_ap), reads=[ss_t, kb.eps_tok], writes=[ss_t])
        kb.op("act", lambda: nc.scalar.activation(out=ss[:, 4:8], in_=ss[:, 4:8], func=AF.Exp, scale=-0.5),
              reads=[ss_t], writes=[ss_t])
        for qi in range(4):
            kb.op("dve", lambda: nc.vector.scalar_tensor_tensor(
                out=dst[:, qi, :], in0=att[:, qi, :], scalar=ss[:, 4 + qi:5 + qi], in1=sub_sb[:],
                op0=ALU.mult, op1=ALU.mult), reads=[att_t, ss_t, sub_t],
                **(dict(writes=[dtok]) if qi == 0 else dict(pwrites=[dtok])))
        return True

    attention_phase([dict(K=[(K1_sb, 68)], Q=[(Q1d, 68)]), dict(K=[(K2_sb, 68)], Q=[(Q2d, 68)])], fin_diff, Ost4, False,
                    extra_q_reads=[qd_t])
    kb_barrier(kb)
    kb.pop_scope()

    o_gather_and_project(w_o1)
    phase_ffn(kb, h_res, h_tok, g_ffn1, w_gu1, w_d1)
    gb = kb.sb("f_g", [128, D], F32); gb_t = kb.tok()
    kb.dma("sp", gb[:], g_fin.partition_broadcast(128), owner=gb_t, writes=[gb_t])
    junk = kb.sb("f_junk", [128, D], BF16); junk_t = kb.tok()
    st4 = kb.sb("f_st", [128, 4], F32); st_t = kb.tok()
    obf = [(kb.sb("f_o%d" % i, [128, D], F32), kb.tok()) for i in range(2)]
    for tt in range(NT):
        kb.op("act", lambda: nc.scalar.activation(out=junk[:], in_=h_res[:, tt, :], func=AF.Square,
                                                  accum_out=st4[:, 0:1]), reads=[h_tok[tt]], writes=[junk_t, st_t])
        rms_scale(kb, st4[:, 0:1], st4[:, 1:2], D, st_t, st_t)
        o_sb, o_t = obf[tt % 2]
        kb.op("dve", lambda: nc.vector.scalar_tensor_tensor(out=o_sb[:], in0=h_res[:, tt, :], scalar=st4[:, 1:2],
                                                            in1=gb[:], op0=ALU.mult, op1=ALU.mult),
              reads=[h_tok[tt], st_t, gb_t], writes=[o_t])
        kb.dma("sp", out[tt * 128:(tt + 1) * 128, :], o_sb[:], owner=o_t, reads=[o_t], pwrites=[outd_t, qd_t])
    kb.finish([t for _, t in obf])
    return kb


def kernel(x, positions, attn_norm, ffn_norm, final_norm,
           mla_w_dq, mla_q_norm, mla_w_uq, mla_w_dkv, mla_kv_norm, mla_w_ukv, mla_w_o,
           diff_kv_norm, diff_w_k, diff_w_v, diff_w_q,
           diff_lambda_q1, diff_lambda_k1, diff_lambda_q2, diff_lambda_k2,
           diff_subln, diff_w_o, ffn_w_gate_up, ffn_w_down):
    f32 = np.float32
    x = np.asarray(x, f32)
    positions = np.asarray(positions, np.int32)
    A = lambda a: np.ascontiguousarray(np.asarray(a, f32))
    inv = (10000.0 ** (-np.arange(32, dtype=f32) * 2.0 / 64)).astype(f32)
    invf = np.concatenate([inv, inv]).reshape(64, 1).astype(f32)
    ident = np.eye(128, dtype=f32)
    mask = np.where(np.arange(128)[:, None] > np.arange(128)[None, :], NEG, 0.0).astype(f32)
    kcoef = np.array([[128, 0, 0, 0], [0, 1, 0, 0], [0, 0, 1, 0], [0, 0, 1, 0]], f32)
    qbase = np.array([[0, 0, 1, 0], [0, 0, 1, 0], [-128, 0, 0, 0], [0, -1, 0, 0]], f32)
    lam = np.ascontiguousarray(np.stack([A(diff_lambda_q1[0]), A(diff_lambda_k1[0]),
                                         A(diff_lambda_q2[0]), A(diff_lambda_k2[0])]))
    shared = dict(pos_all=np.ascontiguousarray(positions), invf=invf, ident=ident, mask=mask, kcoef=kcoef,
                  g_attn0=A(attn_norm[0]), g_q=A(mla_q_norm[0]), g_kvl=A(mla_kv_norm[0]),
                  w_dq=A(mla_w_dq[0]), w_dkv=A(mla_w_dkv[0]), w_o0=A(mla_w_o[0]), g_ffn0=A(ffn_norm[0]),
                  w_gu0=A(ffn_w_gate_up[0]), w_d0=A(ffn_w_down[0]), g_dkv=A(diff_kv_norm), g_attn1=A(attn_norm[1]),
                  lam=lam, subln=A(diff_subln[0]).reshape(1, 128), w_o1=A(diff_w_o[0]), g_ffn1=A(ffn_norm[1]),
                  w_gu1=A(ffn_w_gate_up[1]), w_d1=A(ffn_w_down[1]), g_fin=A(final_norm).reshape(1, D))
    w_uq = np.asarray(mla_w_uq[0], f32); w_ukv = np.asarray(mla_w_ukv[0], f32)
    w_k = np.asarray(diff_w_k, f32); w_v = np.asarray(diff_w_v, f32); w_q = np.asarray(diff_w_q[0], f32)
    maps = []
    for c in range(NCORES):
        sl = slice(c * TPC, (c + 1) * TPC)
        slope = 2.0 ** (-8.0 * (c + 1) / H)
        m = dict(shared)
        m.update(x=np.ascontiguousarray(x[0, sl]), pos=np.ascontiguousarray(positions[:, sl]),
                 qcoef=np.ascontiguousarray(qbase * f32(slope)),
                 w_uq_h=np.ascontiguousarray(w_uq[:, c * 192:(c + 1) * 192]),
                 w_ukv_h=np.ascontiguousarray(w_ukv[:, c * 256:(c + 1) * 256]),
                 w_k_h=np.ascontiguousarray(w_k[:, c * 128:(c + 1) * 128]),
                 w_v_h=np.ascontiguousarray(w_v[:, c * 128:(c + 1) * 128]),
                 w_q_h=np.ascontiguousarray(w_q[:, c * 128:(c + 1) * 128]))
        maps.append(m)
    res = _run(build_fused(), maps)
    out = np.concatenate([np.asarray(res[r]["out"]) for r in range(NCORES)], axis=0)
    return out.reshape(1, S, D).astype(f32)
```

```python
from contextlib import ExitStack
import numpy as np
import ml_dtypes
import concourse.bass as bass
import concourse.mybir as mybir
from concourse.bass_utils import run_bass_kernel_spmd

F32 = mybir.dt.float32
BF16 = mybir.dt.bfloat16
I32 = mybir.dt.int32
AF = mybir.ActivationFunctionType
ALU = mybir.AluOpType

NCORES = 8
S = 16384
D = 1024
TPC = S // NCORES
NT = TPC // 128
H = 8
DFF = 2816
EPS = 1e-6
NEG = -30000.0
TWO_PI = 6.283185307179586
C1 = 6.28125
C2 = TWO_PI - C1
PI = 3.141592653589793
PI_SAFE = 3.1415925
SAME_ENGINE_SYNC = True


class Tok:
    __slots__ = ("w", "r", "sem", "cnt")

    def __init__(self):
        self.w = {}
        self.r = {}
        self.sem = None
        self.cnt = 0


class KB:
    def __init__(self):
        self.nc = bass.Bass("TRN2", target_bir_lowering=False)
        self.st = ExitStack()
        self.sems = []
        self.eng = {}
        nc = self.nc
        for n, h in (("pe", nc.tensor), ("act", nc.scalar), ("dve", nc.vector),
                     ("pool", nc.gpsimd), ("sp", nc.sync)):
            si = self.new_sem("e_" + n)
            self.eng[n] = {"h": h, "sem": si, "cnt": 0, "seen": {}}
        self.nbank = 0
        self.banks = []
        self.dma_toks = []
        self.scopes = []
        self.scope_toks = []
        self.free_sems = []

    def tok(self):
        t = Tok()
        self.dma_toks.append(t)
        if self.scope_toks:
            self.scope_toks[-1].append(t)
        return t

    def push_scope(self):
        self.scopes.append(ExitStack())
        self.scope_toks.append([])

    def pop_scope(self):
        for t in self.scope_toks.pop():
            if t.sem is not None:
                self.free_sems.append((t.sem, t.cnt))
                t.sem = None
        self.scopes.pop().close()

    def new_sem(self, name):
        s = self.st.enter_context(self.nc.semaphore(name))
        self.sems.append(s)
        return len(self.sems) - 1

    def sb(self, name, shape, dt):
        st = self.scopes[-1] if self.scopes else self.st
        self.nsb = getattr(self, "nsb", 0) + 1
        return st.enter_context(self.nc.sbuf_tensor("%s_%d" % (name, self.nsb), shape, dt))

    def ps(self, name, shape, dt):
        return self.st.enter_context(self.nc.psum_tensor(name, shape, dt))

    def dram(self, name, shape, dt, kind):
        return self.nc.dram_tensor(name, shape, dt, kind=kind).ap()

    def make_banks(self):
        for i in range(8):
            t = self.ps("bank%d" % i, [128, 512], F32)
            self.banks.append((t, Tok()))

    def bank(self):
        b = self.banks[self.nbank % 8]
        self.nbank += 1
        return b

    def _wait(self, e, deps, dma=False):
        E = self.eng[e]
        for sem, val in deps.items():
            if sem == E["sem"] and not dma and (e == "pe" or not SAME_ENGINE_SYNC):
                continue
            if E["seen"].get(sem, 0) >= val:
                continue
            E["h"].wait_ge(self.sems[sem], val)
            E["seen"][sem] = val

    def _deps(self, reads, writes, pwrites=(), skip_sem=None):
        deps = {}
        for t in reads:
            for s, v in t.w.items():
                if v > deps.get(s, 0):
                    deps[s] = v
        for t in writes:
            for d in (t.w, t.r):
                for s, v in d.items():
                    if v > deps.get(s, 0):
                        deps[s] = v
        for t in pwrites:
            for d in (t.w, t.r):
                for s, v in d.items():
                    if v > deps.get(s, 0):
                        deps[s] = v
        if skip_sem is not None:
            deps.pop(skip_sem, None)
        return deps

    def _post(self, ticket, reads, writes, pwrites=()):
        s, v = ticket
        for t in reads:
            t.r[s] = v
        for t in writes:
            t.w = {s: v}
            t.r = {}
        for t in pwrites:
            t.w[s] = v

    def op(self, e, fn, reads=(), writes=(), pwrites=()):
        E = self.eng[e]
        self._wait(e, self._deps(reads, writes, pwrites))
        inst = fn()
        E["cnt"] += 1
        inst.then_inc(self.sems[E["sem"]], 1)
        self._post((E["sem"], E["cnt"]), reads, writes, pwrites)

    def dma(self, q, out, in_, owner, reads=(), writes=(), pwrites=()):
        E = self.eng[q]
        if owner.sem is None:
            if self.free_sems:
                owner.sem, owner.cnt = self.free_sems.pop()
            else:
                owner.sem = self.new_sem("d%d" % len(self.sems))
        self._wait(q, self._deps(reads, writes, pwrites, skip_sem=owner.sem), dma=True)
        owner.cnt += 16
        E["h"].dma_start(out=out, in_=in_).then_inc(self.sems[owner.sem], 16)
        self._post((owner.sem, owner.cnt), reads, writes, pwrites)

    def finish(self, toks):
        deps = {}
        for t in toks:
            for d in (t.w, t.r):
                for s, v in d.items():
                    if v > deps.get(s, 0):
                        deps[s] = v
        self._wait("sp", deps, dma=True)


def rms_scale(kb, ss_ap, r_ap, n, tok_ss, tok_r):
    nc = kb.nc
    kb.op("act", lambda: nc.scalar.activation(out=r_ap, in_=ss_ap, func=AF.Ln, scale=1.0 / n, bias=kb.eps_ap),
          reads=[tok_ss, kb.eps_tok], writes=[tok_r])
    kb.op("act", lambda: nc.scalar.activation(out=r_ap, in_=r_ap, func=AF.Exp, scale=-0.5),
          reads=[tok_r], writes=[tok_r])


def setup_consts(kb, ident_dram):
    nc = kb.nc
    kb.ident = kb.sb("ident_sb", [128, 128], BF16)
    kb.ident_tok = Tok()
    kb.dma("pool", kb.ident[:], ident_dram, owner=kb.ident_tok, writes=[kb.ident_tok])
    kb.eps_t = kb.sb("eps_t", [128, 1], F32)
    kb.eps_tok = Tok()
    kb.eps_ap = kb.eps_t[:, 0:1]
    kb.op("dve", lambda: nc.vector.memset(kb.eps_t[:], EPS), writes=[kb.eps_tok])


def transpose_to(kb, src_ap, src_tok, nblk, kpart, dst_fn, dst_tok, evac="act", pw=False):
    nc = kb.nc
    bt, btok = kb.bank()
    bv = bt[:].bitcast(BF16)
    for j in range(nblk):
        kb.op("pe", lambda j=j: nc.tensor.transpose(bv[0:kpart, j * 128:(j + 1) * 128], src_ap(j), kb.ident[:]),
              reads=[src_tok, kb.ident_tok], writes=[btok])
    srcv = bv[0:kpart, 0:nblk * 128].rearrange("p (j t) -> p j t", j=nblk)
    wr = dict(pwrites=[dst_tok]) if pw else dict(writes=[dst_tok])
    if evac == "act":
        kb.op("act", lambda: nc.scalar.copy(out=dst_fn(), in_=srcv), reads=[btok], **wr)
    else:
        kb.op("dve", lambda: nc.vector.tensor_copy(out=dst_fn(), in_=srcv), reads=[btok], **wr)


def build_L1():
    kb = KB()
    nc = kb.nc
    x = kb.dram("x", [TPC, D], F32, "ExternalInput")
    pos = kb.dram("pos", [1, TPC], I32, "ExternalInput")
    invf = kb.dram("invf", [64, 1], F32, "ExternalInput")
    ident_d = kb.dram("ident", [128, 128], F32, "ExternalInput")
    g_attn = kb.dram("g_attn", [D], F32, "ExternalInput")
    g_q = kb.dram("g_q", [512], F32, "ExternalInput")
    g_kv = kb.dram("g_kv", [256], F32, "ExternalInput")
    w_dq = kb.dram("w_dq", [D, 512], F32, "ExternalInput")
    w_uq = kb.dram("w_uq", [512, 1536], F32, "ExternalInput")
    w_dkv = kb.dram("w_dkv", [D, 320], F32, "ExternalInput")
    w_ukv = kb.dram("w_ukv", [256, 2048], F32, "ExternalInput")
    QTa_o = kb.dram("QTa", [H, 128, TPC], BF16, "ExternalOutput")
    QTb_o = kb.dram("QTb", [H, 64, TPC], BF16, "ExternalOutput")
    KTa_o = kb.dram("KTa", [H, 128, TPC], BF16, "ExternalOutput")
    KTb_o = kb.dram("KTb", [64, TPC], BF16, "ExternalOutput")
    V_o = kb.dram("V", [H, 128, NT, 129], BF16, "ExternalOutput")

    kb.make_banks()
    setup_consts(kb, ident_d)
    scale = (128 + 64) ** -0.5

    gA = kb.sb("gA", [128, 8], F32); gA_t = Tok()
    gQ = kb.sb("gQ", [128, 4], F32); gQ_t = Tok()
    gK = kb.sb("gK", [128, 2], F32); gK_t = Tok()
    with nc.allow_non_contiguous_dma(reason="tiny gain vectors"):
        kb.dma("sp", gA[:], g_attn.rearrange("(j p) -> p j", p=128), owner=gA_t, writes=[gA_t])
        kb.dma("sp", gQ[:], g_q.rearrange("(j p) -> p j", p=128), owner=gQ_t, writes=[gQ_t])
        kb.dma("sp", gK[:], g_kv.rearrange("(j p) -> p j", p=128), owner=gK_t, writes=[gK_t])

    stage = kb.sb("stage", [128, 6144], F32); stage_t = Tok()
    Wdq = kb.sb("Wdq", [128, 8, 512], BF16)
    Wdkv = kb.sb("Wdkv", [128, 8, 320], BF16)
    Wkrot = kb.sb("Wkrot", [128, 8, 64], BF16)
    Wuq = kb.sb("Wuq", [128, 4, 1536], BF16)
    Wqrot = kb.sb("Wqrot", [128, 4, 8, 64], BF16)
    Wukv = kb.sb("Wukv", [128, 2, 2048], BF16)

    def prep(w_dram, g_sb, g_tok, out_bf, kc, ncol, extra=None):
        sview = stage[:, 0:kc * ncol].rearrange("p (j n) -> p j n", j=kc)
        kb.dma("sp", sview, w_dram.rearrange("(j p) n -> p j n", p=128), owner=stage_t, writes=[stage_t])
        wt = Tok()
        for j in range(kc):
            kb.op("dve", lambda j=j: nc.vector.tensor_scalar(
                out=out_bf[:, j, :], in0=sview[:, j, :], scalar1=g_sb[:, j:j + 1],
                scalar2=(None if extra is None else float(extra)), op0=ALU.mult,
                **({} if extra is None else {"op1": ALU.mult})),
                reads=[stage_t, g_tok], writes=[wt])
        return wt

    Wdq_t = prep(w_dq, gA, gA_t, Wdq, 8, 512)
    Wdkv_t = prep(w_dkv, gA, gA_t, Wdkv, 8, 320)
    Wuq_t = prep(w_uq, gQ, gQ_t, Wuq, 4, 1536, extra=scale)
    Wukv_t = prep(w_ukv, gK, gK_t, Wukv, 2, 2048)
    Wkrot_t = Tok()
    kb.op("dve", lambda: nc.vector.tensor_scalar(out=Wkrot[:, :, 0:32], in0=Wdkv[:, :, 288:320], scalar1=-1.0,
                                                 scalar2=None, op0=ALU.mult), reads=[Wdkv_t], writes=[Wkrot_t])
    kb.op("dve", lambda: nc.vector.tensor_copy(out=Wkrot[:, :, 32:64], in_=Wdkv[:, :, 256:288]),
          reads=[Wdkv_t], writes=[Wkrot_t])
    Wqrot_t = Tok()
    Wuq_v = Wuq[:].rearrange("p j (h d) -> p j h d", h=8)
    for j in range(4):
        kb.op("dve", lambda j=j: nc.vector.tensor_scalar(out=Wqrot[:, j, :, 0:32], in0=Wuq_v[:, j, :, 160:192],
                                                         scalar1=-1.0, scalar2=None, op0=ALU.mult),
              reads=[Wuq_t], writes=[Wqrot_t])
        kb.op("dve", lambda j=j: nc.vector.tensor_copy(out=Wqrot[:, j, :, 32:64], in_=Wuq_v[:, j, :, 128:160]),
              reads=[Wuq_t], writes=[Wqrot_t])

    cosT = kb.sb("cosT", [64, TPC], F32); sinT = kb.sb("sinT", [64, TPC], F32); tab_t = Tok()
    invf_sb = kb.sb("invf_sb", [64, 1], F32); invf_t = Tok()
    kb.dma("sp", invf_sb[:], invf, owner=invf_t, writes=[invf_t])
    posi = kb.sb("posi", [64, TPC], I32); posi_t = Tok()
    kb.dma("sp", posi[:], pos.partition_broadcast(64), owner=posi_t, writes=[posi_t])
    A0 = stage[0:64, 0:TPC]; A1 = stage[0:64, TPC:2 * TPC]; A2 = stage[0:64, 2 * TPC:3 * TPC]
    A1i = A1.bitcast(I32)
    V = nc.vector
    sq = [
        (lambda: V.tensor_copy(out=A0, in_=posi[:]), [posi_t]),
        (lambda: V.tensor_scalar(out=A0, in0=A0, scalar1=invf_sb[:, 0:1], scalar2=None, op0=ALU.mult), [invf_t]),
        (lambda: V.tensor_scalar(out=A2, in0=A0, scalar1=1.0 / TWO_PI, scalar2=None, op0=ALU.mult), []),
        (lambda: V.tensor_copy(out=A1i, in_=A2), []),
        (lambda: V.tensor_copy(out=A2, in_=A1i), []),
        (lambda: V.scalar_tensor_tensor(out=A0, in0=A2, scalar=-C1, in1=A0, op0=ALU.mult, op1=ALU.add), []),
        (lambda: V.scalar_tensor_tensor(out=A0, in0=A2, scalar=-C2, in1=A0, op0=ALU.mult, op1=ALU.add), []),
        (lambda: V.tensor_scalar(out=A2, in0=A0, scalar1=PI, scalar2=-TWO_PI, op0=ALU.is_gt, op1=ALU.mult), []),
        (lambda: V.tensor_tensor(out=A0, in0=A0, in1=A2, op=ALU.add), []),
        (lambda: V.tensor_scalar(out=A2, in0=A0, scalar1=-PI, scalar2=TWO_PI, op0=ALU.is_lt, op1=ALU.mult), []),
        (lambda: V.tensor_tensor(out=A0, in0=A0, in1=A2, op=ALU.add), []),
        (lambda: V.tensor_scalar(out=A1, in0=A0, scalar1=PI / 2, scalar2=None, op0=ALU.add), []),
        (lambda: V.tensor_scalar(out=A2, in0=A1, scalar1=PI, scalar2=-TWO_PI, op0=ALU.is_gt, op1=ALU.mult), []),
        (lambda: V.tensor_tensor(out=A1, in0=A1, in1=A2, op=ALU.add), []),
        (lambda: V.tensor_scalar(out=A0, in0=A0, scalar1=PI_SAFE, scalar2=-PI_SAFE, op0=ALU.min, op1=ALU.max), []),
        (lambda: V.tensor_scalar(out=A1, in0=A1, scalar1=PI_SAFE, scalar2=-PI_SAFE, op0=ALU.min, op1=ALU.max), []),
    ]
    for fn, rd in sq:
        kb.op("dve", fn, reads=[stage_t] + rd, writes=[stage_t])
    kb.op("act", lambda: nc.scalar.activation(out=sinT[:], in_=A0, func=AF.Sin), reads=[stage_t], writes=[tab_t])
    kb.op("act", lambda: nc.scalar.activation(out=cosT[:], in_=A1, func=AF.Sin), reads=[stage_t], pwrites=[tab_t])

    xt = [kb.sb("xt%d" % i, [128, D], F32) for i in range(2)]; xt_t = [Tok(), Tok()]
    junk = kb.sb("junk", [128, D], BF16); junk_t = Tok()
    xs = kb.sb("xs", [128, D], BF16); xs_t = Tok()
    st4 = kb.sb("st4", [128, 8], F32); st_t = Tok()
    hnT = kb.sb("hnT", [128, 8, 512], BF16); hnT_t = Tok()
    cqs = kb.sb("cqs", [128, 512], BF16); cqs_t = Tok()
    ckvs = kb.sb("ckvs", [128, 256], BF16); ckvs_t = Tok()
    cqT = kb.sb("cqT", [128, 4, 512], BF16); cqT_t = Tok()
    ckvT = kb.sb("ckvT", [128, 2, 512], BF16); ckvT_t = Tok()
    Vaug = kb.sb("Vaug", [128, H, NT, 129], BF16); Vaug_t = Tok()
    kb.op("pool", lambda: nc.gpsimd.memset(Vaug[:], 1.0), writes=[Vaug_t])
    qa_st = kb.sb("qa_st", [128, H, 512], BF16); qa_t = Tok()
    qb_st = kb.sb("qb_st", [64, H, 512], BF16); qb_t = Tok()
    ka_st = kb.sb("ka_st", [128, H, 512], BF16); ka_t = Tok()
    kb_st = kb.sb("kb_st", [64, 512], BF16); kbs_t = Tok()
    tmp1 = kb.sb("tmp1", [64, 512], F32); tmp1_t = Tok()
    tmp2 = kb.sb("tmp2", [64, 512], F32); tmp2_t = Tok()

    Wukv_v = Wukv[:].rearrange("p j (h d) -> p j h d", h=8)

    def rope_combine(raw_b, raw_tok, rot_b, rot_tok, c, out_ap, out_tok, pw):
        cs = cosT[:, c * 512:(c + 1) * 512]
        sn = sinT[:, c * 512:(c + 1) * 512]
        kb.op("dve", lambda: nc.vector.tensor_tensor(out=tmp1[:], in0=raw_b[0:64, :], in1=cs, op=ALU.mult),
              reads=[raw_tok, tab_t], writes=[tmp1_t])
        kb.op("dve", lambda: nc.vector.tensor_tensor(out=tmp2[:], in0=rot_b[0:64, :], in1=sn, op=ALU.mult),
              reads=[rot_tok, tab_t], writes=[tmp2_t])
        wr = dict(pwrites=[out_tok]) if pw else dict(writes=[out_tok])
        kb.op("pool", lambda: nc.gpsimd.tensor_tensor(out=out_ap, in0=tmp1[:], in1=tmp2[:], op=ALU.add),
              reads=[tmp1_t, tmp2_t], **wr)

    for c in range(NT // 4):
        for i in range(4):
            tt = 4 * c + i
            xb = xt[tt % 2]; xbt = xt_t[tt % 2]
            kb.dma("sp", xb[:], x[tt * 128:(tt + 1) * 128, :], owner=xbt, writes=[xbt])
            kb.op("act", lambda: nc.scalar.activation(out=junk[:], in_=xb[:], func=AF.Square,
                                                      accum_out=st4[:, 0:1]),
                  reads=[xbt], writes=[junk_t, st_t])
            rms_scale(kb, st4[:, 0:1], st4[:, 1:2], D, st_t, st_t)
            kb.op("dve", lambda: nc.vector.tensor_scalar(out=xs[:], in0=xb[:], scalar1=st4[:, 1:2], scalar2=None,
                                                         op0=ALU.mult), reads=[xbt, st_t], writes=[xs_t])
            transpose_to(kb, lambda j: xs[:, j * 128:(j + 1) * 128], xs_t, 8, 128,
                         lambda: hnT[:, :, i * 128:(i + 1) * 128], hnT_t, evac="act", pw=(i > 0))
            cq_b, cq_bt = kb.bank()
            for j in range(8):
                kb.op("pe", lambda j=j: nc.tensor.matmul(cq_b[:, 0:512], hnT[:, j, i * 128:(i + 1) * 128],
                                                         Wdq[:, j, :], start=(j == 0), stop=(j == 7)),
                      reads=[hnT_t, Wdq_t], writes=[cq_bt])
            ck_b, ck_bt = kb.bank()
            for j in range(8):
                kb.op("pe", lambda j=j: nc.tensor.matmul(ck_b[:, 0:256], hnT[:, j, i * 128:(i + 1) * 128],
                                                         Wdkv[:, j, 0:256], start=(j == 0), stop=(j == 7)),
                      reads=[hnT_t, Wdkv_t], writes=[ck_bt])
            kb.op("act", lambda: nc.scalar.activation(out=junk[:, 0:512], in_=cq_b[:, 0:512], func=AF.Square,
                                                      accum_out=st4[:, 2:3]), reads=[cq_bt], writes=[junk_t, st_t])
            rms_scale(kb, st4[:, 2:3], st4[:, 3:4], 512, st_t, st_t)
            kb.op("dve", lambda: nc.vector.tensor_scalar(out=cqs[:], in0=cq_b[:, 0:512], scalar1=st4[:, 3:4],
                                                         scalar2=None, op0=ALU.mult),
                  reads=[cq_bt, st_t], writes=[cqs_t])
            kb.op("act", lambda: nc.scalar.activation(out=junk[:, 0:256], in_=ck_b[:, 0:256], func=AF.Square,
                                                      accum_out=st4[:, 4:5]), reads=[ck_bt], writes=[junk_t, st_t])
            rms_scale(kb, st4[:, 4:5], st4[:, 5:6], 256, st_t, st_t)
            kb.op("dve", lambda: nc.vector.tensor_scalar(out=ckvs[:], in0=ck_b[:, 0:256], scalar1=st4[:, 5:6],
                                                         scalar2=None, op0=ALU.mult),
                  reads=[ck_bt, st_t], writes=[ckvs_t])
            transpose_to(kb, lambda j: cqs[:, j * 128:(j + 1) * 128], cqs_t, 4, 128,
                         lambda: cqT[:, :, i * 128:(i + 1) * 128], cqT_t, evac="dve", pw=(i > 0))
            transpose_to(kb, lambda j: ckvs[:, j * 128:(j + 1) * 128], ckvs_t, 2, 128,
                         lambda: ckvT[:, :, i * 128:(i + 1) * 128], ckvT_t, evac="dve", pw=(i > 0))
            for hh in range(2):
                vb, vbt = kb.bank()
                for j in range(2):
                    kb.op("pe", lambda j=j: nc.tensor.matmul(vb[:, 0:512].rearrange("p (h d) -> p h d", h=4),
                                                             ckvT[:, j, i * 128:(i + 1) * 128],
                                                             Wukv_v[:, j, 4 * hh:4 * hh + 4, 128:256],
                                                             start=(j == 0), stop=(j == 1)),
                          reads=[ckvT_t, Wukv_t], writes=[vbt])
                kb.op("act", lambda: nc.scalar.copy(out=Vaug[:, 4 * hh:4 * hh + 4, tt, 0:128],
                                                    in_=vb[:, 0:512].rearrange("p (h d) -> p h d", h=4)),
                      reads=[vbt], pwrites=[Vaug_t])
        csl = slice(c * 512, (c + 1) * 512)
        kp_b, kp_bt = kb.bank()
        for j in range(8):
            kb.op("pe", lambda j=j: nc.tensor.matmul(kp_b[0:64, :], Wdkv[:, j, 256:320], hnT[:, j, :],
                                                     start=(j == 0), stop=(j == 7)),
                  reads=[hnT_t, Wdkv_t], writes=[kp_bt])
        kr_b, kr_bt = kb.bank()
        for j in range(8):
            kb.op("pe", lambda j=j: nc.tensor.matmul(kr_b[0:64, :], Wkrot[:, j, :], hnT[:, j, :],
                                                     start=(j == 0), stop=(j == 7)),
                  reads=[hnT_t, Wkrot_t], writes=[kr_bt])
        rope_combine(kp_b, kp_bt, kr_b, kr_bt, c, kb_st[:], kbs_t, False)
        kb.dma("pool", KTb_o[:, csl], kb_st[:], owner=kbs_t, reads=[kbs_t])
        for h in range(H):
            qa_b, qa_bt = kb.bank()
            for j in range(4):
                kb.op("pe", lambda j=j: nc.tensor.matmul(qa_b[:, :], Wuq[:, j, h * 192:h * 192 + 128], cqT[:, j, :],
                                                         start=(j == 0), stop=(j == 3)),
                      reads=[cqT_t, Wuq_t], writes=[qa_bt])
            kb.op("act", lambda: nc.scalar.copy(out=qa_st[:, h, :], in_=qa_b[:, :]), reads=[qa_bt],
                  **(dict(writes=[qa_t]) if h == 0 else dict(pwrites=[qa_t])))
            qp_b, qp_bt = kb.bank()
            for j in range(4):
                kb.op("pe", lambda j=j: nc.tensor.matmul(qp_b[0:64, :], Wuq[:, j, h * 192 + 128:h * 192 + 192],
                                                         cqT[:, j, :], start=(j == 0), stop=(j == 3)),
                      reads=[cqT_t, Wuq_t], writes=[qp_bt])
            qr_b, qr_bt = kb.bank()
            for j in range(4):
                kb.op("pe", lambda j=j: nc.tensor.matmul(qr_b[0:64, :], Wqrot[:, j, h, :], cqT[:, j, :],
                                                         start=(j == 0), stop=(j == 3)),
                      reads=[cqT_t, Wqrot_t], writes=[qr_bt])
            rope_combine(qp_b, qp_bt, qr_b, qr_bt, c, qb_st[:, h, :], qb_t, h > 0)
            ka_b, ka_bt = kb.bank()
            for j in range(2):
                kb.op("pe", lambda j=j: nc.tensor.matmul(ka_b[:, :], Wukv[:, j, h * 256:h * 256 + 128], ckvT[:, j, :],
                                                         start=(j == 0), stop=(j == 1)),
                      reads=[ckvT_t, Wukv_t], writes=[ka_bt])
            kb.op("dve", lambda: nc.vector.tensor_copy(out=ka_st[:, h, :], in_=ka_b[:, :]), reads=[ka_bt],
                  **(dict(writes=[ka_t]) if h == 0 else dict(pwrites=[ka_t])))
        kb.dma("pool", QTa_o[:, :, csl].rearrange("h p t -> p h t"), qa_st[:], owner=qa_t, reads=[qa_t])
        kb.dma("pool", QTb_o[:, :, csl].rearrange("h p t -> p h t"), qb_st[:], owner=qb_t, reads=[qb_t])
        kb.dma("pool", KTa_o[:, :, csl].rearrange("h p t -> p h t"), ka_st[:], owner=ka_t, reads=[ka_t])
    for h in range(H):
        kb.dma("pool", V_o[h], Vaug[:, h, :, :], owner=Vaug_t, reads=[Vaug_t])
    kb.finish([kbs_t, qa_t, qb_t, ka_t, Vaug_t])
    return kb


def attention_core(kb, passes, V_sb, kv_tok_of_block, epilogue, nq_chunks=S // 512, q_resident=None, qc_order=None, extra_q_reads=(), filler=0):
    nc = kb.nc
    npass = len(passes)
    kb.ac_n = getattr(kb, "ac_n", 0) + 1
    pfx = "ac%d_" % kb.ac_n
    zeros = kb.sb(pfx + "zeros", [128, 512], BF16); zeros_t = Tok()
    kb.op("pool", lambda: nc.gpsimd.memset(zeros[:], 0.0), writes=[zeros_t])
    NQS = 3
    order = list(range(nq_chunks)) if qc_order is None else list(qc_order)
    qslots = []
    for pi, ps_ in enumerate(passes if q_resident is None else []):
        sl = []
        for s in range(NQS):
            parts = [kb.sb(pfx + "q%d_%d_%d" % (pi, s, k), [rows, 512], BF16) for k, (_, rows) in enumerate(ps_["Q"])]
            sl.append((parts, Tok()))
        qslots.append(sl)
    NP = 3
    pbuf = [(kb.sb(pfx + "pT%d" % i, [128, 512], BF16), Tok()) for i in range(NP)]
    sbank = [kb.banks[0], kb.banks[1], kb.banks[7]]
    NS = 3
    osets = [(kb.banks[2], kb.banks[3]), (kb.banks[4], kb.banks[5])]

    def load_q(oi):
        qc = order[oi]
        for pi, ps_ in enumerate(passes):
            parts, tok = qslots[pi][oi % NQS]
            for k, (qd, rows) in enumerate(ps_["Q"]):
                kb.dma("pool", parts[k][:], qd[:, qc * 512:(qc + 1) * 512], owner=tok, reads=list(extra_q_reads),
                       **(dict(writes=[tok]) if k == 0 else dict(pwrites=[tok])))

    units = [(qc, pi) for qc in order for pi in range(npass)]
    oidx = {qc: i for i, qc in enumerate(order)}
    tiles = []
    for ui, (qc, pi) in enumerate(units):
        for kbi in range(4 * qc + 4):
            tiles.append((ui, qc, pi, kbi))

    def emit_qk(ti):
        ui, qc, pi, kbi = tiles[ti]
        sb_t, sb_tok = sbank[ti % NS]
        if q_resident is None:
            parts, qtok = qslots[pi][oidx[qc] % NQS]
            qsl = lambda k, rows, lo: parts[k][0:rows, lo:512]
        else:
            qtok = q_resident(qc)
            qsl = lambda k, rows, lo: passes[pi]["Q"][k][0][0:rows, qc * 512 + lo:(qc + 1) * 512]
        j = kbi - 4 * qc
        lo = 128 * j if j > 0 else 0
        kparts = passes[pi]["K"]
        n = len(kparts)
        for k, (ksb, rows) in enumerate(kparts):
            kb.op("pe", lambda k=k, ksb=ksb, rows=rows: nc.tensor.matmul(
                sb_t[:, lo:512], ksb[0:rows, kbi * 128:(kbi + 1) * 128], qsl(k, rows, lo),
                start=(k == 0), stop=(k == n - 1 and j < 0)),
                reads=[kv_tok_of_block(kbi), qtok], writes=[sb_tok])
        if j >= 0:
            kb.op("pe", lambda: nc.tensor.matmul(sb_t[:, lo:lo + 128], kb.ident[:], kb.mask[:], start=False, stop=True),
                  reads=[kb.ident_tok, kb.mask_tok], writes=[sb_tok])
        if filler:
            fb, fbt = kb.banks[6]
            kb.op("pe", lambda: nc.tensor.matmul(fb[:, 0:filler], zeros[:, 0:128], zeros[:, 0:filler], start=True, stop=True),
                  reads=[zeros_t], writes=[fbt])

    def emit_exp(ti):
        ui, qc, pi, kbi = tiles[ti]
        sb_t, sb_tok = sbank[ti % NS]
        pb, ptok = pbuf[ti % NP]
        j = kbi - 4 * qc
        lo = 128 * j if j > 0 else 0
        kb.op("act", lambda: nc.scalar.activation(out=pb[:, lo:512], in_=sb_t[:, lo:512], func=AF.Exp),
              reads=[sb_tok], writes=[ptok])

    def emit_pv(ti):
        ui, qc, pi, kbi = tiles[ti]
        pb, ptok = pbuf[ti % NP]
        oset = osets[ui % 2]
        j = kbi - 4 * qc
        last = (kbi == 4 * qc + 3)
        if kbi == 0:
            for (ob, obtok) in oset:
                kb.op("pe", lambda ob=ob: nc.tensor.matmul(ob[:, :], zeros[:, 0:128], zeros[:, :], start=True, stop=False),
                      reads=[zeros_t], writes=[obtok])
        for i in range(max(j, 0), 4):
            ob, obtok = oset[i // 2]
            c0 = (i % 2) * 129
            kb.op("pe", lambda i=i, ob=ob, c0=c0: nc.tensor.matmul(
                ob[:, c0:c0 + 129], pb[:, 128 * i:128 * (i + 1)], V_sb[:, kbi, :], start=False,
                stop=(kbi == 4 * qc + i)),
                reads=[ptok, kv_tok_of_block(kbi)], writes=[obtok])
        if last:
            epilogue(qc, pi, oset)

    if q_resident is None:
        load_q(0)
        if len(order) > 1:
            load_q(1)
    nt = len(tiles)

    def qk_with_prefetch(x):
        ui, qc, pi, kbi = tiles[x]
        if q_resident is None and kbi == 0 and pi == 0 and oidx[qc] + 2 < len(order):
            load_q(oidx[qc] + 2)
        emit_qk(x)

    for x in range(min(2, nt)):
        qk_with_prefetch(x)
    for ti in range(nt):
        if ti + 2 < nt:
            qk_with_prefetch(ti + 2)
        emit_exp(ti)
        if ti >= 1:
            emit_pv(ti - 1)
    emit_pv(nt - 1)


def setup_mask(kb, mask_dram):
    kb.mask = kb.sb("mask_sb", [128, 128], BF16)
    kb.mask_tok = Tok()
    kb.dma("pool", kb.mask[:], mask_dram, owner=kb.mask_tok, writes=[kb.mask_tok])


def build_L2(nq_chunks=S // 512):
    kb = KB()
    nc = kb.nc
    QTa = kb.dram("QTa", [128, S], BF16, "ExternalInput")
    QTb = kb.dram("QTb", [64, S], BF16, "ExternalInput")
    KTa = kb.dram("KTa", [128, S], BF16, "ExternalInput")
    KTb = kb.dram("KTb", [64, S], BF16, "ExternalInput")
    Vd = kb.dram("V", [128, S // 128, 129], BF16, "ExternalInput")
    ident_d = kb.dram("ident", [128, 128], F32, "ExternalInput")
    mask_d = kb.dram("mask", [128, 128], F32, "ExternalInput")
    O = kb.dram("O", [S, 128], BF16, "ExternalOutput")
    kb.make_banks()
    setup_consts(kb, ident_d)
    setup_mask(kb, mask_d)
    KTa_sb = kb.sb("KTa_sb", [128, S], BF16)
    KTb_sb = kb.sb("KTb_sb", [64, S], BF16)
    V_sb = kb.sb("V_sb", [128, S // 128, 129], BF16)
    ptoks = [Tok() for _ in range(8)]
    for r in range(8):
        t = ptoks[r]
        kb.dma("sp", KTa_sb[:, r * 2048:(r + 1) * 2048], KTa[:, r * 2048:(r + 1) * 2048], owner=t, writes=[t])
        kb.dma("sp", KTb_sb[:, r * 2048:(r + 1) * 2048], KTb[:, r * 2048:(r + 1) * 2048], owner=t, pwrites=[t])
        kb.dma("sp", V_sb[:, r * 16:(r + 1) * 16, :], Vd[:, r * 16:(r + 1) * 16, :], owner=t, pwrites=[t])
    ost = [(kb.sb("ost%d" % i, [128, 4, 128], BF16), Tok()) for i in range(2)]
    rc = kb.sb("rc", [128, 4], F32); rc_t = Tok()
    Ov = O.rearrange("(c i p) d -> c p i d", i=4, p=128)

    def epilogue(qc, pi, oset):
        o_sb, o_tok = ost[qc % 2]
        for b, (ob, obtok) in enumerate(oset):
            sums = ob[:, 0:258].rearrange("p (i c) -> p i c", c=129)[:, :, 128:129]
            kb.op("dve", lambda: nc.vector.reciprocal(out=rc[:, 2 * b:2 * b + 2].rearrange("p (i c) -> p i c", c=1),
                                                     in_=sums), reads=[obtok], writes=[rc_t])
            for i in range(2):
                kb.op("dve", lambda i=i: nc.vector.tensor_scalar(
                    out=o_sb[:, 2 * b + i, :], in0=ob[:, i * 129:i * 129 + 128],
                    scalar1=rc[:, 2 * b + i:2 * b + i + 1], scalar2=None, op0=ALU.mult),
                    reads=[obtok, rc_t], **(dict(writes=[o_tok]) if (b == 0 and i == 0) else dict(pwrites=[o_tok])))
        kb.dma("sp", Ov[qc], o_sb[:], owner=o_tok, reads=[o_tok])

    attention_core(kb, [dict(K=[(KTa_sb, 128), (KTb_sb, 64)], Q=[(QTa, 128), (QTb, 64)])], V_sb,
                   lambda kbi: ptoks[kbi // 16], epilogue, nq_chunks=nq_chunks)
    kb.finish([t for _, t in ost])
    return kb


def kb_barrier(kb):
    deps = {}
    for n, E in kb.eng.items():
        if E["cnt"] > 0:
            deps[E["sem"]] = E["cnt"]
    for t in kb.dma_toks:
        if t.sem is not None and t.cnt > 0:
            deps[t.sem] = t.cnt
    for n in kb.eng:
        kb._wait(n, dict(deps), dma=True)


def load_gain(kb, name, g_dram, kc):
    nc = kb.nc
    g = kb.sb(name, [128, kc], F32)
    t = kb.tok()
    with nc.allow_non_contiguous_dma(reason="tiny gain vector"):
        kb.dma("sp", g[:], g_dram.rearrange("(j p) -> p j", p=128), owner=t, writes=[t])
    return g, t


def prep_weight(kb, w_view, g_sb, g_tok, out_bf, kc, ncol, stage, stage_tok, extra=None, q="sp"):
    nc = kb.nc
    sview = stage[:, 0:kc * ncol].rearrange("p (j n) -> p j n", j=kc)
    kb.dma(q, sview, w_view.rearrange("(j p) n -> p j n", p=128), owner=stage_tok, writes=[stage_tok])
    wt = kb.tok()
    for j in range(kc):
        if g_sb is None:
            kb.op("dve", lambda j=j: nc.vector.tensor_copy(out=out_bf[:, j, :], in_=sview[:, j, :]),
                  reads=[stage_tok], **(dict(writes=[wt]) if j == 0 else dict(pwrites=[wt])))
        else:
            kb.op("dve", lambda j=j: nc.vector.tensor_scalar(
                out=out_bf[:, j, :], in0=sview[:, j, :], scalar1=g_sb[:, j:j + 1],
                scalar2=(None if extra is None else float(extra)), op0=ALU.mult,
                **({} if extra is None else {"op1": ALU.mult})),
                reads=[stage_tok, g_tok], **(dict(writes=[wt]) if j == 0 else dict(pwrites=[wt])))
    return wt


def norm_transpose(kb, h_res, h_tok, hnT_all, hnT_tok, work, tiles=None):
    nc = kb.nc
    junk, junk_t, xs, xs_t, st4, st_t = work
    for tt in (range(NT) if tiles is None else tiles):
        kb.op("act", lambda: nc.scalar.activation(out=junk[:], in_=h_res[:, tt, :], func=AF.Square,
                                                  accum_out=st4[:, 0:1]), reads=[h_tok[tt]], writes=[junk_t, st_t])
        rms_scale(kb, st4[:, 0:1], st4[:, 1:2], D, st_t, st_t)
        kb.op("dve", lambda: nc.vector.tensor_scalar(out=xs[:], in0=h_res[:, tt, :], scalar1=st4[:, 1:2],
                                                     scalar2=None, op0=ALU.mult),
              reads=[h_tok[tt], st_t], writes=[xs_t])
        transpose_to(kb, lambda j: xs[:, j * 128:(j + 1) * 128], xs_t, 8, 128,
                     lambda: hnT_all[:, :, tt * 128:(tt + 1) * 128], hnT_tok, evac="act", pw=(tt > 0))


def phase_attn_out(kb, resid_load, O_d, w_o, h_res, h_tok):
    nc = kb.nc
    kb.push_scope()
    stage = kb.sb("pa_stage", [128, 8 * 1024], F32); stage_t = kb.tok()
    Wo = kb.sb("pa_Wo", [128, 8, 1024], BF16)
    Wo_t = prep_weight(kb, w_o, None, None, Wo, 8, 1024, stage, stage_t)
    ot = [(kb.sb("pa_o%d" % i, [128, D], BF16), kb.tok()) for i in range(2)]
    oT = kb.sb("pa_oT", [128, 8, 128], BF16); oT_t = kb.tok()
    for tt in range(NT):
        resid_load(tt)
        ob, obt = ot[tt % 2]
        kb.dma("pool", ob[:], O_d[tt * 128:(tt + 1) * 128, :], owner=obt, writes=[obt])
        transpose_to(kb, lambda j: ob[:, j * 128:(j + 1) * 128], obt, 8, 128, lambda: oT[:, :, :], oT_t, evac="act")
        for half in range(2):
            pb, pbt = kb.bank()
            for j in range(8):
                kb.op("pe", lambda j=j: nc.tensor.matmul(pb[:, :], oT[:, j, :], Wo[:, j, half * 512:(half + 1) * 512],
                                                         start=(j == 0), stop=(j == 7)),
                      reads=[oT_t, Wo_t], writes=[pbt])
            hs = h_res[:, tt, half * 512:(half + 1) * 512]
            kb.op("dve", lambda: nc.vector.tensor_tensor(out=hs, in0=pb[:, :], in1=hs, op=ALU.add),
                  reads=[pbt], pwrites=[h_tok[tt]])
    kb_barrier(kb)
    kb.pop_scope()


def phase_ffn(kb, h_res, h_tok, g_ffn, w_gu, w_d):
    nc = kb.nc
    kb.push_scope()
    gF, gF_t = load_gain(kb, "ff_g", g_ffn, 8)
    junk = kb.sb("ff_junk", [128, D], BF16); xs = kb.sb("ff_xs", [128, D], BF16); st4 = kb.sb("ff_st", [128, 4], F32)
    work = (junk, kb.tok(), xs, kb.tok(), st4, kb.tok())
    hnT = kb.sb("ff_hnT", [128, 8, TPC], BF16); hnT_t = kb.tok()
    norm_transpose(kb, h_res, h_tok, hnT, hnT_t, work)
    HB = 256
    nhb = DFF // HB
    stages = [(kb.sb("ff_stage%d" % i, [128, 8 * HB], F32), kb.tok()) for i in range(3)]
    Wg = [kb.sb("ff_Wg%d" % i, [128, 8, HB], BF16) for i in range(2)]
    Wu = [kb.sb("ff_Wu%d" % i, [128, 8, HB], BF16) for i in range(2)]
    Wd = [kb.sb("ff_Wd%d" % i, [128, 2, D], BF16) for i in range(2)]
    sg = [(kb.sb("ff_sg%d" % i, [128, 512], F32), kb.tok()) for i in range(2)]
    act = [(kb.sb("ff_act%d" % i, [128, 2, 512], BF16), kb.tok()) for i in range(2)]
    si = 0
    wts = {}

    def load_w(hb):
        nonlocal si
        s0, s0t = stages[si % 3]; si += 1
        Wg_t = prep_weight(kb, w_gu[:, hb * HB:(hb + 1) * HB], gF, gF_t, Wg[hb % 2], 8, HB, s0, s0t)
        s1, s1t = stages[si % 3]; si += 1
        Wu_t = prep_weight(kb, w_gu[:, DFF + hb * HB:DFF + (hb + 1) * HB], gF, gF_t, Wu[hb % 2], 8, HB, s1, s1t)
        s2, s2t = stages[si % 3]; si += 1
        Wd_t = prep_weight(kb, w_d[hb * HB:(hb + 1) * HB, :], None, None, Wd[hb % 2], 2, D, s2, s2t)
        wts[hb] = (Wg_t, Wu_t, Wd_t)

    units = [(hb, c) for hb in range(nhb) for c in range(TPC // 512)]

    def stage1(u):
        hb, c = units[u]
        if c == 0:
            load_w(hb)
        Wg_t, Wu_t, Wd_t = wts[hb]
        ab, abt = act[u % 2]
        for sub in range(2):
            gb, gbt = kb.bank()
            for j in range(8):
                kb.op("pe", lambda j=j: nc.tensor.matmul(gb[:, :], Wg[hb % 2][:, j, sub * 128:(sub + 1) * 128],
                                                         hnT[:, j, c * 512:(c + 1) * 512], start=(j == 0), stop=(j == 7)),
                      reads=[hnT_t, Wg_t], writes=[gbt])
            ub, ubt = kb.bank()
            for j in range(8):
                kb.op("pe", lambda j=j: nc.tensor.matmul(ub[:, :], Wu[hb % 2][:, j, sub * 128:(sub + 1) * 128],
                                                         hnT[:, j, c * 512:(c + 1) * 512], start=(j == 0), stop=(j == 7)),
                      reads=[hnT_t, Wu_t], writes=[ubt])
            sgb, sgt = sg[sub]
            kb.op("act", lambda: nc.scalar.activation(out=sgb[:], in_=gb[:, :], func=AF.Silu),
                  reads=[gbt], writes=[sgt])
            kb.op("dve", lambda: nc.vector.tensor_tensor(out=ab[:, sub, :], in0=ub[:, :], in1=sgb[:], op=ALU.mult),
                  reads=[ubt, sgt], **(dict(writes=[abt]) if sub == 0 else dict(pwrites=[abt])))

    def stage2(u):
        hb, c = units[u]
        Wg_t, Wu_t, Wd_t = wts[hb]
        ab, abt = act[u % 2]
        for i in range(4):
            tt = c * 4 + i
            for half in range(2):
                db, dbt = kb.bank()
                for sub in range(2):
                    kb.op("pe", lambda sub=sub: nc.tensor.matmul(db[:, :], ab[:, sub, i * 128:(i + 1) * 128],
                                                                 Wd[hb % 2][:, sub, half * 512:(half + 1) * 512],
                                                                 start=(sub == 0), stop=(sub == 1)),
                          reads=[abt, Wd_t], writes=[dbt])
                hs = h_res[:, tt, half * 512:(half + 1) * 512]
                kb.op("dve", lambda: nc.vector.tensor_tensor(out=hs, in0=db[:, :], in1=hs, op=ALU.add),
                      reads=[dbt], pwrites=[h_tok[tt]])

    stage1(0)
    for u in range(len(units)):
        if u + 1 < len(units):
            stage1(u + 1)
        stage2(u)
    kb_barrier(kb)
    kb.pop_scope()


def make_h(kb):
    h_res = kb.sb("h_res", [128, NT, D], F32)
    h_tok = [kb.tok() for _ in range(NT)]
    return h_res, h_tok


def build_L3():
    kb = KB()
    nc = kb.nc
    x = kb.dram("x", [TPC, D], F32, "ExternalInput")
    O_d = kb.dram("O", [TPC, D], BF16, "ExternalInput")
    pos = kb.dram("pos", [1, TPC], I32, "ExternalInput")
    coef = kb.dram("coef", [4, 8], F32, "ExternalInput")
    ident_d = kb.dram("ident", [128, 128], F32, "ExternalInput")
    w_o = kb.dram("w_o", [D, D], F32, "ExternalInput")
    g_ffn = kb.dram("g_ffn", [D], F32, "ExternalInput")
    w_gu = kb.dram("w_gu", [D, 2 * DFF], F32, "ExternalInput")
    w_d = kb.dram("w_d", [DFF, D], F32, "ExternalInput")
    g_kv = kb.dram("g_kv", [D], F32, "ExternalInput")
    g_a1 = kb.dram("g_a1", [D], F32, "ExternalInput")
    w_k = kb.dram("w_k", [D, D], F32, "ExternalInput")
    w_v = kb.dram("w_v", [D, D], F32, "ExternalInput")
    w_q = kb.dram("w_q", [D, D], F32, "ExternalInput")
    h1_o = kb.dram("h1", [TPC, D], F32, "ExternalOutput")
    Q1_o = kb.dram("Q1", [H, 68, TPC], BF16, "ExternalOutput")
    Q2_o = kb.dram("Q2", [H, 68, TPC], BF16, "ExternalOutput")
    K1_o = kb.dram("K1", [H, 68, TPC], BF16, "ExternalOutput")
    K2_o = kb.dram("K2", [H, 68, TPC], BF16, "ExternalOutput")
    V_o = kb.dram("V", [H, 128, NT, 129], BF16, "ExternalOutput")
    kb.make_banks()
    setup_consts(kb, ident_d)
    h_res, h_tok = make_h(kb)

    def resid_load(tt):
        kb.dma("sp", h_res[:, tt, :], x[tt * 128:(tt + 1) * 128, :], owner=h_tok[tt], writes=[h_tok[tt]])

    phase_attn_out(kb, resid_load, O_d, w_o, h_res, h_tok)
    phase_ffn(kb, h_res, h_tok, g_ffn, w_gu, w_d)
    for tt in range(NT):
        kb.dma("sp", h1_o[tt * 128:(tt + 1) * 128, :], h_res[:, tt, :], owner=h_tok[tt], reads=[h_tok[tt]])

    kb.push_scope()
    cf = kb.sb("d_cf", [4, 8], F32); cf_t = kb.tok()
    kb.dma("sp", cf[:], coef, owner=cf_t, writes=[cf_t])
    pi_ = kb.sb("d_pi", [4, TPC], I32); pi_t = kb.tok()
    kb.dma("sp", pi_[:], pos.partition_broadcast(4), owner=pi_t, writes=[pi_t])
    ai = kb.sb("d_ai", [4, TPC], I32)
    af = kb.sb("d_af", [4, TPC], F32); bf_ = kb.sb("d_bf", [4, TPC], F32); aug_t = kb.tok()
    t1 = kb.sb("d_t1", [4, TPC], F32)
    kaug = kb.sb("d_kaug", [4, TPC], BF16); qbase = kb.sb("d_qbase", [4, TPC], F32)
    qaug = [(kb.sb("d_qaug%d" % i, [4, TPC], BF16), kb.tok()) for i in range(2)]
    kaug_t = kb.tok()
    V_ = nc.vector
    seq = [
        lambda: V_.tensor_scalar(out=ai[:], in0=pi_[:], scalar1=7, scalar2=None, op0=ALU.arith_shift_right),
        lambda: V_.tensor_copy(out=af[:], in_=ai[:]),
        lambda: V_.tensor_scalar(out=ai[:], in0=pi_[:], scalar1=127, scalar2=None, op0=ALU.bitwise_and),
        lambda: V_.tensor_copy(out=bf_[:], in_=ai[:]),
        lambda: V_.tensor_scalar(out=t1[:], in0=af[:], scalar1=cf[:, 0:1], scalar2=cf[:, 2:3], op0=ALU.mult, op1=ALU.add),
        lambda: V_.scalar_tensor_tensor(out=kaug[:], in0=bf_[:], scalar=cf[:, 1:2], in1=t1[:], op0=ALU.mult, op1=ALU.add),
        lambda: V_.tensor_scalar(out=t1[:], in0=af[:], scalar1=cf[:, 3:4], scalar2=cf[:, 5:6], op0=ALU.mult, op1=ALU.add),
        lambda: V_.scalar_tensor_tensor(out=qbase[:], in0=bf_[:], scalar=cf[:, 4:5], in1=t1[:], op0=ALU.mult, op1=ALU.add),
    ]
    for fn in seq:
        kb.op("dve", fn, reads=[pi_t, cf_t], writes=[aug_t])
    kb.op("dve", lambda: V_.tensor_copy(out=kaug[:], in_=kaug[:]), reads=[aug_t], writes=[kaug_t])
    for h in range(H):
        slope = 2.0 ** (-8.0 * (h + 1) / H)
        qa_, qa_t_ = qaug[h % 2]
        kb.op("dve", lambda: V_.tensor_scalar(out=qa_[:], in0=qbase[:], scalar1=slope, scalar2=None, op0=ALU.mult),
              reads=[aug_t], writes=[qa_t_])
        kb.dma("pool", Q1_o[h, 64:68, :], qa_[:], owner=qa_t_, reads=[qa_t_])
        kb.dma("pool", Q2_o[h, 64:68, :], qa_[:], owner=qa_t_, reads=[qa_t_])
        kb.dma("pool", K1_o[h, 64:68, :], kaug[:], owner=kaug_t, reads=[kaug_t])
        kb.dma("pool", K2_o[h, 64:68, :], kaug[:], owner=kaug_t, reads=[kaug_t])
    kb_barrier(kb)
    kb.pop_scope()

    kb.push_scope()
    gK, gK_t = load_gain(kb, "d_gk", g_kv, 8)
    gA, gA_t = load_gain(kb, "d_ga", g_a1, 8)
    Wk = kb.sb("d_Wk", [128, 8, D], BF16); Wv = kb.sb("d_Wv", [128, 8, D], BF16); Wq = kb.sb("d_Wq", [128, 8, D], BF16)
    kb.push_scope()
    stage = kb.sb("d_stage", [128, 8 * 1024], F32); stage_t = kb.tok()
    Wk_t = prep_weight(kb, w_k, gK, gK_t, Wk, 8, D, stage, stage_t)
    Wv_t = prep_weight(kb, w_v, gK, gK_t, Wv, 8, D, stage, stage_t)
    Wq_t = prep_weight(kb, w_q, gA, gA_t, Wq, 8, D, stage, stage_t, extra=64 ** -0.5)
    kb_barrier(kb)
    kb.pop_scope()
    junk = kb.sb("d_junk", [128, D], BF16); xs = kb.sb("d_xs", [128, D], BF16); st4 = kb.sb("d_st", [128, 4], F32)
    work = (junk, kb.tok(), xs, kb.tok(), st4, kb.tok())
    hT = kb.sb("d_hT", [128, 8, TPC], BF16); hT_t = kb.tok()
    norm_transpose(kb, h_res, h_tok, hT, hT_t, work)
    vst = [(kb.sb("d_vst%d" % i, [128, H, 129], BF16), kb.tok()) for i in range(2)]
    for vs_, vs_t in vst:
        kb.op("pool", lambda: nc.gpsimd.memset(vs_[:], 1.0), writes=[vs_t])
    kst = [(kb.sb("d_kst%d" % i, [128, H, 512], BF16), kb.tok()) for i in range(1)]
    qst = [(kb.sb("d_qst%d" % i, [128, H, 512], BF16), kb.tok()) for i in range(1)]
    for c in range(TPC // 512):
        csl = slice(c * 512, (c + 1) * 512)
        for (W, W_t, stl, o1, o2, ev) in ((Wk, Wk_t, kst, K1_o, K2_o, "act"), (Wq, Wq_t, qst, Q1_o, Q2_o, "dve")):
            sb_, sb_t = stl[0]
            for h in range(H):
                pb, pbt = kb.bank()
                for j in range(8):
                    kb.op("pe", lambda j=j: nc.tensor.matmul(pb[:, :], W[:, j, h * 128:(h + 1) * 128], hT[:, j, csl],
                                                             start=(j == 0), stop=(j == 7)),
                          reads=[hT_t, W_t], writes=[pbt])
                wr = dict(writes=[sb_t]) if h == 0 else dict(pwrites=[sb_t])
                if ev == "act":
                    kb.op("act", lambda: nc.scalar.copy(out=sb_[:, h, :], in_=pb[:, :]), reads=[pbt], **wr)
                else:
                    kb.op("dve", lambda: nc.vector.tensor_copy(out=sb_[:, h, :], in_=pb[:, :]), reads=[pbt], **wr)
            kb.dma("pool", o1[:, 0:64, csl].rearrange("h p t -> p h t"), sb_[0:64, :, :], owner=sb_t, reads=[sb_t])
            kb.dma("pool", o2[:, 0:64, csl].rearrange("h p t -> p h t"), sb_[64:128, :, :], owner=sb_t, reads=[sb_t])
        for i in range(4):
            tt = c * 4 + i
            vs_, vs_t = vst[tt % 2]
            for hh in range(2):
                vb, vbt = kb.bank()
                for j in range(8):
                    kb.op("pe", lambda j=j: nc.tensor.matmul(vb[:, :], hT[:, j, tt * 128:(tt + 1) * 128],
                                                             Wv[:, j, hh * 512:(hh + 1) * 512], start=(j == 0), stop=(j == 7)),
                          reads=[hT_t, Wv_t], writes=[vbt])
                kb.op("act", lambda: nc.scalar.copy(out=vs_[:, 4 * hh:4 * hh + 4, 0:128],
                                                    in_=vb[:, :].rearrange("p (h d) -> p h d", h=4)),
                      reads=[vbt], pwrites=[vs_t])
            kb.dma("pool", V_o[:, :, tt, :].rearrange("h p c -> p h c"), vs_[:], owner=vs_t, reads=[vs_t])
    kb.finish(h_tok + [kaug_t] + [t for _, t in qaug] + [t for _, t in vst] + [t for _, t in kst] + [t for _, t in qst])
    kb_barrier(kb)
    kb.pop_scope()
    return kb


def build_L4(lambda_init, nq_chunks=S // 512):
    kb = KB()
    nc = kb.nc
    Q1 = kb.dram("Q1", [68, S], BF16, "ExternalInput")
    Q2 = kb.dram("Q2", [68, S], BF16, "ExternalInput")
    K1 = kb.dram("K1", [68, S], BF16, "ExternalInput")
    K2 = kb.dram("K2", [68, S], BF16, "ExternalInput")
    Vd = kb.dram("V", [128, S // 128, 129], BF16, "ExternalInput")
    ident_d = kb.dram("ident", [128, 128], F32, "ExternalInput")
    mask_d = kb.dram("mask", [128, 128], F32, "ExternalInput")
    lam_d = kb.dram("lam", [4, 64], F32, "ExternalInput")
    subln_d = kb.dram("subln", [1, 128], F32, "ExternalInput")
    O = kb.dram("O", [S, 128], BF16, "ExternalOutput")
    kb.make_banks()
    setup_consts(kb, ident_d)
    setup_mask(kb, mask_d)
    lam_sb = kb.sb("lam_sb", [128, 4, 64], F32); lam_t = kb.tok()
    kb.dma("sp", lam_sb[:].rearrange("p a d -> p (a d)"), lam_d.rearrange("a d -> (a d)").partition_broadcast(128),
           owner=lam_t, writes=[lam_t])
    lj = kb.sb("lam_j", [128, 64], F32); lv = kb.sb("lam_v", [128, 4], F32); lv_t = kb.tok()
    kb.op("dve", lambda: nc.vector.tensor_tensor(out=lj[:], in0=lam_sb[:, 0, :], in1=lam_sb[:, 1, :], op=ALU.mult),
          reads=[lam_t], writes=[lv_t])
    kb.op("dve", lambda: nc.vector.tensor_reduce(out=lv[:, 0:1], in_=lj[:], axis=mybir.AxisListType.X, op=ALU.add),
          reads=[lv_t], writes=[lv_t])
    kb.op("dve", lambda: nc.vector.tensor_tensor(out=lj[:], in0=lam_sb[:, 2, :], in1=lam_sb[:, 3, :], op=ALU.mult),
          reads=[lam_t, lv_t], writes=[lv_t])
    kb.op("dve", lambda: nc.vector.tensor_reduce(out=lv[:, 1:2], in_=lj[:], axis=mybir.AxisListType.X, op=ALU.add),
          reads=[lv_t], writes=[lv_t])
    kb.op("act", lambda: nc.scalar.activation(out=lv[:, 0:2], in_=lv[:, 0:2], func=AF.Exp), reads=[lv_t], writes=[lv_t])
    kb.op("dve", lambda: nc.vector.scalar_tensor_tensor(out=lv[:, 2:3], in0=lv[:, 1:2], scalar=-float(lambda_init),
                                                        in1=lv[:, 0:1], op0=ALU.add, op1=ALU.subtract),
          reads=[lv_t], writes=[lv_t])
    sub_sb = kb.sb("sub_sb", [128, 128], F32); sub_t = kb.tok()
    kb.dma("sp", sub_sb[:], subln_d.partition_broadcast(128), owner=sub_t, writes=[sub_t])
    kb.op("dve", lambda: nc.vector.tensor_scalar(out=sub_sb[:], in0=sub_sb[:], scalar1=float(1.0 - lambda_init),
                                                 scalar2=None, op0=ALU.mult), reads=[sub_t], writes=[sub_t])

    K1_sb = kb.sb("K1_sb", [68, S], BF16)
    K2_sb = kb.sb("K2_sb", [68, S], BF16)
    V_sb = kb.sb("V_sb", [128, S // 128, 129], BF16)
    ptoks = [kb.tok() for _ in range(8)]
    for r in range(8):
        t = ptoks[r]
        kb.dma("sp", K1_sb[:, r * 2048:(r + 1) * 2048], K1[:, r * 2048:(r + 1) * 2048], owner=t, writes=[t])
        kb.dma("sp", K2_sb[:, r * 2048:(r + 1) * 2048], K2[:, r * 2048:(r + 1) * 2048], owner=t, pwrites=[t])
        kb.dma("sp", V_sb[:, r * 16:(r + 1) * 16, :], Vd[:, r * 16:(r + 1) * 16, :], owner=t, pwrites=[t])
    o1 = kb.sb("o1n", [128, 4, 128], F32); o1_t = kb.tok()
    att = kb.sb("attd", [128, 4, 128], F32); att_t = kb.tok()
    junk = kb.sb("junk4", [128, 128], F32); junk_t = kb.tok()
    ost = [(kb.sb("ost%d" % i, [128, 4, 128], BF16), kb.tok()) for i in range(2)]
    rc = kb.sb("rc", [128, 8], F32); rc_t = kb.tok()
    ss = kb.sb("ss4", [128, 8], F32); ss_t = kb.tok()
    Ov = O.rearrange("(c i p) d -> c p i d", i=4, p=128)

    def epilogue(qc, pi, oset):
        for b, (ob, obtok) in enumerate(oset):
            sums = ob[:, 0:258].rearrange("p (i c) -> p i c", c=129)[:, :, 128:129]
            kb.op("dve", lambda: nc.vector.reciprocal(out=rc[:, 2 * b:2 * b + 2].rearrange("p (i c) -> p i c", c=1),
                                                     in_=sums), reads=[obtok], writes=[rc_t])
            for i in range(2):
                qi = 2 * b + i
                if pi == 0:
                    kb.op("dve", lambda: nc.vector.tensor_scalar(
                        out=o1[:, qi, :], in0=ob[:, i * 129:i * 129 + 128], scalar1=rc[:, qi:qi + 1], scalar2=None,
                        op0=ALU.mult), reads=[obtok, rc_t], **(dict(writes=[o1_t]) if qi == 0 else dict(pwrites=[o1_t])))
                else:
                    kb.op("dve", lambda: nc.vector.tensor_scalar(
                        out=att[:, qi, :], in0=ob[:, i * 129:i * 129 + 128], scalar1=rc[:, qi:qi + 1],
                        scalar2=lv[:, 2:3], op0=ALU.mult, op1=ALU.mult),
                        reads=[obtok, rc_t, lv_t], **(dict(writes=[att_t]) if qi == 0 else dict(pwrites=[att_t])))
                    kb.op("pool", lambda: nc.gpsimd.tensor_tensor(out=att[:, qi, :], in0=att[:, qi, :], in1=o1[:, qi, :],
                                                                  op=ALU.add), reads=[o1_t], pwrites=[att_t])
        if pi == 1:
            o_sb, o_tok = ost[qc % 2]
            for qi in range(4):
                kb.op("act", lambda: nc.scalar.activation(out=junk[:], in_=att[:, qi, :], func=AF.Square,
                                                          accum_out=ss[:, qi:qi + 1]),
                      reads=[att_t], writes=[junk_t], **(dict(pwrites=[ss_t])))
            kb.op("act", lambda: nc.scalar.activation(out=ss[:, 4:8], in_=ss[:, 0:4], func=AF.Ln, scale=1.0 / 128,
                                                      bias=kb.eps_ap), reads=[ss_t, kb.eps_tok], writes=[ss_t])
            kb.op("act", lambda: nc.scalar.activation(out=ss[:, 4:8], in_=ss[:, 4:8], func=AF.Exp, scale=-0.5),
                  reads=[ss_t], writes=[ss_t])
            for qi in range(4):
                kb.op("dve", lambda: nc.vector.scalar_tensor_tensor(
                    out=o_sb[:, qi, :], in0=att[:, qi, :], scalar=ss[:, 4 + qi:5 + qi], in1=sub_sb[:],
                    op0=ALU.mult, op1=ALU.mult), reads=[att_t, ss_t, sub_t],
                    **(dict(writes=[o_tok]) if qi == 0 else dict(pwrites=[o_tok])))
            kb.dma("sp", Ov[qc], o_sb[:], owner=o_tok, reads=[o_tok])

    attention_core(kb, [dict(K=[(K1_sb, 68)], Q=[(Q1, 68)]), dict(K=[(K2_sb, 68)], Q=[(Q2, 68)])], V_sb,
                   lambda kbi: ptoks[kbi // 16], epilogue, nq_chunks=nq_chunks)
    kb.finish([t for _, t in ost])
    return kb


def build_L5():
    kb = KB()
    nc = kb.nc
    h1 = kb.dram("h1", [TPC, D], F32, "ExternalInput")
    O_d = kb.dram("O", [TPC, D], BF16, "ExternalInput")
    ident_d = kb.dram("ident", [128, 128], F32, "ExternalInput")
    w_o = kb.dram("w_o", [D, D], F32, "ExternalInput")
    g_ffn = kb.dram("g_ffn", [D], F32, "ExternalInput")
    w_gu = kb.dram("w_gu", [D, 2 * DFF], F32, "ExternalInput")
    w_d = kb.dram("w_d", [DFF, D], F32, "ExternalInput")
    g_fin = kb.dram("g_fin", [1, D], F32, "ExternalInput")
    out = kb.dram("out", [TPC, D], F32, "ExternalOutput")
    kb.make_banks()
    setup_consts(kb, ident_d)
    h_res, h_tok = make_h(kb)

    def resid_load(tt):
        kb.dma("sp", h_res[:, tt, :], h1[tt * 128:(tt + 1) * 128, :], owner=h_tok[tt], writes=[h_tok[tt]])

    phase_attn_out(kb, resid_load, O_d, w_o, h_res, h_tok)
    phase_ffn(kb, h_res, h_tok, g_ffn, w_gu, w_d)
    gb = kb.sb("f_g", [128, D], F32); gb_t = kb.tok()
    kb.dma("sp", gb[:], g_fin.partition_broadcast(128), owner=gb_t, writes=[gb_t])
    junk = kb.sb("f_junk", [128, D], BF16); junk_t = kb.tok()
    st4 = kb.sb("f_st", [128, 4], F32); st_t = kb.tok()
    ob = [(kb.sb("f_o%d" % i, [128, D], F32), kb.tok()) for i in range(2)]
    for tt in range(NT):
        kb.op("act", lambda: nc.scalar.activation(out=junk[:], in_=h_res[:, tt, :], func=AF.Square,
                                                  accum_out=st4[:, 0:1]), reads=[h_tok[tt]], writes=[junk_t, st_t])
        rms_scale(kb, st4[:, 0:1], st4[:, 1:2], D, st_t, st_t)
        o_sb, o_t = ob[tt % 2]
        kb.op("dve", lambda: nc.vector.scalar_tensor_tensor(out=o_sb[:], in0=h_res[:, tt, :], scalar=st4[:, 1:2],
                                                            in1=gb[:], op0=ALU.mult, op1=ALU.mult),
              reads=[h_tok[tt], st_t, gb_t], writes=[o_t])
        kb.dma("sp", out[tt * 128:(tt + 1) * 128, :], o_sb[:], owner=o_t, reads=[o_t])
    kb.finish([t for _, t in ob])
    return kb


def _run(kb, maps):
    res = run_bass_kernel_spmd(kb.nc, maps, core_ids=list(range(NCORES)))
    return res.results


def _cat_tokens(results, key, h):
    return np.ascontiguousarray(np.concatenate([np.asarray(results[r][key])[h] for r in range(NCORES)], axis=-1))


def _cat_v(results, h):
    return np.ascontiguousarray(np.concatenate([np.asarray(results[r]["V"])[h] for r in range(NCORES)], axis=1))


def kernel_unfused(x, positions, attn_norm, ffn_norm, final_norm,
           mla_w_dq, mla_q_norm, mla_w_uq, mla_w_dkv, mla_kv_norm, mla_w_ukv, mla_w_o,
           diff_kv_norm, diff_w_k, diff_w_v, diff_w_q,
           diff_lambda_q1, diff_lambda_k1, diff_lambda_q2, diff_lambda_k2,
           diff_subln, diff_w_o, ffn_w_gate_up, ffn_w_down):
    f32 = np.float32
    x = np.asarray(x, f32)
    positions = np.asarray(positions, np.int32)
    A = lambda a: np.ascontiguousarray(np.asarray(a, f32))
    inv = (10000.0 ** (-np.arange(32, dtype=f32) * 2.0 / 64)).astype(f32)
    invf = np.concatenate([inv, inv]).reshape(64, 1).astype(f32)
    ident = np.eye(128, dtype=f32)
    mask = np.where(np.arange(128)[:, None] > np.arange(128)[None, :], NEG, 0.0).astype(f32)
    coef = np.array([[128, 0, 0, 0, 0, 1],
                     [0, 1, 0, 0, 0, 1],
                     [0, 0, 1, -128, 0, 0],
                     [0, 0, 1, 0, -1, 0]], f32)
    coef = np.ascontiguousarray(np.concatenate([coef, np.zeros((4, 2), f32)], axis=1))
    sl = [slice(r * TPC, (r + 1) * TPC) for r in range(NCORES)]
    xs = [np.ascontiguousarray(x[0, s]) for s in sl]
    ps = [np.ascontiguousarray(positions[:, s]) for s in sl]

    r1 = _run(build_L1(), [dict(x=xs[r], pos=ps[r], invf=invf, ident=ident, g_attn=A(attn_norm[0]),
                                g_q=A(mla_q_norm[0]), g_kv=A(mla_kv_norm[0]), w_dq=A(mla_w_dq[0]),
                                w_uq=A(mla_w_uq[0]), w_dkv=A(mla_w_dkv[0]), w_ukv=A(mla_w_ukv[0]))
                           for r in range(NCORES)])
    ktb = np.ascontiguousarray(np.concatenate([np.asarray(r1[r]["KTb"]) for r in range(NCORES)], axis=-1))
    r2 = _run(build_L2(), [dict(QTa=_cat_tokens(r1, "QTa", h), QTb=_cat_tokens(r1, "QTb", h),
                                KTa=_cat_tokens(r1, "KTa", h), KTb=ktb, V=_cat_v(r1, h), ident=ident, mask=mask)
                           for h in range(NCORES)])
    del r1
    O0 = np.concatenate([np.asarray(r2[h]["O"]) for h in range(H)], axis=1)
    del r2
    r3 = _run(build_L3(), [dict(x=xs[r], O=np.ascontiguousarray(O0[sl[r]]), pos=ps[r], coef=coef, ident=ident,
                                w_o=A(mla_w_o[0]), g_ffn=A(ffn_norm[0]), w_gu=A(ffn_w_gate_up[0]),
                                w_d=A(ffn_w_down[0]), g_kv=A(diff_kv_norm), g_a1=A(attn_norm[1]),
                                w_k=A(diff_w_k), w_v=A(diff_w_v), w_q=A(diff_w_q[0]))
                           for r in range(NCORES)])
    lambda_init = 0.8 - 0.6 * float(np.exp(-0.3 * 1))
    lam = np.ascontiguousarray(np.stack([A(diff_lambda_q1[0]), A(diff_lambda_k1[0]),
                                         A(diff_lambda_q2[0]), A(diff_lambda_k2[0])]))
    r4 = _run(build_L4(lambda_init), [dict(Q1=_cat_tokens(r3, "Q1", h), Q2=_cat_tokens(r3, "Q2", h),
                                           K1=_cat_tokens(r3, "K1", h), K2=_cat_tokens(r3, "K2", h),
                                           V=_cat_v(r3, h), ident=ident, mask=mask, lam=lam,
                                           subln=A(diff_subln[0]).reshape(1, 128))
                                      for h in range(NCORES)])
    h1 = [np.asarray(r3[r]["h1"]) for r in range(NCORES)]
    del r3
    O1 = np.concatenate([np.asarray(r4[h]["O"]) for h in range(H)], axis=1)
    del r4
    r5 = _run(build_L5(), [dict(h1=h1[r], O=np.ascontiguousarray(O1[sl[r]]), ident=ident, w_o=A(diff_w_o[0]),
                                g_ffn=A(ffn_norm[1]), w_gu=A(ffn_w_gate_up[1]), w_d=A(ffn_w_down[1]),
                                g_fin=A(final_norm).reshape(1, D))
                           for r in range(NCORES)])
    out = np.concatenate([np.asarray(r5[r]["out"]) for r in range(NCORES)], axis=0)
    return out.reshape(1, S, D).astype(f32)


GROWS = 1024


def kb_gather(kb, gin, gout, reads, writes):
    nc = kb.nc
    E = kb.eng["pool"]
    kb._wait("pool", kb._deps(reads, writes), dma=True)
    kb.cc_cnt += 1
    nc.gpsimd.collective_compute("AllGather", ALU.bypass, replica_groups=[list(range(NCORES))],
                                 ins=[gin.opt()], outs=[gout.opt()]).then_inc(kb.sems[kb.cc_sem])
    kb._post((kb.cc_sem, kb.cc_cnt), reads, writes)


def rope_tables(kb, pos_ap, n, scr, posi, cos_out, sin_out, invf_sb, invf_t, out_tok, scr_tok):
    nc = kb.nc
    V = nc.vector
    A0, A1, A2 = scr
    A1i = A1.bitcast(I32)
    kb.dma("sp", posi, pos_ap.partition_broadcast(64), owner=scr_tok, writes=[scr_tok])
    sq = [
        lambda: V.tensor_copy(out=A0, in_=posi),
        lambda: V.tensor_scalar(out=A0, in0=A0, scalar1=invf_sb[:, 0:1], scalar2=None, op0=ALU.mult),
        lambda: V.tensor_scalar(out=A2, in0=A0, scalar1=1.0 / TWO_PI, scalar2=None, op0=ALU.mult),
        lambda: V.tensor_copy(out=A1i, in_=A2),
        lambda: V.tensor_copy(out=A2, in_=A1i),
        lambda: V.scalar_tensor_tensor(out=A0, in0=A2, scalar=-C1, in1=A0, op0=ALU.mult, op1=ALU.add),
        lambda: V.scalar_tensor_tensor(out=A0, in0=A2, scalar=-C2, in1=A0, op0=ALU.mult, op1=ALU.add),
        lambda: V.tensor_scalar(out=A2, in0=A0, scalar1=PI, scalar2=-TWO_PI, op0=ALU.is_gt, op1=ALU.mult),
        lambda: V.tensor_tensor(out=A0, in0=A0, in1=A2, op=ALU.add),
        lambda: V.tensor_scalar(out=A2, in0=A0, scalar1=-PI, scalar2=TWO_PI, op0=ALU.is_lt, op1=ALU.mult),
        lambda: V.tensor_tensor(out=A0, in0=A0, in1=A2, op=ALU.add),
        lambda: V.tensor_scalar(out=A1, in0=A0, scalar1=PI / 2, scalar2=None, op0=ALU.add),
        lambda: V.tensor_scalar(out=A2, in0=A1, scalar1=PI, scalar2=-TWO_PI, op0=ALU.is_gt, op1=ALU.mult),
        lambda: V.tensor_tensor(out=A1, in0=A1, in1=A2, op=ALU.add),
        lambda: V.tensor_scalar(out=A0, in0=A0, scalar1=PI_SAFE, scalar2=-PI_SAFE, op0=ALU.min, op1=ALU.max),
        lambda: V.tensor_scalar(out=A1, in0=A1, scalar1=PI_SAFE, scalar2=-PI_SAFE, op0=ALU.min, op1=ALU.max),
    ]
    for fn in sq:
        kb.op("dve", fn, reads=[invf_t], writes=[scr_tok])
    kb.op("act", lambda: nc.scalar.activation(out=sin_out, in_=A0, func=AF.Sin), reads=[scr_tok], writes=[out_tok])
    kb.op("act", lambda: nc.scalar.activation(out=cos_out, in_=A1, func=AF.Sin), reads=[scr_tok], pwrites=[out_tok])


def rope_apply(kb, raw_b, raw_tok, rot_b, rot_tok, cs, sn, tab_t, tmp, out_ap, out_tok, pw):
    nc = kb.nc
    (tmp1, tmp1_t), (tmp2, tmp2_t) = tmp
    kb.op("dve", lambda: nc.vector.tensor_tensor(out=tmp1[:], in0=raw_b[0:64, :], in1=cs, op=ALU.mult),
          reads=[raw_tok, tab_t], writes=[tmp1_t])
    kb.op("dve", lambda: nc.vector.tensor_tensor(out=tmp2[:], in0=rot_b[0:64, :], in1=sn, op=ALU.mult),
          reads=[rot_tok, tab_t], writes=[tmp2_t])
    wr = dict(pwrites=[out_tok]) if pw else dict(writes=[out_tok])
    kb.op("pool", lambda: nc.gpsimd.tensor_tensor(out=out_ap, in0=tmp1[:], in1=tmp2[:], op=ALU.add),
          reads=[tmp1_t, tmp2_t], **wr)


def build_fused(stop=None):
    kb = KB()
    nc = kb.nc
    kb.cc_sem = kb.new_sem("cc")
    kb.cc_cnt = 0
    pid = nc.partition_id()
    di = lambda n, sh, dt=F32: kb.dram(n, sh, dt, "ExternalInput")
    x = di("x", [TPC, D]); pos = di("pos", [1, TPC], I32); pos_all = di("pos_all", [1, S], I32)
    invf = di("invf", [64, 1]); ident_d = di("ident", [128, 128]); mask_d = di("mask", [128, 128])
    kcoef = di("kcoef", [4, 4]); qcoef = di("qcoef", [4, 4])
    g_attn0 = di("g_attn0", [D]); g_q = di("g_q", [512]); g_kvl = di("g_kvl", [256])
    w_dq = di("w_dq", [D, 512]); w_dkv = di("w_dkv", [D, 320])
    w_uq_h = di("w_uq_h", [512, 192]); w_ukv_h = di("w_ukv_h", [256, 256])
    w_o0 = di("w_o0", [D, D]); g_ffn0 = di("g_ffn0", [D]); w_gu0 = di("w_gu0", [D, 2 * DFF]); w_d0 = di("w_d0", [DFF, D])
    g_dkv = di("g_dkv", [D]); g_attn1 = di("g_attn1", [D])
    w_k_h = di("w_k_h", [D, 128]); w_v_h = di("w_v_h", [D, 128]); w_q_h = di("w_q_h", [D, 128])
    lam_d = di("lam", [4, 64]); subln_d = di("subln", [1, 128])
    w_o1 = di("w_o1", [D, D]); g_ffn1 = di("g_ffn1", [D]); w_gu1 = di("w_gu1", [D, 2 * DFF]); w_d1 = di("w_d1", [DFF, D])
    g_fin = di("g_fin", [1, D])
    out = kb.dram("out", [TPC, D], F32, "ExternalOutput")
    gin = kb.dram("gin", [GROWS, 512], F32, "Internal")
    gout = kb.dram("gout", [NCORES * GROWS, 512], F32, "Internal")
    gin_bf = gin.bitcast(BF16)
    gout_bf = gout.bitcast(BF16).rearrange("(r a) b -> r a b", r=NCORES)
    gin_O = gin_bf.rearrange("a (b d) -> (a b) d", d=128)
    gout_O = gout_bf.rearrange("r a (b d) -> r (a b) d", d=128)
    out_bf = out.bitcast(BF16)
    gin_t = kb.tok(); gout_t = kb.tok(); outd_t = kb.tok()
    lambda_init = 0.8 - 0.6 * float(np.exp(-0.3 * 1))

    kb.make_banks()
    setup_consts(kb, ident_d)
    setup_mask(kb, mask_d)
    invf_sb = kb.sb("invf_sb", [64, 1], F32); invf_t = kb.tok()
    kb.dma("sp", invf_sb[:], invf, owner=invf_t, writes=[invf_t])

    kb.push_scope()
    gA, gA_t = load_gain(kb, "a_gA", g_attn0, 8)
    stage = kb.sb("a_stage", [128, 6144], F32); stage_t = kb.tok()
    Wdq = kb.sb("a_Wdq", [128, 8, 512], BF16); Wdkv = kb.sb("a_Wdkv", [128, 8, 320], BF16)
    Wkrot = kb.sb("a_Wkrot", [128, 8, 64], BF16)
    Wdq_t = prep_weight(kb, w_dq, gA, gA_t, Wdq, 8, 512, stage, stage_t)
    Wdkv_t = prep_weight(kb, w_dkv, gA, gA_t, Wdkv, 8, 320, stage, stage_t)
    Wkrot_t = kb.tok()
    kb.op("dve", lambda: nc.vector.tensor_scalar(out=Wkrot[:, :, 0:32], in0=Wdkv[:, :, 288:320], scalar1=-1.0,
                                                 scalar2=None, op0=ALU.mult), reads=[Wdkv_t], writes=[Wkrot_t])
    kb.op("dve", lambda: nc.vector.tensor_copy(out=Wkrot[:, :, 32:64], in_=Wdkv[:, :, 256:288]),
          reads=[Wdkv_t], pwrites=[Wkrot_t])
    gQ, gQ_t = load_gain(kb, "a_gQ", g_q, 4)
    gK, gK_t = load_gain(kb, "a_gK", g_kvl, 2)
    cosT = kb.sb("a_cos", [64, TPC], F32); sinT = kb.sb("a_sin", [64, TPC], F32); tab_t = kb.tok()
    posi = kb.sb("a_posi", [64, TPC], I32)
    rope_tables(kb, pos, TPC, (stage[0:64, 0:TPC], stage[0:64, TPC:2 * TPC], stage[0:64, 2 * TPC:3 * TPC]),
                posi[:], cosT[:], sinT[:], invf_sb, invf_t, tab_t, stage_t)
    xt = [(kb.sb("a_xt%d" % i, [128, D], F32), kb.tok()) for i in range(2)]
    junk = kb.sb("a_junk", [128, D], BF16); junk_t = kb.tok()
    xs = kb.sb("a_xs", [128, D], BF16); xs_t = kb.tok()
    st4 = kb.sb("a_st4", [128, 8], F32); st_t = kb.tok()
    hnT = kb.sb("a_hnT", [128, 8, 512], BF16); hnT_t = kb.tok()
    cqs = kb.sb("a_cqs", [128, 512], BF16); cqs_t = kb.tok()
    ckvs = kb.sb("a_ckvs", [128, 256], BF16); ckvs_t = kb.tok()
    cqT = [(kb.sb("a_cqT%d" % i, [128, 4, 512], BF16), kb.tok()) for i in range(2)]
    ckvT = [(kb.sb("a_ckvT%d" % i, [128, 2, 512], BF16), kb.tok()) for i in range(2)]
    kpe_st = [(kb.sb("a_kpe%d" % i, [64, 512], BF16), kb.tok()) for i in range(2)]
    tmp = ((kb.sb("a_tmp1", [64, 512], F32), kb.tok()), (kb.sb("a_tmp2", [64, 512], F32), kb.tok()))
    for c in range(NT // 4):
        cq_c, cq_ct = cqT[c % 2]; ck_c, ck_ct = ckvT[c % 2]; kp_c, kp_ct = kpe_st[c % 2]
        for i in range(4):
            tt = 4 * c + i
            xb, xbt = xt[tt % 2]
            kb.dma("sp", xb[:], x[tt * 128:(tt + 1) * 128, :], owner=xbt, writes=[xbt])
            kb.op("act", lambda: nc.scalar.activation(out=junk[:], in_=xb[:], func=AF.Square, accum_out=st4[:, 0:1]),
                  reads=[xbt], writes=[junk_t, st_t])
            rms_scale(kb, st4[:, 0:1], st4[:, 1:2], D, st_t, st_t)
            kb.op("dve", lambda: nc.vector.tensor_scalar(out=xs[:], in0=xb[:], scalar1=st4[:, 1:2], scalar2=None,
                                                         op0=ALU.mult), reads=[xbt, st_t], writes=[xs_t])
            transpose_to(kb, lambda j: xs[:, j * 128:(j + 1) * 128], xs_t, 8, 128,
                         lambda: hnT[:, :, i * 128:(i + 1) * 128], hnT_t, evac="act", pw=(i > 0))
            cq_b, cq_bt = kb.bank()
            for j in range(8):
                kb.op("pe", lambda j=j: nc.tensor.matmul(cq_b[:, 0:512], hnT[:, j, i * 128:(i + 1) * 128],
                                                         Wdq[:, j, :], start=(j == 0), stop=(j == 7)),
                      reads=[hnT_t, Wdq_t], writes=[cq_bt])
            ck_b, ck_bt = kb.bank()
            for j in range(8):
                kb.op("pe", lambda j=j: nc.tensor.matmul(ck_b[:, 0:256], hnT[:, j, i * 128:(i + 1) * 128],
                                                         Wdkv[:, j, 0:256], start=(j == 0), stop=(j == 7)),
                      reads=[hnT_t, Wdkv_t], writes=[ck_bt])
            kb.op("act", lambda: nc.scalar.activation(out=junk[:, 0:512], in_=cq_b[:, 0:512], func=AF.Square,
                                                      accum_out=st4[:, 2:3]), reads=[cq_bt], writes=[junk_t, st_t])
            rms_scale(kb, st4[:, 2:3], st4[:, 3:4], 512, st_t, st_t)
            kb.op("dve", lambda: nc.vector.tensor_scalar(out=cqs[:], in0=cq_b[:, 0:512], scalar1=st4[:, 3:4],
                                                         scalar2=None, op0=ALU.mult), reads=[cq_bt, st_t], writes=[cqs_t])
            kb.op("act", lambda: nc.scalar.activation(out=junk[:, 0:256], in_=ck_b[:, 0:256], func=AF.Square,
                                                      accum_out=st4[:, 4:5]), reads=[ck_bt], writes=[junk_t, st_t])
            rms_scale(kb, st4[:, 4:5], st4[:, 5:6], 256, st_t, st_t)
            kb.op("dve", lambda: nc.vector.tensor_scalar(out=ckvs[:], in0=ck_b[:, 0:256], scalar1=st4[:, 5:6],
                                                         scalar2=None, op0=ALU.mult), reads=[ck_bt, st_t], writes=[ckvs_t])
            transpose_to(kb, lambda j: cqs[:, j * 128:(j + 1) * 128], cqs_t, 4, 128,
                         lambda: cq_c[:, :, i * 128:(i + 1) * 128], cq_ct, evac="dve", pw=(i > 0))
            transpose_to(kb, lambda j: ckvs[:, j * 128:(j + 1) * 128], ckvs_t, 2, 128,
                         lambda: ck_c[:, :, i * 128:(i + 1) * 128], ck_ct, evac="dve", pw=(i > 0))
        kp_b, kp_bt = kb.bank()
        for j in range(8):
            kb.op("pe", lambda j=j: nc.tensor.matmul(kp_b[0:64, :], Wdkv[:, j, 256:320], hnT[:, j, :],
                                                     start=(j == 0), stop=(j == 7)), reads=[hnT_t, Wdkv_t], writes=[kp_bt])
        kr_b, kr_bt = kb.bank()
        for j in range(8):
            kb.op("pe", lambda j=j: nc.tensor.matmul(kr_b[0:64, :], Wkrot[:, j, :], hnT[:, j, :],
                                                     start=(j == 0), stop=(j == 7)), reads=[hnT_t, Wkrot_t], writes=[kr_bt])
        rope_apply(kb, kp_b, kp_bt, kr_b, kr_bt, cosT[:, c * 512:(c + 1) * 512], sinT[:, c * 512:(c + 1) * 512], tab_t,
                   tmp, kp_c[:], kp_ct, False)
        hf, cc = c // 2, c % 2
        dst, dtok = (gin_bf, gin_t) if hf == 0 else (out_bf, outd_t)
        csl = slice(cc * 512, (cc + 1) * 512)
        kb.dma("pool", dst[0:512, csl].rearrange("(j p) t -> p j t", p=128), cq_c[:], owner=cq_ct, reads=[cq_ct], pwrites=[dtok])
        kb.dma("pool", dst[512:768, csl].rearrange("(j p) t -> p j t", p=128), ck_c[:], owner=ck_ct, reads=[ck_ct], pwrites=[dtok])
        kb.dma("pool", dst[768:832, csl], kp_c[:], owner=kp_ct, reads=[kp_ct], pwrites=[dtok])
        if c == 1:
            kb_gather(kb, gin, gout, reads=[gin_t], writes=[gout_t])
    kb_barrier(kb)
    kb.pop_scope()

    if stop == "A":
        return kb
    kb.push_scope()
    QTa_sb = kb.sb("QTa_sb", [128, S], BF16); QTb_sb = kb.sb("QTb_sb", [128, S], BF16)
    KTa_sb = kb.sb("KTa_sb", [128, S], BF16); KTb_sb = kb.sb("KTb_sb", [128, S], BF16)
    V_sb = kb.sb("V_sb", [128, S // 128, 129], BF16)
    ctok = [kb.tok() for _ in range(S // 512)]
    vinit_t = kb.tok()
    kb.op("pool", lambda: nc.gpsimd.memset(V_sb[:], 1.0), writes=[vinit_t])
    kb.op("pool", lambda: nc.gpsimd.memset(QTb_sb[64:128, :], 0.0), pwrites=[vinit_t])
    kb.op("pool", lambda: nc.gpsimd.memset(KTb_sb[64:128, :], 0.0), pwrites=[vinit_t])
    kb.push_scope()
    gQ, gQ_t = load_gain(kb, "b_gQ", g_q, 4)
    gK, gK_t = load_gain(kb, "b_gK", g_kvl, 2)
    stage = kb.sb("b_stage", [128, 1024], F32); stage_t = kb.tok()
    Wuq = kb.sb("b_Wuq", [128, 4, 192], BF16); Wqrot = kb.sb("b_Wqrot", [128, 4, 64], BF16)
    Wukv = kb.sb("b_Wukv", [128, 2, 256], BF16)
    Wuq_t = prep_weight(kb, w_uq_h, gQ, gQ_t, Wuq, 4, 192, stage, stage_t, extra=(128 + 64) ** -0.5)
    Wukv_t = prep_weight(kb, w_ukv_h, gK, gK_t, Wukv, 2, 256, stage, stage_t)
    Wqrot_t = kb.tok()
    kb.op("dve", lambda: nc.vector.tensor_scalar(out=Wqrot[:, :, 0:32], in0=Wuq[:, :, 160:192], scalar1=-1.0,
                                                 scalar2=None, op0=ALU.mult), reads=[Wuq_t], writes=[Wqrot_t])
    kb.op("dve", lambda: nc.vector.tensor_copy(out=Wqrot[:, :, 32:64], in_=Wuq[:, :, 128:160]),
          reads=[Wuq_t], pwrites=[Wqrot_t])
    cqc = [(kb.sb("b_cq%d" % i, [128, 4, 512], BF16), kb.tok()) for i in range(2)]
    ckc = [(kb.sb("b_ck%d" % i, [128, 2, 512], BF16), kb.tok()) for i in range(2)]
    tabs = [(kb.sb("b_cos%d" % i, [64, 512], F32), kb.sb("b_sin%d" % i, [64, 512], F32), kb.tok()) for i in range(2)]
    scr = [(kb.sb("b_A0_%d" % i, [64, 512], F32), kb.sb("b_A1_%d" % i, [64, 512], F32),
            kb.sb("b_A2_%d" % i, [64, 512], F32), kb.sb("b_pi_%d" % i, [64, 512], I32), kb.tok()) for i in range(1)]
    tmp = ((kb.sb("b_tmp1", [64, 512], F32), kb.tok()), (kb.sb("b_tmp2", [64, 512], F32), kb.tok()))
    n = 0
    for hf in range(2):
        if hf == 1:
            kb.dma("pool", gin_bf[0:832, :], out_bf[0:832, 0:1024], owner=gin_t, reads=[outd_t], writes=[gin_t])
            kb_gather(kb, gin, gout, reads=[gin_t], writes=[gout_t])
        if stop == "G1":
            kb_barrier(kb)
            return kb
        if stop == "Ba" and hf == 1:
            kb_barrier(kb)
            return kb
        for r in range(NCORES):
            for cc in range(2):
                gc = r * 4 + hf * 2 + cc
                tsl = slice(gc * 512, (gc + 1) * 512)
                csl = slice(cc * 512, (cc + 1) * 512)
                cq_c, cq_ct = cqc[n % 2]; ck_c, ck_ct = ckc[n % 2]
                cs_, sn_, tb_t = tabs[n % 2]; A0, A1, A2, pi_, sc_t = scr[0]
                n += 1
                kb.dma("sp", cq_c[:], gout_bf[r, 0:512, csl].rearrange("(j p) t -> p j t", p=128), owner=cq_ct,
                       reads=[gout_t], writes=[cq_ct])
                kb.dma("sp", ck_c[:], gout_bf[r, 512:768, csl].rearrange("(j p) t -> p j t", p=128), owner=ck_ct,
                       reads=[gout_t], writes=[ck_ct])
                ct = ctok[gc]
                kb.dma("sp", KTb_sb[0:64, tsl], gout_bf[r, 768:832, csl], owner=ct, reads=[gout_t, vinit_t], writes=[ct])
                rope_tables(kb, pos_all[:, tsl], 512, (A0[:], A1[:], A2[:]), pi_[:], cs_[:], sn_[:], invf_sb, invf_t, tb_t, sc_t)
                qa_b, qa_bt = kb.bank()
                for j in range(4):
                    kb.op("pe", lambda j=j: nc.tensor.matmul(qa_b[:, :], Wuq[:, j, 0:128], cq_c[:, j, :],
                                                             start=(j == 0), stop=(j == 3)), reads=[cq_ct, Wuq_t], writes=[qa_bt])
                kb.op("act", lambda: nc.scalar.copy(out=QTa_sb[:, tsl], in_=qa_b[:, :]), reads=[qa_bt], pwrites=[ct])
                qp_b, qp_bt = kb.bank()
                for j in range(4):
                    kb.op("pe", lambda j=j: nc.tensor.matmul(qp_b[0:64, :], Wuq[:, j, 128:192], cq_c[:, j, :],
                                                             start=(j == 0), stop=(j == 3)), reads=[cq_ct, Wuq_t], writes=[qp_bt])
                qr_b, qr_bt = kb.bank()
                for j in range(4):
                    kb.op("pe", lambda j=j: nc.tensor.matmul(qr_b[0:64, :], Wqrot[:, j, :], cq_c[:, j, :],
                                                             start=(j == 0), stop=(j == 3)), reads=[cq_ct, Wqrot_t], writes=[qr_bt])
                rope_apply(kb, qp_b, qp_bt, qr_b, qr_bt, cs_[:], sn_[:], tb_t, tmp, QTb_sb[0:64, tsl], ct, True)
                ka_b, ka_bt = kb.bank()
                for j in range(2):
                    kb.op("pe", lambda j=j: nc.tensor.matmul(ka_b[:, :], Wukv[:, j, 0:128], ck_c[:, j, :],
                                                             start=(j == 0), stop=(j == 1)), reads=[ck_ct, Wukv_t], writes=[ka_bt])
                kb.op("dve", lambda: nc.vector.tensor_copy(out=KTa_sb[:, tsl], in_=ka_b[:, :]), reads=[ka_bt], pwrites=[ct])
                vb, vbt = kb.bank()
                for i in range(4):
                    for j in range(2):
                        kb.op("pe", lambda i=i, j=j: nc.tensor.matmul(vb[:, i * 128:(i + 1) * 128], ck_c[:, j, i * 128:(i + 1) * 128],
                                                                      Wukv[:, j, 128:256], start=(j == 0), stop=(j == 1)),
                              reads=[ck_ct, Wukv_t], writes=[vbt])
                kb.op("act", lambda: nc.scalar.copy(out=V_sb[:, gc * 4:gc * 4 + 4, 0:128],
                                                    in_=vb[:, :].rearrange("p (i d) -> p i d", i=4)),
                      reads=[vbt, vinit_t], pwrites=[ct])
    kb_barrier(kb)
    kb.pop_scope()

    if stop == "B":
        return kb
    qorder = [qc for qc in range(S // 512) if (qc % 4) < 2] + [qc for qc in range(S // 512) if (qc % 4) >= 2]

    def o_row0(qc):
        return (qc // 4) * 1024 + (qc % 2) * 512

    outO = out_bf[0:512, :].rearrange("a (b d) -> (a b) d", d=128)

    def attention_phase(passes, finalize_fn, ost, q_res, extra_q_reads=()):
        done = {0: 0, 1: 0}

        def epilogue(qc, pi, oset):
            hf = (qc % 4) // 2
            o_sb, o_tok = ost[done[0] % 2 if hf == 0 else done[1] % 2]
            fin = finalize_fn(qc, pi, oset, o_sb, o_tok)
            if fin:
                r0 = o_row0(qc)
                dstv, dtok = (gin_O, gin_t) if hf == 0 else (outO, outd_t)
                with nc.allow_non_contiguous_dma(reason="256B rows"):
                    kb.dma("pool", dstv[r0:r0 + 512, :].rearrange("(i p) d -> p i d", p=128), o_sb[:], owner=o_tok,
                           reads=[o_tok], pwrites=[dtok])
                done[hf] += 1
                if hf == 0 and done[0] == 16:
                    kb_gather(kb, gin, gout, reads=[gin_t], writes=[gout_t])

        attention_core(kb, passes, V_sb, lambda kbi: ctok[kbi // 4], epilogue,
                       q_resident=((lambda qc: ctok[qc]) if q_res else None), qc_order=qorder,
                       extra_q_reads=extra_q_reads, filler=(0 if q_res else 384))

    Ost = [(kb.sb("Ost%d" % i, [128, 4, 128], BF16), kb.tok()) for i in range(2)]
    rc = kb.sb("rc", [128, 8], F32); rc_t = kb.tok()
    gathers = []

    def fin_mla(qc, pi, oset, dst, dtok):
        for b, (ob, obtok) in enumerate(oset):
            sums = ob[:, 0:258].rearrange("p (i c) -> p i c", c=129)[:, :, 128:129]
            kb.op("dve", lambda: nc.vector.reciprocal(out=rc[:, 2 * b:2 * b + 2].rearrange("p (i c) -> p i c", c=1),
                                                     in_=sums), reads=[obtok], writes=[rc_t])
            for i in range(2):
                kb.op("dve", lambda i=i: nc.vector.tensor_scalar(
                    out=dst[:, 2 * b + i, :], in0=ob[:, i * 129:i * 129 + 128],
                    scalar1=rc[:, 2 * b + i:2 * b + i + 1], scalar2=None, op0=ALU.mult),
                    reads=[obtok, rc_t], **(dict(writes=[dtok]) if (b == 0 and i == 0) else dict(pwrites=[dtok])))
        return True

    attention_phase([dict(K=[(KTa_sb, 128), (KTb_sb, 128)], Q=[(QTa_sb, 128), (QTb_sb, 128)])], fin_mla, Ost, True)
    kb_barrier(kb)
    kb.pop_scope()

    if stop == "L2":
        return kb
    h_res, h_tok = make_h(kb)
    for tt in range(NT):
        kb.dma("sp", h_res[:, tt, :], x[tt * 128:(tt + 1) * 128, :], owner=h_tok[tt], writes=[h_tok[tt]])

    def o_gather_and_project(w_o):
        kb.push_scope()
        stage = kb.sb("pa_stage", [128, 8 * 1024], F32); stage_t = kb.tok()
        Wo = kb.sb("pa_Wo", [128, 8, 1024], BF16)
        Wo_t = prep_weight(kb, w_o, None, None, Wo, 8, 1024, stage, stage_t)
        oall = [(kb.sb("pa_oall%d" % k, [128, D], BF16), kb.tok()) for k in range(8)]
        oT = kb.sb("pa_oT", [128, 8, 128], BF16); oT_t = kb.tok()

        def load_tiles():
            for k in range(8):
                ob, obt = oall[k]
                with nc.allow_non_contiguous_dma(reason="256B head rows"):
                    kb.dma("sp", ob[:].rearrange("p (h d) -> p h d", h=H),
                           gout_O[:, bass.ds(pid * 1024 + k * 128, 128), :].rearrange("h p d -> p h d"),
                           owner=obt, reads=[gout_t], writes=[obt])

        def project(hf):
            for k in range(8):
                tt = hf * 8 + k
                ob, obt = oall[k]
                transpose_to(kb, lambda j: ob[:, j * 128:(j + 1) * 128], obt, 8, 128, lambda: oT[:, :, :], oT_t, evac="act")
                for half in range(2):
                    pb, pbt = kb.bank()
                    for j in range(8):
                        kb.op("pe", lambda j=j: nc.tensor.matmul(pb[:, :], oT[:, j, :], Wo[:, j, half * 512:(half + 1) * 512],
                                                                 start=(j == 0), stop=(j == 7)),
                              reads=[oT_t, Wo_t], writes=[pbt])
                    hs = h_res[:, tt, half * 512:(half + 1) * 512]
                    kb.op("dve", lambda: nc.vector.tensor_tensor(out=hs, in0=pb[:, :], in1=hs, op=ALU.add),
                          reads=[pbt], pwrites=[h_tok[tt]])

        load_tiles()
        kb.dma("pool", gin_bf, out_bf[0:512, :].rearrange("a (b c) -> (a b) c", c=1024), owner=gin_t,
               reads=[outd_t], writes=[gin_t])
        kb_gather(kb, gin, gout, reads=[gin_t], writes=[gout_t])
        project(0)
        load_tiles()
        project(1)
        kb_barrier(kb)
        kb.pop_scope()

    o_gather_and_project(w_o0)
    if stop == "C1":
        kb_barrier(kb)
        return kb
    phase_ffn(kb, h_res, h_tok, g_ffn0, w_gu0, w_d0)
    if stop == "C":
        return kb

    kb.push_scope()
    junk = kb.sb("d_junk", [128, D], BF16); xs = kb.sb("d_xs", [128, D], BF16); st4 = kb.sb("d_st", [128, 4], F32)
    work = (junk, kb.tok(), xs, kb.tok(), st4, kb.tok())
    hT = kb.sb("d_hT", [128, 8, TPC], BF16); hT_t = kb.tok()
    norm_transpose(kb, h_res, h_tok, hT, hT_t, work, tiles=range(0, 8))
    kb.dma("pool", gin_bf.rearrange("(j p) t -> p j t", p=128), hT[:, :, 0:1024], owner=gin_t, reads=[hT_t], writes=[gin_t])
    kb_gather(kb, gin, gout, reads=[gin_t], writes=[gout_t])
    norm_transpose(kb, h_res, h_tok, hT, hT_t, work, tiles=range(8, 16))
    stg = out_bf[0:512, :].rearrange("a (b c) -> (a b) c", c=1024)
    kb.dma("pool", stg.rearrange("(j p) t -> p j t", p=128), hT[:, :, 1024:2048], owner=outd_t, reads=[hT_t], writes=[outd_t])
    kb_barrier(kb)
    kb.pop_scope()

    kb.push_scope()
    K1_sb = kb.sb("K1_sb", [68, S], BF16); K2_sb = kb.sb("K2_sb", [68, S], BF16)
    V_sb = kb.sb("V4_sb", [128, S // 128, 129], BF16)
    ctok = [kb.tok() for _ in range(S // 512)]
    vinit_t = kb.tok()
    kb.op("pool", lambda: nc.gpsimd.memset(V_sb[:], 1.0), writes=[vinit_t])
    Q1d = out_bf[512:1056, :].rearrange("(q a) b -> q (a b)", q=68)
    Q2d = out_bf[1056:1600, :].rearrange("(q a) b -> q (a b)", q=68)
    qd_t = kb.tok()
    kb.push_scope()
    gDK, gDK_t = load_gain(kb, "d_gk", g_dkv, 8)
    gA1, gA1_t = load_gain(kb, "d_ga", g_attn1, 8)
    stage = kb.sb("d_stage", [128, 1024], F32); stage_t = kb.tok()
    Wk = kb.sb("d_Wk", [128, 8, 128], BF16); Wv = kb.sb("d_Wv", [128, 8, 128], BF16); Wq = kb.sb("d_Wq", [128, 8, 128], BF16)
    Wk_t = prep_weight(kb, w_k_h, gDK, gDK_t, Wk, 8, 128, stage, stage_t)
    Wv_t = prep_weight(kb, w_v_h, gDK, gDK_t, Wv, 8, 128, stage, stage_t)
    Wq_t = prep_weight(kb, w_q_h, gA1, gA1_t, Wq, 8, 128, stage, stage_t, extra=64 ** -0.5)
    cf = kb.sb("d_cf", [68, 8], F32); cf_t = kb.tok()
    kb.dma("sp", cf[64:68, 0:4], kcoef, owner=cf_t, writes=[cf_t])
    kb.dma("sp", cf[64:68, 4:8], qcoef, owner=cf_t, pwrites=[cf_t])
    xc = [(kb.sb("d_xc%d" % i, [128, 8, 512], BF16), kb.tok()) for i in range(2)]
    qst = [(kb.sb("d_qst%d" % i, [68, 2, 512], BF16), kb.tok()) for i in range(2)]
    pI = kb.sb("d_pI", [68, 512], I32); aI = kb.sb("d_aI", [68, 512], I32)
    aF = kb.sb("d_aF", [68, 512], F32); bF = kb.sb("d_bF", [68, 512], F32); t1 = kb.sb("d_t1", [68, 512], F32)
    aug_t = kb.tok()
    V_ = nc.vector
    R = slice(64, 68)
    n = 0
    for hf in range(2):
        if hf == 1:
            kb.dma("pool", gin_bf, stg, owner=gin_t, reads=[outd_t], writes=[gin_t])
            kb_gather(kb, gin, gout, reads=[gin_t], writes=[gout_t])
        for r in range(NCORES):
            for cc in range(2):
                gc = r * 4 + hf * 2 + cc
                tsl = slice(gc * 512, (gc + 1) * 512)
                csl = slice(cc * 512, (cc + 1) * 512)
                x_c, x_ct = xc[n % 2]; q_s, q_st = qst[n % 2]
                n += 1
                ct = ctok[gc]
                kb.dma("sp", x_c[:], gout_bf[r, :, csl].rearrange("(j p) t -> p j t", p=128), owner=x_ct,
                       reads=[gout_t], writes=[x_ct])
                kb.dma("sp", pI[R, :], pos_all[:, tsl].partition_broadcast(4), owner=aug_t, writes=[aug_t])
                seq = [
                    lambda: V_.tensor_scalar(out=aI[R, :], in0=pI[R, :], scalar1=7, scalar2=None, op0=ALU.arith_shift_right),
                    lambda: V_.tensor_copy(out=aF[R, :], in_=aI[R, :]),
                    lambda: V_.tensor_scalar(out=aI[R, :], in0=pI[R, :], scalar1=127, scalar2=None, op0=ALU.bitwise_and),
                    lambda: V_.tensor_copy(out=bF[R, :], in_=aI[R, :]),
                    lambda: V_.tensor_scalar(out=t1[R, :], in0=aF[R, :], scalar1=cf[R, 0:1], scalar2=cf[R, 2:3], op0=ALU.mult, op1=ALU.add),
                ]
                for fn in seq:
                    kb.op("dve", fn, reads=[cf_t], writes=[aug_t])
                kb.op("dve", lambda: V_.scalar_tensor_tensor(out=K1_sb[R, tsl], in0=bF[R, :], scalar=cf[R, 1:2], in1=t1[R, :],
                                                             op0=ALU.mult, op1=ALU.add), reads=[aug_t, cf_t], writes=[ct])
                kb.op("dve", lambda: V_.tensor_copy(out=K2_sb[R, tsl], in_=K1_sb[R, tsl]), reads=[ct], pwrites=[ct])
                kb.op("dve", lambda: V_.tensor_scalar(out=t1[R, :], in0=aF[R, :], scalar1=cf[R, 4:5], scalar2=cf[R, 6:7],
                                                      op0=ALU.mult, op1=ALU.add), reads=[aug_t, cf_t], writes=[aug_t])
                kb.op("dve", lambda: V_.scalar_tensor_tensor(out=q_s[R, 0, :], in0=bF[R, :], scalar=cf[R, 5:6], in1=t1[R, :],
                                                             op0=ALU.mult, op1=ALU.add), reads=[aug_t, cf_t], writes=[q_st])
                kb.op("dve", lambda: V_.tensor_copy(out=q_s[R, 1, :], in_=q_s[R, 0, :]), reads=[q_st], pwrites=[q_st])
                for i in range(2):
                    kb_, kbt_ = kb.bank()
                    for j in range(8):
                        kb.op("pe", lambda j=j: nc.tensor.matmul(kb_[0:64, :], Wk[:, j, i * 64:(i + 1) * 64], x_c[:, j, :],
                                                                 start=(j == 0), stop=(j == 7)), reads=[x_ct, Wk_t], writes=[kbt_])
                    Ki = K1_sb if i == 0 else K2_sb
                    kb.op("act", lambda: nc.scalar.copy(out=Ki[0:64, tsl], in_=kb_[0:64, :]), reads=[kbt_], pwrites=[ct])
                    qb_, qbt_ = kb.bank()
                    for j in range(8):
                        kb.op("pe", lambda j=j: nc.tensor.matmul(qb_[0:64, :], Wq[:, j, i * 64:(i + 1) * 64], x_c[:, j, :],
                                                                 start=(j == 0), stop=(j == 7)), reads=[x_ct, Wq_t], writes=[qbt_])
                    kb.op("dve", lambda: nc.vector.tensor_copy(out=q_s[0:64, i, :], in_=qb_[0:64, :]), reads=[qbt_], pwrites=[q_st])
                vb, vbt = kb.bank()
                for i in range(4):
                    for j in range(8):
                        kb.op("pe", lambda i=i, j=j: nc.tensor.matmul(vb[:, i * 128:(i + 1) * 128], x_c[:, j, i * 128:(i + 1) * 128],
                                                                      Wv[:, j, :], start=(j == 0), stop=(j == 7)),
                              reads=[x_ct, Wv_t], writes=[vbt])
                kb.op("act", lambda: nc.scalar.copy(out=V_sb[:, gc * 4:gc * 4 + 4, 0:128],
                                                    in_=vb[:, :].rearrange("p (i d) -> p i d", i=4)),
                      reads=[vbt, vinit_t], pwrites=[ct])
                kb.dma("pool", Q1d[:, tsl], q_s[:, 0, :], owner=q_st, reads=[q_st], pwrites=[qd_t])
                kb.dma("pool", Q2d[:, tsl], q_s[:, 1, :], owner=q_st, reads=[q_st], pwrites=[qd_t])
    kb_barrier(kb)
    kb.pop_scope()

    lam_sb = kb.sb("lam_sb", [128, 4, 64], F32); lam_t = kb.tok()
    kb.dma("sp", lam_sb[:].rearrange("p a d -> p (a d)"), lam_d.rearrange("a d -> (a d)").partition_broadcast(128),
           owner=lam_t, writes=[lam_t])
    lj = kb.sb("lam_j", [128, 64], F32); lv = kb.sb("lam_v", [128, 4], F32); lv_t = kb.tok()
    kb.op("dve", lambda: nc.vector.tensor_tensor(out=lj[:], in0=lam_sb[:, 0, :], in1=lam_sb[:, 1, :], op=ALU.mult),
          reads=[lam_t], writes=[lv_t])
    kb.op("dve", lambda: nc.vector.tensor_reduce(out=lv[:, 0:1], in_=lj[:], axis=mybir.AxisListType.X, op=ALU.add),
          reads=[lv_t], writes=[lv_t])
    kb.op("dve", lambda: nc.vector.tensor_tensor(out=lj[:], in0=lam_sb[:, 2, :], in1=lam_sb[:, 3, :], op=ALU.mult),
          reads=[lam_t, lv_t], writes=[lv_t])
    kb.op("dve", lambda: nc.vector.tensor_reduce(out=lv[:, 1:2], in_=lj[:], axis=mybir.AxisListType.X, op=ALU.add),
          reads=[lv_t], writes=[lv_t])
    kb.op("act", lambda: nc.scalar.activation(out=lv[:, 0:2], in_=lv[:, 0:2], func=AF.Exp), reads=[lv_t], writes=[lv_t])
    kb.op("dve", lambda: nc.vector.scalar_tensor_tensor(out=lv[:, 2:3], in0=lv[:, 1:2], scalar=-float(lambda_init),
                                                        in1=lv[:, 0:1], op0=ALU.add, op1=ALU.subtract),
          reads=[lv_t], writes=[lv_t])
    sub_sb = kb.sb("sub_sb", [128, 128], F32); sub_t = kb.tok()
    kb.dma("sp", sub_sb[:], subln_d.partition_broadcast(128), owner=sub_t, writes=[sub_t])
    kb.op("dve", lambda: nc.vector.tensor_scalar(out=sub_sb[:], in0=sub_sb[:], scalar1=float(1.0 - lambda_init),
                                                 scalar2=None, op0=ALU.mult), reads=[sub_t], writes=[sub_t])
    o1 = kb.sb("o1n", [128, 4, 128], F32); o1_t = kb.tok()
    att = kb.sb("attd", [128, 4, 128], F32); att_t = kb.tok()
    junk4 = kb.sb("junk4", [128, 128], F32); junk4_t = kb.tok()
    Ost4 = [(kb.sb("Ost4_%d" % i, [128, 4, 128], BF16), kb.tok()) for i in range(2)]
    rc4 = kb.sb("rc4", [128, 8], F32); rc4_t = kb.tok()
    ss = kb.sb("ss4", [128, 8], F32); ss_t = kb.tok()

    def fin_diff(qc, pi, oset, dst, dtok):
        for b, (ob, obtok) in enumerate(oset):
            sums = ob[:, 0:258].rearrange("p (i c) -> p i c", c=129)[:, :, 128:129]
            kb.op("dve", lambda: nc.vector.reciprocal(out=rc4[:, 2 * b:2 * b + 2].rearrange("p (i c) -> p i c", c=1),
                                                     in_=sums), reads=[obtok], writes=[rc4_t])
            for i in range(2):
                qi = 2 * b + i
                if pi == 0:
                    kb.op("dve", lambda: nc.vector.tensor_scalar(
                        out=o1[:, qi, :], in0=ob[:, i * 129:i * 129 + 128], scalar1=rc4[:, qi:qi + 1], scalar2=None,
                        op0=ALU.mult), reads=[obtok, rc4_t], **(dict(writes=[o1_t]) if qi == 0 else dict(pwrites=[o1_t])))
                else:
                    kb.op("dve", lambda: nc.vector.tensor_scalar(
                        out=att[:, qi, :], in0=ob[:, i * 129:i * 129 + 128], scalar1=rc4[:, qi:qi + 1],
                        scalar2=lv[:, 2:3], op0=ALU.mult, op1=ALU.mult),
                        reads=[obtok, rc4_t, lv_t], **(dict(writes=[att_t]) if qi == 0 else dict(pwrites=[att_t])))
                    kb.op("pool", lambda: nc.gpsimd.tensor_tensor(out=att[:, qi, :], in0=att[:, qi, :], in1=o1[:, qi, :],
                                                                  op=ALU.add), reads=[o1_t], pwrites=[att_t])
        if pi == 0:
            return False
        for qi in range(4):
            kb.op("act", lambda: nc.scalar.activation(out=junk4[:], in_=att[:, qi, :], func=AF.Square,
                                                      accum_out=ss[:, qi:qi + 1]),
                  reads=[att_t], writes=[junk4_t], pwrites=[ss_t])
        kb.op("act", lambda: nc.scalar.activation(out=ss[:, 4:8], in_=ss[:, 0:4], func=AF.Ln, scale=1.0 / 128,
                                                  bias=kb.eps_ap), reads=[ss_t, kb.eps_tok], writes=[ss_t])
        kb.op("act", lambda: nc.scalar.activation(out=ss[:, 4:8], in_=ss[:, 4:8], func=AF.Exp, scale=-0.5),
              reads=[ss_t], writes=[ss_t])
        for qi in range(4):
            kb.op("dve", lambda: nc.vector.scalar_tensor_tensor(
                out=dst[:, qi, :], in0=att[:, qi, :], scalar=ss[:, 4 + qi:5 + qi], in1=sub_sb[:],
                op0=ALU.mult, op1=ALU.mult), reads=[att_t, ss_t, sub_t],
                **(dict(writes=[dtok]) if qi == 0 else dict(pwrites=[dtok])))
        return True

    attention_phase([dict(K=[(K1_sb, 68)], Q=[(Q1d, 68)]), dict(K=[(K2_sb, 68)], Q=[(Q2d, 68)])], fin_diff, Ost4, False,
                    extra_q_reads=[qd_t])
    kb_barrier(kb)
    kb.pop_scope()

    o_gather_and_project(w_o1)
    phase_ffn(kb, h_res, h_tok, g_ffn1, w_gu1, w_d1)
    gb = kb.sb("f_g", [128, D], F32); gb_t = kb.tok()
    kb.dma("sp", gb[:], g_fin.partition_broadcast(128), owner=gb_t, writes=[gb_t])
    junk = kb.sb("f_junk", [128, D], BF16); junk_t = kb.tok()
    st4 = kb.sb("f_st", [128, 4], F32); st_t = kb.tok()
    obf = [(kb.sb("f_o%d" % i, [128, D], F32), kb.tok()) for i in range(2)]
    for tt in range(NT):
        kb.op("act", lambda: nc.scalar.activation(out=junk[:], in_=h_res[:, tt, :], func=AF.Square,
                                                  accum_out=st4[:, 0:1]), reads=[h_tok[tt]], writes=[junk_t, st_t])
        rms_scale(kb, st4[:, 0:1], st4[:, 1:2], D, st_t, st_t)
        o_sb, o_t = obf[tt % 2]
        kb.op("dve", lambda: nc.vector.scalar_tensor_tensor(out=o_sb[:], in0=h_res[:, tt, :], scalar=st4[:, 1:2],
                                                            in1=gb[:], op0=ALU.mult, op1=ALU.mult),
              reads=[h_tok[tt], st_t, gb_t], writes=[o_t])
        kb.dma("sp", out[tt * 128:(tt + 1) * 128, :], o_sb[:], owner=o_t, reads=[o_t], pwrites=[outd_t, qd_t])
    kb.finish([t for _, t in obf])
    return kb


def kernel(x, positions, attn_norm, ffn_norm, final_norm,
           mla_w_dq, mla_q_norm, mla_w_uq, mla_w_dkv, mla_kv_norm, mla_w_ukv, mla_w_o,
           diff_kv_norm, diff_w_k, diff_w_v, diff_w_q,
           diff_lambda_q1, diff_lambda_k1, diff_lambda_q2, diff_lambda_k2,
           diff_subln, diff_w_o, ffn_w_gate_up, ffn_w_down):
    f32 = np.float32
    x = np.asarray(x, f32)
    positions = np.asarray(positions, np.int32)
    A = lambda a: np.ascontiguousarray(np.asarray(a, f32))
    inv = (10000.0 ** (-np.arange(32, dtype=f32) * 2.0 / 64)).astype(f32)
    invf = np.concatenate([inv, inv]).reshape(64, 1).astype(f32)
    ident = np.eye(128, dtype=f32)
    mask = np.where(np.arange(128)[:, None] > np.arange(128)[None, :], NEG, 0.0).astype(f32)
    kcoef = np.array([[128, 0, 0, 0], [0, 1, 0, 0], [0, 0, 1, 0], [0, 0, 1, 0]], f32)
    qbase = np.array([[0, 0, 1, 0], [0, 0, 1, 0], [-128, 0, 0, 0], [0, -1, 0, 0]], f32)
    lam = np.ascontiguousarray(np.stack([A(diff_lambda_q1[0]), A(diff_lambda_k1[0]),
                                         A(diff_lambda_q2[0]), A(diff_lambda_k2[0])]))
    shared = dict(pos_all=np.ascontiguousarray(positions), invf=invf, ident=ident, mask=mask, kcoef=kcoef,
                  g_attn0=A(attn_norm[0]), g_q=A(mla_q_norm[0]), g_kvl=A(mla_kv_norm[0]),
                  w_dq=A(mla_w_dq[0]), w_dkv=A(mla_w_dkv[0]), w_o0=A(mla_w_o[0]), g_ffn0=A(ffn_norm[0]),
                  w_gu0=A(ffn_w_gate_up[0]), w_d0=A(ffn_w_down[0]), g_dkv=A(diff_kv_norm), g_attn1=A(attn_norm[1]),
                  lam=lam, subln=A(diff_subln[0]).reshape(1, 128), w_o1=A(diff_w_o[0]), g_ffn1=A(ffn_norm[1]),
                  w_gu1=A(ffn_w_gate_up[1]), w_d1=A(ffn_w_down[1]), g_fin=A(final_norm).reshape(1, D))
    w_uq = np.asarray(mla_w_uq[0], f32); w_ukv = np.asarray(mla_w_ukv[0], f32)
    w_k = np.asarray(diff_w_k, f32); w_v = np.asarray(diff_w_v, f32); w_q = np.asarray(diff_w_q[0], f32)
    maps = []
    for c in range(NCORES):
        sl = slice(c * TPC, (c + 1) * TPC)
        slope = 2.0 ** (-8.0 * (c + 1) / H)
        m = dict(shared)
        m.update(x=np.ascontiguousarray(x[0, sl]), pos=np.ascontiguousarray(positions[:, sl]),
                 qcoef=np.ascontiguousarray(qbase * f32(slope)),
                 w_uq_h=np.ascontiguousarray(w_uq[:, c * 192:(c + 1) * 192]),
                 w_ukv_h=np.ascontiguousarray(w_ukv[:, c * 256:(c + 1) * 256]),
                 w_k_h=np.ascontiguousarray(w_k[:, c * 128:(c + 1) * 128]),
                 w_v_h=np.ascontiguousarray(w_v[:, c * 128:(c + 1) * 128]),
                 w_q_h=np.ascontiguousarray(w_q[:, c * 128:(c + 1) * 128]))
        maps.append(m)
    res = _run(build_fused(), maps)
    out = np.concatenate([np.asarray(res[r]["out"]) for r in range(NCORES)], axis=0)
    return out.reshape(1, S, D).astype(f32)
```

```python
from contextlib import ExitStack
import numpy as np
import ml_dtypes
import concourse.bass as bass
import concourse.mybir as mybir
from concourse.bass_utils import run_bass_kernel_spmd

F32 = mybir.dt.float32
BF16 = mybir.dt.bfloat16
I32 = mybir.dt.int32
AF = mybir.ActivationFunctionType
ALU = mybir.AluOpType

NCORES = 8
S = 16384
D = 1024
TPC = S // NCORES
NT = TPC // 128
H = 8
DFF = 2816
EPS = 1e-6
NEG = -30000.0
TWO_PI = 6.283185307179586
C1 = 6.28125
C2 = TWO_PI - C1
PI = 3.141592653589793
PI_SAFE = 3.1415925
SAME_ENGINE_SYNC = True


class Tok:
    __slots__ = ("w", "r", "sem", "cnt")

    def __init__(self):
        self.w = {}
        self.r = {}
        self.sem = None
        self.cnt = 0


class KB:
    def __init__(self):
        self.nc = bass.Bass("TRN2", target_bir_lowering=False)
        self.st = ExitStack()
        self.sems = []
        self.eng = {}
        nc = self.nc
        for n, h in (("pe", nc.tensor), ("act", nc.scalar), ("dve", nc.vector),
                     ("pool", nc.gpsimd), ("sp", nc.sync)):
            si = self.new_sem("e_" + n)
            self.eng[n] = {"h": h, "sem": si, "cnt": 0, "seen": {}}
        self.nbank = 0
        self.banks = []
        self.dma_toks = []
        self.scopes = []
        self.scope_toks = []
        self.free_sems = []

    def tok(self):
        t = Tok()
        self.dma_toks.append(t)
        if self.scope_toks:
            self.scope_toks[-1].append(t)
        return t

    def push_scope(self):
        self.scopes.append(ExitStack())
        self.scope_toks.append([])

    def pop_scope(self):
        for t in self.scope_toks.pop():
            if t.sem is not None:
                self.free_sems.append((t.sem, t.cnt))
                t.sem = None
        self.scopes.pop().close()

    def new_sem(self, name):
        s = self.st.enter_context(self.nc.semaphore(name))
        self.sems.append(s)
        return len(self.sems) - 1

    def sb(self, name, shape, dt):
        st = self.scopes[-1] if self.scopes else self.st
        self.nsb = getattr(self, "nsb", 0) + 1
        return st.enter_context(self.nc.sbuf_tensor("%s_%d" % (name, self.nsb), shape, dt))

    def ps(self, name, shape, dt):
        return self.st.enter_context(self.nc.psum_tensor(name, shape, dt))

    def dram(self, name, shape, dt, kind):
        return self.nc.dram_tensor(name, shape, dt, kind=kind).ap()

    def make_banks(self):
        for i in range(8):
            t = self.ps("bank%d" % i, [128, 512], F32)
            self.banks.append((t, Tok()))

    def bank(self):
        b = self.banks[self.nbank % 8]
        self.nbank += 1
        return b

    def _wait(self, e, deps, dma=False):
        E = self.eng[e]
        for sem, val in deps.items():
            if sem == E["sem"] and not dma and (e == "pe" or not SAME_ENGINE_SYNC):
                continue
            if E["seen"].get(sem, 0) >= val:
                continue
            E["h"].wait_ge(self.sems[sem], val)
            E["seen"][sem] = val

    def _deps(self, reads, writes, pwrites=(), skip_sem=None):
        deps = {}
        for t in reads:
            for s, v in t.w.items():
                if v > deps.get(s, 0):
                    deps[s] = v
        for t in writes:
            for d in (t.w, t.r):
                for s, v in d.items():
                    if v > deps.get(s, 0):
                        deps[s] = v
        for t in pwrites:
            for d in (t.w, t.r):
                for s, v in d.items():
                    if v > deps.get(s, 0):
                        deps[s] = v
        if skip_sem is not None:
            deps.pop(skip_sem, None)
        return deps

    def _post(self, ticket, reads, writes, pwrites=()):
        s, v = ticket
        for t in reads:
            t.r[s] = v
        for t in writes:
            t.w = {s: v}
            t.r = {}
        for t in pwrites:
            t.w[s] = v

    def op(self, e, fn, reads=(), writes=(), pwrites=()):
        E = self.eng[e]
        self._wait(e, self._deps(reads, writes, pwrites))
        inst = fn()
        E["cnt"] += 1
        inst.then_inc(self.sems[E["sem"]], 1)
        self._post((E["sem"], E["cnt"]), reads, writes, pwrites)

    def dma(self, q, out, in_, owner, reads=(), writes=(), pwrites=()):
        E = self.eng[q]
        if owner.sem is None:
            if self.free_sems:
                owner.sem, owner.cnt = self.free_sems.pop()
            else:
                owner.sem = self.new_sem("d%d" % len(self.sems))
        self._wait(q, self._deps(reads, writes, pwrites, skip_sem=owner.sem), dma=True)
        owner.cnt += 16
        E["h"].dma_start(out=out, in_=in_).then_inc(self.sems[owner.sem], 16)
        self._post((owner.sem, owner.cnt), reads, writes, pwrites)

    def finish(self, toks):
        deps = {}
        for t in toks:
            for d in (t.w, t.r):
                for s, v in d.items():
                    if v > deps.get(s, 0):
                        deps[s] = v
        self._wait("sp", deps, dma=True)


def rms_scale(kb, ss_ap, r_ap, n, tok_ss, tok_r):
    nc = kb.nc
    kb.op("act", lambda: nc.scalar.activation(out=r_ap, in_=ss_ap, func=AF.Ln, scale=1.0 / n, bias=kb.eps_ap),
          reads=[tok_ss, kb.eps_tok], writes=[tok_r])
    kb.op("act", lambda: nc.scalar.activation(out=r_ap, in_=r_ap, func=AF.Exp, scale=-0.5),
          reads=[tok_r], writes=[tok_r])


def setup_consts(kb, ident_dram):
    nc = kb.nc
    kb.ident = kb.sb("ident_sb", [128, 128], BF16)
    kb.ident_tok = Tok()
    kb.dma("pool", kb.ident[:], ident_dram, owner=kb.ident_tok, writes=[kb.ident_tok])
    kb.eps_t = kb.sb("eps_t", [128, 1], F32)
    kb.eps_tok = Tok()
    kb.eps_ap = kb.eps_t[:, 0:1]
    kb.op("dve", lambda: nc.vector.memset(kb.eps_t[:], EPS), writes=[kb.eps_tok])


def transpose_to(kb, src_ap, src_tok, nblk, kpart, dst_fn, dst_tok, evac="act", pw=False):
    nc = kb.nc
    bt, btok = kb.bank()
    bv = bt[:].bitcast(BF16)
    for j in range(nblk):
        kb.op("pe", lambda j=j: nc.tensor.transpose(bv[0:kpart, j * 128:(j + 1) * 128], src_ap(j), kb.ident[:]),
              reads=[src_tok, kb.ident_tok], writes=[btok])
    srcv = bv[0:kpart, 0:nblk * 128].rearrange("p (j t) -> p j t", j=nblk)
    wr = dict(pwrites=[dst_tok]) if pw else dict(writes=[dst_tok])
    if evac == "act":
        kb.op("act", lambda: nc.scalar.copy(out=dst_fn(), in_=srcv), reads=[btok], **wr)
    else:
        kb.op("dve", lambda: nc.vector.tensor_copy(out=dst_fn(), in_=srcv), reads=[btok], **wr)


def build_L1():
    kb = KB()
    nc = kb.nc
    x = kb.dram("x", [TPC, D], F32, "ExternalInput")
    pos = kb.dram("pos", [1, TPC], I32, "ExternalInput")
    invf = kb.dram("invf", [64, 1], F32, "ExternalInput")
    ident_d = kb.dram("ident", [128, 128], F32, "ExternalInput")
    g_attn = kb.dram("g_attn", [D], F32, "ExternalInput")
    g_q = kb.dram("g_q", [512], F32, "ExternalInput")
    g_kv = kb.dram("g_kv", [256], F32, "ExternalInput")
    w_dq = kb.dram("w_dq", [D, 512], F32, "ExternalInput")
    w_uq = kb.dram("w_uq", [512, 1536], F32, "ExternalInput")
    w_dkv = kb.dram("w_dkv", [D, 320], F32, "ExternalInput")
    w_ukv = kb.dram("w_ukv", [256, 2048], F32, "ExternalInput")
    QTa_o = kb.dram("QTa", [H, 128, TPC], BF16, "ExternalOutput")
    QTb_o = kb.dram("QTb", [H, 64, TPC], BF16, "ExternalOutput")
    KTa_o = kb.dram("KTa", [H, 128, TPC], BF16, "ExternalOutput")
    KTb_o = kb.dram("KTb", [64, TPC], BF16, "ExternalOutput")
    V_o = kb.dram("V", [H, 128, NT, 129], BF16, "ExternalOutput")

    kb.make_banks()
    setup_consts(kb, ident_d)
    scale = (128 + 64) ** -0.5

    gA = kb.sb("gA", [128, 8], F32); gA_t = Tok()
    gQ = kb.sb("gQ", [128, 4], F32); gQ_t = Tok()
    gK = kb.sb("gK", [128, 2], F32); gK_t = Tok()
    with nc.allow_non_contiguous_dma(reason="tiny gain vectors"):
        kb.dma("sp", gA[:], g_attn.rearrange("(j p) -> p j", p=128), owner=gA_t, writes=[gA_t])
        kb.dma("sp", gQ[:], g_q.rearrange("(j p) -> p j", p=128), owner=gQ_t, writes=[gQ_t])
        kb.dma("sp", gK[:], g_kv.rearrange("(j p) -> p j", p=128), owner=gK_t, writes=[gK_t])

    stage = kb.sb("stage", [128, 6144], F32); stage_t = Tok()
    Wdq = kb.sb("Wdq", [128, 8, 512], BF16)
    Wdkv = kb.sb("Wdkv", [128, 8, 320], BF16)
    Wkrot = kb.sb("Wkrot", [128, 8, 64], BF16)
    Wuq = kb.sb("Wuq", [128, 4, 1536], BF16)
    Wqrot = kb.sb("Wqrot", [128, 4, 8, 64], BF16)
    Wukv = kb.sb("Wukv", [128, 2, 2048], BF16)

    def prep(w_dram, g_sb, g_tok, out_bf, kc, ncol, extra=None):
        sview = stage[:, 0:kc * ncol].rearrange("p (j n) -> p j n", j=kc)
        kb.dma("sp", sview, w_dram.rearrange("(j p) n -> p j n", p=128), owner=stage_t, writes=[stage_t])
        wt = Tok()
        for j in range(kc):
            kb.op("dve", lambda j=j: nc.vector.tensor_scalar(
                out=out_bf[:, j, :], in0=sview[:, j, :], scalar1=g_sb[:, j:j + 1],
                scalar2=(None if extra is None else float(extra)), op0=ALU.mult,
                **({} if extra is None else {"op1": ALU.mult})),
                reads=[stage_t, g_tok], writes=[wt])
        return wt

    Wdq_t = prep(w_dq, gA, gA_t, Wdq, 8, 512)
    Wdkv_t = prep(w_dkv, gA, gA_t, Wdkv, 8, 320)
    Wuq_t = prep(w_uq, gQ, gQ_t, Wuq, 4, 1536, extra=scale)
    Wukv_t = prep(w_ukv, gK, gK_t, Wukv, 2, 2048)
    Wkrot_t = Tok()
    kb.op("dve", lambda: nc.vector.tensor_scalar(out=Wkrot[:, :, 0:32], in0=Wdkv[:, :, 288:320], scalar1=-1.0,
                                                 scalar2=None, op0=ALU.mult), reads=[Wdkv_t], writes=[Wkrot_t])
    kb.op("dve", lambda: nc.vector.tensor_copy(out=Wkrot[:, :, 32:64], in_=Wdkv[:, :, 256:288]),
          reads=[Wdkv_t], writes=[Wkrot_t])
    Wqrot_t = Tok()
    Wuq_v = Wuq[:].rearrange("p j (h d) -> p j h d", h=8)
    for j in range(4):
        kb.op("dve", lambda j=j: nc.vector.tensor_scalar(out=Wqrot[:, j, :, 0:32], in0=Wuq_v[:, j, :, 160:192],
                                                         scalar1=-1.0, scalar2=None, op0=ALU.mult),
              reads=[Wuq_t], writes=[Wqrot_t])
        kb.op("dve", lambda j=j: nc.vector.tensor_copy(out=Wqrot[:, j, :, 32:64], in_=Wuq_v[:, j, :, 128:160]),
              reads=[Wuq_t], writes=[Wqrot_t])

    cosT = kb.sb("cosT", [64, TPC], F32); sinT = kb.sb("sinT", [64, TPC], F32); tab_t = Tok()
    invf_sb = kb.sb("invf_sb", [64, 1], F32); invf_t = Tok()
    kb.dma("sp", invf_sb[:], invf, owner=invf_t, writes=[invf_t])
    posi = kb.sb("posi", [64, TPC], I32); posi_t = Tok()
    kb.dma("sp", posi[:], pos.partition_broadcast(64), owner=posi_t, writes=[posi_t])
    A0 = stage[0:64, 0:TPC]; A1 = stage[0:64, TPC:2 * TPC]; A2 = stage[0:64, 2 * TPC:3 * TPC]
    A1i = A1.bitcast(I32)
    V = nc.vector
    sq = [
        (lambda: V.tensor_copy(out=A0, in_=posi[:]), [posi_t]),
        (lambda: V.tensor_scalar(out=A0, in0=A0, scalar1=invf_sb[:, 0:1], scalar2=None, op0=ALU.mult), [invf_t]),
        (lambda: V.tensor_scalar(out=A2, in0=A0, scalar1=1.0 / TWO_PI, scalar2=None, op0=ALU.mult), []),
        (lambda: V.tensor_copy(out=A1i, in_=A2), []),
        (lambda: V.tensor_copy(out=A2, in_=A1i), []),
        (lambda: V.scalar_tensor_tensor(out=A0, in0=A2, scalar=-C1, in1=A0, op0=ALU.mult, op1=ALU.add), []),
        (lambda: V.scalar_tensor_tensor(out=A0, in0=A2, scalar=-C2, in1=A0, op0=ALU.mult, op1=ALU.add), []),
        (lambda: V.tensor_scalar(out=A2, in0=A0, scalar1=PI, scalar2=-TWO_PI, op0=ALU.is_gt, op1=ALU.mult), []),
        (lambda: V.tensor_tensor(out=A0, in0=A0, in1=A2, op=ALU.add), []),
        (lambda: V.tensor_scalar(out=A2, in0=A0, scalar1=-PI, scalar2=TWO_PI, op0=ALU.is_lt, op1=ALU.mult), []),
        (lambda: V.tensor_tensor(out=A0, in0=A0, in1=A2, op=ALU.add), []),
        (lambda: V.tensor_scalar(out=A1, in0=A0, scalar1=PI / 2, scalar2=None, op0=ALU.add), []),
        (lambda: V.tensor_scalar(out=A2, in0=A1, scalar1=PI, scalar2=-TWO_PI, op0=ALU.is_gt, op1=ALU.mult), []),
        (lambda: V.tensor_tensor(out=A1, in0=A1, in1=A2, op=ALU.add), []),
        (lambda: V.tensor_scalar(out=A0, in0=A0, scalar1=PI_SAFE, scalar2=-PI_SAFE, op0=ALU.min, op1=ALU.max), []),
        (lambda: V.tensor_scalar(out=A1, in0=A1, scalar1=PI_SAFE, scalar2=-PI_SAFE, op0=ALU.min, op1=ALU.max), []),
    ]
    for fn, rd in sq:
        kb.op("dve", fn, reads=[stage_t] + rd, writes=[stage_t])
    kb.op("act", lambda: nc.scalar.activation(out=sinT[:], in_=A0, func=AF.Sin), reads=[stage_t], writes=[tab_t])
    kb.op("act", lambda: nc.scalar.activation(out=cosT[:], in_=A1, func=AF.Sin), reads=[stage_t], pwrites=[tab_t])

    xt = [kb.sb("xt%d" % i, [128, D], F32) for i in range(2)]; xt_t = [Tok(), Tok()]
    junk = kb.sb("junk", [128, D], BF16); junk_t = Tok()
    xs = kb.sb("xs", [128, D], BF16); xs_t = Tok()
    st4 = kb.sb("st4", [128, 8], F32); st_t = Tok()
    hnT = kb.sb("hnT", [128, 8, 512], BF16); hnT_t = Tok()
    cqs = kb.sb("cqs", [128, 512], BF16); cqs_t = Tok()
    ckvs = kb.sb("ckvs", [128, 256], BF16); ckvs_t = Tok()
    cqT = kb.sb("cqT", [128, 4, 512], BF16); cqT_t = Tok()
    ckvT = kb.sb("ckvT", [128, 2, 512], BF16); ckvT_t = Tok()
    Vaug = kb.sb("Vaug", [128, H, NT, 129], BF16); Vaug_t = Tok()
    kb.op("pool", lambda: nc.gpsimd.memset(Vaug[:], 1.0), writes=[Vaug_t])
    qa_st = kb.sb("qa_st", [128, H, 512], BF16); qa_t = Tok()
    qb_st = kb.sb("qb_st", [64, H, 512], BF16); qb_t = Tok()
    ka_st = kb.sb("ka_st", [128, H, 512], BF16); ka_t = Tok()
    kb_st = kb.sb("kb_st", [64, 512], BF16); kbs_t = Tok()
    tmp1 = kb.sb("tmp1", [64, 512], F32); tmp1_t = Tok()
    tmp2 = kb.sb("tmp2", [64, 512], F32); tmp2_t = Tok()

    Wukv_v = Wukv[:].rearrange("p j (h d) -> p j h d", h=8)

    def rope_combine(raw_b, raw_tok, rot_b, rot_tok, c, out_ap, out_tok, pw):
        cs = cosT[:, c * 512:(c + 1) * 512]
        sn = sinT[:, c * 512:(c + 1) * 512]
        kb.op("dve", lambda: nc.vector.tensor_tensor(out=tmp1[:], in0=raw_b[0:64, :], in1=cs, op=ALU.mult),
              reads=[raw_tok, tab_t], writes=[tmp1_t])
        kb.op("dve", lambda: nc.vector.tensor_tensor(out=tmp2[:], in0=rot_b[0:64, :], in1=sn, op=ALU.mult),
              reads=[rot_tok, tab_t], writes=[tmp2_t])
        wr = dict(pwrites=[out_tok]) if pw else dict(writes=[out_tok])
        kb.op("pool", lambda: nc.gpsimd.tensor_tensor(out=out_ap, in0=tmp1[:], in1=tmp2[:], op=ALU.add),
              reads=[tmp1_t, tmp2_t], **wr)

    for c in range(NT // 4):
        for i in range(4):
            tt = 4 * c + i
            xb = xt[tt % 2]; xbt = xt_t[tt % 2]
            kb.dma("sp", xb[:], x[tt * 128:(tt + 1) * 128, :], owner=xbt, writes=[xbt])
            kb.op("act", lambda: nc.scalar.activation(out=junk[:], in_=xb[:], func=AF.Square,
                                                      accum_out=st4[:, 0:1]),
                  reads=[xbt], writes=[junk_t, st_t])
            rms_scale(kb, st4[:, 0:1], st4[:, 1:2], D, st_t, st_t)
            kb.op("dve", lambda: nc.vector.tensor_scalar(out=xs[:], in0=xb[:], scalar1=st4[:, 1:2], scalar2=None,
                                                         op0=ALU.mult), reads=[xbt, st_t], writes=[xs_t])
            transpose_to(kb, lambda j: xs[:, j * 128:(j + 1) * 128], xs_t, 8, 128,
                         lambda: hnT[:, :, i * 128:(i + 1) * 128], hnT_t, evac="act", pw=(i > 0))
            cq_b, cq_bt = kb.bank()
            for j in range(8):
                kb.op("pe", lambda j=j: nc.tensor.matmul(cq_b[:, 0:512], hnT[:, j, i * 128:(i + 1) * 128],
                                                         Wdq[:, j, :], start=(j == 0), stop=(j == 7)),
                      reads=[hnT_t, Wdq_t], writes=[cq_bt])
            ck_b, ck_bt = kb.bank()
            for j in range(8):
                kb.op("pe", lambda j=j: nc.tensor.matmul(ck_b[:, 0:256], hnT[:, j, i * 128:(i + 1) * 128],
                                                         Wdkv[:, j, 0:256], start=(j == 0), stop=(j == 7)),
                      reads=[hnT_t, Wdkv_t], writes=[ck_bt])
            kb.op("act", lambda: nc.scalar.activation(out=junk[:, 0:512], in_=cq_b[:, 0:512], func=AF.Square,
                                                      accum_out=st4[:, 2:3]), reads=[cq_bt], writes=[junk_t, st_t])
            rms_scale(kb, st4[:, 2:3], st4[:, 3:4], 512, st_t, st_t)
            kb.op("dve", lambda: nc.vector.tensor_scalar(out=cqs[:], in0=cq_b[:, 0:512], scalar1=st4[:, 3:4],
                                                         scalar2=None, op0=ALU.mult),
                  reads=[cq_bt, st_t], writes=[cqs_t])
            kb.op("act", lambda: nc.scalar.activation(out=junk[:, 0:256], in_=ck_b[:, 0:256], func=AF.Square,
                                                      accum_out=st4[:, 4:5]), reads=[ck_bt], writes=[junk_t, st_t])
            rms_scale(kb, st4[:, 4:5], st4[:, 5:6], 256, st_t, st_t)
            kb.op("dve", lambda: nc.vector.tensor_scalar(out=ckvs[:], in0=ck_b[:, 0:256], scalar1=st4[:, 5:6],
                                                         scalar2=None, op0=ALU.mult),
                  reads=[ck_bt, st_t], writes=[ckvs_t])
            transpose_to(kb, lambda j: cqs[:, j * 128:(j + 1) * 128], cqs_t, 4, 128,
                         lambda: cqT[:, :, i * 128:(i + 1) * 128], cqT_t, evac="dve", pw=(i > 0))
            transpose_to(kb, lambda j: ckvs[:, j * 128:(j + 1) * 128], ckvs_t, 2, 128,
                         lambda: ckvT[:, :, i * 128:(i + 1) * 128], ckvT_t, evac="dve", pw=(i > 0))
            for hh in range(2):
                vb, vbt = kb.bank()
                for j in range(2):
                    kb.op("pe", lambda j=j: nc.tensor.matmul(vb[:, 0:512].rearrange("p (h d) -> p h d", h=4),
                                                             ckvT[:, j, i * 128:(i + 1) * 128],
                                                             Wukv_v[:, j, 4 * hh:4 * hh + 4, 128:256],
                                                             start=(j == 0), stop=(j == 1)),
                          reads=[ckvT_t, Wukv_t], writes=[vbt])
                kb.op("act", lambda: nc.scalar.copy(out=Vaug[:, 4 * hh:4 * hh + 4, tt, 0:128],
                                                    in_=vb[:, 0:512].rearrange("p (h d) -> p h d", h=4)),
                      reads=[vbt], pwrites=[Vaug_t])
        csl = slice(c * 512, (c + 1) * 512)
        kp_b, kp_bt = kb.bank()
        for j in range(8):
            kb.op("pe", lambda j=j: nc.tensor.matmul(kp_b[0:64, :], Wdkv[:, j, 256:320], hnT[:, j, :],
                                                     start=(j == 0), stop=(j == 7)),
                  reads=[hnT_t, Wdkv_t], writes=[kp_bt])
        kr_b, kr_bt = kb.bank()
        for j in range(8):
            kb.op("pe", lambda j=j: nc.tensor.matmul(kr_b[0:64, :], Wkrot[:, j, :], hnT[:, j, :],
                                                     start=(j == 0), stop=(j == 7)),
                  reads=[hnT_t, Wkrot_t], writes=[kr_bt])
        rope_combine(kp_b, kp_bt, kr_b, kr_bt, c, kb_st[:], kbs_t, False)
        kb.dma("pool", KTb_o[:, csl], kb_st[:], owner=kbs_t, reads=[kbs_t])
        for h in range(H):
            qa_b, qa_bt = kb.bank()
            for j in range(4):
                kb.op("pe", lambda j=j: nc.tensor.matmul(qa_b[:, :], Wuq[:, j, h * 192:h * 192 + 128], cqT[:, j, :],
                                                         start=(j == 0), stop=(j == 3)),
                      reads=[cqT_t, Wuq_t], writes=[qa_bt])
            kb.op("act", lambda: nc.scalar.copy(out=qa_st[:, h, :], in_=qa_b[:, :]), reads=[qa_bt],
                  **(dict(writes=[qa_t]) if h == 0 else dict(pwrites=[qa_t])))
            qp_b, qp_bt = kb.bank()
            for j in range(4):
                kb.op("pe", lambda j=j: nc.tensor.matmul(qp_b[0:64, :], Wuq[:, j, h * 192 + 128:h * 192 + 192],
                                                         cqT[:, j, :], start=(j == 0), stop=(j == 3)),
                      reads=[cqT_t, Wuq_t], writes=[qp_bt])
            qr_b, qr_bt = kb.bank()
            for j in range(4):
                kb.op("pe", lambda j=j: nc.tensor.matmul(qr_b[0:64, :], Wqrot[:, j, h, :], cqT[:, j, :],
                                                         start=(j == 0), stop=(j == 3)),
                      reads=[cqT_t, Wqrot_t], writes=[qr_bt])
            rope_combine(qp_b, qp_bt, qr_b, qr_bt, c, qb_st[:, h, :], qb_t, h > 0)
            ka_b, ka_bt = kb.bank()
            for j in range(2):
                kb.op("pe", lambda j=j: nc.tensor.matmul(ka_b[:, :], Wukv[:, j, h * 256:h * 256 + 128], ckvT[:, j, :],
                                                         start=(j == 0), stop=(j == 1)),
                      reads=[ckvT_t, Wukv_t], writes=[ka_bt])
            kb.op("dve", lambda: nc.vector.tensor_copy(out=ka_st[:, h, :], in_=ka_b[:, :]), reads=[ka_bt],
                  **(dict(writes=[ka_t]) if h == 0 else dict(pwrites=[ka_t])))
        kb.dma("pool", QTa_o[:, :, csl].rearrange("h p t -> p h t"), qa_st[:], owner=qa_t, reads=[qa_t])
        kb.dma("pool", QTb_o[:, :, csl].rearrange("h p t -> p h t"), qb_st[:], owner=qb_t, reads=[qb_t])
        kb.dma("pool", KTa_o[:, :, csl].rearrange("h p t -> p h t"), ka_st[:], owner=ka_t, reads=[ka_t])
    for h in range(H):
        kb.dma("pool", V_o[h], Vaug[:, h, :, :], owner=Vaug_t, reads=[Vaug_t])
    kb.finish([kbs_t, qa_t, qb_t, ka_t, Vaug_t])
    return kb


def attention_core(kb, passes, V_sb, kv_tok_of_block, epilogue, nq_chunks=S // 512, q_resident=None, qc_order=None, extra_q_reads=(), filler=0):
    nc = kb.nc
    npass = len(passes)
    kb.ac_n = getattr(kb, "ac_n", 0) + 1
    pfx = "ac%d_" % kb.ac_n
    zeros = kb.sb(pfx + "zeros", [128, 512], BF16); zeros_t = Tok()
    kb.op("pool", lambda: nc.gpsimd.memset(zeros[:], 0.0), writes=[zeros_t])
    NQS = 3
    order = list(range(nq_chunks)) if qc_order is None else list(qc_order)
    qslots = []
    for pi, ps_ in enumerate(passes if q_resident is None else []):
        sl = []
        for s in range(NQS):
            parts = [kb.sb(pfx + "q%d_%d_%d" % (pi, s, k), [rows, 512], BF16) for k, (_, rows) in enumerate(ps_["Q"])]
            sl.append((parts, Tok()))
        qslots.append(sl)
    NP = 3
    pbuf = [(kb.sb(pfx + "pT%d" % i, [128, 512], BF16), Tok()) for i in range(NP)]
    sbank = [kb.banks[0], kb.banks[1], kb.banks[7]]
    NS = 3
    osets = [(kb.banks[2], kb.banks[3]), (kb.banks[4], kb.banks[5])]

    def load_q(oi):
        qc = order[oi]
        for pi, ps_ in enumerate(passes):
            parts, tok = qslots[pi][oi % NQS]
            for k, (qd, rows) in enumerate(ps_["Q"]):
                kb.dma("pool", parts[k][:], qd[:, qc * 512:(qc + 1) * 512], owner=tok, reads=list(extra_q_reads),
                       **(dict(writes=[tok]) if k == 0 else dict(pwrites=[tok])))

    units = [(qc, pi) for qc in order for pi in range(npass)]
    oidx = {qc: i for i, qc in enumerate(order)}
    tiles = []
    for ui, (qc, pi) in enumerate(units):
        for kbi in range(4 * qc + 4):
            tiles.append((ui, qc, pi, kbi))

    def emit_qk(ti):
        ui, qc, pi, kbi = tiles[ti]
        sb_t, sb_tok = sbank[ti % NS]
        if q_resident is None:
            parts, qtok = qslots[pi][oidx[qc] % NQS]
            qsl = lambda k, rows, lo: parts[k][0:rows, lo:512]
        else:
            qtok = q_resident(qc)
            qsl = lambda k, rows, lo: passes[pi]["Q"][k][0][0:rows, qc * 512 + lo:(qc + 1) * 512]
        j = kbi - 4 * qc
        lo = 128 * j if j > 0 else 0
        kparts = passes[pi]["K"]
        n = len(kparts)
        for k, (ksb, rows) in enumerate(kparts):
            kb.op("pe", lambda k=k, ksb=ksb, rows=rows: nc.tensor.matmul(
                sb_t[:, lo:512], ksb[0:rows, kbi * 128:(kbi + 1) * 128], qsl(k, rows, lo),
                start=(k == 0), stop=(k == n - 1 and j < 0)),
                reads=[kv_tok_of_block(kbi), qtok], writes=[sb_tok])
        if j >= 0:
            kb.op("pe", lambda: nc.tensor.matmul(sb_t[:, lo:lo + 128], kb.ident[:], kb.mask[:], start=False, stop=True),
                  reads=[kb.ident_tok, kb.mask_tok], writes=[sb_tok])
        if filler:
            fb, fbt = kb.banks[6]
            kb.op("pe", lambda: nc.tensor.matmul(fb[:, 0:filler], zeros[:, 0:128], zeros[:, 0:filler], start=True, stop=True),
                  reads=[zeros_t], writes=[fbt])

    def emit_exp(ti):
        ui, qc, pi, kbi = tiles[ti]
        sb_t, sb_tok = sbank[ti % NS]
        pb, ptok = pbuf[ti % NP]
        j = kbi - 4 * qc
        lo = 128 * j if j > 0 else 0
        kb.op("act", lambda: nc.scalar.activation(out=pb[:, lo:512], in_=sb_t[:, lo:512], func=AF.Exp),
              reads=[sb_tok], writes=[ptok])

    def emit_pv(ti):
        ui, qc, pi, kbi = tiles[ti]
        pb, ptok = pbuf[ti % NP]
        oset = osets[ui % 2]
        j = kbi - 4 * qc
        last = (kbi == 4 * qc + 3)
        if kbi == 0:
            for (ob, obtok) in oset:
                kb.op("pe", lambda ob=ob: nc.tensor.matmul(ob[:, :], zeros[:, 0:128], zeros[:, :], start=True, stop=False),
                      reads=[zeros_t], writes=[obtok])
        for i in range(max(j, 0), 4):
            ob, obtok = oset[i // 2]
            c0 = (i % 2) * 129
            kb.op("pe", lambda i=i, ob=ob, c0=c0: nc.tensor.matmul(
                ob[:, c0:c0 + 129], pb[:, 128 * i:128 * (i + 1)], V_sb[:, kbi, :], start=False,
                stop=(kbi == 4 * qc + i)),
                reads=[ptok, kv_tok_of_block(kbi)], writes=[obtok])
        if last:
            epilogue(qc, pi, oset)

    if q_resident is None:
        load_q(0)
        if len(order) > 1:
            load_q(1)
    nt = len(tiles)

    def qk_with_prefetch(x):
        ui, qc, pi, kbi = tiles[x]
        if q_resident is None and kbi == 0 and pi == 0 and oidx[qc] + 2 < len(order):
            load_q(oidx[qc] + 2)
        emit_qk(x)

    for x in range(min(2, nt)):
        qk_with_prefetch(x)
    for ti in range(nt):
        if ti + 2 < nt:
            qk_with_prefetch(ti + 2)
        emit_exp(ti)
        if ti >= 1:
            emit_pv(ti - 1)
    emit_pv(nt - 1)


def setup_mask(kb, mask_dram):
    kb.mask = kb.sb("mask_sb", [128, 128], BF16)
    kb.mask_tok = Tok()
    kb.dma("pool", kb.mask[:], mask_dram, owner=kb.mask_tok, writes=[kb.mask_tok])


def build_L2(nq_chunks=S // 512):
    kb = KB()
    nc = kb.nc
    QTa = kb.dram("QTa", [128, S], BF16, "ExternalInput")
    QTb = kb.dram("QTb", [64, S], BF16, "ExternalInput")
    KTa = kb.dram("KTa", [128, S], BF16, "ExternalInput")
    KTb = kb.dram("KTb", [64, S], BF16, "ExternalInput")
    Vd = kb.dram("V", [128, S // 128, 129], BF16, "ExternalInput")
    ident_d = kb.dram("ident", [128, 128], F32, "ExternalInput")
    mask_d = kb.dram("mask", [128, 128], F32, "ExternalInput")
    O = kb.dram("O", [S, 128], BF16, "ExternalOutput")
    kb.make_banks()
    setup_consts(kb, ident_d)
    setup_mask(kb, mask_d)
    KTa_sb = kb.sb("KTa_sb", [128, S], BF16)
    KTb_sb = kb.sb("KTb_sb", [64, S], BF16)
    V_sb = kb.sb("V_sb", [128, S // 128, 129], BF16)
    ptoks = [Tok() for _ in range(8)]
    for r in range(8):
        t = ptoks[r]
        kb.dma("sp", KTa_sb[:, r * 2048:(r + 1) * 2048], KTa[:, r * 2048:(r + 1) * 2048], owner=t, writes=[t])
        kb.dma("sp", KTb_sb[:, r * 2048:(r + 1) * 2048], KTb[:, r * 2048:(r + 1) * 2048], owner=t, pwrites=[t])
        kb.dma("sp", V_sb[:, r * 16:(r + 1) * 16, :], Vd[:, r * 16:(r + 1) * 16, :], owner=t, pwrites=[t])
    ost = [(kb.sb("ost%d" % i, [128, 4, 128], BF16), Tok()) for i in range(2)]
    rc = kb.sb("rc", [128, 4], F32); rc_t = Tok()
    Ov = O.rearrange("(c i p) d -> c p i d", i=4, p=128)

    def epilogue(qc, pi, oset):
        o_sb, o_tok = ost[qc % 2]
        for b, (ob, obtok) in enumerate(oset):
            sums = ob[:, 0:258].rearrange("p (i c) -> p i c", c=129)[:, :, 128:129]
            kb.op("dve", lambda: nc.vector.reciprocal(out=rc[:, 2 * b:2 * b + 2].rearrange("p (i c) -> p i c", c=1),
                                                     in_=sums), reads=[obtok], writes=[rc_t])
            for i in range(2):
                kb.op("dve", lambda i=i: nc.vector.tensor_scalar(
                    out=o_sb[:, 2 * b + i, :], in0=ob[:, i * 129:i * 129 + 128],
                    scalar1=rc[:, 2 * b + i:2 * b + i + 1], scalar2=None, op0=ALU.mult),
                    reads=[obtok, rc_t], **(dict(writes=[o_tok]) if (b == 0 and i == 0) else dict(pwrites=[o_tok])))
        kb.dma("sp", Ov[qc], o_sb[:], owner=o_tok, reads=[o_tok])

    attention_core(kb, [dict(K=[(KTa_sb, 128), (KTb_sb, 64)], Q=[(QTa, 128), (QTb, 64)])], V_sb,
                   lambda kbi: ptoks[kbi // 16], epilogue, nq_chunks=nq_chunks)
    kb.finish([t for _, t in ost])
    return kb


def kb_barrier(kb):
    deps = {}
    for n, E in kb.eng.items():
        if E["cnt"] > 0:
            deps[E["sem"]] = E["cnt"]
    for t in kb.dma_toks:
        if t.sem is not None and t.cnt > 0:
            deps[t.sem] = t.cnt
    for n in kb.eng:
        kb._wait(n, dict(deps), dma=True)


def load_gain(kb, name, g_dram, kc):
    nc = kb.nc
    g = kb.sb(name, [128, kc], F32)
    t = kb.tok()
    with nc.allow_non_contiguous_dma(reason="tiny gain vector"):
        kb.dma("sp", g[:], g_dram.rearrange("(j p) -> p j", p=128), owner=t, writes=[t])
    return g, t


def prep_weight(kb, w_view, g_sb, g_tok, out_bf, kc, ncol, stage, stage_tok, extra=None, q="sp"):
    nc = kb.nc
    sview = stage[:, 0:kc * ncol].rearrange("p (j n) -> p j n", j=kc)
    kb.dma(q, sview, w_view.rearrange("(j p) n -> p j n", p=128), owner=stage_tok, writes=[stage_tok])
    wt = kb.tok()
    for j in range(kc):
        if g_sb is None:
            kb.op("dve", lambda j=j: nc.vector.tensor_copy(out=out_bf[:, j, :], in_=sview[:, j, :]),
                  reads=[stage_tok], **(dict(writes=[wt]) if j == 0 else dict(pwrites=[wt])))
        else:
            kb.op("dve", lambda j=j: nc.vector.tensor_scalar(
                out=out_bf[:, j, :], in0=sview[:, j, :], scalar1=g_sb[:, j:j + 1],
                scalar2=(None if extra is None else float(extra)), op0=ALU.mult,
                **({} if extra is None else {"op1": ALU.mult})),
                reads=[stage_tok, g_tok], **(dict(writes=[wt]) if j == 0 else dict(pwrites=[wt])))
    return wt


def norm_transpose(kb, h_res, h_tok, hnT_all, hnT_tok, work, tiles=None):
    nc = kb.nc
    works = work if isinstance(work, list) else [work]
    for tt in (range(NT) if tiles is None else tiles):
        junk, junk_t, xs, xs_t, st4, st_t = works[tt % len(works)]
        kb.op("act", lambda: nc.scalar.activation(out=junk[:], in_=h_res[:, tt, :], func=AF.Square,
                                                  accum_out=st4[:, 0:1]), reads=[h_tok[tt]], writes=[junk_t, st_t])
        rms_scale(kb, st4[:, 0:1], st4[:, 1:2], D, st_t, st_t)
        kb.op("dve", lambda: nc.vector.tensor_scalar(out=xs[:], in0=h_res[:, tt, :], scalar1=st4[:, 1:2],
                                                     scalar2=None, op0=ALU.mult),
              reads=[h_tok[tt], st_t], writes=[xs_t])
        transpose_to(kb, lambda j: xs[:, j * 128:(j + 1) * 128], xs_t, 8, 128,
                     lambda: hnT_all[:, :, tt * 128:(tt + 1) * 128], hnT_tok, evac="act", pw=(tt > 0))


def phase_attn_out(kb, resid_load, O_d, w_o, h_res, h_tok):
    nc = kb.nc
    kb.push_scope()
    stage = kb.sb("pa_stage", [128, 8 * 1024], F32); stage_t = kb.tok()
    Wo = kb.sb("pa_Wo", [128, 8, 1024], BF16)
    Wo_t = prep_weight(kb, w_o, None, None, Wo, 8, 1024, stage, stage_t)
    ot = [(kb.sb("pa_o%d" % i, [128, D], BF16), kb.tok()) for i in range(2)]
    oT = kb.sb("pa_oT", [128, 8, 128], BF16); oT_t = kb.tok()
    for tt in range(NT):
        resid_load(tt)
        ob, obt = ot[tt % 2]
        kb.dma("pool", ob[:], O_d[tt * 128:(tt + 1) * 128, :], owner=obt, writes=[obt])
        transpose_to(kb, lambda j: ob[:, j * 128:(j + 1) * 128], obt, 8, 128, lambda: oT[:, :, :], oT_t, evac="act")
        for half in range(2):
            pb, pbt = kb.bank()
            for j in range(8):
                kb.op("pe", lambda j=j: nc.tensor.matmul(pb[:, :], oT[:, j, :], Wo[:, j, half * 512:(half + 1) * 512],
                                                         start=(j == 0), stop=(j == 7)),
                      reads=[oT_t, Wo_t], writes=[pbt])
            hs = h_res[:, tt, half * 512:(half + 1) * 512]
            kb.op("dve", lambda: nc.vector.tensor_tensor(out=hs, in0=pb[:, :], in1=hs, op=ALU.add),
                  reads=[pbt], pwrites=[h_tok[tt]])
    kb_barrier(kb)
    kb.pop_scope()


def phase_ffn(kb, h_res, h_tok, g_ffn, w_gu, w_d):
    nc = kb.nc
    kb.push_scope()
    gF, gF_t = load_gain(kb, "ff_g", g_ffn, 8)
    work = [(kb.sb("ff_junk%d" % i, [128, D], BF16), kb.tok(), kb.sb("ff_xs%d" % i, [128, D], BF16), kb.tok(),
             kb.sb("ff_st%d" % i, [128, 4], F32), kb.tok()) for i in range(2)]
    hnT = kb.sb("ff_hnT", [128, 8, TPC], BF16); hnT_t = kb.tok()
    norm_transpose(kb, h_res, h_tok, hnT, hnT_t, work)
    HB = 256
    nhb = DFF // HB
    stages = [(kb.sb("ff_stage%d" % i, [128, 8 * HB], F32), kb.tok()) for i in range(3)]
    Wg = [kb.sb("ff_Wg%d" % i, [128, 8, HB], BF16) for i in range(2)]
    Wu = [kb.sb("ff_Wu%d" % i, [128, 8, HB], BF16) for i in range(2)]
    Wd = [kb.sb("ff_Wd%d" % i, [128, 2, D], BF16) for i in range(2)]
    sg = [(kb.sb("ff_sg%d" % i, [128, 512], F32), kb.tok()) for i in range(2)]
    act = [(kb.sb("ff_act%d" % i, [128, 2, 512], BF16), kb.tok()) for i in range(2)]
    si = 0
    wts = {}

    def load_w(hb):
        nonlocal si
        s0, s0t = stages[si % 3]; si += 1
        Wg_t = prep_weight(kb, w_gu[:, hb * HB:(hb + 1) * HB], gF, gF_t, Wg[hb % 2], 8, HB, s0, s0t)
        s1, s1t = stages[si % 3]; si += 1
        Wu_t = prep_weight(kb, w_gu[:, DFF + hb * HB:DFF + (hb + 1) * HB], gF, gF_t, Wu[hb % 2], 8, HB, s1, s1t)
        s2, s2t = stages[si % 3]; si += 1
        Wd_t = prep_weight(kb, w_d[hb * HB:(hb + 1) * HB, :], None, None, Wd[hb % 2], 2, D, s2, s2t)
        wts[hb] = (Wg_t, Wu_t, Wd_t)

    units = [(hb, c) for hb in range(nhb) for c in range(TPC // 512)]

    def stage1(u):
        hb, c = units[u]
        if c == 0:
            load_w(hb)
        Wg_t, Wu_t, Wd_t = wts[hb]
        ab, abt = act[u % 2]
        for sub in range(2):
            gb, gbt = kb.bank()
            for j in range(8):
                kb.op("pe", lambda j=j: nc.tensor.matmul(gb[:, :], Wg[hb % 2][:, j, sub * 128:(sub + 1) * 128],
                                                         hnT[:, j, c * 512:(c + 1) * 512], start=(j == 0), stop=(j == 7)),
                      reads=[hnT_t, Wg_t], writes=[gbt])
            ub, ubt = kb.bank()
            for j in range(8):
                kb.op("pe", lambda j=j: nc.tensor.matmul(ub[:, :], Wu[hb % 2][:, j, sub * 128:(sub + 1) * 128],
                                                         hnT[:, j, c * 512:(c + 1) * 512], start=(j == 0), stop=(j == 7)),
                      reads=[hnT_t, Wu_t], writes=[ubt])
            sgb, sgt = sg[sub]
            kb.op("act", lambda: nc.scalar.activation(out=sgb[:], in_=gb[:, :], func=AF.Silu),
                  reads=[gbt], writes=[sgt])
            kb.op("dve", lambda: nc.vector.tensor_tensor(out=ab[:, sub, :], in0=ub[:, :], in1=sgb[:], op=ALU.mult),
                  reads=[ubt, sgt], **(dict(writes=[abt]) if sub == 0 else dict(pwrites=[abt])))

    def stage2(u):
        hb, c = units[u]
        Wg_t, Wu_t, Wd_t = wts[hb]
        ab, abt = act[u % 2]
        for i in range(4):
            tt = c * 4 + i
            for half in range(2):
                db, dbt = kb.bank()
                for sub in range(2):
                    kb.op("pe", lambda sub=sub: nc.tensor.matmul(db[:, :], ab[:, sub, i * 128:(i + 1) * 128],
                                                                 Wd[hb % 2][:, sub, half * 512:(half + 1) * 512],
                                                                 start=(sub == 0), stop=(sub == 1)),
                          reads=[abt, Wd_t], writes=[dbt])
                hs = h_res[:, tt, half * 512:(half + 1) * 512]
                kb.op("dve", lambda: nc.vector.tensor_tensor(out=hs, in0=db[:, :], in1=hs, op=ALU.add),
                      reads=[dbt], pwrites=[h_tok[tt]])

    stage1(0)
    for u in range(len(units)):
        if u + 1 < len(units):
            stage1(u + 1)
        stage2(u)
    kb_barrier(kb)
    kb.pop_scope()


def make_h(kb):
    h_res = kb.sb("h_res", [128, NT, D], F32)
    h_tok = [kb.tok() for _ in range(NT)]
    return h_res, h_tok


def build_L3():
    kb = KB()
    nc = kb.nc
    x = kb.dram("x", [TPC, D], F32, "ExternalInput")
    O_d = kb.dram("O", [TPC, D], BF16, "ExternalInput")
    pos = kb.dram("pos", [1, TPC], I32, "ExternalInput")
    coef = kb.dram("coef", [4, 8], F32, "ExternalInput")
    ident_d = kb.dram("ident", [128, 128], F32, "ExternalInput")
    w_o = kb.dram("w_o", [D, D], F32, "ExternalInput")
    g_ffn = kb.dram("g_ffn", [D], F32, "ExternalInput")
    w_gu = kb.dram("w_gu", [D, 2 * DFF], F32, "ExternalInput")
    w_d = kb.dram("w_d", [DFF, D], F32, "ExternalInput")
    g_kv = kb.dram("g_kv", [D], F32, "ExternalInput")
    g_a1 = kb.dram("g_a1", [D], F32, "ExternalInput")
    w_k = kb.dram("w_k", [D, D], F32, "ExternalInput")
    w_v = kb.dram("w_v", [D, D], F32, "ExternalInput")
    w_q = kb.dram("w_q", [D, D], F32, "ExternalInput")
    h1_o = kb.dram("h1", [TPC, D], F32, "ExternalOutput")
    Q1_o = kb.dram("Q1", [H, 68, TPC], BF16, "ExternalOutput")
    Q2_o = kb.dram("Q2", [H, 68, TPC], BF16, "ExternalOutput")
    K1_o = kb.dram("K1", [H, 68, TPC], BF16, "ExternalOutput")
    K2_o = kb.dram("K2", [H, 68, TPC], BF16, "ExternalOutput")
    V_o = kb.dram("V", [H, 128, NT, 129], BF16, "ExternalOutput")
    kb.make_banks()
    setup_consts(kb, ident_d)
    h_res, h_tok = make_h(kb)

    def resid_load(tt):
        kb.dma("sp", h_res[:, tt, :], x[tt * 128:(tt + 1) * 128, :], owner=h_tok[tt], writes=[h_tok[tt]])

    phase_attn_out(kb, resid_load, O_d, w_o, h_res, h_tok)
    phase_ffn(kb, h_res, h_tok, g_ffn, w_gu, w_d)
    for tt in range(NT):
        kb.dma("sp", h1_o[tt * 128:(tt + 1) * 128, :], h_res[:, tt, :], owner=h_tok[tt], reads=[h_tok[tt]])

    kb.push_scope()
    cf = kb.sb("d_cf", [4, 8], F32); cf_t = kb.tok()
    kb.dma("sp", cf[:], coef, owner=cf_t, writes=[cf_t])
    pi_ = kb.sb("d_pi", [4, TPC], I32); pi_t = kb.tok()
    kb.dma("sp", pi_[:], pos.partition_broadcast(4), owner=pi_t, writes=[pi_t])
    ai = kb.sb("d_ai", [4, TPC], I32)
    af = kb.sb("d_af", [4, TPC], F32); bf_ = kb.sb("d_bf", [4, TPC], F32); aug_t = kb.tok()
    t1 = kb.sb("d_t1", [4, TPC], F32)
    kaug = kb.sb("d_kaug", [4, TPC], BF16); qbase = kb.sb("d_qbase", [4, TPC], F32)
    qaug = [(kb.sb("d_qaug%d" % i, [4, TPC], BF16), kb.tok()) for i in range(2)]
    kaug_t = kb.tok()
    V_ = nc.vector
    seq = [
        lambda: V_.tensor_scalar(out=ai[:], in0=pi_[:], scalar1=7, scalar2=None, op0=ALU.arith_shift_right),
        lambda: V_.tensor_copy(out=af[:], in_=ai[:]),
        lambda: V_.tensor_scalar(out=ai[:], in0=pi_[:], scalar1=127, scalar2=None, op0=ALU.bitwise_and),
        lambda: V_.tensor_copy(out=bf_[:], in_=ai[:]),
        lambda: V_.tensor_scalar(out=t1[:], in0=af[:], scalar1=cf[:, 0:1], scalar2=cf[:, 2:3], op0=ALU.mult, op1=ALU.add),
        lambda: V_.scalar_tensor_tensor(out=kaug[:], in0=bf_[:], scalar=cf[:, 1:2], in1=t1[:], op0=ALU.mult, op1=ALU.add),
        lambda: V_.tensor_scalar(out=t1[:], in0=af[:], scalar1=cf[:, 3:4], scalar2=cf[:, 5:6], op0=ALU.mult, op1=ALU.add),
        lambda: V_.scalar_tensor_tensor(out=qbase[:], in0=bf_[:], scalar=cf[:, 4:5], in1=t1[:], op0=ALU.mult, op1=ALU.add),
    ]
    for fn in seq:
        kb.op("dve", fn, reads=[pi_t, cf_t], writes=[aug_t])
    kb.op("dve", lambda: V_.tensor_copy(out=kaug[:], in_=kaug[:]), reads=[aug_t], writes=[kaug_t])
    for h in range(H):
        slope = 2.0 ** (-8.0 * (h + 1) / H)
        qa_, qa_t_ = qaug[h % 2]
        kb.op("dve", lambda: V_.tensor_scalar(out=qa_[:], in0=qbase[:], scalar1=slope, scalar2=None, op0=ALU.mult),
              reads=[aug_t], writes=[qa_t_])
        kb.dma("pool", Q1_o[h, 64:68, :], qa_[:], owner=qa_t_, reads=[qa_t_])
        kb.dma("pool", Q2_o[h, 64:68, :], qa_[:], owner=qa_t_, reads=[qa_t_])
        kb.dma("pool", K1_o[h, 64:68, :], kaug[:], owner=kaug_t, reads=[kaug_t])
        kb.dma("pool", K2_o[h, 64:68, :], kaug[:], owner=kaug_t, reads=[kaug_t])
    kb_barrier(kb)
    kb.pop_scope()

    kb.push_scope()
    gK, gK_t = load_gain(kb, "d_gk", g_kv, 8)
    gA, gA_t = load_gain(kb, "d_ga", g_a1, 8)
    Wk = kb.sb("d_Wk", [128, 8, D], BF16); Wv = kb.sb("d_Wv", [128, 8, D], BF16); Wq = kb.sb("d_Wq", [128, 8, D], BF16)
    kb.push_scope()
    stage = kb.sb("d_stage", [128, 8 * 1024], F32); stage_t = kb.tok()
    Wk_t = prep_weight(kb, w_k, gK, gK_t, Wk, 8, D, stage, stage_t)
    Wv_t = prep_weight(kb, w_v, gK, gK_t, Wv, 8, D, stage, stage_t)
    Wq_t = prep_weight(kb, w_q, gA, gA_t, Wq, 8, D, stage, stage_t, extra=64 ** -0.5)
    kb_barrier(kb)
    kb.pop_scope()
    junk = kb.sb("d_junk", [128, D], BF16); xs = kb.sb("d_xs", [128, D], BF16); st4 = kb.sb("d_st", [128, 4], F32)
    work = (junk, kb.tok(), xs, kb.tok(), st4, kb.tok())
    hT = kb.sb("d_hT", [128, 8, TPC], BF16); hT_t = kb.tok()
    norm_transpose(kb, h_res, h_tok, hT, hT_t, work)
    vst = [(kb.sb("d_vst%d" % i, [128, H, 129], BF16), kb.tok()) for i in range(2)]
    for vs_, vs_t in vst:
        kb.op("pool", lambda: nc.gpsimd.memset(vs_[:], 1.0), writes=[vs_t])
    kst = [(kb.sb("d_kst%d" % i, [128, H, 512], BF16), kb.tok()) for i in range(1)]
    qst = [(kb.sb("d_qst%d" % i, [128, H, 512], BF16), kb.tok()) for i in range(1)]
    for c in range(TPC // 512):
        csl = slice(c * 512, (c + 1) * 512)
        for (W, W_t, stl, o1, o2, ev) in ((Wk, Wk_t, kst, K1_o, K2_o, "act"), (Wq, Wq_t, qst, Q1_o, Q2_o, "dve")):
            sb_, sb_t = stl[0]
            for h in range(H):
                pb, pbt = kb.bank()
                for j in range(8):
                    kb.op("pe", lambda j=j: nc.tensor.matmul(pb[:, :], W[:, j, h * 128:(h + 1) * 128], hT[:, j, csl],
                                                             start=(j == 0), stop=(j == 7)),
                          reads=[hT_t, W_t], writes=[pbt])
                wr = dict(writes=[sb_t]) if h == 0 else dict(pwrites=[sb_t])
                if ev == "act":
                    kb.op("act", lambda: nc.scalar.copy(out=sb_[:, h, :], in_=pb[:, :]), reads=[pbt], **wr)
                else:
                    kb.op("dve", lambda: nc.vector.tensor_copy(out=sb_[:, h, :], in_=pb[:, :]), reads=[pbt], **wr)
            kb.dma("pool", o1[:, 0:64, csl].rearrange("h p t -> p h t"), sb_[0:64, :, :], owner=sb_t, reads=[sb_t])
            kb.dma("pool", o2[:, 0:64, csl].rearrange("h p t -> p h t"), sb_[64:128, :, :], owner=sb_t, reads=[sb_t])
        for i in range(4):
            tt = c * 4 + i
            vs_, vs_t = vst[tt % 2]
            for hh in range(2):
                vb, vbt = kb.bank()
                for j in range(8):
                    kb.op("pe", lambda j=j: nc.tensor.matmul(vb[:, :], hT[:, j, tt * 128:(tt + 1) * 128],
                                                             Wv[:, j, hh * 512:(hh + 1) * 512], start=(j == 0), stop=(j == 7)),
                          reads=[hT_t, Wv_t], writes=[vbt])
                kb.op("act", lambda: nc.scalar.copy(out=vs_[:, 4 * hh:4 * hh + 4, 0:128],
                                                    in_=vb[:, :].rearrange("p (h d) -> p h d", h=4)),
                      reads=[vbt], pwrites=[vs_t])
            kb.dma("pool", V_o[:, :, tt, :].rearrange("h p c -> p h c"), vs_[:], owner=vs_t, reads=[vs_t])
    kb.finish(h_tok + [kaug_t] + [t for _, t in qaug] + [t for _, t in vst] + [t for _, t in kst] + [t for _, t in qst])
    kb_barrier(kb)
    kb.pop_scope()
    return kb


def build_L4(lambda_init, nq_chunks=S // 512):
    kb = KB()
    nc = kb.nc
    Q1 = kb.dram("Q1", [68, S], BF16, "ExternalInput")
    Q2 = kb.dram("Q2", [68, S], BF16, "ExternalInput")
    K1 = kb.dram("K1", [68, S], BF16, "ExternalInput")
    K2 = kb.dram("K2", [68, S], BF16, "ExternalInput")
    Vd = kb.dram("V", [128, S // 128, 129], BF16, "ExternalInput")
    ident_d = kb.dram("ident", [128, 128], F32, "ExternalInput")
    mask_d = kb.dram("mask", [128, 128], F32, "ExternalInput")
    lam_d = kb.dram("lam", [4, 64], F32, "ExternalInput")
    subln_d = kb.dram("subln", [1, 128], F32, "ExternalInput")
    O = kb.dram("O", [S, 128], BF16, "ExternalOutput")
    kb.make_banks()
    setup_consts(kb, ident_d)
    setup_mask(kb, mask_d)
    lam_sb = kb.sb("lam_sb", [128, 4, 64], F32); lam_t = kb.tok()
    kb.dma("sp", lam_sb[:].rearrange("p a d -> p (a d)"), lam_d.rearrange("a d -> (a d)").partition_broadcast(128),
           owner=lam_t, writes=[lam_t])
    lj = kb.sb("lam_j", [128, 64], F32); lv = kb.sb("lam_v", [128, 4], F32); lv_t = kb.tok()
    kb.op("dve", lambda: nc.vector.tensor_tensor(out=lj[:], in0=lam_sb[:, 0, :], in1=lam_sb[:, 1, :], op=ALU.mult),
          reads=[lam_t], writes=[lv_t])
    kb.op("dve", lambda: nc.vector.tensor_reduce(out=lv[:, 0:1], in_=lj[:], axis=mybir.AxisListType.X, op=ALU.add),
          reads=[lv_t], writes=[lv_t])
    kb.op("dve", lambda: nc.vector.tensor_tensor(out=lj[:], in0=lam_sb[:, 2, :], in1=lam_sb[:, 3, :], op=ALU.mult),
          reads=[lam_t, lv_t], writes=[lv_t])
    kb.op("dve", lambda: nc.vector.tensor_reduce(out=lv[:, 1:2], in_=lj[:], axis=mybir.AxisListType.X, op=ALU.add),
          reads=[lv_t], writes=[lv_t])
    kb.op("act", lambda: nc.scalar.activation(out=lv[:, 0:2], in_=lv[:, 0:2], func=AF.Exp), reads=[lv_t], writes=[lv_t])
    kb.op("dve", lambda: nc.vector.scalar_tensor_tensor(out=lv[:, 2:3], in0=lv[:, 1:2], scalar=-float(lambda_init),
                                                        in1=lv[:, 0:1], op0=ALU.add, op1=ALU.subtract),
          reads=[lv_t], writes=[lv_t])
    sub_sb = kb.sb("sub_sb", [128, 128], F32); sub_t = kb.tok()
    kb.dma("sp", sub_sb[:], subln_d.partition_broadcast(128), owner=sub_t, writes=[sub_t])
    kb.op("dve", lambda: nc.vector.tensor_scalar(out=sub_sb[:], in0=sub_sb[:], scalar1=float(1.0 - lambda_init),
                                                 scalar2=None, op0=ALU.mult), reads=[sub_t], writes=[sub_t])

    K1_sb = kb.sb("K1_sb", [68, S], BF16)
    K2_sb = kb.sb("K2_sb", [68, S], BF16)
    V_sb = kb.sb("V_sb", [128, S // 128, 129], BF16)
    ptoks = [kb.tok() for _ in range(8)]
    for r in range(8):
        t = ptoks[r]
        kb.dma("sp", K1_sb[:, r * 2048:(r + 1) * 2048], K1[:, r * 2048:(r + 1) * 2048], owner=t, writes=[t])
        kb.dma("sp", K2_sb[:, r * 2048:(r + 1) * 2048], K2[:, r * 2048:(r + 1) * 2048], owner=t, pwrites=[t])
        kb.dma("sp", V_sb[:, r * 16:(r + 1) * 16, :], Vd[:, r * 16:(r + 1) * 16, :], owner=t, pwrites=[t])
    o1 = kb.sb("o1n", [128, 4, 128], F32); o1_t = kb.tok()
    att = kb.sb("attd", [128, 4, 128], F32); att_t = kb.tok()
    junk = kb.sb("junk4", [128, 128], F32); junk_t = kb.tok()
    ost = [(kb.sb("ost%d" % i, [128, 4, 128], BF16), kb.tok()) for i in range(2)]
    rc = kb.sb("rc", [128, 8], F32); rc_t = kb.tok()
    ss = kb.sb("ss4", [128, 8], F32); ss_t = kb.tok()
    Ov = O.rearrange("(c i p) d -> c p i d", i=4, p=128)

    def epilogue(qc, pi, oset):
        for b, (ob, obtok) in enumerate(oset):
            sums = ob[:, 0:258].rearrange("p (i c) -> p i c", c=129)[:, :, 128:129]
            kb.op("dve", lambda: nc.vector.reciprocal(out=rc[:, 2 * b:2 * b + 2].rearrange("p (i c) -> p i c", c=1),
                                                     in_=sums), reads=[obtok], writes=[rc_t])
            for i in range(2):
                qi = 2 * b + i
                if pi == 0:
                    kb.op("dve", lambda: nc.vector.tensor_scalar(
                        out=o1[:, qi, :], in0=ob[:, i * 129:i * 129 + 128], scalar1=rc[:, qi:qi + 1], scalar2=None,
                        op0=ALU.mult), reads=[obtok, rc_t], **(dict(writes=[o1_t]) if qi == 0 else dict(pwrites=[o1_t])))
                else:
                    kb.op("dve", lambda: nc.vector.tensor_scalar(
                        out=att[:, qi, :], in0=ob[:, i * 129:i * 129 + 128], scalar1=rc[:, qi:qi + 1],
                        scalar2=lv[:, 2:3], op0=ALU.mult, op1=ALU.mult),
                        reads=[obtok, rc_t, lv_t], **(dict(writes=[att_t]) if qi == 0 else dict(pwrites=[att_t])))
                    kb.op("pool", lambda: nc.gpsimd.tensor_tensor(out=att[:, qi, :], in0=att[:, qi, :], in1=o1[:, qi, :],
                                                                  op=ALU.add), reads=[o1_t], pwrites=[att_t])
        if pi == 1:
            o_sb, o_tok = ost[qc % 2]
            for qi in range(4):
                kb.op("act", lambda: nc.scalar.activation(out=junk[:], in_=att[:, qi, :], func=AF.Square,
                                                          accum_out=ss[:, qi:qi + 1]),
                      reads=[att_t], writes=[junk_t], **(dict(pwrites=[ss_t])))
            kb.op("act", lambda: nc.scalar.activation(out=ss[:, 4:8], in_=ss[:, 0:4], func=AF.Ln, scale=1.0 / 128,
                                                      bias=kb.eps_ap), reads=[ss_t, kb.eps_tok], writes=[ss_t])
            kb.op("act", lambda: nc.scalar.activation(out=ss[:, 4:8], in_=ss[:, 4:8], func=AF.Exp, scale=-0.5),
                  reads=[ss_t], writes=[ss_t])
            for qi in range(4):
                kb.op("dve", lambda: nc.vector.scalar_tensor_tensor(
                    out=o_sb[:, qi, :], in0=att[:, qi, :], scalar=ss[:, 4 + qi:5 + qi], in1=sub_sb[:],
                    op0=ALU.mult, op1=ALU.mult), reads=[att_t, ss_t, sub_t],
                    **(dict(writes=[o_tok]) if qi == 0 else dict(pwrites=[o_tok])))
            kb.dma("sp", Ov[qc], o_sb[:], owner=o_tok, reads=[o_tok])

    attention_core(kb, [dict(K=[(K1_sb, 68)], Q=[(Q1, 68)]), dict(K=[(K2_sb, 68)], Q=[(Q2, 68)])], V_sb,
                   lambda kbi: ptoks[kbi // 16], epilogue, nq_chunks=nq_chunks)
    kb.finish([t for _, t in ost])
    return kb


def build_L5():
    kb = KB()
    nc = kb.nc
    h1 = kb.dram("h1", [TPC, D], F32, "ExternalInput")
    O_d = kb.dram("O", [TPC, D], BF16, "ExternalInput")
    ident_d = kb.dram("ident", [128, 128], F32, "ExternalInput")
    w_o = kb.dram("w_o", [D, D], F32, "ExternalInput")
    g_ffn = kb.dram("g_ffn", [D], F32, "ExternalInput")
    w_gu = kb.dram("w_gu", [D, 2 * DFF], F32, "ExternalInput")
    w_d = kb.dram("w_d", [DFF, D], F32, "ExternalInput")
    g_fin = kb.dram("g_fin", [1, D], F32, "ExternalInput")
    out = kb.dram("out", [TPC, D], F32, "ExternalOutput")
    kb.make_banks()
    setup_consts(kb, ident_d)
    h_res, h_tok = make_h(kb)

    def resid_load(tt):
        kb.dma("sp", h_res[:, tt, :], h1[tt * 128:(tt + 1) * 128, :], owner=h_tok[tt], writes=[h_tok[tt]])

    phase_attn_out(kb, resid_load, O_d, w_o, h_res, h_tok)
    phase_ffn(kb, h_res, h_tok, g_ffn, w_gu, w_d)
    gb = kb.sb("f_g", [128, D], F32); gb_t = kb.tok()
    kb.dma("sp", gb[:], g_fin.partition_broadcast(128), owner=gb_t, writes=[gb_t])
    junk = kb.sb("f_junk", [128, D], BF16); junk_t = kb.tok()
    st4 = kb.sb("f_st", [128, 4], F32); st_t = kb.tok()
    ob = [(kb.sb("f_o%d" % i, [128, D], F32), kb.tok()) for i in range(2)]
    for tt in range(NT):
        kb.op("act", lambda: nc.scalar.activation(out=junk[:], in_=h_res[:, tt, :], func=AF.Square,
                                                  accum_out=st4[:, 0:1]), reads=[h_tok[tt]], writes=[junk_t, st_t])
        rms_scale(kb, st4[:, 0:1], st4[:, 1:2], D, st_t, st_t)
        o_sb, o_t = ob[tt % 2]
        kb.op("dve", lambda: nc.vector.scalar_tensor_tensor(out=o_sb[:], in0=h_res[:, tt, :], scalar=st4[:, 1:2],
                                                            in1=gb[:], op0=ALU.mult, op1=ALU.mult),
              reads=[h_tok[tt], st_t, gb_t], writes=[o_t])
        kb.dma("sp", out[tt * 128:(tt + 1) * 128, :], o_sb[:], owner=o_t, reads=[o_t])
    kb.finish([t for _, t in ob])
    return kb


def _run(kb, maps):
    res = run_bass_kernel_spmd(kb.nc, maps, core_ids=list(range(NCORES)))
    return res.results


def _cat_tokens(results, key, h):
    return np.ascontiguousarray(np.concatenate([np.asarray(results[r][key])[h] for r in range(NCORES)], axis=-1))


def _cat_v(results, h):
    return np.ascontiguousarray(np.concatenate([np.asarray(results[r]["V"])[h] for r in range(NCORES)], axis=1))


def kernel_unfused(x, positions, attn_norm, ffn_norm, final_norm,
           mla_w_dq, mla_q_norm, mla_w_uq, mla_w_dkv, mla_kv_norm, mla_w_ukv, mla_w_o,
           diff_kv_norm, diff_w_k, diff_w_v, diff_w_q,
           diff_lambda_q1, diff_lambda_k1, diff_lambda_q2, diff_lambda_k2,
           diff_subln, diff_w_o, ffn_w_gate_up, ffn_w_down):
    f32 = np.float32
    x = np.asarray(x, f32)
    positions = np.asarray(positions, np.int32)
    A = lambda a: np.ascontiguousarray(np.asarray(a, f32))
    inv = (10000.0 ** (-np.arange(32, dtype=f32) * 2.0 / 64)).astype(f32)
    invf = np.concatenate([inv, inv]).reshape(64, 1).astype(f32)
    ident = np.eye(128, dtype=f32)
    mask = np.where(np.arange(128)[:, None] > np.arange(128)[None, :], NEG, 0.0).astype(f32)
    coef = np.array([[128, 0, 0, 0, 0, 1],
                     [0, 1, 0, 0, 0, 1],
                     [0, 0, 1, -128, 0, 0],
                     [0, 0, 1, 0, -1, 0]], f32)
    coef = np.ascontiguousarray(np.concatenate([coef, np.zeros((4, 2), f32)], axis=1))
    sl = [slice(r * TPC, (r + 1) * TPC) for r in range(NCORES)]
    xs = [np.ascontiguousarray(x[0, s]) for s in sl]
    ps = [np.ascontiguousarray(positions[:, s]) for s in sl]

    r1 = _run(build_L1(), [dict(x=xs[r], pos=ps[r], invf=invf, ident=ident, g_attn=A(attn_norm[0]),
                                g_q=A(mla_q_norm[0]), g_kv=A(mla_kv_norm[0]), w_dq=A(mla_w_dq[0]),
                                w_uq=A(mla_w_uq[0]), w_dkv=A(mla_w_dkv[0]), w_ukv=A(mla_w_ukv[0]))
                           for r in range(NCORES)])
    ktb = np.ascontiguousarray(np.concatenate([np.asarray(r1[r]["KTb"]) for r in range(NCORES)], axis=-1))
    r2 = _run(build_L2(), [dict(QTa=_cat_tokens(r1, "QTa", h), QTb=_cat_tokens(r1, "QTb", h),
                                KTa=_cat_tokens(r1, "KTa", h), KTb=ktb, V=_cat_v(r1, h), ident=ident, mask=mask)
                           for h in range(NCORES)])
    del r1
    O0 = np.concatenate([np.asarray(r2[h]["O"]) for h in range(H)], axis=1)
    del r2
    r3 = _run(build_L3(), [dict(x=xs[r], O=np.ascontiguousarray(O0[sl[r]]), pos=ps[r], coef=coef, ident=ident,
                                w_o=A(mla_w_o[0]), g_ffn=A(ffn_norm[0]), w_gu=A(ffn_w_gate_up[0]),
                                w_d=A(ffn_w_down[0]), g_kv=A(diff_kv_norm), g_a1=A(attn_norm[1]),
                                w_k=A(diff_w_k), w_v=A(diff_w_v), w_q=A(diff_w_q[0]))
                           for r in range(NCORES)])
    lambda_init = 0.8 - 0.6 * float(np.exp(-0.3 * 1))
    lam = np.ascontiguousarray(np.stack([A(diff_lambda_q1[0]), A(diff_lambda_k1[0]),
                                         A(diff_lambda_q2[0]), A(diff_lambda_k2[0])]))
    r4 = _run(build_L4(lambda_init), [dict(Q1=_cat_tokens(r3, "Q1", h), Q2=_cat_tokens(r3, "Q2", h),
                                           K1=_cat_tokens(r3, "K1", h), K2=_cat_tokens(r3, "K2", h),
                                           V=_cat_v(r3, h), ident=ident, mask=mask, lam=lam,
                                           subln=A(diff_subln[0]).reshape(1, 128))
                                      for h in range(NCORES)])
    h1 = [np.asarray(r3[r]["h1"]) for r in range(NCORES)]
    del r3
    O1 = np.concatenate([np.asarray(r4[h]["O"]) for h in range(H)], axis=1)
    del r4
    r5 = _run(build_L5(), [dict(h1=h1[r], O=np.ascontiguousarray(O1[sl[r]]), ident=ident, w_o=A(diff_w_o[0]),
                                g_ffn=A(ffn_norm[1]), w_gu=A(ffn_w_gate_up[1]), w_d=A(ffn_w_down[1]),
                                g_fin=A(final_norm).reshape(1, D))
                           for r in range(NCORES)])
    out = np.concatenate([np.asarray(r5[r]["out"]) for r in range(NCORES)], axis=0)
    return out.reshape(1, S, D).astype(f32)


GROWS = 1024


def kb_gather(kb, gin, gout, reads, writes):
    nc = kb.nc
    E = kb.eng["pool"]
    kb._wait("pool", kb._deps(reads, writes), dma=True)
    kb.cc_cnt += 1
    nc.gpsimd.collective_compute("AllGather", ALU.bypass, replica_groups=[list(range(NCORES))],
                                 ins=[gin.opt()], outs=[gout.opt()]).then_inc(kb.sems[kb.cc_sem])
    kb._post((kb.cc_sem, kb.cc_cnt), reads, writes)


def rope_tables(kb, pos_ap, n, scr, posi, cos_out, sin_out, invf_sb, invf_t, out_tok, scr_tok):
    nc = kb.nc
    V = nc.vector
    A0, A1, A2 = scr
    A1i = A1.bitcast(I32)
    kb.dma("sp", posi, pos_ap.partition_broadcast(64), owner=scr_tok, writes=[scr_tok])
    sq = [
        lambda: V.tensor_copy(out=A0, in_=posi),
        lambda: V.tensor_scalar(out=A0, in0=A0, scalar1=invf_sb[:, 0:1], scalar2=None, op0=ALU.mult),
        lambda: V.tensor_scalar(out=A2, in0=A0, scalar1=1.0 / TWO_PI, scalar2=None, op0=ALU.mult),
        lambda: V.tensor_copy(out=A1i, in_=A2),
        lambda: V.tensor_copy(out=A2, in_=A1i),
        lambda: V.scalar_tensor_tensor(out=A0, in0=A2, scalar=-C1, in1=A0, op0=ALU.mult, op1=ALU.add),
        lambda: V.scalar_tensor_tensor(out=A0, in0=A2, scalar=-C2, in1=A0, op0=ALU.mult, op1=ALU.add),
        lambda: V.tensor_scalar(out=A2, in0=A0, scalar1=PI, scalar2=-TWO_PI, op0=ALU.is_gt, op1=ALU.mult),
        lambda: V.tensor_tensor(out=A0, in0=A0, in1=A2, op=ALU.add),
        lambda: V.tensor_scalar(out=A2, in0=A0, scalar1=-PI, scalar2=TWO_PI, op0=ALU.is_lt, op1=ALU.mult),
        lambda: V.tensor_tensor(out=A0, in0=A0, in1=A2, op=ALU.add),
        lambda: V.tensor_scalar(out=A1, in0=A0, scalar1=PI / 2, scalar2=None, op0=ALU.add),
        lambda: V.tensor_scalar(out=A2, in0=A1, scalar1=PI, scalar2=-TWO_PI, op0=ALU.is_gt, op1=ALU.mult),
        lambda: V.tensor_tensor(out=A1, in0=A1, in1=A2, op=ALU.add),
        lambda: V.tensor_scalar(out=A0, in0=A0, scalar1=PI_SAFE, scalar2=-PI_SAFE, op0=ALU.min, op1=ALU.max),
        lambda: V.tensor_scalar(out=A1, in0=A1, scalar1=PI_SAFE, scalar2=-PI_SAFE, op0=ALU.min, op1=ALU.max),
    ]
    for fn in sq:
        kb.op("dve", fn, reads=[invf_t], writes=[scr_tok])
    kb.op("act", lambda: nc.scalar.activation(out=sin_out, in_=A0, func=AF.Sin), reads=[scr_tok], writes=[out_tok])
    kb.op("act", lambda: nc.scalar.activation(out=cos_out, in_=A1, func=AF.Sin), reads=[scr_tok], pwrites=[out_tok])


def rope_apply(kb, raw_b, raw_tok, rot_b, rot_tok, cs, sn, tab_t, tmp, out_ap, out_tok, pw):
    nc = kb.nc
    (tmp1, tmp1_t), (tmp2, tmp2_t) = tmp
    kb.op("dve", lambda: nc.vector.tensor_tensor(out=tmp1[:], in0=raw_b[0:64, :], in1=cs, op=ALU.mult),
          reads=[raw_tok, tab_t], writes=[tmp1_t])
    kb.op("dve", lambda: nc.vector.tensor_tensor(out=tmp2[:], in0=rot_b[0:64, :], in1=sn, op=ALU.mult),
          reads=[rot_tok, tab_t], writes=[tmp2_t])
    wr = dict(pwrites=[out_tok]) if pw else dict(writes=[out_tok])
    kb.op("pool", lambda: nc.gpsimd.tensor_tensor(out=out_ap, in0=tmp1[:], in1=tmp2[:], op=ALU.add),
          reads=[tmp1_t, tmp2_t], **wr)


def build_fused(stop=None):
    kb = KB()
    nc = kb.nc
    kb.cc_sem = kb.new_sem("cc")
    kb.cc_cnt = 0
    pid = nc.partition_id()
    di = lambda n, sh, dt=F32: kb.dram(n, sh, dt, "ExternalInput")
    x = di("x", [TPC, D]); pos = di("pos", [1, TPC], I32); pos_all = di("pos_all", [1, S], I32)
    invf = di("invf", [64, 1]); ident_d = di("ident", [128, 128]); mask_d = di("mask", [128, 128])
    kcoef = di("kcoef", [4, 4]); qcoef = di("qcoef", [4, 4])
    g_attn0 = di("g_attn0", [D]); g_q = di("g_q", [512]); g_kvl = di("g_kvl", [256])
    w_dq = di("w_dq", [D, 512]); w_dkv = di("w_dkv", [D, 320])
    w_uq_h = di("w_uq_h", [512, 192]); w_ukv_h = di("w_ukv_h", [256, 256])
    w_o0 = di("w_o0", [D, D]); g_ffn0 = di("g_ffn0", [D]); w_gu0 = di("w_gu0", [D, 2 * DFF]); w_d0 = di("w_d0", [DFF, D])
    g_dkv = di("g_dkv", [D]); g_attn1 = di("g_attn1", [D])
    w_k_h = di("w_k_h", [D, 128]); w_v_h = di("w_v_h", [D, 128]); w_q_h = di("w_q_h", [D, 128])
    lam_d = di("lam", [4, 64]); subln_d = di("subln", [1, 128])
    w_o1 = di("w_o1", [D, D]); g_ffn1 = di("g_ffn1", [D]); w_gu1 = di("w_gu1", [D, 2 * DFF]); w_d1 = di("w_d1", [DFF, D])
    g_fin = di("g_fin", [1, D])
    out = kb.dram("out", [TPC, D], F32, "ExternalOutput")
    gin = kb.dram("gin", [GROWS, 512], F32, "Internal")
    gout = kb.dram("gout", [NCORES * GROWS, 512], F32, "Internal")
    gin_bf = gin.bitcast(BF16)
    gout_bf = gout.bitcast(BF16).rearrange("(r a) b -> r a b", r=NCORES)
    gin_O = gin_bf.rearrange("a (b d) -> (a b) d", d=128)
    gout_O = gout_bf.rearrange("r a (b d) -> r (a b) d", d=128)
    out_bf = out.bitcast(BF16)
    gin_t = kb.tok(); gout_t = kb.tok(); outd_t = kb.tok()
    lambda_init = 0.8 - 0.6 * float(np.exp(-0.3 * 1))

    kb.make_banks()
    setup_consts(kb, ident_d)
    setup_mask(kb, mask_d)
    invf_sb = kb.sb("invf_sb", [64, 1], F32); invf_t = kb.tok()
    kb.dma("sp", invf_sb[:], invf, owner=invf_t, writes=[invf_t])

    kb.push_scope()
    gA, gA_t = load_gain(kb, "a_gA", g_attn0, 8)
    stage = kb.sb("a_stage", [128, 6144], F32); stage_t = kb.tok()
    Wdq = kb.sb("a_Wdq", [128, 8, 512], BF16); Wdkv = kb.sb("a_Wdkv", [128, 8, 320], BF16)
    Wkrot = kb.sb("a_Wkrot", [128, 8, 64], BF16)
    Wdq_t = prep_weight(kb, w_dq, gA, gA_t, Wdq, 8, 512, stage, stage_t)
    Wdkv_t = prep_weight(kb, w_dkv, gA, gA_t, Wdkv, 8, 320, stage, stage_t)
    Wkrot_t = kb.tok()
    kb.op("dve", lambda: nc.vector.tensor_scalar(out=Wkrot[:, :, 0:32], in0=Wdkv[:, :, 288:320], scalar1=-1.0,
                                                 scalar2=None, op0=ALU.mult), reads=[Wdkv_t], writes=[Wkrot_t])
    kb.op("dve", lambda: nc.vector.tensor_copy(out=Wkrot[:, :, 32:64], in_=Wdkv[:, :, 256:288]),
          reads=[Wdkv_t], pwrites=[Wkrot_t])
    gQ, gQ_t = load_gain(kb, "a_gQ", g_q, 4)
    gK, gK_t = load_gain(kb, "a_gK", g_kvl, 2)
    cosT = kb.sb("a_cos", [64, TPC], F32); sinT = kb.sb("a_sin", [64, TPC], F32); tab_t = kb.tok()
    posi = kb.sb("a_posi", [64, TPC], I32)
    rope_tables(kb, pos, TPC, (stage[0:64, 0:TPC], stage[0:64, TPC:2 * TPC], stage[0:64, 2 * TPC:3 * TPC]),
                posi[:], cosT[:], sinT[:], invf_sb, invf_t, tab_t, stage_t)
    xt = [(kb.sb("a_xt%d" % i, [128, D], F32), kb.tok()) for i in range(2)]
    junk2 = [(kb.sb("a_junk%d" % i, [128, D], BF16), kb.tok()) for i in range(2)]
    xs2 = [(kb.sb("a_xs%d" % i, [128, D], BF16), kb.tok()) for i in range(2)]
    st42 = [(kb.sb("a_st4%d" % i, [128, 8], F32), kb.tok()) for i in range(2)]
    hnT = kb.sb("a_hnT", [128, 8, 512], BF16); hnT_t = kb.tok()
    cqs2 = [(kb.sb("a_cqs%d" % i, [128, 512], BF16), kb.tok()) for i in range(2)]
    ckvs2 = [(kb.sb("a_ckvs%d" % i, [128, 256], BF16), kb.tok()) for i in range(2)]
    cqT = [(kb.sb("a_cqT%d" % i, [128, 4, 512], BF16), kb.tok()) for i in range(2)]
    ckvT = [(kb.sb("a_ckvT%d" % i, [128, 2, 512], BF16), kb.tok()) for i in range(2)]
    kpe_st = [(kb.sb("a_kpe%d" % i, [64, 512], BF16), kb.tok()) for i in range(2)]
    tmp = ((kb.sb("a_tmp1", [64, 512], F32), kb.tok()), (kb.sb("a_tmp2", [64, 512], F32), kb.tok()))
    for c in range(NT // 4):
        cq_c, cq_ct = cqT[c % 2]; ck_c, ck_ct = ckvT[c % 2]; kp_c, kp_ct = kpe_st[c % 2]
        for i in range(4):
            tt = 4 * c + i
            xb, xbt = xt[tt % 2]
            junk, junk_t = junk2[tt % 2]; xs, xs_t = xs2[tt % 2]; st4, st_t = st42[tt % 2]
            cqs, cqs_t = cqs2[tt % 2]; ckvs, ckvs_t = ckvs2[tt % 2]
            kb.dma("sp", xb[:], x[tt * 128:(tt + 1) * 128, :], owner=xbt, writes=[xbt])
            kb.op("act", lambda: nc.scalar.activation(out=junk[:], in_=xb[:], func=AF.Square, accum_out=st4[:, 0:1]),
                  reads=[xbt], writes=[junk_t, st_t])
            rms_scale(kb, st4[:, 0:1], st4[:, 1:2], D, st_t, st_t)
            kb.op("dve", lambda: nc.vector.tensor_scalar(out=xs[:], in0=xb[:], scalar1=st4[:, 1:2], scalar2=None,
                                                         op0=ALU.mult), reads=[xbt, st_t], writes=[xs_t])
            transpose_to(kb, lambda j: xs[:, j * 128:(j + 1) * 128], xs_t, 8, 128,
                         lambda: hnT[:, :, i * 128:(i + 1) * 128], hnT_t, evac="act", pw=(i > 0))
            cq_b, cq_bt = kb.bank()
            for j in range(8):
                kb.op("pe", lambda j=j: nc.tensor.matmul(cq_b[:, 0:512], hnT[:, j, i * 128:(i + 1) * 128],
                                                         Wdq[:, j, :], start=(j == 0), stop=(j == 7)),
                      reads=[hnT_t, Wdq_t], writes=[cq_bt])
            ck_b, ck_bt = kb.bank()
            for j in range(8):
                kb.op("pe", lambda j=j: nc.tensor.matmul(ck_b[:, 0:256], hnT[:, j, i * 128:(i + 1) * 128],
                                                         Wdkv[:, j, 0:256], start=(j == 0), stop=(j == 7)),
                      reads=[hnT_t, Wdkv_t], writes=[ck_bt])
            kb.op("act", lambda: nc.scalar.activation(out=junk[:, 0:512], in_=cq_b[:, 0:512], func=AF.Square,
                                                      accum_out=st4[:, 2:3]), reads=[cq_bt], writes=[junk_t, st_t])
            rms_scale(kb, st4[:, 2:3], st4[:, 3:4], 512, st_t, st_t)
            kb.op("dve", lambda: nc.vector.tensor_scalar(out=cqs[:], in0=cq_b[:, 0:512], scalar1=st4[:, 3:4],
                                                         scalar2=None, op0=ALU.mult), reads=[cq_bt, st_t], writes=[cqs_t])
            kb.op("act", lambda: nc.scalar.activation(out=junk[:, 0:256], in_=ck_b[:, 0:256], func=AF.Square,
                                                      accum_out=st4[:, 4:5]), reads=[ck_bt], writes=[junk_t, st_t])
            rms_scale(kb, st4[:, 4:5], st4[:, 5:6], 256, st_t, st_t)
            kb.op("dve", lambda: nc.vector.tensor_scalar(out=ckvs[:], in0=ck_b[:, 0:256], scalar1=st4[:, 5:6],
                                                         scalar2=None, op0=ALU.mult), reads=[ck_bt, st_t], writes=[ckvs_t])
            transpose_to(kb, lambda j: cqs[:, j * 128:(j + 1) * 128], cqs_t, 4, 128,
                         lambda: cq_c[:, :, i * 128:(i + 1) * 128], cq_ct, evac="dve", pw=(i > 0))
            transpose_to(kb, lambda j: ckvs[:, j * 128:(j + 1) * 128], ckvs_t, 2, 128,
                         lambda: ck_c[:, :, i * 128:(i + 1) * 128], ck_ct, evac="dve", pw=(i > 0))
        kp_b, kp_bt = kb.bank()
        for j in range(8):
            kb.op("pe", lambda j=j: nc.tensor.matmul(kp_b[0:64, :], Wdkv[:, j, 256:320], hnT[:, j, :],
                                                     start=(j == 0), stop=(j == 7)), reads=[hnT_t, Wdkv_t], writes=[kp_bt])
        kr_b, kr_bt = kb.bank()
        for j in range(8):
            kb.op("pe", lambda j=j: nc.tensor.matmul(kr_b[0:64, :], Wkrot[:, j, :], hnT[:, j, :],
                                                     start=(j == 0), stop=(j == 7)), reads=[hnT_t, Wkrot_t], writes=[kr_bt])
        rope_apply(kb, kp_b, kp_bt, kr_b, kr_bt, cosT[:, c * 512:(c + 1) * 512], sinT[:, c * 512:(c + 1) * 512], tab_t,
                   tmp, kp_c[:], kp_ct, False)
        hf, cc = c // 2, c % 2
        dst, dtok = (gin_bf, gin_t) if hf == 0 else (out_bf, outd_t)
        csl = slice(cc * 512, (cc + 1) * 512)
        kb.dma("pool", dst[0:512, csl].rearrange("(j p) t -> p j t", p=128), cq_c[:], owner=cq_ct, reads=[cq_ct], pwrites=[dtok])
        kb.dma("pool", dst[512:768, csl].rearrange("(j p) t -> p j t", p=128), ck_c[:], owner=ck_ct, reads=[ck_ct], pwrites=[dtok])
        kb.dma("pool", dst[768:832, csl], kp_c[:], owner=kp_ct, reads=[kp_ct], pwrites=[dtok])
        if c == 1:
            kb_gather(kb, gin, gout, reads=[gin_t], writes=[gout_t])
    kb_barrier(kb)
    kb.pop_scope()

    if stop == "A":
        return kb
    kb.push_scope()
    QTa_sb = kb.sb("QTa_sb", [128, S], BF16); QTb_sb = kb.sb("QTb_sb", [128, S], BF16)
    KTa_sb = kb.sb("KTa_sb", [128, S], BF16); KTb_sb = kb.sb("KTb_sb", [128, S], BF16)
    V_sb = kb.sb("V_sb", [128, S // 128, 129], BF16)
    ctok = [kb.tok() for _ in range(S // 512)]
    vinit_t = kb.tok()
    kb.op("pool", lambda: nc.gpsimd.memset(V_sb[:], 1.0), writes=[vinit_t])
    kb.op("pool", lambda: nc.gpsimd.memset(QTb_sb[64:128, :], 0.0), pwrites=[vinit_t])
    kb.op("pool", lambda: nc.gpsimd.memset(KTb_sb[64:128, :], 0.0), pwrites=[vinit_t])
    kb.push_scope()
    gQ, gQ_t = load_gain(kb, "b_gQ", g_q, 4)
    gK, gK_t = load_gain(kb, "b_gK", g_kvl, 2)
    stage = kb.sb("b_stage", [128, 1024], F32); stage_t = kb.tok()
    Wuq = kb.sb("b_Wuq", [128, 4, 192], BF16); Wqrot = kb.sb("b_Wqrot", [128, 4, 64], BF16)
    Wukv = kb.sb("b_Wukv", [128, 2, 256], BF16)
    Wuq_t = prep_weight(kb, w_uq_h, gQ, gQ_t, Wuq, 4, 192, stage, stage_t, extra=(128 + 64) ** -0.5)
    Wukv_t = prep_weight(kb, w_ukv_h, gK, gK_t, Wukv, 2, 256, stage, stage_t)
    Wqrot_t = kb.tok()
    kb.op("dve", lambda: nc.vector.tensor_scalar(out=Wqrot[:, :, 0:32], in0=Wuq[:, :, 160:192], scalar1=-1.0,
                                                 scalar2=None, op0=ALU.mult), reads=[Wuq_t], writes=[Wqrot_t])
    kb.op("dve", lambda: nc.vector.tensor_copy(out=Wqrot[:, :, 32:64], in_=Wuq[:, :, 128:160]),
          reads=[Wuq_t], pwrites=[Wqrot_t])
    cqc = [(kb.sb("b_cq%d" % i, [128, 4, 512], BF16), kb.tok()) for i in range(2)]
    ckc = [(kb.sb("b_ck%d" % i, [128, 2, 512], BF16), kb.tok()) for i in range(2)]
    tabs = [(kb.sb("b_cos%d" % i, [64, 512], F32), kb.sb("b_sin%d" % i, [64, 512], F32), kb.tok()) for i in range(2)]
    scr = [(kb.sb("b_A0_%d" % i, [64, 512], F32), kb.sb("b_A1_%d" % i, [64, 512], F32),
            kb.sb("b_A2_%d" % i, [64, 512], F32), kb.sb("b_pi_%d" % i, [64, 512], I32), kb.tok()) for i in range(1)]
    tmp = ((kb.sb("b_tmp1", [64, 512], F32), kb.tok()), (kb.sb("b_tmp2", [64, 512], F32), kb.tok()))
    n = 0
    for hf in range(2):
        if hf == 1:
            kb.dma("pool", gin_bf[0:832, :], out_bf[0:832, 0:1024], owner=gin_t, reads=[outd_t], writes=[gin_t])
            kb_gather(kb, gin, gout, reads=[gin_t], writes=[gout_t])
        if stop == "G1":
            kb_barrier(kb)
            return kb
        if stop == "Ba" and hf == 1:
            kb_barrier(kb)
            return kb
        for r in range(NCORES):
            for cc in range(2):
                gc = r * 4 + hf * 2 + cc
                tsl = slice(gc * 512, (gc + 1) * 512)
                csl = slice(cc * 512, (cc + 1) * 512)
                cq_c, cq_ct = cqc[n % 2]; ck_c, ck_ct = ckc[n % 2]
                cs_, sn_, tb_t = tabs[n % 2]; A0, A1, A2, pi_, sc_t = scr[0]
                n += 1
                kb.dma("sp", cq_c[:], gout_bf[r, 0:512, csl].rearrange("(j p) t -> p j t", p=128), owner=cq_ct,
                       reads=[gout_t], writes=[cq_ct])
                kb.dma("sp", ck_c[:], gout_bf[r, 512:768, csl].rearrange("(j p) t -> p j t", p=128), owner=ck_ct,
                       reads=[gout_t], writes=[ck_ct])
                ct = ctok[gc]
                kb.dma("sp", KTb_sb[0:64, tsl], gout_bf[r, 768:832, csl], owner=ct, reads=[gout_t, vinit_t], writes=[ct])
                rope_tables(kb, pos_all[:, tsl], 512, (A0[:], A1[:], A2[:]), pi_[:], cs_[:], sn_[:], invf_sb, invf_t, tb_t, sc_t)
                qa_b, qa_bt = kb.bank()
                for j in range(4):
                    kb.op("pe", lambda j=j: nc.tensor.matmul(qa_b[:, :], Wuq[:, j, 0:128], cq_c[:, j, :],
                                                             start=(j == 0), stop=(j == 3)), reads=[cq_ct, Wuq_t], writes=[qa_bt])
                kb.op("act", lambda: nc.scalar.copy(out=QTa_sb[:, tsl], in_=qa_b[:, :]), reads=[qa_bt], pwrites=[ct])
                qp_b, qp_bt = kb.bank()
                for j in range(4):
                    kb.op("pe", lambda j=j: nc.tensor.matmul(qp_b[0:64, :], Wuq[:, j, 128:192], cq_c[:, j, :],
                                                             start=(j == 0), stop=(j == 3)), reads=[cq_ct, Wuq_t], writes=[qp_bt])
                qr_b, qr_bt = kb.bank()
                for j in range(4):
                    kb.op("pe", lambda j=j: nc.tensor.matmul(qr_b[0:64, :], Wqrot[:, j, :], cq_c[:, j, :],
                                                             start=(j == 0), stop=(j == 3)), reads=[cq_ct, Wqrot_t], writes=[qr_bt])
                rope_apply(kb, qp_b, qp_bt, qr_b, qr_bt, cs_[:], sn_[:], tb_t, tmp, QTb_sb[0:64, tsl], ct, True)
                ka_b, ka_bt = kb.bank()
                for j in range(2):
                    kb.op("pe", lambda j=j: nc.tensor.matmul(ka_b[:, :], Wukv[:, j, 0:128], ck_c[:, j, :],
                                                             start=(j == 0), stop=(j == 1)), reads=[ck_ct, Wukv_t], writes=[ka_bt])
                kb.op("dve", lambda: nc.vector.tensor_copy(out=KTa_sb[:, tsl], in_=ka_b[:, :]), reads=[ka_bt], pwrites=[ct])
                vb, vbt = kb.bank()
                for i in range(4):
                    for j in range(2):
                        kb.op("pe", lambda i=i, j=j: nc.tensor.matmul(vb[:, i * 128:(i + 1) * 128], ck_c[:, j, i * 128:(i + 1) * 128],
                                                                      Wukv[:, j, 128:256], start=(j == 0), stop=(j == 1)),
                              reads=[ck_ct, Wukv_t], writes=[vbt])
                kb.op("act", lambda: nc.scalar.copy(out=V_sb[:, gc * 4:gc * 4 + 4, 0:128],
                                                    in_=vb[:, :].rearrange("p (i d) -> p i d", i=4)),
                      reads=[vbt, vinit_t], pwrites=[ct])
    kb_barrier(kb)
    kb.pop_scope()

    if stop == "B":
        return kb
    qorder = [qc for qc in range(S // 512) if (qc % 4) < 2] + [qc for qc in range(S // 512) if (qc % 4) >= 2]

    def o_row0(qc):
        return (qc // 4) * 1024 + (qc % 2) * 512

    outO = out_bf[0:512, :].rearrange("a (b d) -> (a b) d", d=128)

    def attention_phase(passes, finalize_fn, ost, q_res, extra_q_reads=()):
        done = {0: 0, 1: 0}

        def epilogue(qc, pi, oset):
            hf = (qc % 4) // 2
            o_sb, o_tok = ost[done[0] % 2 if hf == 0 else done[1] % 2]
            fin = finalize_fn(qc, pi, oset, o_sb, o_tok)
            if fin:
                r0 = o_row0(qc)
                dstv, dtok = (gin_O, gin_t) if hf == 0 else (outO, outd_t)
                with nc.allow_non_contiguous_dma(reason="256B rows"):
                    kb.dma("pool", dstv[r0:r0 + 512, :].rearrange("(i p) d -> p i d", p=128), o_sb[:], owner=o_tok,
                           reads=[o_tok], pwrites=[dtok])
                done[hf] += 1
                if hf == 0 and done[0] == 16:
                    kb_gather(kb, gin, gout, reads=[gin_t], writes=[gout_t])

        attention_core(kb, passes, V_sb, lambda kbi: ctok[kbi // 4], epilogue,
                       q_resident=((lambda qc: ctok[qc]) if q_res else None), qc_order=qorder,
                       extra_q_reads=extra_q_reads, filler=(0 if q_res else 96))

    Ost = [(kb.sb("Ost%d" % i, [128, 4, 128], BF16), kb.tok()) for i in range(2)]
    rc = kb.sb("rc", [128, 8], F32); rc_t = kb.tok()
    gathers = []

    def fin_mla(qc, pi, oset, dst, dtok):
        for b, (ob, obtok) in enumerate(oset):
            sums = ob[:, 0:258].rearrange("p (i c) -> p i c", c=129)[:, :, 128:129]
            kb.op("dve", lambda: nc.vector.reciprocal(out=rc[:, 2 * b:2 * b + 2].rearrange("p (i c) -> p i c", c=1),
                                                     in_=sums), reads=[obtok], writes=[rc_t])
            for i in range(2):
                kb.op("dve", lambda i=i: nc.vector.tensor_scalar(
                    out=dst[:, 2 * b + i, :], in0=ob[:, i * 129:i * 129 + 128],
                    scalar1=rc[:, 2 * b + i:2 * b + i + 1], scalar2=None, op0=ALU.mult),
                    reads=[obtok, rc_t], **(dict(writes=[dtok]) if (b == 0 and i == 0) else dict(pwrites=[dtok])))
        return True

    attention_phase([dict(K=[(KTa_sb, 128), (KTb_sb, 128)], Q=[(QTa_sb, 128), (QTb_sb, 128)])], fin_mla, Ost, True)
    kb_barrier(kb)
    kb.pop_scope()

    if stop == "L2":
        return kb
    h_res, h_tok = make_h(kb)
    for tt in range(NT):
        kb.dma("sp", h_res[:, tt, :], x[tt * 128:(tt + 1) * 128, :], owner=h_tok[tt], writes=[h_tok[tt]])

    def o_gather_and_project(w_o):
        kb.push_scope()
        stage = kb.sb("pa_stage", [128, 8 * 1024], F32); stage_t = kb.tok()
        Wo = kb.sb("pa_Wo", [128, 8, 1024], BF16)
        Wo_t = prep_weight(kb, w_o, None, None, Wo, 8, 1024, stage, stage_t)
        oall = [(kb.sb("pa_oall%d" % k, [128, D], BF16), kb.tok()) for k in range(8)]
        oT = kb.sb("pa_oT", [128, 8, 128], BF16); oT_t = kb.tok()

        def load_tiles():
            for k in range(8):
                ob, obt = oall[k]
                with nc.allow_non_contiguous_dma(reason="256B head rows"):
                    kb.dma("sp", ob[:].rearrange("p (h d) -> p h d", h=H),
                           gout_O[:, bass.ds(pid * 1024 + k * 128, 128), :].rearrange("h p d -> p h d"),
                           owner=obt, reads=[gout_t], writes=[obt])

        def project(hf):
            for k in range(8):
                tt = hf * 8 + k
                ob, obt = oall[k]
                transpose_to(kb, lambda j: ob[:, j * 128:(j + 1) * 128], obt, 8, 128, lambda: oT[:, :, :], oT_t, evac="act")
                for half in range(2):
                    pb, pbt = kb.bank()
                    for j in range(8):
                        kb.op("pe", lambda j=j: nc.tensor.matmul(pb[:, :], oT[:, j, :], Wo[:, j, half * 512:(half + 1) * 512],
                                                                 start=(j == 0), stop=(j == 7)),
                              reads=[oT_t, Wo_t], writes=[pbt])
                    hs = h_res[:, tt, half * 512:(half + 1) * 512]
                    kb.op("dve", lambda: nc.vector.tensor_tensor(out=hs, in0=pb[:, :], in1=hs, op=ALU.add),
                          reads=[pbt], pwrites=[h_tok[tt]])

        load_tiles()
        kb.dma("pool", gin_bf, out_bf[0:512, :].rearrange("a (b c) -> (a b) c", c=1024), owner=gin_t,
               reads=[outd_t], writes=[gin_t])
        kb_gather(kb, gin, gout, reads=[gin_t], writes=[gout_t])
        project(0)
        load_tiles()
        project(1)
        kb_barrier(kb)
        kb.pop_scope()

    o_gather_and_project(w_o0)
    if stop == "C1":
        kb_barrier(kb)
        return kb
    phase_ffn(kb, h_res, h_tok, g_ffn0, w_gu0, w_d0)
    if stop == "C":
        return kb

    kb.push_scope()
    work = [(kb.sb("d_junk%d" % i, [128, D], BF16), kb.tok(), kb.sb("d_xs%d" % i, [128, D], BF16), kb.tok(),
             kb.sb("d_st%d" % i, [128, 4], F32), kb.tok()) for i in range(2)]
    hT = kb.sb("d_hT", [128, 8, TPC], BF16); hT_t = kb.tok()
    norm_transpose(kb, h_res, h_tok, hT, hT_t, work, tiles=range(0, 8))
    kb.dma("pool", gin_bf.rearrange("(j p) t -> p j t", p=128), hT[:, :, 0:1024], owner=gin_t, reads=[hT_t], writes=[gin_t])
    kb_gather(kb, gin, gout, reads=[gin_t], writes=[gout_t])
    norm_transpose(kb, h_res, h_tok, hT, hT_t, work, tiles=range(8, 16))
    stg = out_bf[0:512, :].rearrange("a (b c) -> (a b) c", c=1024)
    kb.dma("pool", stg.rearrange("(j p) t -> p j t", p=128), hT[:, :, 1024:2048], owner=outd_t, reads=[hT_t], writes=[outd_t])
    kb_barrier(kb)
    kb.pop_scope()

    kb.push_scope()
    K1_sb = kb.sb("K1_sb", [68, S], BF16); K2_sb = kb.sb("K2_sb", [68, S], BF16)
    V_sb = kb.sb("V4_sb", [128, S // 128, 129], BF16)
    ctok = [kb.tok() for _ in range(S // 512)]
    vinit_t = kb.tok()
    kb.op("pool", lambda: nc.gpsimd.memset(V_sb[:], 1.0), writes=[vinit_t])
    Q1d = out_bf[512:1056, :].rearrange("(q a) b -> q (a b)", q=68)
    Q2d = out_bf[1056:1600, :].rearrange("(q a) b -> q (a b)", q=68)
    qd_t = kb.tok()
    kb.push_scope()
    gDK, gDK_t = load_gain(kb, "d_gk", g_dkv, 8)
    gA1, gA1_t = load_gain(kb, "d_ga", g_attn1, 8)
    stage = kb.sb("d_stage", [128, 1024], F32); stage_t = kb.tok()
    Wk = kb.sb("d_Wk", [128, 8, 128], BF16); Wv = kb.sb("d_Wv", [128, 8, 128], BF16); Wq = kb.sb("d_Wq", [128, 8, 128], BF16)
    Wk_t = prep_weight(kb, w_k_h, gDK, gDK_t, Wk, 8, 128, stage, stage_t)
    Wv_t = prep_weight(kb, w_v_h, gDK, gDK_t, Wv, 8, 128, stage, stage_t)
    Wq_t = prep_weight(kb, w_q_h, gA1, gA1_t, Wq, 8, 128, stage, stage_t, extra=64 ** -0.5)
    cf = kb.sb("d_cf", [68, 8], F32); cf_t = kb.tok()
    kb.dma("sp", cf[64:68, 0:4], kcoef, owner=cf_t, writes=[cf_t])
    kb.dma("sp", cf[64:68, 4:8], qcoef, owner=cf_t, pwrites=[cf_t])
    xc = [(kb.sb("d_xc%d" % i, [128, 8, 512], BF16), kb.tok()) for i in range(2)]
    qst = [(kb.sb("d_qst%d" % i, [68, 2, 512], BF16), kb.tok()) for i in range(2)]
    pI = kb.sb("d_pI", [68, 512], I32); aI = kb.sb("d_aI", [68, 512], I32)
    aF = kb.sb("d_aF", [68, 512], F32); bF = kb.sb("d_bF", [68, 512], F32); t1 = kb.sb("d_t1", [68, 512], F32)
    aug_t = kb.tok()
    V_ = nc.vector
    R = slice(64, 68)
    n = 0
    for hf in range(2):
        if hf == 1:
            kb.dma("pool", gin_bf, stg, owner=gin_t, reads=[outd_t], writes=[gin_t])
            kb_gather(kb, gin, gout, reads=[gin_t], writes=[gout_t])
        for r in range(NCORES):
            for cc in range(2):
                gc = r * 4 + hf * 2 + cc
                tsl = slice(gc * 512, (gc + 1) * 512)
                csl = slice(cc * 512, (cc + 1) * 512)
                x_c, x_ct = xc[n % 2]; q_s, q_st = qst[n % 2]
                n += 1
                ct = ctok[gc]
                kb.dma("sp", x_c[:], gout_bf[r, :, csl].rearrange("(j p) t -> p j t", p=128), owner=x_ct,
                       reads=[gout_t], writes=[x_ct])
                kb.dma("sp", pI[R, :], pos_all[:, tsl].partition_broadcast(4), owner=aug_t, writes=[aug_t])
                seq = [
                    lambda: V_.tensor_scalar(out=aI[R, :], in0=pI[R, :], scalar1=7, scalar2=None, op0=ALU.arith_shift_right),
                    lambda: V_.tensor_copy(out=aF[R, :], in_=aI[R, :]),
                    lambda: V_.tensor_scalar(out=aI[R, :], in0=pI[R, :], scalar1=127, scalar2=None, op0=ALU.bitwise_and),
                    lambda: V_.tensor_copy(out=bF[R, :], in_=aI[R, :]),
                    lambda: V_.tensor_scalar(out=t1[R, :], in0=aF[R, :], scalar1=cf[R, 0:1], scalar2=cf[R, 2:3], op0=ALU.mult, op1=ALU.add),
                ]
                for fn in seq:
                    kb.op("dve", fn, reads=[cf_t], writes=[aug_t])
                kb.op("dve", lambda: V_.scalar_tensor_tensor(out=K1_sb[R, tsl], in0=bF[R, :], scalar=cf[R, 1:2], in1=t1[R, :],
                                                             op0=ALU.mult, op1=ALU.add), reads=[aug_t, cf_t], writes=[ct])
                kb.op("dve", lambda: V_.tensor_copy(out=K2_sb[R, tsl], in_=K1_sb[R, tsl]), reads=[ct], pwrites=[ct])
                kb.op("dve", lambda: V_.tensor_scalar(out=t1[R, :], in0=aF[R, :], scalar1=cf[R, 4:5], scalar2=cf[R, 6:7],
                                                      op0=ALU.mult, op1=ALU.add), reads=[aug_t, cf_t], writes=[aug_t])
                kb.op("dve", lambda: V_.scalar_tensor_tensor(out=q_s[R, 0, :], in0=bF[R, :], scalar=cf[R, 5:6], in1=t1[R, :],
                                                             op0=ALU.mult, op1=ALU.add), reads=[aug_t, cf_t], writes=[q_st])
                kb.op("dve", lambda: V_.tensor_copy(out=q_s[R, 1, :], in_=q_s[R, 0, :]), reads=[q_st], pwrites=[q_st])
                for i in range(2):
                    kb_, kbt_ = kb.bank()
                    for j in range(8):
                        kb.op("pe", lambda j=j: nc.tensor.matmul(kb_[0:64, :], Wk[:, j, i * 64:(i + 1) * 64], x_c[:, j, :],
                                                                 start=(j == 0), stop=(j == 7)), reads=[x_ct, Wk_t], writes=[kbt_])
                    Ki = K1_sb if i == 0 else K2_sb
                    kb.op("act", lambda: nc.scalar.copy(out=Ki[0:64, tsl], in_=kb_[0:64, :]), reads=[kbt_], pwrites=[ct])
                    qb_, qbt_ = kb.bank()
                    for j in range(8):
                        kb.op("pe", lambda j=j: nc.tensor.matmul(qb_[0:64, :], Wq[:, j, i * 64:(i + 1) * 64], x_c[:, j, :],
                                                                 start=(j == 0), stop=(j == 7)), reads=[x_ct, Wq_t], writes=[qbt_])
                    kb.op("dve", lambda: nc.vector.tensor_copy(out=q_s[0:64, i, :], in_=qb_[0:64, :]), reads=[qbt_], pwrites=[q_st])
                vb, vbt = kb.bank()
                for i in range(4):
                    for j in range(8):
                        kb.op("pe", lambda i=i, j=j: nc.tensor.matmul(vb[:, i * 128:(i + 1) * 128], x_c[:, j, i * 128:(i + 1) * 128],
                                                                      Wv[:, j, :], start=(j == 0), stop=(j == 7)),
                              reads=[x_ct, Wv_t], writes=[vbt])
                kb.op("act", lambda: nc.scalar.copy(out=V_sb[:, gc * 4:gc * 4 + 4, 0:128],
                                                    in_=vb[:, :].rearrange("p (i d) -> p i d", i=4)),
                      reads=[vbt, vinit_t], pwrites=[ct])
                kb.dma("pool", Q1d[:, tsl], q_s[:, 0, :], owner=q_st, reads=[q_st], pwrites=[qd_t])
                kb.dma("pool", Q2d[:, tsl], q_s[:, 1, :], owner=q_st, reads=[q_st], pwrites=[qd_t])
    kb_barrier(kb)
    kb.pop_scope()

    lam_sb = kb.sb("lam_sb", [128, 4, 64], F32); lam_t = kb.tok()
    kb.dma("sp", lam_sb[:].rearrange("p a d -> p (a d)"), lam_d.rearrange("a d -> (a d)").partition_broadcast(128),
           owner=lam_t, writes=[lam_t])
    lj = kb.sb("lam_j", [128, 64], F32); lv = kb.sb("lam_v", [128, 4], F32); lv_t = kb.tok()
    kb.op("dve", lambda: nc.vector.tensor_tensor(out=lj[:], in0=lam_sb[:, 0, :], in1=lam_sb[:, 1, :], op=ALU.mult),
          reads=[lam_t], writes=[lv_t])
    kb.op("dve", lambda: nc.vector.tensor_reduce(out=lv[:, 0:1], in_=lj[:], axis=mybir.AxisListType.X, op=ALU.add),
          reads=[lv_t], writes=[lv_t])
    kb.op("dve", lambda: nc.vector.tensor_tensor(out=lj[:], in0=lam_sb[:, 2, :], in1=lam_sb[:, 3, :], op=ALU.mult),
          reads=[lam_t, lv_t], writes=[lv_t])
    kb.op("dve", lambda: nc.vector.tensor_reduce(out=lv[:, 1:2], in_=lj[:], axis=mybir.AxisListType.X, op=ALU.add),
          reads=[lv_t], writes=[lv_t])
    kb.op("act", lambda: nc.scalar.activation(out=lv[:, 0:2], in_=lv[:, 0:2], func=AF.Exp), reads=[lv_t], writes=[lv_t])
    kb.op("dve", lambda: nc.vector.scalar_tensor_tensor(out=lv[:, 2:3], in0=lv[:, 1:2], scalar=-float(lambda_init),
                                                        in1=lv[:, 0:1], op0=ALU.add, op1=ALU.subtract),
          reads=[lv_t], writes=[lv_t])
    sub_sb = kb.sb("sub_sb", [128, 128], F32); sub_t = kb.tok()
    kb.dma("sp", sub_sb[:], subln_d.partition_broadcast(128), owner=sub_t, writes=[sub_t])
    kb.op("dve", lambda: nc.vector.tensor_scalar(out=sub_sb[:], in0=sub_sb[:], scalar1=float(1.0 - lambda_init),
                                                 scalar2=None, op0=ALU.mult), reads=[sub_t], writes=[sub_t])
    o1 = kb.sb("o1n", [128, 4, 128], F32); o1_t = kb.tok()
    att = kb.sb("attd", [128, 4, 128], F32); att_t = kb.tok()
    junk4 = kb.sb("junk4", [128, 128], F32); junk4_t = kb.tok()
    Ost4 = [(kb.sb("Ost4_%d" % i, [128, 4, 128], BF16), kb.tok()) for i in range(2)]
    rc4 = kb.sb("rc4", [128, 8], F32); rc4_t = kb.tok()
    ss = kb.sb("ss4", [128, 8], F32); ss_t = kb.tok()

    def fin_diff(qc, pi, oset, dst, dtok):
        for b, (ob, obtok) in enumerate(oset):
            sums = ob[:, 0:258].rearrange("p (i c) -> p i c", c=129)[:, :, 128:129]
            kb.op("dve", lambda: nc.vector.reciprocal(out=rc4[:, 2 * b:2 * b + 2].rearrange("p (i c) -> p i c", c=1),
                                                     in_=sums), reads=[obtok], writes=[rc4_t])
            for i in range(2):
                qi = 2 * b + i
                if pi == 0:
                    kb.op("dve", lambda: nc.vector.tensor_scalar(
                        out=o1[:, qi, :], in0=ob[:, i * 129:i * 129 + 128], scalar1=rc4[:, qi:qi + 1], scalar2=None,
                        op0=ALU.mult), reads=[obtok, rc4_t], **(dict(writes=[o1_t]) if qi == 0 else dict(pwrites=[o1_t])))
                else:
                    kb.op("dve", lambda: nc.vector.tensor_scalar(
                        out=att[:, qi, :], in0=ob[:, i * 129:i * 129 + 128], scalar1=rc4[:, qi:qi + 1],
                        scalar2=lv[:, 2:3], op0=ALU.mult, op1=ALU.mult),
                        reads=[obtok, rc4_t, lv_t], **(dict(writes=[att_t]) if qi == 0 else dict(pwrites=[att_t])))
                    kb.op("pool", lambda: nc.gpsimd.tensor_tensor(out=att[:, qi, :], in0=att[:, qi, :], in1=o1[:, qi, :],
                                                                  op=ALU.add), reads=[o1_t], pwrites=[att_t])
        if pi == 0:
            return False
        for qi in range(4):
            kb.op("act", lambda: nc.scalar.activation(out=junk4[:], in_=att[:, qi, :], func=AF.Square,
                                                      accum_out=ss[:, qi:qi + 1]),
                  reads=[att_t], writes=[junk4_t], pwrites=[ss_t])
        kb.op("act", lambda: nc.scalar.activation(out=ss[:, 4:8], in_=ss[:, 0:4], func=AF.Ln, scale=1.0 / 128,
                                                  bias=kb.eps_ap), reads=[ss_t, kb.eps_tok], writes=[ss_t])
        kb.op("act", lambda: nc.scalar.activation(out=ss[:, 4:8], in_=ss[:, 4:8], func=AF.Exp, scale=-0.5),
              reads=[ss_t], writes=[ss_t])
        for qi in range(4):
            kb.op("dve", lambda: nc.vector.scalar_tensor_tensor(
                out=dst[:, qi, :], in0=att[:, qi, :], scalar=ss[:, 4 + qi:5 + qi], in1=sub_sb[:],
                op0=ALU.mult, op1=ALU.mult), reads=[att_t, ss_t, sub_t],
                **(dict(writes=[dtok]) if qi == 0 else dict(pwrites=[dtok])))
        return True

    attention_phase([dict(K=[(K1_sb, 68)], Q=[(Q1d, 68)]), dict(K=[(K2_sb, 68)], Q=[(Q2d, 68)])], fin_diff, Ost4, False,
                    extra_q_reads=[qd_t])
    kb_barrier(kb)
    kb.pop_scope()

    o_gather_and_project(w_o1)
    phase_ffn(kb, h_res, h_tok, g_ffn1, w_gu1, w_d1)
    gb = kb.sb("f_g", [128, D], F32); gb_t = kb.tok()
    kb.dma("sp", gb[:], g_fin.partition_broadcast(128), owner=gb_t, writes=[gb_t])
    junk = kb.sb("f_junk", [128, D], BF16); junk_t = kb.tok()
    st4 = kb.sb("f_st", [128, 4], F32); st_t = kb.tok()
    obf = [(kb.sb("f_o%d" % i, [128, D], F32), kb.tok()) for i in range(2)]
    for tt in range(NT):
        kb.op("act", lambda: nc.scalar.activation(out=junk[:], in_=h_res[:, tt, :], func=AF.Square,
                                                  accum_out=st4[:, 0:1]), reads=[h_tok[tt]], writes=[junk_t, st_t])
        rms_scale(kb, st4[:, 0:1], st4[:, 1:2], D, st_t, st_t)
        o_sb, o_t = obf[tt % 2]
        kb.op("dve", lambda: nc.vector.scalar_tensor_tensor(out=o_sb[:], in0=h_res[:, tt, :], scalar=st4[:, 1:2],
                                                            in1=gb[:], op0=ALU.mult, op1=ALU.mult),
              reads=[h_tok[tt], st_t, gb_t], writes=[o_t])
        kb.dma("sp", out[tt * 128:(tt + 1) * 128, :], o_sb[:], owner=o_t, reads=[o_t], pwrites=[outd_t, qd_t])
    kb.finish([t for _, t in obf])
    return kb


def kernel(x, positions, attn_norm, ffn_norm, final_norm,
           mla_w_dq, mla_q_norm, mla_w_uq, mla_w_dkv, mla_kv_norm, mla_w_ukv, mla_w_o,
           diff_kv_norm, diff_w_k, diff_w_v, diff_w_q,
           diff_lambda_q1, diff_lambda_k1, diff_lambda_q2, diff_lambda_k2,
           diff_subln, diff_w_o, ffn_w_gate_up, ffn_w_down):
    f32 = np.float32
    x = np.asarray(x, f32)
    positions = np.asarray(positions, np.int32)
    A = lambda a: np.ascontiguousarray(np.asarray(a, f32))
    inv = (10000.0 ** (-np.arange(32, dtype=f32) * 2.0 / 64)).astype(f32)
    invf = np.concatenate([inv, inv]).reshape(64, 1).astype(f32)
    ident = np.eye(128, dtype=f32)
    mask = np.where(np.arange(128)[:, None] > np.arange(128)[None, :], NEG, 0.0).astype(f32)
    kcoef = np.array([[128, 0, 0, 0], [0, 1, 0, 0], [0, 0, 1, 0], [0, 0, 1, 0]], f32)
    qbase = np.array([[0, 0, 1, 0], [0, 0, 1, 0], [-128, 0, 0, 0], [0, -1, 0, 0]], f32)
    lam = np.ascontiguousarray(np.stack([A(diff_lambda_q1[0]), A(diff_lambda_k1[0]),
                                         A(diff_lambda_q2[0]), A(diff_lambda_k2[0])]))
    shared = dict(pos_all=np.ascontiguousarray(positions), invf=invf, ident=ident, mask=mask, kcoef=kcoef,
                  g_attn0=A(attn_norm[0]), g_q=A(mla_q_norm[0]), g_kvl=A(mla_kv_norm[0]),
                  w_dq=A(mla_w_dq[0]), w_dkv=A(mla_w_dkv[0]), w_o0=A(mla_w_o[0]), g_ffn0=A(ffn_norm[0]),
                  w_gu0=A(ffn_w_gate_up[0]), w_d0=A(ffn_w_down[0]), g_dkv=A(diff_kv_norm), g_attn1=A(attn_norm[1]),
                  lam=lam, subln=A(diff_subln[0]).reshape(1, 128), w_o1=A(diff_w_o[0]), g_ffn1=A(ffn_norm[1]),
                  w_gu1=A(ffn_w_gate_up[1]), w_d1=A(ffn_w_down[1]), g_fin=A(final_norm).reshape(1, D))
    w_uq = np.asarray(mla_w_uq[0], f32); w_ukv = np.asarray(mla_w_ukv[0], f32)
    w_k = np.asarray(diff_w_k, f32); w_v = np.asarray(diff_w_v, f32); w_q = np.asarray(diff_w_q[0], f32)
    maps = []
    for c in range(NCORES):
        sl = slice(c * TPC, (c + 1) * TPC)
        slope = 2.0 ** (-8.0 * (c + 1) / H)
        m = dict(shared)
        m.update(x=np.ascontiguousarray(x[0, sl]), pos=np.ascontiguousarray(positions[:, sl]),
                 qcoef=np.ascontiguousarray(qbase * f32(slope)),
                 w_uq_h=np.ascontiguousarray(w_uq[:, c * 192:(c + 1) * 192]),
                 w_ukv_h=np.ascontiguousarray(w_ukv[:, c * 256:(c + 1) * 256]),
                 w_k_h=np.ascontiguousarray(w_k[:, c * 128:(c + 1) * 128]),
                 w_v_h=np.ascontiguousarray(w_v[:, c * 128:(c + 1) * 128]),
                 w_q_h=np.ascontiguousarray(w_q[:, c * 128:(c + 1) * 128]))
        maps.append(m)
    res = _run(build_fused(), maps)
    out = np.concatenate([np.asarray(res[r]["out"]) for r in range(NCORES)], axis=0)
    return out.reshape(1, S, D).astype(f32)
```
